# Optimizing a Trainium2 kernel written in Bass

```python
import math
import jax, jax.numpy as jnp
from jax import lax
import numpy as np

D_MODEL = 1024
BATCH = 8
SEQ = 2048
DEPTH = 2
DEC_BATCH = 16
DEC_SEQ = 32
PAST_LEN = 4096

CHUNK = 64
Q_BLOCK = 128
MLA_HEADS = 8
MLA_NOPE = 64
MLA_ROPE = 32
MLA_V = 64
MLA_Q_RANK = 256
MLA_KV_RANK = 128
ROPE_BASE = 10000.0
SB_HEADS = 8
SB_DIM = 64
DIFF_HEADS = 4
DIFF_DIM = 64
BRANCH_W = 512
N_BRANCH = 3
D_FF = 2816
CONV_W = 3
EPS = 1e-6

MLA_SCALE = (MLA_NOPE + MLA_ROPE) ** -0.5
SB_SCALE = SB_DIM ** -0.5
DIFF_SCALE = DIFF_DIM ** -0.5
SPLITS = (MLA_Q_RANK, MLA_KV_RANK, MLA_ROPE,
          SB_HEADS * SB_DIM, SB_HEADS * SB_DIM, SB_HEADS * SB_DIM,
          2 * DIFF_HEADS * DIFF_DIM, 2 * DIFF_HEADS * DIFF_DIM, 2 * DIFF_HEADS * DIFF_DIM,
          N_BRANCH * D_MODEL)
N_IN = sum(SPLITS)
SPLIT_IDX = tuple(int(i) for i in np.cumsum(SPLITS)[:-1])

kernel_name = 'hybrid_mla_stickbreak_diffattn_convffn_stream_step'


def rmsnorm(x, g):
    xf = x.astype(jnp.float32)
    y = xf * lax.rsqrt(jnp.mean(xf * xf, axis=-1, keepdims=True) + EPS)
    return (y * g.astype(jnp.float32)).astype(x.dtype)


def rope(x, pos):
    half = x.shape[-1] // 2
    inv = ROPE_BASE ** (-jnp.arange(half, dtype=jnp.float32) / half)
    ang = pos.astype(jnp.float32)[:, None] * inv[None, :]
    cos = jnp.cos(ang)[:, None, :]
    sin = jnp.sin(ang)[:, None, :]
    xf = x.astype(jnp.float32)
    x1, x2 = xf[..., :half], xf[..., half:]
    return jnp.concatenate([x1 * cos - x2 * sin, x1 * sin + x2 * cos], axis=-1).astype(x.dtype)


def chunk_mask(q_pos, k_pos):
    return (k_pos[None, :] // CHUNK) <= (q_pos[:, None] // CHUNK)


def alibi_slopes():
    return 2.0 ** (-8.0 * jnp.arange(1, DIFF_HEADS + 1, dtype=jnp.float32) / DIFF_HEADS)


def sweep(fn, q_args, q_pos):
    S = q_pos.shape[0]
    if S <= Q_BLOCK or S % Q_BLOCK != 0:
        return fn(*q_args, q_pos)
    nb = S // Q_BLOCK
    blocks = tuple(jnp.moveaxis(a.reshape((a.shape[0], nb, Q_BLOCK) + a.shape[2:]), 1, 0) for a in q_args)
    pos_blocks = q_pos.reshape(nb, Q_BLOCK)
    out = lax.map(lambda t: fn(*t[0], t[1]), (blocks, pos_blocks))
    o = jnp.moveaxis(out, 0, 1)
    return o.reshape((o.shape[0], nb * Q_BLOCK) + o.shape[3:])


def mla_block(qn, qr, q_pos, kn, kr, v, k_pos):
    s = (jnp.einsum('bqhd,bkhd->bhqk', qn, kn) + jnp.einsum('bqhr,bkr->bhqk', qr, kr)).astype(jnp.float32) * MLA_SCALE
    s = jnp.where(chunk_mask(q_pos, k_pos), s, -jnp.inf)
    p = jax.nn.softmax(s, axis=-1).astype(v.dtype)
    return jnp.einsum('bhqk,bkhd->bqhd', p, v)


def sb_block(q, q_pos, k, v, k_pos):
    z = jnp.einsum('bqhd,bkhd->bhqk', q, k).astype(jnp.float32) * SB_SCALE
    valid = k_pos[None, :] < q_pos[:, None]
    log_1mb = jnp.where(valid, jax.nn.log_sigmoid(-z), 0.0)
    suffix = lax.cumsum(log_1mb, axis=3, reverse=True) - log_1mb
    a = jnp.where(valid, jnp.exp(jax.nn.log_sigmoid(z) + suffix), 0.0).astype(v.dtype)
    return jnp.einsum('bhqk,bkhd->bqhd', a, v)


def diff_block(q, q_pos, k, v, k_pos, lam):
    s = jnp.einsum('bqhmd,bkhmd->bhmqk', q, k).astype(jnp.float32) * DIFF_SCALE
    dist = jnp.abs(q_pos[:, None] - k_pos[None, :]).astype(jnp.float32)
    s = s - alibi_slopes()[:, None, None, None] * dist
    s = jnp.where(chunk_mask(q_pos, k_pos), s, -jnp.inf)
    p = jax.nn.softmax(s, axis=-1)
    pd = (p[:, :, 0] - lam * p[:, :, 1]).astype(v.dtype)
    return jnp.einsum('bhqk,bkhe->bqhe', pd, v)


def mixer_layer(x, q_pos, past, w, layer_idx):
    B, S, _ = x.shape
    h = rmsnorm(x, w['mix_norm_g'])
    z = h @ w['w_in']
    z_cq, z_ckv, z_kr, z_sq, z_sk, z_sv, z_dq, z_dk, z_dv, z_g = jnp.split(z, SPLIT_IDX, axis=-1)
    cq = rmsnorm(z_cq, w['mla_q_norm_g'])
    q = (cq @ w['mla_w_uq']).reshape(B, S, MLA_HEADS, MLA_NOPE + MLA_ROPE)
    qn = rmsnorm(q[..., :MLA_NOPE], w['mla_qn_g'])
    qr = rope(rmsnorm(q[..., MLA_NOPE:], w['mla_qr_g']), q_pos)
    ckv_new = rmsnorm(z_ckv, w['mla_kv_norm_g'])
    kr_new = rope(rmsnorm(z_kr, w['mla_kr_g'])[:, :, None, :], q_pos)[:, :, 0, :]
    sbq = z_sq.reshape(B, S, SB_HEADS, SB_DIM)
    sbk_new = z_sk.reshape(B, S, SB_HEADS, SB_DIM)
    sbv_new = z_sv.reshape(B, S, SB_HEADS, SB_DIM)
    dq = rmsnorm(z_dq.reshape(B, S, DIFF_HEADS, 2, DIFF_DIM), w['diff_qn_g'])
    dk_new = rmsnorm(z_dk.reshape(B, S, DIFF_HEADS, 2, DIFF_DIM), w['diff_kn_g'])
    dv_new = z_dv.reshape(B, S, DIFF_HEADS, 2 * DIFF_DIM)
    new_rows = (ckv_new, kr_new, sbk_new, sbv_new, dk_new, dv_new)
    if past is None:
        full = new_rows
    else:
        full = tuple(jnp.concatenate([c, n], axis=1) for c, n in zip(past, new_rows))
    ckv, kr, sbk, sbv, dk, dv = full
    K = ckv.shape[1]
    k_pos = jnp.arange(K, dtype=jnp.int32)
    kn = rmsnorm((ckv @ w['mla_w_uk']).reshape(B, K, MLA_HEADS, MLA_NOPE), w['mla_kn_g'])
    mv = (ckv @ w['mla_w_uv']).reshape(B, K, MLA_HEADS, MLA_V)
    o_mla = sweep(lambda a, b, qp: mla_block(a, b, qp, kn, kr, mv, k_pos), (qn, qr), q_pos)
    o_sb = sweep(lambda a, qp: sb_block(a, qp, sbk, sbv, k_pos), (sbq,), q_pos)
    lam_init = 0.8 - 0.6 * math.exp(-0.3 * layer_idx)
    lv = w['diff_lambda'].astype(jnp.float32)
    lam = jnp.exp(jnp.sum(lv[0] * lv[1])) - jnp.exp(jnp.sum(lv[2] * lv[3])) + lam_init
    o_diff = sweep(lambda a, qp: diff_block(a, qp, dk, dv, k_pos, lam), (dq,), q_pos)
    o_diff = rmsnorm(o_diff, w['diff_subln_g']) * (1.0 - lam_init)
    g_mla, g_sb, g_diff = jnp.split(jax.nn.sigmoid(z_g), N_BRANCH, axis=-1)
    merged = (g_mla * (o_mla.reshape(B, S, BRANCH_W) @ w['w_br_mla'])
              + g_sb * (o_sb.reshape(B, S, BRANCH_W) @ w['w_br_sb'])
              + g_diff * (o_diff.reshape(B, S, BRANCH_W) @ w['w_br_diff']))
    return merged @ w['w_out'], new_rows


def ffn_layer(x, conv_past, w):
    h = rmsnorm(x, w['ffn_norm_g'])
    a, u = jnp.split(h @ w['ffn_w_up'], 2, axis=-1)
    B, S, _ = a.shape
    pad = jnp.zeros((B, CONV_W - 1, D_FF), a.dtype) if conv_past is None else conv_past
    ap = jnp.concatenate([pad, a], axis=1)
    cw = w['ffn_conv_w']
    c = ap[:, 0:S] * cw[0]
    for k in range(1, CONV_W):
        c = c + ap[:, k:k + S] * cw[k]
    c = c + w['ffn_conv_b']
    out = (jax.nn.silu(c) * u) @ w['ffn_w_down']
    return out, ap[:, -(CONV_W - 1):]


def setup_inputs(seed: int = 0) -> dict:
    key = jax.random.key(seed)
    ks = iter(jax.random.split(key, 48))

    def nrm(shape, scale=1.0):
        return jax.random.normal(next(ks), shape, jnp.float32) * scale

    def gain(shape):
        return 1.0 + 0.02 * nrm(shape)

    L = DEPTH
    return {
        'x_prompt': nrm((BATCH, SEQ, D_MODEL)),
        'x_sample': nrm((DEC_BATCH, DEC_SEQ, D_MODEL)),
        'cache_mla_ckv': nrm((L, DEC_BATCH, PAST_LEN, MLA_KV_RANK)),
        'cache_mla_krope': nrm((L, DEC_BATCH, PAST_LEN, MLA_ROPE)),
        'cache_sb_k': nrm((L, DEC_BATCH, PAST_LEN, SB_HEADS, SB_DIM)),
        'cache_sb_v': nrm((L, DEC_BATCH, PAST_LEN, SB_HEADS, SB_DIM)),
        'cache_diff_k': nrm((L, DEC_BATCH, PAST_LEN, DIFF_HEADS, 2, DIFF_DIM)),
        'cache_diff_v': nrm((L, DEC_BATCH, PAST_LEN, DIFF_HEADS, 2 * DIFF_DIM)),
        'state_ffn_conv': nrm((L, DEC_BATCH, CONV_W - 1, D_FF)),
        'mix_norm_g': gain((L, D_MODEL)),
        'w_in': nrm((L, D_MODEL, N_IN), D_MODEL ** -0.5),
        'mla_q_norm_g': gain((L, MLA_Q_RANK)),
        'mla_w_uq': nrm((L, MLA_Q_RANK, MLA_HEADS * (MLA_NOPE + MLA_ROPE)), MLA_Q_RANK ** -0.5),
        'mla_kv_norm_g': gain((L, MLA_KV_RANK)),
        'mla_w_uk': nrm((L, MLA_KV_RANK, MLA_HEADS * MLA_NOPE), MLA_KV_RANK ** -0.5),
        'mla_w_uv': nrm((L, MLA_KV_RANK, MLA_HEADS * MLA_V), MLA_KV_RANK ** -0.5),
        'mla_qn_g': gain((L, MLA_NOPE)),
        'mla_kn_g': gain((L, MLA_NOPE)),
        'mla_qr_g': gain((L, MLA_ROPE)),
        'mla_kr_g': gain((L, MLA_ROPE)),
        'diff_qn_g': gain((L, DIFF_DIM)),
        'diff_kn_g': gain((L, DIFF_DIM)),
        'diff_lambda': nrm((L, 4, DIFF_DIM), 0.1),
        'diff_subln_g': gain((L, 2 * DIFF_DIM)),
        'w_br_mla': nrm((L, BRANCH_W, D_MODEL), BRANCH_W ** -0.5),
        'w_br_sb': nrm((L, BRANCH_W, D_MODEL), BRANCH_W ** -0.5),
        'w_br_diff': nrm((L, BRANCH_W, D_MODEL), BRANCH_W ** -0.5),
        'w_out': nrm((L, D_MODEL, D_MODEL), D_MODEL ** -0.5),
        'ffn_norm_g': gain((L, D_MODEL)),
        'ffn_w_up': nrm((L, D_MODEL, 2 * D_FF), D_MODEL ** -0.5),
        'ffn_conv_w': nrm((L, CONV_W, D_FF), CONV_W ** -0.5),
        'ffn_conv_b': nrm((L, D_FF), 0.01),
        'ffn_w_down': nrm((L, D_FF, D_MODEL), D_FF ** -0.5),
    }


def reference(x_prompt, x_sample, cache_mla_ckv, cache_mla_krope, cache_sb_k, cache_sb_v,
              cache_diff_k, cache_diff_v, state_ffn_conv,
              mix_norm_g, w_in, mla_q_norm_g, mla_w_uq, mla_kv_norm_g, mla_w_uk, mla_w_uv,
              mla_qn_g, mla_kn_g, mla_qr_g, mla_kr_g, diff_qn_g, diff_kn_g, diff_lambda,
              diff_subln_g, w_br_mla, w_br_sb, w_br_diff, w_out, ffn_norm_g, ffn_w_up,
              ffn_conv_w, ffn_conv_b, ffn_w_down):
    weights = dict(mix_norm_g=mix_norm_g, w_in=w_in, mla_q_norm_g=mla_q_norm_g, mla_w_uq=mla_w_uq,
                   mla_kv_norm_g=mla_kv_norm_g, mla_w_uk=mla_w_uk, mla_w_uv=mla_w_uv,
                   mla_qn_g=mla_qn_g, mla_kn_g=mla_kn_g, mla_qr_g=mla_qr_g, mla_kr_g=mla_kr_g,
                   diff_qn_g=diff_qn_g, diff_kn_g=diff_kn_g, diff_lambda=diff_lambda,
                   diff_subln_g=diff_subln_g, w_br_mla=w_br_mla, w_br_sb=w_br_sb,
                   w_br_diff=w_br_diff, w_out=w_out, ffn_norm_g=ffn_norm_g, ffn_w_up=ffn_w_up,
                   ffn_conv_w=ffn_conv_w, ffn_conv_b=ffn_conv_b, ffn_w_down=ffn_w_down)
    past_len = cache_mla_ckv.shape[2]
    pos_p = jnp.arange(x_prompt.shape[1], dtype=jnp.int32)
    pos_s = past_len + jnp.arange(x_sample.shape[1], dtype=jnp.int32)
    xp, xs = x_prompt, x_sample
    rows_p, rows_s, conv_p, conv_s = [], [], [], []
    for l in range(DEPTH):
        w = {name: arr[l] for name, arr in weights.items()}
        yp, rp = mixer_layer(xp, pos_p, None, w, l)
        xp = xp + yp
        fp, cp = ffn_layer(xp, None, w)
        xp = xp + fp
        past = (cache_mla_ckv[l], cache_mla_krope[l], cache_sb_k[l], cache_sb_v[l],
                cache_diff_k[l], cache_diff_v[l])
        ys, rs = mixer_layer(xs, pos_s, past, w, l)
        xs = xs + ys
        fs, cs = ffn_layer(xs, state_ffn_conv[l], w)
        xs = xs + fs
        rows_p.append(rp)
        rows_s.append(rs)
        conv_p.append(cp)
        conv_s.append(cs)
    p_ckv = jnp.stack([r[0] for r in rows_p])
    p_krope = jnp.stack([r[1] for r in rows_p])
    p_sbk = jnp.stack([r[2] for r in rows_p])
    p_sbv = jnp.stack([r[3] for r in rows_p])
    p_dk = jnp.stack([r[4] for r in rows_p])
    p_dv = jnp.stack([r[5] for r in rows_p])
    p_conv = jnp.stack(conv_p)
    s_ckv = jnp.stack([r[0] for r in rows_s])
    s_krope = jnp.stack([r[1] for r in rows_s])
    s_sbk = jnp.stack([r[2] for r in rows_s])
    s_sbv = jnp.stack([r[3] for r in rows_s])
    s_dk = jnp.stack([r[4] for r in rows_s])
    s_dv = jnp.stack([r[5] for r in rows_s])
    s_conv = jnp.stack(conv_s)
    return (xp, xs, p_ckv, p_krope, p_sbk, p_sbv, p_dk, p_dv, p_conv,
            s_ckv, s_krope, s_sbk, s_sbv, s_dk, s_dv, s_conv)
```

```python
import math
from contextlib import ExitStack
import numpy as np
import concourse.bass as bass
import concourse.mybir as mybir
from concourse.bass_utils import run_bass_kernel_spmd

F32 = mybir.dt.float32
BF16 = mybir.dt.bfloat16
AF = mybir.ActivationFunctionType
ALU = mybir.AluOpType
AX = mybir.AxisListType

L = 2
D = 1024
SP_ = 2048
NS = 2
SS = 32
PAST = 4096
NTOK = SP_ + NS * SS
NT = 18
DFF = 2816
NIN = 6560
EPS = 1e-6
MLA_SCALE = 96 ** -0.5
SB_SCALE = 64 ** -0.5
DIFF_SCALE = 64 ** -0.5
OFF = dict(cq=0, ckv=256, kr=384, sq=416, sk=928, sv=1440, dq=1952, dk=2464, dv=2976, g=3488)
ENG = ['pe', 'act', 'dve', 'pool', 'sp']
NDS = 20
_DEV = {'stop': None, 'off': False, 'maxops': None, 'nops': 0, 'log': None}


class _Stop(Exception):
    pass


def _ck(name):
    if _DEV['log'] is not None and not _DEV['off']:
        _DEV['log'].append((name, _DEV['nops']))
    if _DEV['stop'] == name:
        _DEV['off'] = True


class Buf:
    __slots__ = ('w', 'r', 'excl')

    def __init__(self):
        self.w = None
        self.r = {}
        self.excl = False


class TT:
    def __init__(self, h, n=1):
        self.h = h
        self.b = [Buf() for _ in range(n)]

    def __getitem__(self, k):
        return self.h[k]


class Sync:
    def __init__(self, nc, es):
        self.nc = nc
        self.e = dict(pe=nc.tensor, act=nc.scalar, dve=nc.vector, pool=nc.gpsimd, sp=nc.sync)
        self.sem = {k: es.enter_context(nc.semaphore("sem_" + k)) for k in ENG}
        self.cnt = {k: 0 for k in ENG}
        self.dsem = {q: [es.enter_context(nc.semaphore("ds_%s%d" % (q, i))) for i in range(NDS)] for q in ('sp', 'pool')}
        self.dcnt = {q: [0] * NDS for q in ('sp', 'pool')}
        self.dnext = {'sp': 0, 'pool': 0}
        self.known = {k: {} for k in ENG}
        self.pend = {k: False for k in ENG}

    def _semof(self, k):
        return self.sem[k] if isinstance(k, str) else self.dsem[k[0]][k[1]]

    def _wait(self, eng, toks):
        kn = self.known[eng]
        best = {}
        for (k, v) in toks:
            if best.get(k, 0) < v:
                best[k] = v
        for k, v in best.items():
            if kn.get(k, 0) < v:
                self.e[eng].wait_ge(self._semof(k), v)
                kn[k] = v

    def _deps(self, eng, reads, writes):
        toks = set()
        for b in reads:
            if b.w is not None:
                toks.add(b.w)
            if b.excl:
                for kv in b.r.items():
                    if kv[0] != eng:
                        toks.add(kv)
        for b in writes:
            if b.w is not None:
                toks.add(b.w)
            for kv in b.r.items():
                toks.add(kv)
        if eng == 'pe':
            toks = {t for t in toks if t[0] != 'pe'}
        return toks

    def op(self, eng, fn, reads=(), writes=(), inc=True):
        if _DEV['off']:
            return
        _DEV['nops'] += 1
        if _DEV['maxops'] is not None and _DEV['nops'] > _DEV['maxops'] and not self.pend[eng]:
            _DEV['off'] = True
            return
        self._wait(eng, self._deps(eng, reads, writes))
        ins = fn(self.e[eng])
        c = self.cnt[eng] + 1
        if inc:
            ins.then_inc(self.sem[eng], 1)
            self.cnt[eng] = c
            self.pend[eng] = False
        else:
            self.pend[eng] = True
        for b in reads:
            b.r[eng] = c
        for b in writes:
            b.w = (eng, c)
            b.r = {}

    def dma(self, q, out, in_, reads=(), writes=()):
        if _DEV['off']:
            return
        toks = self._deps(q, reads, writes)
        i = self.dnext[q]
        self.dnext[q] = (i + 1) % NDS
        key = (q, i)
        if self.dcnt[q][i] > 0:
            toks.add((key, self.dcnt[q][i]))
        self._wait(q, toks)
        self.e[q].dma_start(out=out, in_=in_).then_inc(self.dsem[q][i], 16)
        self.dcnt[q][i] += 16
        v = self.dcnt[q][i]
        for b in reads:
            b.r[key] = v
        for b in writes:
            b.w = (key, v)
            b.r = {}

    def all_tokens(self):
        toks = {(k, self.cnt[k]) for k in ENG if self.cnt[k] > 0}
        for q in ('sp', 'pool'):
            for i, c in enumerate(self.dcnt[q]):
                if c > 0:
                    toks.add(((q, i), c))
        return toks

    def barrier(self):
        if _DEV['off']:
            return
        for k in ENG:
            assert not self.pend[k]
        toks = self.all_tokens()
        for eng in ENG:
            self._wait(eng, {t for t in toks if t[0] != eng})


def build_program():
    nc = bass.Bass("TRN2", target_bir_lowering=False)

    def din(name, shape):
        return nc.dram_tensor(name, list(shape), F32, kind="ExternalInput").ap()

    def dout(name, shape):
        return nc.dram_tensor(name, list(shape), F32, kind="ExternalOutput").ap()

    xin = din("xin", [NTOK, D])
    c_ckv = din("c_ckv", [L, NS, PAST, 128])
    c_kr = din("c_kr", [L, NS, PAST, 32])
    c_sbk = din("c_sbk", [L, NS, PAST, 512])
    c_sbv = din("c_sbv", [L, NS, PAST, 512])
    c_dk = din("c_dk", [L, NS, PAST, 512])
    c_dv = din("c_dv", [L, NS, PAST, 512])
    c_conv = din("c_conv", [L, NS, 128, 22 * 2])
    W = {}
    for name, shape in [("mix_norm_g", [L, D]), ("w_in", [L, D, NIN]), ("mla_q_norm_g", [L, 256]),
                        ("mla_w_uq", [L, 256, 768]), ("mla_kv_norm_g", [L, 128]), ("mla_w_uk", [L, 128, 512]),
                        ("mla_w_uv", [L, 128, 512]), ("mla_qn_g", [L, 64]), ("mla_kn_g", [L, 64]),
                        ("mla_qr_g", [L, 32]), ("mla_kr_g", [L, 32]), ("diff_qn_g", [L, 64]),
                        ("diff_kn_g", [L, 64]), ("diff_lambda", [L, 256]), ("diff_subln_g", [L, 128]),
                        ("w_br_mla", [L, 512, D]), ("w_br_sb", [L, 512, D]), ("w_br_diff", [L, 512, D]),
                        ("w_out", [L, D, D]), ("ffn_norm_g", [L, D]), ("ffn_w_up", [L, D, 2 * DFF]),
                        ("ffn_conv_w", [L, 128, 22 * 3]), ("ffn_conv_b", [L, 128, 22]), ("ffn_w_down", [L, DFF, D])]:
        W[name] = din(name, shape)
    k_cs = din("k_cs", [NTOK, 32])
    k_msb = din("k_msb", [128, 128])
    k_mch = din("k_mch", [128, 128])
    k_mdf = din("k_mdf", [128, 4 * 128])
    k_tri = din("k_tri", [128, 128])
    k_bdp = din("k_bdp", [128, 4 * 17])
    k_bds = din("k_bds", [128, 4 * 33])

    y = dout("y", [NTOK, D])
    o_ckv = dout("o_ckv", [L, NTOK, 128])
    o_kr = dout("o_kr", [L, NTOK, 32])
    o_sbk = dout("o_sbk", [L, NTOK, 512])
    o_sbv = dout("o_sbv", [L, NTOK, 512])
    o_dk = dout("o_dk", [L, NTOK, 512])
    o_dv = dout("o_dv", [L, NTOK, 512])
    o_conv = dout("o_conv", [L, 3, 128, 22 * 2])

    with ExitStack() as es:
        E = es.enter_context
        S = Sync(nc, es)
        cnt = [0]

        def sb(es_, shape, dt, n=1):
            cnt[0] += 1
            return TT(es_.enter_context(nc.sbuf_tensor("t%d" % cnt[0], list(shape), dt)), n)

        X = sb(es, [128, NT, D], F32, NT)
        HT = sb(es, [128, 8, NTOK], BF16, NT)
        OT = sb(es, [128, 4, NTOK], BF16, 5)
        ident = sb(es, [128, 128], BF16)
        ones = sb(es, [128, 128], BF16)
        zeros = sb(es, [128, 128], BF16)
        onesf = sb(es, [128, 128], F32)
        tri = sb(es, [128, 128], BF16)
        msb = sb(es, [128, 128], F32)
        mch = sb(es, [128, 128], F32)
        mdf = sb(es, [128, 4, 128], F32)
        bdp = sb(es, [128, 4, 17], F32)
        bds = sb(es, [128, 4, 33], F32)
        cs = sb(es, [128, NT, 32], F32)
        gbig = sb(es, [128, D], F32)
        gsm = sb(es, [128, 1024], F32)
        gcol = sb(es, [128, 8], F32)
        PS = [TT(E(nc.psum_tensor("ps%d" % i, [128, 512], F32))) for i in range(8)]
        for p_ in PS:
            p_.b[0].excl = True
        rot = {'mm': 0, 'tp': 0}

        def mmbank():
            rot['mm'] = (rot['mm'] + 1) % len(mm_list[0])
            return PS[mm_list[0][rot['mm']]]

        def tpbank():
            rot['tp'] ^= 1
            return PS[2 + rot['tp']]

        def tsl(tt):
            if tt < 16:
                return 128, slice(tt * 128, tt * 128 + 128)
            return 32, slice(2048 + 32 * (tt - 16), 2048 + 32 * (tt - 16) + 32)

        GROUPS = [(g * 512, 512, [4 * g + i for i in range(4)]) for g in range(4)] + [(2048, 64, [16, 17])]
        mm_list = [[0, 1]]

        S.op('pool', lambda e: e.memset(ident[:], 0.0), [], ident.b)
        S.op('pool', lambda e: e.affine_select(out=ident[:], in_=ident[:], pattern=[[-1, 128]], compare_op=ALU.not_equal,
                                               fill=1.0, base=0, channel_multiplier=1), ident.b, ident.b)
        S.op('pool', lambda e: e.memset(ones[:], 1.0), [], ones.b)
        S.op('pool', lambda e: e.memset(zeros[:], 0.0), [], zeros.b)
        S.op('pool', lambda e: e.memset(onesf[:], 1.0), [], onesf.b)
        S.dma('pool', tri[:], k_tri, [], tri.b)
        S.dma('sp', msb[:], k_msb, [], msb.b)
        S.dma('sp', mch[:], k_mch, [], mch.b)
        S.dma('sp', mdf[:].rearrange("p a b -> p (a b)"), k_mdf, [], mdf.b)
        S.dma('sp', bdp[:].rearrange("p a b -> p (a b)"), k_bdp, [], bdp.b)
        S.dma('sp', bds[:].rearrange("p a b -> p (a b)"), k_bds, [], bds.b)
        S.dma('sp', cs[:, 0:16, :], k_cs[0:2048, :].rearrange("(t p) n -> p t n", p=128), [], cs.b)
        S.dma('sp', cs[0:32, 16, :], k_cs[2048:2080, :], [], cs.b)
        S.dma('sp', cs[0:32, 17, :], k_cs[2080:2112, :], [], cs.b)
        zrhs = sb(es, [128, 512], BF16)
        S.op('pool', lambda e: e.memset(zrhs[:], 0.0), [], zrhs.b)

        GS = dict(q_norm=(0, 256), kv_norm=(256, 128), kr=(384, 32), qn=(416, 64), kn=(480, 64), qr=(544, 32),
                  dqn=(576, 64), dkn=(640, 64), lam=(704, 256))

        def gs(name, rows=128):
            o, n = GS[name]
            return gsm[0:rows, o:o + n]

        def rstd_from_ss(ss_t, rows, G, d):
            S.op('act', lambda e: e.activation(out=ss_t[0:rows, 0:G], in_=ss_t[0:rows, 0:G], func=AF.Ln, scale=1.0 / d, bias=EPS),
                 ss_t.b, ss_t.b)
            S.op('act', lambda e: e.activation(out=ss_t[0:rows, 0:G], in_=ss_t[0:rows, 0:G], func=AF.Exp, scale=-0.5),
                 ss_t.b, ss_t.b)

        def gnorm(ws, src, srcb, rows, G, d, gain, out, outb, post_scale=1.0):
            sq, ss, tmp = ws['sq'], ws['ss'], ws['tmp']
            sqv = sq[0:rows, 0:G * d].rearrange("p (g d) -> p g d", d=d)
            S.op('act', lambda e: e.activation(out=sqv, in_=src, func=AF.Square), srcb, sq.b)
            S.op('dve', lambda e: e.tensor_reduce(out=ss[0:rows, 0:G], in_=sqv, axis=AX.X, op=ALU.add), sq.b, ss.b)
            rstd_from_ss(ss, rows, G, d)
            tv = tmp[0:rows, 0:G * d].rearrange("p (g d) -> p g d", d=d)
            S.op('dve', lambda e: e.tensor_tensor(out=tv, in0=src, in1=ss[0:rows, 0:G].unsqueeze(2).to_broadcast([rows, G, d]),
                                                  op=ALU.mult), list(srcb) + ss.b, tmp.b)
            gb = gain.unsqueeze(1).to_broadcast([rows, G, d])
            if post_scale == 1.0:
                S.op('dve', lambda e: e.tensor_tensor(out=out, in0=tv, in1=gb, op=ALU.mult), tmp.b + gsm.b, outb)
            else:
                S.op('dve', lambda e: e.scalar_tensor_tensor(out=out, in0=tv, scalar=float(post_scale), in1=gb, op0=ALU.mult,
                                                             op1=ALU.mult), tmp.b + gsm.b, outb)

        def rope(ws, src, srcb, rows, H, tt, out, outb):
            t1, t2 = ws['r1'], ws['r2']
            cosb = cs[0:rows, tt, 0:16].unsqueeze(1).to_broadcast([rows, H, 16])
            sinb = cs[0:rows, tt, 16:32].unsqueeze(1).to_broadcast([rows, H, 16])
            a1 = t1[0:rows, 0:H * 16].rearrange("p (h d) -> p h d", d=16)
            a2 = t2[0:rows, 0:H * 16].rearrange("p (h d) -> p h d", d=16)
            x1 = src[:, :, 0:16]
            x2 = src[:, :, 16:32]
            S.op('dve', lambda e: e.tensor_tensor(out=a1, in0=x1, in1=cosb, op=ALU.mult), list(srcb) + cs.b, t1.b)
            S.op('dve', lambda e: e.tensor_tensor(out=a2, in0=x2, in1=sinb, op=ALU.mult), list(srcb) + cs.b, t2.b)
            S.op('dve', lambda e: e.tensor_tensor(out=out[:, :, 0:16], in0=a1, in1=a2, op=ALU.subtract), t1.b + t2.b, outb)
            S.op('dve', lambda e: e.tensor_tensor(out=a1, in0=x1, in1=sinb, op=ALU.mult), list(srcb) + cs.b, t1.b)
            S.op('dve', lambda e: e.tensor_tensor(out=a2, in0=x2, in1=cosb, op=ALU.mult), list(srcb) + cs.b, t2.b)
            S.op('dve', lambda e: e.tensor_tensor(out=out[:, :, 16:32], in0=a1, in1=a2, op=ALU.add), t1.b + t2.b, outb)

        def transpose_to(src, srcb, rows, ncols, dst, dstb, eng='act'):
            pb = tpbank()
            pv = pb.h[:, :].bitcast(BF16)
            S.op('pe', lambda e: e.transpose(out=pv[0:ncols, 0:rows], in_=src, identity=ident[0:rows, 0:rows]), list(srcb) + ident.b, pb.b)
            if eng == 'act':
                S.op('act', lambda e: e.copy(out=dst, in_=pv[0:ncols, 0:rows]), pb.b, dstb)
            else:
                S.op('dve', lambda e: e.tensor_copy(out=dst, in_=pv[0:ncols, 0:rows]), pb.b, dstb)

        def load_w(dst, src_ap):
            S.dma('pool', dst.h[:], src_ap, [], dst.b)

        def norm_phase(l, gname, first):
            with ExitStack() as ps:
                sq = sb(ps, [128, D], F32)
                ssr = [sb(ps, [128, 1], F32) for _ in range(2)]
                hb = [sb(ps, [128, D], BF16) for _ in range(2)]
                S.dma('sp', gbig[:], W[gname][l].partition_broadcast(128), [], gbig.b)
                for tt in range(NT):
                    rows, sl = tsl(tt)
                    if first:
                        S.dma('sp', X[0:rows, tt, :], xin[sl, :], [], [X.b[tt]])
                    ss = ssr[tt % 2]
                    h = hb[tt % 2]
                    S.op('act', lambda e: e.activation(out=sq[0:rows, :], in_=X[0:rows, tt, :], func=AF.Square,
                                                       accum_out=ss[0:rows, :]), [X.b[tt]], sq.b + ss.b)
                    rstd_from_ss(ss, rows, 1, D)
                    S.op('dve', lambda e: e.scalar_tensor_tensor(out=h[0:rows, :], in0=X[0:rows, tt, :], scalar=ss[0:rows, 0:1],
                                                                 in1=gbig[0:rows, :], op0=ALU.mult, op1=ALU.mult),
                         [X.b[tt]] + ss.b + gbig.b, h.b)
                    pb = tpbank()
                    pv = pb.h[:, :].bitcast(BF16).rearrange("p (k n) -> p k n", n=128)
                    for k in range(8):
                        S.op('pe', lambda e: e.transpose(out=pv[:, k, 0:rows], in_=h[0:rows, k * 128:(k + 1) * 128],
                                                         identity=ident[0:rows, 0:rows]), h.b + ident.b, pb.b, inc=(k == 7))
                    S.op('act' if tt % 2 else 'dve',
                         (lambda e: e.copy(out=HT[:, :, sl], in_=pv[:, :, 0:rows])) if tt % 2 else
                         (lambda e: e.tensor_copy(out=HT[:, :, sl], in_=pv[:, :, 0:rows])), pb.b, [HT.b[tt]])
                S.barrier()

        def _layers():
          for l in range(L):
            lam_init = 0.8 - 0.6 * math.exp(-0.3 * l)
            for nm, wn in [('q_norm', 'mla_q_norm_g'), ('kv_norm', 'mla_kv_norm_g'), ('kr', 'mla_kr_g'), ('qn', 'mla_qn_g'),
                           ('kn', 'mla_kn_g'), ('qr', 'mla_qr_g'), ('dqn', 'diff_qn_g'), ('dkn', 'diff_kn_g'),
                           ('lam', 'diff_lambda')]:
                S.dma('sp', gs(nm), W[wn][l].partition_broadcast(128), [], gsm.b)
            S.dma('sp', gcol[:, 0:1], W['diff_subln_g'][l].rearrange("(p o) -> p o", o=1), [], gcol.b)
            with ExitStack() as ps:
                lt = sb(ps, [128, 128], F32)
                l2 = sb(ps, [128, 2], F32)
                lv = gs('lam')
                S.op('dve', lambda e: e.tensor_tensor(out=lt[:, 0:64], in0=lv[:, 0:64], in1=lv[:, 64:128], op=ALU.mult), gsm.b, lt.b)
                S.op('dve', lambda e: e.tensor_tensor(out=lt[:, 64:128], in0=lv[:, 128:192], in1=lv[:, 192:256], op=ALU.mult), gsm.b, lt.b)
                S.op('dve', lambda e: e.tensor_reduce(out=l2[:, 0:2], in_=lt[:, :].rearrange("p (a b) -> p a b", b=64), axis=AX.X,
                                                      op=ALU.add), lt.b, l2.b)
                S.op('act', lambda e: e.activation(out=l2[:, 0:2], in_=l2[:, 0:2], func=AF.Exp), l2.b, l2.b)
                S.op('dve', lambda e: e.tensor_tensor(out=gcol[:, 1:2], in0=l2[:, 0:1], in1=l2[:, 1:2], op=ALU.subtract), l2.b, gcol.b)
                S.op('dve', lambda e: e.tensor_scalar(out=gcol[:, 1:2], in0=gcol[:, 1:2], scalar1=float(lam_init), scalar2=None,
                                                      op0=ALU.add), gcol.b, gcol.b)
                S.op('dve', lambda e: e.tensor_scalar(out=gcol[:, 2:3], in0=gcol[:, 1:2], scalar1=-1.0, scalar2=None,
                                                      op0=ALU.mult), gcol.b, gcol.b)
                S.op('dve', lambda e: e.tensor_scalar(out=gcol[:, 0:1], in0=gcol[:, 0:1], scalar1=float(1.0 - lam_init), scalar2=None,
                                                      op0=ALU.mult), gcol.b, gcol.b)
                S.barrier()

            norm_phase(l, 'mix_norm_g', l == 0)
            _ck('norm%d' % l)

            for mixer in ('mla', 'sb', 'diff'):
                with ExitStack() as ms:
                    if mixer == 'mla':
                        CQT = sb(ms, [128, 2, NTOK], BF16, NT)
                        CKVT = sb(ms, [128, NTOK], BF16, NT)
                        KRT = sb(ms, [128, NT, 32], BF16, NT)
                        nunits, hpu = 4, 2
                    elif mixer == 'sb':
                        nunits, hpu = 2, 4
                    else:
                        nunits, hpu = 2, 2
                    for u in range(nunits):
                        with ExitStack() as us:
                            ws = dict(sq=sb(us, [128, 512], F32), ss=sb(us, [128, 8], F32), tmp=sb(us, [128, 512], F32),
                                      r1=sb(us, [128, 128], F32), r2=sb(us, [128, 128], F32))
                            stage = [sb(us, [128, 256], F32) for _ in range(3)]
                            stg_i = [0]

                            def nstage():
                                stg_i[0] = (stg_i[0] + 1) % 3
                                return stage[stg_i[0]]
                            tokb = [sb(us, [128, 256], BF16) for _ in range(3)]
                            tok_i = [0]

                            def ntok():
                                tok_i[0] = (tok_i[0] + 1) % 3
                                return tokb[tok_i[0]]
                            PT = [sb(us, [128, 512], BF16) for _ in range(2)]
                            pt_i = [0]

                            def npt():
                                pt_i[0] ^= 1
                                return PT[pt_i[0]]
                            rden = sb(us, [128, 512], F32)
                            if mixer == 'mla':
                                KT = sb(us, [96, 2, NTOK], BF16, NT)
                                V = sb(us, [128, NT, 128], BF16, NT)
                                QT = sb(us, [96, 2, 512], BF16)
                                if u == 0:
                                    w1 = sb(us, [128, 8, 416], BF16)
                                    load_w(w1, W['w_in'][l, :, 0:416].rearrange("(k p) n -> p k n", p=128))
                                wuq = sb(us, [128, 2, 192], BF16)
                                load_w(wuq, W['mla_w_uq'][l, :, u * 192:(u + 1) * 192].rearrange("(k p) n -> p k n", p=128))
                                wuk = sb(us, [128, 128], BF16)
                                load_w(wuk, W['mla_w_uk'][l, :, u * 128:(u + 1) * 128])
                                wuv = sb(us, [128, 128], BF16)
                                load_w(wuv, W['mla_w_uv'][l, :, u * 128:(u + 1) * 128])
                                kcat = [sb(us, [128, 2, 96], BF16) for _ in range(2)]
                                qcat = [sb(us, [128, 2, 96], BF16) for _ in range(2)]
                                pck = sb(us, [128, 4, 128], BF16)
                                pkr = sb(us, [128, 4, 32], BF16)
                                pckT = sb(us, [128, 512], BF16)
                                KTp = sb(us, [96, 2, 512], BF16)
                                Vp = sb(us, [128, 4, 128], BF16)
                            else:
                                c0q = OFF['sq' if mixer == 'sb' else 'dq'] + u * 256
                                c0k = OFF['sk' if mixer == 'sb' else 'dk'] + u * 256
                                c0v = OFF['sv' if mixer == 'sb' else 'dv'] + u * 256
                                wq = sb(us, [128, 8, 256], BF16)
                                wk = sb(us, [128, 8, 256], BF16)
                                wv = sb(us, [128, 8, 256], BF16)
                                load_w(wq, W['w_in'][l, :, c0q:c0q + 256].rearrange("(k p) n -> p k n", p=128))
                                load_w(wk, W['w_in'][l, :, c0k:c0k + 256].rearrange("(k p) n -> p k n", p=128))
                                load_w(wv, W['w_in'][l, :, c0v:c0v + 256].rearrange("(k p) n -> p k n", p=128))
                                KT = sb(us, [128, 2, NTOK], BF16, NT)
                                V = sb(us, [128, NT, 256], BF16, NT)
                                QT = sb(us, [128, 2, 512], BF16)
                                pk = sb(us, [128, 4, 256], BF16)
                                KTp = sb(us, [128, 2, 512], BF16)
                                Vp = sb(us, [128, 4, 256], BF16)
                                if mixer == 'sb':
                                    R = sb(us, [128, 2, 512], F32)
                                    e1 = [sb(us, [128, 512], F32) for _ in range(2)]
                                    spb = [sb(us, [128, 512], BF16) for _ in range(2)]
                                    cumr = sb(us, [128, 512], F32)
                                else:
                                    t0 = sb(us, [128, 512], F32)
                                    t1 = sb(us, [128, 512], F32)
                                    rden1 = sb(us, [128, 512], F32)
                                    osq = sb(us, [128, 512], F32)
                            o_k = o_sbk if mixer == 'sb' else o_dk
                            o_v = o_sbv if mixer == 'sb' else o_dv

                            def proj_tile(tt, qcol0):
                                rows, sl = tsl(tt)
                                if mixer == 'mla':
                                    if u == 0:
                                        pz = mmbank()
                                        for k in range(8):
                                            S.op('pe', lambda e: e.matmul(out=pz[0:rows, 0:416], lhsT=HT[:, k, sl], rhs=w1[:, k, :],
                                                                          start=(k == 0), stop=(k == 7)), [HT.b[tt]] + w1.b, pz.b, inc=(k == 7))
                                        tk = ntok()
                                        gnorm(ws, pz[0:rows, 0:256].rearrange("p (g d) -> p g d", g=1), pz.b, rows, 1, 256, gs('q_norm', rows),
                                              tk[0:rows, 0:256].rearrange("p (g d) -> p g d", g=1), tk.b)
                                        for c in range(2):
                                            transpose_to(tk[0:rows, c * 128:(c + 1) * 128], tk.b, rows, 128, CQT[:, c, sl], [CQT.b[tt]],
                                                         'act' if c else 'dve')
                                        st = nstage()
                                        gnorm(ws, pz[0:rows, 256:384].rearrange("p (g d) -> p g d", g=1), pz.b, rows, 1, 128, gs('kv_norm', rows),
                                              st[0:rows, 0:128].rearrange("p (g d) -> p g d", g=1), st.b)
                                        S.dma('sp', o_ckv[l, sl, :], st[0:rows, 0:128], st.b, [])
                                        tk2 = ntok()
                                        S.op('act', lambda e: e.copy(out=tk2[0:rows, 0:128], in_=st[0:rows, 0:128]), st.b, tk2.b)
                                        transpose_to(tk2[0:rows, 0:128], tk2.b, rows, 128, CKVT[:, sl], [CKVT.b[tt]], 'dve')
                                        gnorm(ws, pz[0:rows, 384:416].rearrange("p (g d) -> p g d", g=1), pz.b, rows, 1, 32, gs('kr', rows),
                                              st[0:rows, 128:160].rearrange("p (g d) -> p g d", g=1), st.b)
                                        rope(ws, st[0:rows, 128:160].rearrange("p (g d) -> p g d", g=1), st.b, rows, 1, tt,
                                             st[0:rows, 160:192].rearrange("p (g d) -> p g d", g=1), st.b)
                                        S.dma('sp', o_kr[l, sl, :], st[0:rows, 160:192], st.b, [])
                                        S.op('act', lambda e: e.copy(out=KRT[0:rows, tt, :], in_=st[0:rows, 160:192]), st.b, [KRT.b[tt]])
                                    pq = mmbank()
                                    for c in range(2):
                                        S.op('pe', lambda e: e.matmul(out=pq[0:rows, 0:192], lhsT=CQT[:, c, sl], rhs=wuq[:, c, :],
                                                                      start=(c == 0), stop=(c == 1)), [CQT.b[tt]] + wuq.b, pq.b, inc=(c == 1))
                                    qv = pq[0:rows, 0:192].rearrange("p (h d) -> p h d", d=96)
                                    qc = qcat[tt % 2]
                                    gnorm(ws, qv[:, :, 0:64], pq.b, rows, 2, 64, gs('qn', rows), qc[0:rows, :, 0:64], qc.b, MLA_SCALE)
                                    st = nstage()
                                    stv = st[0:rows, 0:64].rearrange("p (h d) -> p h d", d=32)
                                    gnorm(ws, qv[:, :, 64:96], pq.b, rows, 2, 32, gs('qr', rows), stv, st.b, MLA_SCALE)
                                    rope(ws, stv, st.b, rows, 2, tt, qc[0:rows, :, 64:96], qc.b)
                                    for hh in range(2):
                                        transpose_to(qc[0:rows, hh, :], qc.b, rows, 96, QT[:, hh, qcol0:qcol0 + rows], QT.b, 'act' if hh else 'dve')
                                    kv_from_latent(CKVT[:, sl], [CKVT.b[tt]], KRT[0:rows, tt, :], [KRT.b[tt]], rows,
                                                   lambda hh: KT[:, hh, sl], [KT.b[tt]], V[0:rows, tt, :], [V.b[tt]], kcat[tt % 2])
                                else:
                                    pq = mmbank()
                                    for k in range(8):
                                        S.op('pe', lambda e: e.matmul(out=pq[0:rows, 0:256], lhsT=HT[:, k, sl], rhs=wq[:, k, :],
                                                                      start=(k == 0), stop=(k == 7)), [HT.b[tt]] + wq.b, pq.b, inc=(k == 7))
                                    tk = ntok()
                                    if mixer == 'sb':
                                        S.op('act', lambda e: e.activation(out=tk[0:rows, :], in_=pq[0:rows, 0:256], func=AF.Copy,
                                                                           scale=SB_SCALE), pq.b, tk.b)
                                    else:
                                        gnorm(ws, pq[0:rows, 0:256].rearrange("p (g d) -> p g d", d=64), pq.b, rows, 4, 64, gs('dqn', rows),
                                              tk[0:rows, :].rearrange("p (g d) -> p g d", d=64), tk.b, DIFF_SCALE)
                                    for c in range(2):
                                        transpose_to(tk[0:rows, c * 128:(c + 1) * 128], tk.b, rows, 128, QT[:, c, qcol0:qcol0 + rows], QT.b,
                                                     'act' if c else 'dve')
                                    pk_ = mmbank()
                                    for k in range(8):
                                        S.op('pe', lambda e: e.matmul(out=pk_[0:rows, 0:256], lhsT=HT[:, k, sl], rhs=wk[:, k, :],
                                                                      start=(k == 0), stop=(k == 7)), [HT.b[tt]] + wk.b, pk_.b, inc=(k == 7))
                                    st = nstage()
                                    if mixer == 'sb':
                                        S.op('act', lambda e: e.copy(out=st[0:rows, :], in_=pk_[0:rows, 0:256]), pk_.b, st.b)
                                    else:
                                        gnorm(ws, pk_[0:rows, 0:256].rearrange("p (g d) -> p g d", d=64), pk_.b, rows, 4, 64, gs('dkn', rows),
                                              st[0:rows, :].rearrange("p (g d) -> p g d", d=64), st.b)
                                    S.dma('sp', o_k[l, sl, u * 256:(u + 1) * 256], st[0:rows, :], st.b, [])
                                    tk = ntok()
                                    S.op('dve', lambda e: e.tensor_copy(out=tk[0:rows, :], in_=st[0:rows, :]), st.b, tk.b)
                                    for c in range(2):
                                        transpose_to(tk[0:rows, c * 128:(c + 1) * 128], tk.b, rows, 128, KT[:, c, sl], [KT.b[tt]],
                                                     'act' if c else 'dve')
                                    pv_ = mmbank()
                                    for k in range(8):
                                        S.op('pe', lambda e: e.matmul(out=pv_[0:rows, 0:256], lhsT=HT[:, k, sl], rhs=wv[:, k, :],
                                                                      start=(k == 0), stop=(k == 7)), [HT.b[tt]] + wv.b, pv_.b, inc=(k == 7))
                                    st = nstage()
                                    S.op('act', lambda e: e.copy(out=st[0:rows, :], in_=pv_[0:rows, 0:256]), pv_.b, st.b)
                                    S.dma('sp', o_v[l, sl, u * 256:(u + 1) * 256], st[0:rows, :], st.b, [])
                                    S.op('dve', lambda e: e.tensor_copy(out=V[0:rows, tt, :], in_=pv_[0:rows, 0:256]), pv_.b, [V.b[tt]])

                            def kv_from_latent(ckvT_ap, ckvT_b, kr_ap, kr_b, rows, kt_dst, kt_b, v_dst, v_b, kc):
                                pk_ = mmbank()
                                S.op('pe', lambda e: e.matmul(out=pk_[0:rows, 0:128], lhsT=ckvT_ap, rhs=wuk[:, :], start=True, stop=True),
                                     list(ckvT_b) + wuk.b, pk_.b)
                                gnorm(ws, pk_[0:rows, 0:128].rearrange("p (h d) -> p h d", d=64), pk_.b, rows, 2, 64, gs('kn', rows),
                                      kc[0:rows, :, 0:64], kc.b)
                                S.op('dve', lambda e: e.tensor_copy(out=kc[0:rows, :, 64:96], in_=kr_ap.unsqueeze(1).to_broadcast([rows, 2, 32])),
                                     list(kr_b), kc.b)
                                for hh in range(2):
                                    transpose_to(kc[0:rows, hh, :], kc.b, rows, 96, kt_dst(hh), kt_b, 'act' if hh else 'dve')
                                pv_ = mmbank()
                                S.op('pe', lambda e: e.matmul(out=pv_[0:rows, 0:128], lhsT=ckvT_ap, rhs=wuv[:, :], start=True, stop=True),
                                     list(ckvT_b) + wuv.b, pv_.b)
                                S.op('act', lambda e: e.copy(out=v_dst, in_=pv_[0:rows, 0:128]), pv_.b, v_b)

                            acc, accd, acc1, accd1 = PS[4], PS[5], PS[6], PS[7]

                            def zero_acc(t_, ncols):
                                S.op('pe', lambda e: e.matmul(out=t_[:, 0:ncols], lhsT=zeros[:, :], rhs=zrhs[:, 0:ncols], start=True, stop=True),
                                     zeros.b + zrhs.b, t_.b)

                            def att_begin(hg, ncols):
                                if mixer == 'mla':
                                    zero_acc(acc, ncols)
                                    zero_acc(accd, ncols)
                                elif mixer == 'sb':
                                    zero_acc(acc, ncols)
                                    S.op('dve', lambda e: e.memset(R[:, :, 0:ncols], 0.0), [], R.b)
                                else:
                                    for t_ in (acc, accd, acc1, accd1):
                                        zero_acc(t_, ncols)

                            def diag_mask(t_, mask_ap, mask_b, nk, c0, ncols):
                                dn = min(128, ncols - c0)
                                S.op('dve', lambda e: e.tensor_tensor(out=t_[0:nk, c0:c0 + dn], in0=t_[0:nk, c0:c0 + dn], in1=mask_ap(nk, dn),
                                                                      op=ALU.mult), t_.b + mask_b, t_.b)

                            def att_blocks(hg, kbs, qc0, ncols, sample):
                                for kb in kbs:
                                    nk, c0 = kb['nk'], kb['c0']
                                    if mixer == 'mla':
                                        for hh in range(2):
                                            st = mmbank()
                                            S.op('pe', lambda e: e.matmul(out=st[0:nk, c0:ncols], lhsT=kb['kt'](hh), rhs=QT[:, hh, qc0 + c0:qc0 + ncols],
                                                                          start=True, stop=True), kb['ktb'] + QT.b, st.b)
                                            pt = npt()
                                            S.op('act', lambda e: e.activation(out=pt[0:nk, c0:ncols], in_=st[0:nk, c0:ncols], func=AF.Exp), st.b, pt.b)
                                            if kb['diag']:
                                                diag_mask(pt, lambda a, b: mch[0:a, 0:b], mch.b, nk, c0, ncols)
                                            S.op('pe', lambda e: e.matmul(out=acc[64 * hh:64 * hh + 64, c0:ncols], lhsT=kb['v'](hh),
                                                                          rhs=pt[0:nk, c0:ncols], start=False, stop=True, skip_group_check=True),
                                                 kb['vb'] + pt.b, acc.b)
                                            S.op('pe', lambda e: e.matmul(out=accd[64 * hh:64 * hh + 64, c0:ncols], lhsT=ones[0:nk, 0:64],
                                                                          rhs=pt[0:nk, c0:ncols], start=False, stop=True, skip_group_check=True),
                                                 ones.b + pt.b, accd.b)
                                    elif mixer == 'sb':
                                        hp = hg
                                        for hh in range(2):
                                            st = mmbank()
                                            S.op('pe', lambda e: e.matmul(out=st[0:nk, c0:ncols], lhsT=kb['kt'](hp)[64 * hh:64 * hh + 64, :],
                                                                          rhs=QT[64 * hh:64 * hh + 64, hp, qc0 + c0:qc0 + ncols], start=True, stop=True),
                                                 kb['ktb'] + QT.b, st.b)
                                            ee = e1[rot['mm'] % 2]
                                            sp_ = spb[rot['mm'] % 2]
                                            S.op('act', lambda e: e.activation(out=ee[0:nk, c0:ncols], in_=st[0:nk, c0:ncols], func=AF.Exp), st.b, ee.b)
                                            S.op('act', lambda e: e.activation(out=sp_[0:nk, c0:ncols], in_=ee[0:nk, c0:ncols], func=AF.Ln, bias=1.0),
                                                 ee.b, sp_.b)
                                            if kb['diag']:
                                                diag_mask(sp_, lambda a, b: msb[0:a, 0:b], msb.b, nk, c0, ncols)
                                            S.op('pe', lambda e: e.matmul(out=acc1[0:nk, c0:ncols], lhsT=tri[0:nk, 0:nk], rhs=sp_[0:nk, c0:ncols],
                                                                          start=True, stop=True), tri.b + sp_.b, acc1.b)
                                            S.op('pe', lambda e: e.matmul(out=accd1[:, c0:ncols], lhsT=ones[0:nk, :], rhs=sp_[0:nk, c0:ncols],
                                                                          start=True, stop=True), ones.b + sp_.b, accd1.b)
                                            S.op('dve', lambda e: e.tensor_tensor(out=cumr[0:nk, c0:ncols], in0=acc1[0:nk, c0:ncols],
                                                                                  in1=R[0:nk, hh, c0:ncols], op=ALU.add), acc1.b + R.b, cumr.b)
                                            S.op('act', lambda e: e.activation(out=cumr[0:nk, c0:ncols], in_=cumr[0:nk, c0:ncols], func=AF.Exp,
                                                                               scale=-1.0), cumr.b, cumr.b)
                                            pt = npt()
                                            S.op('dve', lambda e: e.tensor_tensor(out=pt[0:nk, c0:ncols], in0=ee[0:nk, c0:ncols],
                                                                                  in1=cumr[0:nk, c0:ncols], op=ALU.mult), ee.b + cumr.b, pt.b)
                                            if kb['diag']:
                                                diag_mask(pt, lambda a, b: msb[0:a, 0:b], msb.b, nk, c0, ncols)
                                            S.op('dve', lambda e: e.tensor_tensor(out=R[:, hh, c0:ncols], in0=R[:, hh, c0:ncols], in1=accd1[:, c0:ncols],
                                                                                  op=ALU.add), R.b + accd1.b, R.b)
                                            S.op('pe', lambda e: e.matmul(out=acc[64 * hh:64 * hh + 64, c0:ncols], lhsT=kb['v'](2 * hp + hh),
                                                                          rhs=pt[0:nk, c0:ncols], start=False, stop=True, skip_group_check=True),
                                                 kb['vb'] + pt.b, acc.b)
                                    else:
                                        hd = hg
                                        hgl = 2 * u + hd
                                        for m in range(2):
                                            st = mmbank()
                                            S.op('pe', lambda e: e.matmul(out=st[0:nk, c0:ncols], lhsT=kb['kt'](hd)[64 * m:64 * m + 64, :],
                                                                          rhs=QT[64 * m:64 * m + 64, hd, qc0 + c0:qc0 + ncols], start=True, stop=True),
                                                 kb['ktb'] + QT.b, st.b)
                                            pt = npt()
                                            if sample:
                                                bi = kb['bi']
                                                S.op('act', lambda e: e.activation(out=pt[0:nk, c0:ncols], in_=st[0:nk, c0:ncols], func=AF.Exp,
                                                                                   bias=bds[0:nk, hgl, bi:bi + 1]), st.b + bds.b, pt.b)
                                            else:
                                                cc = c0
                                                while cc < ncols:
                                                    ce = min(ncols, (cc // 256 + 1) * 256)
                                                    dl = (kb['qt0'] + ce // 128 - 1) - kb['kbi']
                                                    S.op('act', lambda e: e.activation(out=pt[0:nk, cc:ce], in_=st[0:nk, cc:ce], func=AF.Exp,
                                                                                       bias=bdp[0:nk, hgl, dl:dl + 1]), st.b + bdp.b, pt.b)
                                                    cc = ce
                                            if kb['diag']:
                                                diag_mask(pt, lambda a, b: mdf[0:a, hgl, 0:b], mdf.b, nk, c0, ncols)
                                            a_, d_ = (acc, accd) if m == 0 else (acc1, accd1)
                                            S.op('pe', lambda e: e.matmul(out=a_[:, c0:ncols], lhsT=kb['v'](hd), rhs=pt[0:nk, c0:ncols],
                                                                          start=False, stop=True, skip_group_check=True), kb['vb'] + pt.b, a_.b)
                                            S.op('pe', lambda e: e.matmul(out=d_[:, c0:ncols], lhsT=ones[0:nk, :], rhs=pt[0:nk, c0:ncols],
                                                                          start=False, stop=True, skip_group_check=True), ones.b + pt.b, d_.b)

                            def att_end(hg, ncols, ocol0, grp_b):
                                if mixer == 'mla':
                                    S.op('dve', lambda e: e.reciprocal(out=rden[:, 0:ncols], in_=accd[:, 0:ncols]), accd.b, rden.b)
                                    S.op('dve', lambda e: e.tensor_tensor(out=OT[:, u, ocol0:ocol0 + ncols], in0=acc[:, 0:ncols], in1=rden[:, 0:ncols],
                                                                          op=ALU.mult), acc.b + rden.b, grp_b)
                                elif mixer == 'sb':
                                    S.op('act', lambda e: e.copy(out=OT[:, 2 * u + hg, ocol0:ocol0 + ncols], in_=acc[:, 0:ncols]), acc.b, grp_b)
                                else:
                                    hgl = 2 * u + hg
                                    S.op('dve', lambda e: e.reciprocal(out=rden[:, 0:ncols], in_=accd[:, 0:ncols]), accd.b, rden.b)
                                    S.op('dve', lambda e: e.reciprocal(out=rden1[:, 0:ncols], in_=accd1[:, 0:ncols]), accd1.b, rden1.b)
                                    S.op('dve', lambda e: e.tensor_tensor(out=t0[:, 0:ncols], in0=acc[:, 0:ncols], in1=rden[:, 0:ncols], op=ALU.mult),
                                         acc.b + rden.b, t0.b)
                                    S.op('dve', lambda e: e.tensor_tensor(out=t1[:, 0:ncols], in0=acc1[:, 0:ncols], in1=rden1[:, 0:ncols], op=ALU.mult),
                                         acc1.b + rden1.b, t1.b)
                                    S.op('dve', lambda e: e.scalar_tensor_tensor(out=t0[:, 0:ncols], in0=t1[:, 0:ncols], scalar=gcol[:, 2:3],
                                                                                 in1=t0[:, 0:ncols], op0=ALU.mult, op1=ALU.add),
                                         t0.b + t1.b + gcol.b, t0.b)
                                    S.op('act', lambda e: e.activation(out=osq[:, 0:ncols], in_=t0[:, 0:ncols], func=AF.Square), t0.b, osq.b)
                                    S.op('pe', lambda e: e.matmul(out=acc[:, 0:ncols], lhsT=onesf[:, :], rhs=osq[:, 0:ncols], start=True, stop=True),
                                         onesf.b + osq.b, acc.b)
                                    S.op('act', lambda e: e.activation(out=t1[:, 0:ncols], in_=acc[:, 0:ncols], func=AF.Ln, scale=1.0 / 128, bias=EPS),
                                         acc.b, t1.b)
                                    S.op('act', lambda e: e.activation(out=t1[:, 0:ncols], in_=t1[:, 0:ncols], func=AF.Exp, scale=-0.5), t1.b, t1.b)
                                    S.op('dve', lambda e: e.scalar_tensor_tensor(out=OT[:, hgl, ocol0:ocol0 + ncols], in0=t0[:, 0:ncols],
                                                                                 scalar=gcol[:, 0:1], in1=t1[:, 0:ncols], op0=ALU.mult, op1=ALU.mult),
                                         t0.b + t1.b + gcol.b, grp_b)

                            nhg = 1 if mixer == 'mla' else 2

                            def kb_store(kbi, c0, diag, qt0, nk=128, bi=0):
                                rows_, sl = tsl(kbi)
                                if mixer == 'mla':
                                    return dict(kt=lambda hh: KT[:, hh, sl], ktb=[KT.b[kbi]], v=lambda hh: V[0:nk, kbi, 64 * hh:64 * hh + 64],
                                                vb=[V.b[kbi]], nk=nk, c0=c0, diag=diag, kbi=kbi, qt0=qt0, bi=bi)
                                if mixer == 'sb':
                                    return dict(kt=lambda hp: KT[:, hp, sl], ktb=[KT.b[kbi]], v=lambda h: V[0:nk, kbi, 64 * h:64 * h + 64],
                                                vb=[V.b[kbi]], nk=nk, c0=c0, diag=diag, kbi=kbi, qt0=qt0, bi=bi)
                                return dict(kt=lambda hd: KT[:, hd, sl], ktb=[KT.b[kbi]], v=lambda hd: V[0:nk, kbi, 128 * hd:128 * hd + 128],
                                            vb=[V.b[kbi]], nk=nk, c0=c0, diag=diag, kbi=kbi, qt0=qt0, bi=bi)

                            def kb_past(j, kbi):
                                sl = slice(j * 128, (j + 1) * 128)
                                bi = 32 - kbi
                                if mixer == 'mla':
                                    return dict(kt=lambda hh: KTp[:, hh, sl], ktb=KTp.b, v=lambda hh: Vp[:, j, 64 * hh:64 * hh + 64],
                                                vb=Vp.b, nk=128, c0=0, diag=False, kbi=kbi, qt0=0, bi=bi)
                                if mixer == 'sb':
                                    return dict(kt=lambda hp: KTp[:, hp, sl], ktb=KTp.b, v=lambda h: Vp[:, j, 64 * h:64 * h + 64],
                                                vb=Vp.b, nk=128, c0=0, diag=False, kbi=kbi, qt0=0, bi=bi)
                                return dict(kt=lambda hd: KTp[:, hd, sl], ktb=KTp.b, v=lambda hd: Vp[:, j, 128 * hd:128 * hd + 128],
                                            vb=Vp.b, nk=128, c0=0, diag=False, kbi=kbi, qt0=0, bi=bi)

                            def build_chunk(s, ch):
                                r0 = ch * 512
                                if mixer == 'mla':
                                    S.dma('pool', pck[:], c_ckv[l, s, r0:r0 + 512, :].rearrange("(k p) n -> p k n", p=128), [], pck.b)
                                    S.dma('pool', pkr[:], c_kr[l, s, r0:r0 + 512, :].rearrange("(k p) n -> p k n", p=128), [], pkr.b)
                                    for j in range(4):
                                        transpose_to(pck[:, j, :], pck.b, 128, 128, pckT[:, j * 128:(j + 1) * 128], pckT.b, 'act' if j % 2 else 'dve')
                                    for j in range(4):
                                        jsl = slice(j * 128, (j + 1) * 128)
                                        kv_from_latent(pckT[:, jsl], pckT.b, pkr[:, j, :], pkr.b, 128,
                                                       lambda hh: KTp[:, hh, jsl], KTp.b, Vp[:, j, :], Vp.b, kcat[j % 2])
                                else:
                                    ck = c_sbk if mixer == 'sb' else c_dk
                                    cv = c_sbv if mixer == 'sb' else c_dv
                                    S.dma('pool', pk[:], ck[l, s, r0:r0 + 512, u * 256:(u + 1) * 256].rearrange("(k p) n -> p k n", p=128), [], pk.b)
                                    S.dma('pool', Vp[:], cv[l, s, r0:r0 + 512, u * 256:(u + 1) * 256].rearrange("(k p) n -> p k n", p=128), [], Vp.b)
                                    for j in range(4):
                                        for c in range(2):
                                            transpose_to(pk[:, j, c * 128:(c + 1) * 128], pk.b, 128, 128, KTp[:, c, j * 128:(j + 1) * 128], KTp.b,
                                                         'act' if c else 'dve')

                            for g in range(4):
                                for i in range(4):
                                    proj_tile(4 * g + i, i * 128)
                                kbs = []
                                for kbi in range(4 * g + 4):
                                    i = kbi - 4 * g
                                    kbs.append(kb_store(kbi, max(i, 0) * 128, i >= 0, 4 * g))
                                if mixer == 'sb':
                                    kbs = kbs[::-1]
                                for hg in range(nhg):
                                    att_begin(hg, 512)
                                    att_blocks(hg, kbs, 0, 512, False)
                                    att_end(hg, 512, g * 512, [OT.b[g]])
                                _ck('%s%d_u%d_g%d' % (mixer, l, u, g))

                            for s in range(NS):
                                proj_tile(16 + s, 0)
                                knew = kb_store(16 + s, 0, mixer != 'mla', 0, nk=32, bi=0)
                                for hg in range(nhg):
                                    att_begin(hg, 32)
                                    if mixer == 'sb':
                                        att_blocks(hg, [knew], 0, 32, True)
                                        for ch in range(7, -1, -1):
                                            build_chunk(s, ch)
                                            att_blocks(hg, [kb_past(j, ch * 4 + j) for j in range(3, -1, -1)], 0, 32, True)
                                    else:
                                        for ch in range(8):
                                            build_chunk(s, ch)
                                            att_blocks(hg, [kb_past(j, ch * 4 + j) for j in range(4)], 0, 32, True)
                                        att_blocks(hg, [knew], 0, 32, True)
                                    att_end(hg, 32, 2048 + 32 * s, [OT.b[4]])
                                _ck('%s%d_u%d_s%d' % (mixer, l, u, s))
                            S.barrier()
                            _ck('%s%d_u%d' % (mixer, l, u))

                    mi = ('mla', 'sb', 'diff').index(mixer)
                    ms.close()
                    with ExitStack() as gs_:
                        mm_list[0] = [0, 1, 4, 5, 6, 7]
                        wg = sb(gs_, [128, 8, 1024], BF16)
                        g0 = OFF['g'] + 1024 * mi
                        load_w(wg, W['w_in'][l, :, g0:g0 + 1024].rearrange("(k p) n -> p k n", p=128))
                        wbr = sb(gs_, [128, 4, 1024], BF16)
                        load_w(wbr, W['w_br_' + mixer][l].rearrange("(k p) n -> p k n", p=128))
                        wo = sb(gs_, [128, 8, 1024], BF16)
                        load_w(wo, W['w_out'][l].rearrange("(k p) n -> p k n", p=128))
                        gT = [sb(gs_, [128, 512], F32) for _ in range(2)]
                        MG = [sb(gs_, [128, 8, 512], BF16) for _ in range(1)]
                        for gi, (c0, n, tiles) in enumerate(GROUPS):
                            hb_ = [HT.b[t] for t in tiles]
                            mg = MG[0]
                            for nn in range(8):
                                nsl = slice(nn * 128, (nn + 1) * 128)
                                pg = mmbank()
                                for k in range(8):
                                    S.op('pe', lambda e: e.matmul(out=pg[:, 0:n], lhsT=wg[:, k, nsl], rhs=HT[:, k, c0:c0 + n],
                                                                  start=(k == 0), stop=(k == 7)), wg.b + hb_, pg.b, inc=(k == 7))
                                gt = gT[nn % 2]
                                S.op('act', lambda e: e.activation(out=gt[:, 0:n], in_=pg[:, 0:n], func=AF.Sigmoid), pg.b, gt.b)
                                py = mmbank()
                                for c in range(4):
                                    S.op('pe', lambda e: e.matmul(out=py[:, 0:n], lhsT=wbr[:, c, nsl], rhs=OT[:, c, c0:c0 + n],
                                                                  start=(c == 0), stop=(c == 3)), wbr.b + [OT.b[gi]], py.b, inc=(c == 3))
                                S.op('dve', lambda e: e.tensor_tensor(out=mg[:, nn, 0:n], in0=py[:, 0:n], in1=gt[:, 0:n], op=ALU.mult),
                                     py.b + gt.b, mg.b)
                            for tt in tiles:
                                rows, sl = tsl(tt)
                                off = sl.start - c0
                                for half in range(2):
                                    hsl = slice(half * 512, (half + 1) * 512)
                                    po = mmbank()
                                    for k in range(8):
                                        S.op('pe', lambda e: e.matmul(out=po[0:rows, :], lhsT=mg[:, k, off:off + rows], rhs=wo[:, k, hsl],
                                                                      start=(k == 0), stop=(k == 7)), mg.b + wo.b, po.b, inc=(k == 7))
                                    S.op('dve', lambda e: e.tensor_tensor(out=X[0:rows, tt, hsl], in0=X[0:rows, tt, hsl], in1=po[0:rows, :],
                                                                          op=ALU.add), [X.b[tt]] + po.b, [X.b[tt]])
                        S.barrier()
                        mm_list[0] = [0, 1]
                        rot['mm'] = 0
                    _ck('%s%d_merge' % (mixer, l))

            norm_phase(l, 'ffn_norm_g', False)
            with ExitStack() as fs:
                mm_list[0] = [0, 1, 4, 5, 6, 7]
                cw = sb(fs, [128, 22, 3], F32)
                cb = sb(fs, [128, 22], F32)
                cst = sb(fs, [128, NS, 22, 2], F32)
                OC = sb(fs, [128, 3, 22, 2], F32)
                S.dma('sp', cw[:].rearrange("p a b -> p (a b)"), W['ffn_conv_w'][l], [], cw.b)
                S.dma('sp', cb[:], W['ffn_conv_b'][l], [], cb.b)
                for s in range(NS):
                    S.dma('sp', cst[:, s, :, :].rearrange("p a b -> p (a b)"), c_conv[l, s], [], cst.b)
                WA = [sb(fs, [128, 8, 512], BF16) for _ in range(2)]
                WU = [sb(fs, [128, 8, 512], BF16) for _ in range(2)]
                WD = [sb(fs, [128, 4, 1024], BF16) for _ in range(2)]
                AT = [sb(fs, [128, 4, 516], F32, 4) for _ in range(1)]
                carry = sb(fs, [128, 4, 2], F32, 4)
                cc_ = [sb(fs, [128, 512], F32) for _ in range(1)]
                sl_ = [sb(fs, [128, 512], F32) for _ in range(1)]
                MM = [sb(fs, [128, 4, 512], BF16) for _ in range(1)]
                fgroups = [(0, 4), (4, 4), (8, 4), (12, 4), (16, 4), (20, 2)]
                for fi, (fc0, nfc) in enumerate(fgroups):
                    wa, wu, wd = WA[fi % 2], WU[fi % 2], WD[fi % 2]
                    nf = nfc * 128
                    S.dma('pool', wa[:, :, 0:nf], W['ffn_w_up'][l, :, fc0 * 128:fc0 * 128 + nf].rearrange("(k p) n -> p k n", p=128), [], wa.b)
                    S.dma('pool', wu[:, :, 0:nf], W['ffn_w_up'][l, :, DFF + fc0 * 128:DFF + fc0 * 128 + nf].rearrange("(k p) n -> p k n", p=128),
                          [], wu.b)
                    S.dma('pool', wd[:, 0:nfc, :], W['ffn_w_down'][l, fc0 * 128:fc0 * 128 + nf, :].rearrange("(c p) n -> p c n", p=128), [], wd.b)
                    for gi, (c0, n, tiles) in enumerate(GROUPS):
                        hb_ = [HT.b[t] for t in tiles]
                        mmt = MM[0]
                        buf = AT[0]
                        for j in range(nfc):
                            fc = fc0 + j
                            jsl = slice(j * 128, (j + 1) * 128)
                            pa = mmbank()
                            for k in range(8):
                                S.op('pe', lambda e: e.matmul(out=pa[:, 0:n], lhsT=wa[:, k, jsl], rhs=HT[:, k, c0:c0 + n],
                                                              start=(k == 0), stop=(k == 7)), wa.b + hb_, pa.b, inc=(k == 7))
                            cc = cc_[0]
                            if gi < 4:
                                S.op('act', lambda e: e.copy(out=buf[:, j, 2:514], in_=pa[:, 0:512]), pa.b, [buf.b[j]])
                                if gi == 0:
                                    S.op('dve', lambda e: e.memset(buf[:, j, 0:2], 0.0), [], [buf.b[j]])
                                else:
                                    S.op('dve', lambda e: e.tensor_copy(out=buf[:, j, 0:2], in_=carry[:, j, :]), [carry.b[j]], [buf.b[j]])
                                S.op('dve', lambda e: e.tensor_copy(out=carry[:, j, :], in_=buf[:, j, 512:514]), [buf.b[j]], [carry.b[j]])
                                segs = [(0, 0, 512)]
                                if gi == 3:
                                    S.op('dve', lambda e: e.tensor_copy(out=OC[:, 0, fc, :], in_=buf[:, j, 512:514]), [buf.b[j]], OC.b)
                            else:
                                for s in range(NS):
                                    S.op('act', lambda e: e.copy(out=buf[:, j, s * 34 + 2:s * 34 + 34], in_=pa[:, s * 32:s * 32 + 32]), pa.b, [buf.b[j]])
                                    S.op('dve', lambda e: e.tensor_copy(out=buf[:, j, s * 34:s * 34 + 2], in_=cst[:, s, fc, :]), cst.b, [buf.b[j]])
                                    S.op('dve', lambda e: e.tensor_copy(out=OC[:, 1 + s, fc, :], in_=buf[:, j, s * 34 + 32:s * 34 + 34]), [buf.b[j]], OC.b)
                                segs = [(0, 0, 32), (34, 32, 32)]
                            for (b0, o0, nn_) in segs:
                                S.op('dve', lambda e: e.tensor_scalar(out=cc[:, o0:o0 + nn_], in0=buf[:, j, b0 + 2:b0 + 2 + nn_], scalar1=cw[:, fc, 2:3],
                                                                      scalar2=cb[:, fc:fc + 1], op0=ALU.mult, op1=ALU.add),
                                     [buf.b[j]] + cw.b + cb.b, cc.b)
                                S.op('dve', lambda e: e.scalar_tensor_tensor(out=cc[:, o0:o0 + nn_], in0=buf[:, j, b0 + 1:b0 + 1 + nn_], scalar=cw[:, fc, 1:2],
                                                                             in1=cc[:, o0:o0 + nn_], op0=ALU.mult, op1=ALU.add),
                                     [buf.b[j]] + cw.b + cc.b, cc.b)
                                S.op('dve', lambda e: e.scalar_tensor_tensor(out=cc[:, o0:o0 + nn_], in0=buf[:, j, b0:b0 + nn_], scalar=cw[:, fc, 0:1],
                                                                             in1=cc[:, o0:o0 + nn_], op0=ALU.mult, op1=ALU.add),
                                     [buf.b[j]] + cw.b + cc.b, cc.b)
                            sl2 = sl_[0]
                            S.op('act', lambda e: e.activation(out=sl2[:, 0:n], in_=cc[:, 0:n], func=AF.Silu), cc.b, sl2.b)
                            pu = mmbank()
                            for k in range(8):
                                S.op('pe', lambda e: e.matmul(out=pu[:, 0:n], lhsT=wu[:, k, jsl], rhs=HT[:, k, c0:c0 + n],
                                                              start=(k == 0), stop=(k == 7)), wu.b + hb_, pu.b, inc=(k == 7))
                            S.op('dve', lambda e: e.tensor_tensor(out=mmt[:, j, 0:n], in0=pu[:, 0:n], in1=sl2[:, 0:n], op=ALU.mult),
                                 pu.b + sl2.b, mmt.b)
                        for tt in tiles:
                            rows, sl = tsl(tt)
                            off = sl.start - c0
                            for half in range(2):
                                hsl = slice(half * 512, (half + 1) * 512)
                                po = mmbank()
                                for j in range(nfc):
                                    S.op('pe', lambda e: e.matmul(out=po[0:rows, :], lhsT=mmt[:, j, off:off + rows], rhs=wd[:, j, hsl],
                                                                  start=(j == 0), stop=(j == nfc - 1)), mmt.b + wd.b, po.b, inc=(j == nfc - 1))
                                S.op('dve', lambda e: e.tensor_tensor(out=X[0:rows, tt, hsl], in0=X[0:rows, tt, hsl], in1=po[0:rows, :],
                                                                      op=ALU.add), [X.b[tt]] + po.b, [X.b[tt]])
                S.dma('sp', o_conv[l].rearrange("s p n -> p s n"), OC[:].rearrange("p s a b -> p s (a b)"), OC.b, [])
                S.barrier()
                mm_list[0] = [0, 1]
                rot['mm'] = 0
            _ck('ffn%d' % l)

        _DEV['off'] = False
        _DEV['nops'] = 0
        _layers()
        _DEV['off'] = False
        mm_list[0] = [0, 1]
        for tt in range(NT):
            rows, sl = tsl(tt)
            S.dma('sp', y[sl, :], X[0:rows, tt, :], [X.b[tt]], [])
        S.barrier()
    return nc


_SLOPES = [2.0 ** (-8.0 * (h + 1) / 4) for h in range(4)]


def _consts():
    half = 16
    inv = (np.float32(10000.0) ** (-np.arange(half, dtype=np.float32) / np.float32(half))).astype(np.float32)
    pos = np.concatenate([np.arange(SP_), PAST + np.arange(SS), PAST + np.arange(SS)]).astype(np.float32)
    ang = (pos[:, None] * inv[None, :]).astype(np.float32)
    k_cs = np.concatenate([np.cos(ang), np.sin(ang)], axis=1).astype(np.float32)
    k = np.arange(128)[:, None]
    q = np.arange(128)[None, :]
    msb = (k < q).astype(np.float32)
    mch = ((k // 64) <= (q // 64)).astype(np.float32)
    mdf = np.zeros((128, 4, 128), np.float32)
    bdp = np.zeros((128, 4, 17), np.float32)
    bds = np.zeros((128, 4, 33), np.float32)
    for h in range(4):
        sl = _SLOPES[h]
        mdf[:, h, :] = mch * np.where(k > q, np.exp(-2.0 * sl * (k - q)), 1.0)
        for d in range(17):
            bdp[:, h, d] = sl * (np.arange(128) - 127 - 128 * d)
        for j in range(33):
            bds[:, h, j] = sl * (np.arange(128) - 31 - 128 * j)
    tri = (k >= q).astype(np.float32)
    return dict(k_cs=k_cs, k_msb=msb, k_mch=mch, k_mdf=mdf.reshape(128, 512), k_tri=tri,
                k_bdp=bdp.reshape(128, 68), k_bds=bds.reshape(128, 132))


_WNAMES = ["mix_norm_g", "w_in", "mla_q_norm_g", "mla_w_uq", "mla_kv_norm_g", "mla_w_uk", "mla_w_uv", "mla_qn_g", "mla_kn_g",
           "mla_qr_g", "mla_kr_g", "diff_qn_g", "diff_kn_g", "diff_lambda", "diff_subln_g", "w_br_mla", "w_br_sb", "w_br_diff",
           "w_out", "ffn_norm_g", "ffn_w_up", "ffn_conv_w", "ffn_conv_b", "ffn_w_down"]


def kernel(**inputs):
    inp = {k: np.asarray(v) for k, v in inputs.items()}
    nc = build_program()
    shared = {}
    for n in _WNAMES:
        a = np.ascontiguousarray(inp[n], dtype=np.float32)
        if n == "diff_lambda":
            a = a.reshape(L, 256)
        elif n == "ffn_conv_w":
            a = np.ascontiguousarray(a.reshape(L, 3, 22, 128).transpose(0, 3, 2, 1)).reshape(L, 128, 66)
        elif n == "ffn_conv_b":
            a = np.ascontiguousarray(a.reshape(L, 22, 128).transpose(0, 2, 1))
        shared[n] = a
    shared.update(_consts())
    in_maps = []
    for c in range(8):
        m = dict(shared)
        m["xin"] = np.ascontiguousarray(np.concatenate([inp["x_prompt"][c], inp["x_sample"][2 * c], inp["x_sample"][2 * c + 1]], axis=0),
                                        dtype=np.float32)
        sl = slice(2 * c, 2 * c + 2)
        m["c_ckv"] = np.ascontiguousarray(inp["cache_mla_ckv"][:, sl])
        m["c_kr"] = np.ascontiguousarray(inp["cache_mla_krope"][:, sl])
        m["c_sbk"] = np.ascontiguousarray(inp["cache_sb_k"][:, sl]).reshape(L, NS, PAST, 512)
        m["c_sbv"] = np.ascontiguousarray(inp["cache_sb_v"][:, sl]).reshape(L, NS, PAST, 512)
        m["c_dk"] = np.ascontiguousarray(inp["cache_diff_k"][:, sl]).reshape(L, NS, PAST, 512)
        m["c_dv"] = np.ascontiguousarray(inp["cache_diff_v"][:, sl]).reshape(L, NS, PAST, 512)
        st = np.asarray(inp["state_ffn_conv"][:, sl], dtype=np.float32)
        m["c_conv"] = np.ascontiguousarray(st.reshape(L, NS, 2, 22, 128).transpose(0, 1, 4, 3, 2)).reshape(L, NS, 128, 44)
        in_maps.append(m)
    res = run_bass_kernel_spmd(nc, in_maps, core_ids=list(range(8))).results

    def gather(name, width):
        p = np.stack([res[c][name][:, 0:SP_] for c in range(8)], axis=1)
        s = np.stack([res[c][name][:, SP_ + SS * j:SP_ + SS * (j + 1)] for c in range(8) for j in range(NS)], axis=1)
        return p, s

    y_p = np.stack([res[c]["y"][0:SP_] for c in range(8)], axis=0)
    y_s = np.stack([res[c]["y"][SP_ + SS * j:SP_ + SS * (j + 1)] for c in range(8) for j in range(NS)], axis=0)
    p_ckv, s_ckv = gather("o_ckv", 128)
    p_kr, s_kr = gather("o_kr", 32)
    p_sbk, s_sbk = gather("o_sbk", 512)
    p_sbv, s_sbv = gather("o_sbv", 512)
    p_dk, s_dk = gather("o_dk", 512)
    p_dv, s_dv = gather("o_dv", 512)

    def conv_of(c, idx):
        oc = res[c]["o_conv"][:, idx].reshape(L, 128, 22, 2)
        return np.ascontiguousarray(oc.transpose(0, 3, 2, 1)).reshape(L, 2, DFF)
    p_conv = np.stack([conv_of(c, 0) for c in range(8)], axis=1)
    s_conv = np.stack([conv_of(c, 1 + j) for c in range(8) for j in range(NS)], axis=1)
    f = np.float32
    return (y_p.astype(f), y_s.astype(f), p_ckv.astype(f), p_kr.astype(f),
            p_sbk.reshape(L, 8, SP_, 8, 64).astype(f), p_sbv.reshape(L, 8, SP_, 8, 64).astype(f),
            p_dk.reshape(L, 8, SP_, 4, 2, 64).astype(f), p_dv.reshape(L, 8, SP_, 4, 128).astype(f), p_conv.astype(f),
            s_ckv.astype(f), s_kr.astype(f), s_sbk.reshape(L, 16, SS, 8, 64).astype(f), s_sbv.reshape(L, 16, SS, 8, 64).astype(f),
            s_dk.reshape(L, 16, SS, 4, 2, 64).astype(f), s_dv.reshape(L, 16, SS, 4, 128).astype(f), s_conv.astype(f))
```

```python
import math
from contextlib import ExitStack
import numpy as np
import concourse.bass as bass
import concourse.mybir as mybir
from concourse.bass_utils import run_bass_kernel_spmd

F32 = mybir.dt.float32
BF16 = mybir.dt.bfloat16
AF = mybir.ActivationFunctionType
ALU = mybir.AluOpType
AX = mybir.AxisListType

L = 2
D = 1024
SP_ = 2048
NS = 2
SS = 32
PAST = 4096
NTOK = SP_ + NS * SS
NT = 18
DFF = 2816
NIN = 6560
EPS = 1e-6
MLA_SCALE = 96 ** -0.5
SB_SCALE = 64 ** -0.5
DIFF_SCALE = 64 ** -0.5
OFF = dict(cq=0, ckv=256, kr=384, sq=416, sk=928, sv=1440, dq=1952, dk=2464, dv=2976, g=3488)
ENG = ['pe', 'act', 'dve', 'pool', 'sp']
NDS = 20
_DEV = {'stop': None, 'off': False, 'maxops': None, 'nops': 0, 'log': None}


class _Stop(Exception):
    pass


def _ck(name):
    if _DEV['log'] is not None and not _DEV['off']:
        _DEV['log'].append((name, _DEV['nops']))
    if _DEV['stop'] == name:
        _DEV['off'] = True


class Buf:
    __slots__ = ('w', 'r', 'excl')

    def __init__(self):
        self.w = None
        self.r = {}
        self.excl = False


class TT:
    def __init__(self, h, n=1):
        self.h = h
        self.b = [Buf() for _ in range(n)]

    def __getitem__(self, k):
        return self.h[k]


class Sync:
    def __init__(self, nc, es):
        self.nc = nc
        self.e = dict(pe=nc.tensor, act=nc.scalar, dve=nc.vector, pool=nc.gpsimd, sp=nc.sync)
        self.sem = {k: es.enter_context(nc.semaphore("sem_" + k)) for k in ENG}
        self.cnt = {k: 0 for k in ENG}
        self.dsem = {q: [es.enter_context(nc.semaphore("ds_%s%d" % (q, i))) for i in range(NDS)] for q in ('sp', 'pool')}
        self.dcnt = {q: [0] * NDS for q in ('sp', 'pool')}
        self.dnext = {'sp': 0, 'pool': 0}
        self.known = {k: {} for k in ENG}
        self.pend = {k: False for k in ENG}
        self.hist = {}

    def _semof(self, k):
        return self.sem[k] if isinstance(k, str) else self.dsem[k[0]][k[1]]

    def _need(self, eng, toks):
        kn = self.known[eng]
        best = {}
        for (k, v) in toks:
            if best.get(k, 0) < v:
                best[k] = v
        need = []
        for k, v in sorted(best.items(), key=lambda kv: str(kv[0])):
            if kn.get(k, 0) < v:
                need.append((k, v))
        implied = {}
        for k, v in need:
            snap = self.hist.get((k, v))
            if snap:
                for k2, v2 in snap.items():
                    if implied.get(k2, 0) < v2:
                        implied[k2] = v2
        out = [(k, v) for (k, v) in need if implied.get(k, 0) < v]
        for k, v in out:
            kn[k] = v
            snap = self.hist.get((k, v))
            if snap:
                for k2, v2 in snap.items():
                    if k2 != eng and kn.get(k2, 0) < v2:
                        kn[k2] = v2
        return out

    def _wait(self, eng, toks, ins_fn=None):
        need = self._need(eng, toks)
        if ins_fn is None:
            for k, v in need:
                self.e[eng].wait_ge(self._semof(k), v)
            return None
        for k, v in need[:-1]:
            self.e[eng].wait_ge(self._semof(k), v)
        ins = ins_fn()
        if need:
            k, v = need[-1]
            ins._wait_ge(self._semof(k), v)
        return ins

    def _deps(self, eng, reads, writes):
        toks = set()
        for b in reads:
            if b.w is not None:
                toks.add(b.w)
            if b.excl:
                for kv in b.r.items():
                    if kv[0] != eng:
                        toks.add(kv)
        for b in writes:
            if b.w is not None:
                toks.add(b.w)
            for kv in b.r.items():
                toks.add(kv)
        if eng == 'pe':
            toks = {t for t in toks if t[0] != 'pe'}
        return toks

    def op(self, eng, fn, reads=(), writes=(), inc=True):
        if _DEV['off']:
            return
        _DEV['nops'] += 1
        if _DEV['maxops'] is not None and _DEV['nops'] > _DEV['maxops'] and not self.pend[eng]:
            _DEV['off'] = True
            return
        ins = self._wait(eng, self._deps(eng, reads, writes), lambda: fn(self.e[eng]))
        c = self.cnt[eng] + 1
        if inc:
            ins.then_inc(self.sem[eng], 1)
            self.cnt[eng] = c
            self.pend[eng] = False
            self.hist[(eng, c)] = dict(self.known[eng])
        else:
            self.pend[eng] = True
        for b in reads:
            b.r[eng] = c
        for b in writes:
            b.w = (eng, c)
            b.r = {}

    def dma(self, q, out, in_, reads=(), writes=()):
        if _DEV['off']:
            return
        toks = self._deps(q, reads, writes)
        i = self.dnext[q]
        self.dnext[q] = (i + 1) % NDS
        key = (q, i)
        if self.dcnt[q][i] > 0:
            toks.add((key, self.dcnt[q][i]))
        ins = self._wait(q, toks, lambda: self.e[q].dma_start(out=out, in_=in_))
        ins.then_inc(self.dsem[q][i], 16)
        self.dcnt[q][i] += 16
        v = self.dcnt[q][i]
        self.hist[(key, v)] = dict(self.known[q])
        for b in reads:
            b.r[key] = v
        for b in writes:
            b.w = (key, v)
            b.r = {}

    def all_tokens(self):
        toks = {(k, self.cnt[k]) for k in ENG if self.cnt[k] > 0}
        for q in ('sp', 'pool'):
            for i, c in enumerate(self.dcnt[q]):
                if c > 0:
                    toks.add(((q, i), c))
        return toks

    def barrier(self):
        if _DEV['off']:
            return
        for k in ENG:
            assert not self.pend[k]
        toks = self.all_tokens()
        for eng in ENG:
            self._wait(eng, {t for t in toks if t[0] != eng})


def build_program():
    nc = bass.Bass("TRN2", target_bir_lowering=False)

    def din(name, shape):
        return nc.dram_tensor(name, list(shape), F32, kind="ExternalInput").ap()

    def dout(name, shape):
        return nc.dram_tensor(name, list(shape), F32, kind="ExternalOutput").ap()

    xin = din("xin", [NTOK, D])
    c_ckv = din("c_ckv", [L, NS, PAST, 128])
    c_kr = din("c_kr", [L, NS, PAST, 32])
    c_sbk = din("c_sbk", [L, NS, PAST, 512])
    c_sbv = din("c_sbv", [L, NS, PAST, 512])
    c_dk = din("c_dk", [L, NS, PAST, 512])
    c_dv = din("c_dv", [L, NS, PAST, 512])
    c_conv = din("c_conv", [L, NS, 128, 22 * 2])
    W = {}
    for name, shape in [("mix_norm_g", [L, D]), ("w_in", [L, D, NIN]), ("mla_q_norm_g", [L, 256]),
                        ("mla_w_uq", [L, 256, 768]), ("mla_kv_norm_g", [L, 128]), ("mla_w_uk", [L, 128, 512]),
                        ("mla_w_uv", [L, 128, 512]), ("mla_qn_g", [L, 64]), ("mla_kn_g", [L, 64]),
                        ("mla_qr_g", [L, 32]), ("mla_kr_g", [L, 32]), ("diff_qn_g", [L, 64]),
                        ("diff_kn_g", [L, 64]), ("diff_lambda", [L, 256]), ("diff_subln_g", [L, 128]),
                        ("w_br_mla", [L, 512, D]), ("w_br_sb", [L, 512, D]), ("w_br_diff", [L, 512, D]),
                        ("w_out", [L, D, D]), ("ffn_norm_g", [L, D]), ("ffn_w_up", [L, D, 2 * DFF]),
                        ("ffn_conv_w", [L, 128, 22 * 3]), ("ffn_conv_b", [L, 128, 22]), ("ffn_w_down", [L, DFF, D])]:
        W[name] = din(name, shape)
    k_cs = din("k_cs", [NTOK, 32])
    k_msb = din("k_msb", [128, 128])
    k_mch = din("k_mch", [128, 128])
    k_mdf = din("k_mdf", [128, 4 * 128])
    k_tri = din("k_tri", [128, 128])
    k_bdp = din("k_bdp", [128, 4 * 17])
    k_bds = din("k_bds", [128, 4 * 33])

    y = dout("y", [NTOK, D])
    o_ckv = dout("o_ckv", [L, NTOK, 128])
    o_kr = dout("o_kr", [L, NTOK, 32])
    o_sbk = dout("o_sbk", [L, NTOK, 512])
    o_sbv = dout("o_sbv", [L, NTOK, 512])
    o_dk = dout("o_dk", [L, NTOK, 512])
    o_dv = dout("o_dv", [L, NTOK, 512])
    o_conv = dout("o_conv", [L, 3, 128, 22 * 2])

    with ExitStack() as es:
        E = es.enter_context
        S = Sync(nc, es)
        cnt = [0]

        def sb(es_, shape, dt, n=1):
            cnt[0] += 1
            return TT(es_.enter_context(nc.sbuf_tensor("t%d" % cnt[0], list(shape), dt)), n)

        X = sb(es, [128, NT, D], F32, NT)
        HT = sb(es, [128, 8, NTOK], BF16, NT)
        OT = sb(es, [128, 4, NTOK], BF16, 5)
        ident = sb(es, [128, 128], BF16)
        ones = sb(es, [128, 128], BF16)
        zeros = sb(es, [128, 128], BF16)
        onesf = sb(es, [128, 128], F32)
        tri = sb(es, [128, 128], BF16)
        msb = sb(es, [128, 128], F32)
        mch = sb(es, [128, 128], F32)
        mdf = sb(es, [128, 4, 128], F32)
        bdp = sb(es, [128, 4, 17], F32)
        bds = sb(es, [128, 4, 33], F32)
        cs = sb(es, [128, NT, 32], F32)
        gbig = sb(es, [128, D], F32)
        gsm = sb(es, [128, 1024], F32)
        gcol = sb(es, [128, 8], F32)
        PS = [TT(E(nc.psum_tensor("ps%d" % i, [128, 512], F32))) for i in range(8)]
        for p_ in PS:
            p_.b[0].excl = True
        rot = {'mm': 0, 'tp': 0}

        def mmbank():
            rot['mm'] = (rot['mm'] + 1) % len(mm_list[0])
            return PS[mm_list[0][rot['mm']]]

        def tpbank():
            rot['tp'] ^= 1
            return PS[2 + rot['tp']]

        def tsl(tt):
            if tt < 16:
                return 128, slice(tt * 128, tt * 128 + 128)
            return 32, slice(2048 + 32 * (tt - 16), 2048 + 32 * (tt - 16) + 32)

        GROUPS = [(g * 512, 512, [4 * g + i for i in range(4)]) for g in range(4)] + [(2048, 64, [16, 17])]
        mm_list = [[0, 1]]

        S.op('pool', lambda e: e.memset(ident[:], 0.0), [], ident.b)
        S.op('pool', lambda e: e.affine_select(out=ident[:], in_=ident[:], pattern=[[-1, 128]], compare_op=ALU.not_equal,
                                               fill=1.0, base=0, channel_multiplier=1), ident.b, ident.b)
        S.op('pool', lambda e: e.memset(ones[:], 1.0), [], ones.b)
        S.op('pool', lambda e: e.memset(zeros[:], 0.0), [], zeros.b)
        S.op('pool', lambda e: e.memset(onesf[:], 1.0), [], onesf.b)
        S.dma('pool', tri[:], k_tri, [], tri.b)
        S.dma('sp', msb[:], k_msb, [], msb.b)
        S.dma('sp', mch[:], k_mch, [], mch.b)
        S.dma('sp', mdf[:].rearrange("p a b -> p (a b)"), k_mdf, [], mdf.b)
        S.dma('sp', bdp[:].rearrange("p a b -> p (a b)"), k_bdp, [], bdp.b)
        S.dma('sp', bds[:].rearrange("p a b -> p (a b)"), k_bds, [], bds.b)
        S.dma('sp', cs[:, 0:16, :], k_cs[0:2048, :].rearrange("(t p) n -> p t n", p=128), [], cs.b)
        S.dma('sp', cs[0:32, 16, :], k_cs[2048:2080, :], [], cs.b)
        S.dma('sp', cs[0:32, 17, :], k_cs[2080:2112, :], [], cs.b)
        zrhs = sb(es, [128, 512], BF16)
        S.op('pool', lambda e: e.memset(zrhs[:], 0.0), [], zrhs.b)

        GS = dict(q_norm=(0, 256), kv_norm=(256, 128), kr=(384, 32), qn=(416, 64), kn=(480, 64), qr=(544, 32),
                  dqn=(576, 64), dkn=(640, 64), lam=(704, 256))

        def gs(name, rows=128):
            o, n = GS[name]
            return gsm[0:rows, o:o + n]

        def rstd_from_ss(ss_t, rows, G, d):
            S.op('act', lambda e: e.activation(out=ss_t[0:rows, 0:G], in_=ss_t[0:rows, 0:G], func=AF.Ln, scale=1.0 / d, bias=EPS),
                 ss_t.b, ss_t.b)
            S.op('act', lambda e: e.activation(out=ss_t[0:rows, 0:G], in_=ss_t[0:rows, 0:G], func=AF.Exp, scale=-0.5),
                 ss_t.b, ss_t.b)

        def gnorm(ws, src, srcb, rows, G, d, gain, out, outb, post_scale=1.0):
            sq, ss, tmp = ws['sq'], ws['ss'], ws['tmp']
            sqv = sq[0:rows, 0:G * d].rearrange("p (g d) -> p g d", d=d)
            S.op('act', lambda e: e.activation(out=sqv, in_=src, func=AF.Square), srcb, sq.b)
            S.op('dve', lambda e: e.tensor_reduce(out=ss[0:rows, 0:G], in_=sqv, axis=AX.X, op=ALU.add), sq.b, ss.b)
            rstd_from_ss(ss, rows, G, d)
            tv = tmp[0:rows, 0:G * d].rearrange("p (g d) -> p g d", d=d)
            S.op('dve', lambda e: e.tensor_tensor(out=tv, in0=src, in1=ss[0:rows, 0:G].unsqueeze(2).to_broadcast([rows, G, d]),
                                                  op=ALU.mult), list(srcb) + ss.b, tmp.b)
            gb = gain.unsqueeze(1).to_broadcast([rows, G, d])
            if post_scale == 1.0:
                S.op('dve', lambda e: e.tensor_tensor(out=out, in0=tv, in1=gb, op=ALU.mult), tmp.b + gsm.b, outb)
            else:
                S.op('dve', lambda e: e.scalar_tensor_tensor(out=out, in0=tv, scalar=float(post_scale), in1=gb, op0=ALU.mult,
                                                             op1=ALU.mult), tmp.b + gsm.b, outb)

        def rope(ws, src, srcb, rows, H, tt, out, outb):
            t1, t2 = ws['r1'], ws['r2']
            cosb = cs[0:rows, tt, 0:16].unsqueeze(1).to_broadcast([rows, H, 16])
            sinb = cs[0:rows, tt, 16:32].unsqueeze(1).to_broadcast([rows, H, 16])
            a1 = t1[0:rows, 0:H * 16].rearrange("p (h d) -> p h d", d=16)
            a2 = t2[0:rows, 0:H * 16].rearrange("p (h d) -> p h d", d=16)
            x1 = src[:, :, 0:16]
            x2 = src[:, :, 16:32]
            S.op('dve', lambda e: e.tensor_tensor(out=a1, in0=x1, in1=cosb, op=ALU.mult), list(srcb) + cs.b, t1.b)
            S.op('dve', lambda e: e.tensor_tensor(out=a2, in0=x2, in1=sinb, op=ALU.mult), list(srcb) + cs.b, t2.b)
            S.op('dve', lambda e: e.tensor_tensor(out=out[:, :, 0:16], in0=a1, in1=a2, op=ALU.subtract), t1.b + t2.b, outb)
            S.op('dve', lambda e: e.tensor_tensor(out=a1, in0=x1, in1=sinb, op=ALU.mult), list(srcb) + cs.b, t1.b)
            S.op('dve', lambda e: e.tensor_tensor(out=a2, in0=x2, in1=cosb, op=ALU.mult), list(srcb) + cs.b, t2.b)
            S.op('dve', lambda e: e.tensor_tensor(out=out[:, :, 16:32], in0=a1, in1=a2, op=ALU.add), t1.b + t2.b, outb)

        def transpose_to(src, srcb, rows, ncols, dst, dstb, eng='act'):
            pb = tpbank()
            pv = pb.h[:, :].bitcast(BF16)
            S.op('pe', lambda e: e.transpose(out=pv[0:ncols, 0:rows], in_=src, identity=ident[0:rows, 0:rows]), list(srcb) + ident.b, pb.b)
            if eng == 'act':
                S.op('act', lambda e: e.copy(out=dst, in_=pv[0:ncols, 0:rows]), pb.b, dstb)
            else:
                S.op('dve', lambda e: e.tensor_copy(out=dst, in_=pv[0:ncols, 0:rows]), pb.b, dstb)

        def load_w(dst, src_ap):
            S.dma('pool', dst.h[:], src_ap, [], dst.b)

        def norm_phase(l, gname, first):
            with ExitStack() as ps:
                sq = sb(ps, [128, D], F32)
                ssr = [sb(ps, [128, 1], F32) for _ in range(2)]
                hb = [sb(ps, [128, D], BF16) for _ in range(2)]
                S.dma('sp', gbig[:], W[gname][l].partition_broadcast(128), [], gbig.b)
                for tt in range(NT):
                    rows, sl = tsl(tt)
                    if first:
                        S.dma('sp', X[0:rows, tt, :], xin[sl, :], [], [X.b[tt]])
                    ss = ssr[tt % 2]
                    h = hb[tt % 2]
                    S.op('act', lambda e: e.activation(out=sq[0:rows, :], in_=X[0:rows, tt, :], func=AF.Square,
                                                       accum_out=ss[0:rows, :]), [X.b[tt]], sq.b + ss.b)
                    rstd_from_ss(ss, rows, 1, D)
                    S.op('dve', lambda e: e.scalar_tensor_tensor(out=h[0:rows, :], in0=X[0:rows, tt, :], scalar=ss[0:rows, 0:1],
                                                                 in1=gbig[0:rows, :], op0=ALU.mult, op1=ALU.mult),
                         [X.b[tt]] + ss.b + gbig.b, h.b)
                    pb = tpbank()
                    pv = pb.h[:, :].bitcast(BF16).rearrange("p (k n) -> p k n", n=128)
                    for k in range(8):
                        S.op('pe', lambda e: e.transpose(out=pv[:, k, 0:rows], in_=h[0:rows, k * 128:(k + 1) * 128],
                                                         identity=ident[0:rows, 0:rows]), h.b + ident.b, pb.b, inc=(k == 7))
                    S.op('act' if tt % 2 else 'dve',
                         (lambda e: e.copy(out=HT[:, :, sl], in_=pv[:, :, 0:rows])) if tt % 2 else
                         (lambda e: e.tensor_copy(out=HT[:, :, sl], in_=pv[:, :, 0:rows])), pb.b, [HT.b[tt]])
                S.barrier()

        def _layers():
          for l in range(L):
            lam_init = 0.8 - 0.6 * math.exp(-0.3 * l)
            for nm, wn in [('q_norm', 'mla_q_norm_g'), ('kv_norm', 'mla_kv_norm_g'), ('kr', 'mla_kr_g'), ('qn', 'mla_qn_g'),
                           ('kn', 'mla_kn_g'), ('qr', 'mla_qr_g'), ('dqn', 'diff_qn_g'), ('dkn', 'diff_kn_g'),
                           ('lam', 'diff_lambda')]:
                S.dma('sp', gs(nm), W[wn][l].partition_broadcast(128), [], gsm.b)
            S.dma('sp', gcol[:, 0:1], W['diff_subln_g'][l].rearrange("(p o) -> p o", o=1), [], gcol.b)
            with ExitStack() as ps:
                lt = sb(ps, [128, 128], F32)
                l2 = sb(ps, [128, 2], F32)
                lv = gs('lam')
                S.op('dve', lambda e: e.tensor_tensor(out=lt[:, 0:64], in0=lv[:, 0:64], in1=lv[:, 64:128], op=ALU.mult), gsm.b, lt.b)
                S.op('dve', lambda e: e.tensor_tensor(out=lt[:, 64:128], in0=lv[:, 128:192], in1=lv[:, 192:256], op=ALU.mult), gsm.b, lt.b)
                S.op('dve', lambda e: e.tensor_reduce(out=l2[:, 0:2], in_=lt[:, :].rearrange("p (a b) -> p a b", b=64), axis=AX.X,
                                                      op=ALU.add), lt.b, l2.b)
                S.op('act', lambda e: e.activation(out=l2[:, 0:2], in_=l2[:, 0:2], func=AF.Exp), l2.b, l2.b)
                S.op('dve', lambda e: e.tensor_tensor(out=gcol[:, 1:2], in0=l2[:, 0:1], in1=l2[:, 1:2], op=ALU.subtract), l2.b, gcol.b)
                S.op('dve', lambda e: e.tensor_scalar(out=gcol[:, 1:2], in0=gcol[:, 1:2], scalar1=float(lam_init), scalar2=None,
                                                      op0=ALU.add), gcol.b, gcol.b)
                S.op('dve', lambda e: e.tensor_scalar(out=gcol[:, 2:3], in0=gcol[:, 1:2], scalar1=-1.0, scalar2=None,
                                                      op0=ALU.mult), gcol.b, gcol.b)
                S.op('dve', lambda e: e.tensor_scalar(out=gcol[:, 0:1], in0=gcol[:, 0:1], scalar1=float(1.0 - lam_init), scalar2=None,
                                                      op0=ALU.mult), gcol.b, gcol.b)
                S.barrier()

            norm_phase(l, 'mix_norm_g', l == 0)
            _ck('norm%d' % l)

            for mixer in ('mla', 'sb', 'diff'):
                with ExitStack() as ms:
                    if mixer == 'mla':
                        CQT = sb(ms, [128, 2, NTOK], BF16, NT)
                        CKVT = sb(ms, [128, NTOK], BF16, NT)
                        KRT = sb(ms, [128, NT, 32], BF16, NT)
                        nunits, hpu = 4, 2
                    elif mixer == 'sb':
                        nunits, hpu = 2, 4
                    else:
                        nunits, hpu = 2, 2
                    for u in range(nunits):
                        with ExitStack() as us:
                            ws = dict(sq=sb(us, [128, 512], F32), ss=sb(us, [128, 8], F32), tmp=sb(us, [128, 512], F32),
                                      r1=sb(us, [128, 128], F32), r2=sb(us, [128, 128], F32))
                            stage = [sb(us, [128, 256], F32) for _ in range(3)]
                            stg_i = [0]

                            def nstage():
                                stg_i[0] = (stg_i[0] + 1) % 3
                                return stage[stg_i[0]]
                            tokb = [sb(us, [128, 256], BF16) for _ in range(3)]
                            tok_i = [0]

                            def ntok():
                                tok_i[0] = (tok_i[0] + 1) % 3
                                return tokb[tok_i[0]]
                            PT = [sb(us, [128, 512], BF16) for _ in range(2)]
                            pt_i = [0]

                            def npt():
                                pt_i[0] ^= 1
                                return PT[pt_i[0]]
                            rden = sb(us, [128, 512], F32)
                            if mixer == 'mla':
                                KT = sb(us, [96, 2, NTOK], BF16, NT)
                                V = sb(us, [128, NT, 128], BF16, NT)
                                QT = sb(us, [96, 2, 512], BF16)
                                if u == 0:
                                    w1 = sb(us, [128, 8, 416], BF16)
                                    load_w(w1, W['w_in'][l, :, 0:416].rearrange("(k p) n -> p k n", p=128))
                                wuq = sb(us, [128, 2, 192], BF16)
                                load_w(wuq, W['mla_w_uq'][l, :, u * 192:(u + 1) * 192].rearrange("(k p) n -> p k n", p=128))
                                wuk = sb(us, [128, 128], BF16)
                                load_w(wuk, W['mla_w_uk'][l, :, u * 128:(u + 1) * 128])
                                wuv = sb(us, [128, 128], BF16)
                                load_w(wuv, W['mla_w_uv'][l, :, u * 128:(u + 1) * 128])
                                kcat = [sb(us, [128, 2, 96], BF16) for _ in range(2)]
                                qcat = [sb(us, [128, 2, 96], BF16) for _ in range(2)]
                                pck = sb(us, [128, 4, 128], BF16)
                                pkr = sb(us, [128, 4, 32], BF16)
                                pckT = sb(us, [128, 512], BF16)
                                KTp = sb(us, [96, 2, 512], BF16)
                                Vp = sb(us, [128, 4, 128], BF16)
                            else:
                                c0q = OFF['sq' if mixer == 'sb' else 'dq'] + u * 256
                                c0k = OFF['sk' if mixer == 'sb' else 'dk'] + u * 256
                                c0v = OFF['sv' if mixer == 'sb' else 'dv'] + u * 256
                                wq = sb(us, [128, 8, 256], BF16)
                                wk = sb(us, [128, 8, 256], BF16)
                                wv = sb(us, [128, 8, 256], BF16)
                                load_w(wq, W['w_in'][l, :, c0q:c0q + 256].rearrange("(k p) n -> p k n", p=128))
                                load_w(wk, W['w_in'][l, :, c0k:c0k + 256].rearrange("(k p) n -> p k n", p=128))
                                load_w(wv, W['w_in'][l, :, c0v:c0v + 256].rearrange("(k p) n -> p k n", p=128))
                                KT = sb(us, [128, 2, NTOK], BF16, NT)
                                V = sb(us, [128, NT, 256], BF16, NT)
                                QT = sb(us, [128, 2, 512], BF16)
                                pk = sb(us, [128, 4, 256], BF16)
                                KTp = sb(us, [128, 2, 512], BF16)
                                Vp = sb(us, [128, 4, 256], BF16)
                                if mixer == 'sb':
                                    R = sb(us, [128, 2, 512], F32)
                                    e1 = [sb(us, [128, 512], F32) for _ in range(2)]
                                    spb = [sb(us, [128, 512], BF16) for _ in range(2)]
                                    cumr = sb(us, [128, 512], F32)
                                else:
                                    t0 = sb(us, [128, 512], F32)
                                    t1 = sb(us, [128, 512], F32)
                                    rden1 = sb(us, [128, 512], F32)
                                    osq = sb(us, [128, 512], F32)
                            o_k = o_sbk if mixer == 'sb' else o_dk
                            o_v = o_sbv if mixer == 'sb' else o_dv

                            def proj_tile(tt, qcol0):
                                rows, sl = tsl(tt)
                                if mixer == 'mla':
                                    if u == 0:
                                        pz = mmbank()
                                        for k in range(8):
                                            S.op('pe', lambda e: e.matmul(out=pz[0:rows, 0:416], lhsT=HT[:, k, sl], rhs=w1[:, k, :],
                                                                          start=(k == 0), stop=(k == 7)), [HT.b[tt]] + w1.b, pz.b, inc=(k == 7))
                                        tk = ntok()
                                        gnorm(ws, pz[0:rows, 0:256].rearrange("p (g d) -> p g d", g=1), pz.b, rows, 1, 256, gs('q_norm', rows),
                                              tk[0:rows, 0:256].rearrange("p (g d) -> p g d", g=1), tk.b)
                                        for c in range(2):
                                            transpose_to(tk[0:rows, c * 128:(c + 1) * 128], tk.b, rows, 128, CQT[:, c, sl], [CQT.b[tt]],
                                                         'act' if c else 'dve')
                                        st = nstage()
                                        gnorm(ws, pz[0:rows, 256:384].rearrange("p (g d) -> p g d", g=1), pz.b, rows, 1, 128, gs('kv_norm', rows),
                                              st[0:rows, 0:128].rearrange("p (g d) -> p g d", g=1), st.b)
                                        S.dma('sp', o_ckv[l, sl, :], st[0:rows, 0:128], st.b, [])
                                        tk2 = ntok()
                                        S.op('act', lambda e: e.copy(out=tk2[0:rows, 0:128], in_=st[0:rows, 0:128]), st.b, tk2.b)
                                        transpose_to(tk2[0:rows, 0:128], tk2.b, rows, 128, CKVT[:, sl], [CKVT.b[tt]], 'dve')
                                        gnorm(ws, pz[0:rows, 384:416].rearrange("p (g d) -> p g d", g=1), pz.b, rows, 1, 32, gs('kr', rows),
                                              st[0:rows, 128:160].rearrange("p (g d) -> p g d", g=1), st.b)
                                        rope(ws, st[0:rows, 128:160].rearrange("p (g d) -> p g d", g=1), st.b, rows, 1, tt,
                                             st[0:rows, 160:192].rearrange("p (g d) -> p g d", g=1), st.b)
                                        S.dma('sp', o_kr[l, sl, :], st[0:rows, 160:192], st.b, [])
                                        S.op('act', lambda e: e.copy(out=KRT[0:rows, tt, :], in_=st[0:rows, 160:192]), st.b, [KRT.b[tt]])
                                    pq = mmbank()
                                    for c in range(2):
                                        S.op('pe', lambda e: e.matmul(out=pq[0:rows, 0:192], lhsT=CQT[:, c, sl], rhs=wuq[:, c, :],
                                                                      start=(c == 0), stop=(c == 1)), [CQT.b[tt]] + wuq.b, pq.b, inc=(c == 1))
                                    qv = pq[0:rows, 0:192].rearrange("p (h d) -> p h d", d=96)
                                    qc = qcat[tt % 2]
                                    gnorm(ws, qv[:, :, 0:64], pq.b, rows, 2, 64, gs('qn', rows), qc[0:rows, :, 0:64], qc.b, MLA_SCALE)
                                    st = nstage()
                                    stv = st[0:rows, 0:64].rearrange("p (h d) -> p h d", d=32)
                                    gnorm(ws, qv[:, :, 64:96], pq.b, rows, 2, 32, gs('qr', rows), stv, st.b, MLA_SCALE)
                                    rope(ws, stv, st.b, rows, 2, tt, qc[0:rows, :, 64:96], qc.b)
                                    for hh in range(2):
                                        transpose_to(qc[0:rows, hh, :], qc.b, rows, 96, QT[:, hh, qcol0:qcol0 + rows], QT.b, 'act' if hh else 'dve')
                                    kv_from_latent(CKVT[:, sl], [CKVT.b[tt]], KRT[0:rows, tt, :], [KRT.b[tt]], rows,
                                                   lambda hh: KT[:, hh, sl], [KT.b[tt]], V[0:rows, tt, :], [V.b[tt]], kcat[tt % 2])
                                else:
                                    pq = mmbank()
                                    for k in range(8):
                                        S.op('pe', lambda e: e.matmul(out=pq[0:rows, 0:256], lhsT=HT[:, k, sl], rhs=wq[:, k, :],
                                                                      start=(k == 0), stop=(k == 7)), [HT.b[tt]] + wq.b, pq.b, inc=(k == 7))
                                    tk = ntok()
                                    if mixer == 'sb':
                                        S.op('act', lambda e: e.activation(out=tk[0:rows, :], in_=pq[0:rows, 0:256], func=AF.Copy,
                                                                           scale=SB_SCALE), pq.b, tk.b)
                                    else:
                                        gnorm(ws, pq[0:rows, 0:256].rearrange("p (g d) -> p g d", d=64), pq.b, rows, 4, 64, gs('dqn', rows),
                                              tk[0:rows, :].rearrange("p (g d) -> p g d", d=64), tk.b, DIFF_SCALE)
                                    for c in range(2):
                                        transpose_to(tk[0:rows, c * 128:(c + 1) * 128], tk.b, rows, 128, QT[:, c, qcol0:qcol0 + rows], QT.b,
                                                     'act' if c else 'dve')
                                    pk_ = mmbank()
                                    for k in range(8):
                                        S.op('pe', lambda e: e.matmul(out=pk_[0:rows, 0:256], lhsT=HT[:, k, sl], rhs=wk[:, k, :],
                                                                      start=(k == 0), stop=(k == 7)), [HT.b[tt]] + wk.b, pk_.b, inc=(k == 7))
                                    st = nstage()
                                    if mixer == 'sb':
                                        S.op('act', lambda e: e.copy(out=st[0:rows, :], in_=pk_[0:rows, 0:256]), pk_.b, st.b)
                                    else:
                                        gnorm(ws, pk_[0:rows, 0:256].rearrange("p (g d) -> p g d", d=64), pk_.b, rows, 4, 64, gs('dkn', rows),
                                              st[0:rows, :].rearrange("p (g d) -> p g d", d=64), st.b)
                                    S.dma('sp', o_k[l, sl, u * 256:(u + 1) * 256], st[0:rows, :], st.b, [])
                                    tk = ntok()
                                    S.op('dve', lambda e: e.tensor_copy(out=tk[0:rows, :], in_=st[0:rows, :]), st.b, tk.b)
                                    for c in range(2):
                                        transpose_to(tk[0:rows, c * 128:(c + 1) * 128], tk.b, rows, 128, KT[:, c, sl], [KT.b[tt]],
                                                     'act' if c else 'dve')
                                    pv_ = mmbank()
                                    for k in range(8):
                                        S.op('pe', lambda e: e.matmul(out=pv_[0:rows, 0:256], lhsT=HT[:, k, sl], rhs=wv[:, k, :],
                                                                      start=(k == 0), stop=(k == 7)), [HT.b[tt]] + wv.b, pv_.b, inc=(k == 7))
                                    st = nstage()
                                    S.op('act', lambda e: e.copy(out=st[0:rows, :], in_=pv_[0:rows, 0:256]), pv_.b, st.b)
                                    S.dma('sp', o_v[l, sl, u * 256:(u + 1) * 256], st[0:rows, :], st.b, [])
                                    S.op('dve', lambda e: e.tensor_copy(out=V[0:rows, tt, :], in_=pv_[0:rows, 0:256]), pv_.b, [V.b[tt]])

                            def kv_from_latent(ckvT_ap, ckvT_b, kr_ap, kr_b, rows, kt_dst, kt_b, v_dst, v_b, kc):
                                pk_ = mmbank()
                                S.op('pe', lambda e: e.matmul(out=pk_[0:rows, 0:128], lhsT=ckvT_ap, rhs=wuk[:, :], start=True, stop=True),
                                     list(ckvT_b) + wuk.b, pk_.b)
                                gnorm(ws, pk_[0:rows, 0:128].rearrange("p (h d) -> p h d", d=64), pk_.b, rows, 2, 64, gs('kn', rows),
                                      kc[0:rows, :, 0:64], kc.b)
                                S.op('dve', lambda e: e.tensor_copy(out=kc[0:rows, :, 64:96], in_=kr_ap.unsqueeze(1).to_broadcast([rows, 2, 32])),
                                     list(kr_b), kc.b)
                                for hh in range(2):
                                    transpose_to(kc[0:rows, hh, :], kc.b, rows, 96, kt_dst(hh), kt_b, 'act' if hh else 'dve')
                                pv_ = mmbank()
                                S.op('pe', lambda e: e.matmul(out=pv_[0:rows, 0:128], lhsT=ckvT_ap, rhs=wuv[:, :], start=True, stop=True),
                                     list(ckvT_b) + wuv.b, pv_.b)
                                S.op('act', lambda e: e.copy(out=v_dst, in_=pv_[0:rows, 0:128]), pv_.b, v_b)

                            acc, accd, acc1, accd1 = PS[4], PS[5], PS[6], PS[7]

                            def zero_acc(t_, ncols):
                                S.op('pe', lambda e: e.matmul(out=t_[:, 0:ncols], lhsT=zeros[:, :], rhs=zrhs[:, 0:ncols], start=True, stop=True),
                                     zeros.b + zrhs.b, t_.b)

                            def att_begin(hg, ncols):
                                if mixer == 'mla':
                                    zero_acc(acc, ncols)
                                    zero_acc(accd, ncols)
                                elif mixer == 'sb':
                                    zero_acc(acc, ncols)
                                    S.op('dve', lambda e: e.memset(R[:, :, 0:ncols], 0.0), [], R.b)
                                else:
                                    for t_ in (acc, accd, acc1, accd1):
                                        zero_acc(t_, ncols)

                            def diag_mask(t_, mask_ap, mask_b, nk, c0, ncols):
                                dn = min(128, ncols - c0)
                                S.op('dve', lambda e: e.tensor_tensor(out=t_[0:nk, c0:c0 + dn], in0=t_[0:nk, c0:c0 + dn], in1=mask_ap(nk, dn),
                                                                      op=ALU.mult), t_.b + mask_b, t_.b)

                            def att_blocks(hg, kbs, qc0, ncols, sample):
                                for kb in kbs:
                                    nk, c0 = kb['nk'], kb['c0']
                                    if mixer == 'mla':
                                        for hh in range(2):
                                            st = mmbank()
                                            S.op('pe', lambda e: e.matmul(out=st[0:nk, c0:ncols], lhsT=kb['kt'](hh), rhs=QT[:, hh, qc0 + c0:qc0 + ncols],
                                                                          start=True, stop=True), kb['ktb'] + QT.b, st.b)
                                            pt = npt()
                                            S.op('act', lambda e: e.activation(out=pt[0:nk, c0:ncols], in_=st[0:nk, c0:ncols], func=AF.Exp), st.b, pt.b)
                                            if kb['diag']:
                                                diag_mask(pt, lambda a, b: mch[0:a, 0:b], mch.b, nk, c0, ncols)
                                            S.op('pe', lambda e: e.matmul(out=acc[64 * hh:64 * hh + 64, c0:ncols], lhsT=kb['v'](hh),
                                                                          rhs=pt[0:nk, c0:ncols], start=False, stop=True, skip_group_check=True),
                                                 kb['vb'] + pt.b, acc.b)
                                            S.op('pe', lambda e: e.matmul(out=accd[64 * hh:64 * hh + 64, c0:ncols], lhsT=ones[0:nk, 0:64],
                                                                          rhs=pt[0:nk, c0:ncols], start=False, stop=True, skip_group_check=True),
                                                 ones.b + pt.b, accd.b)
                                    elif mixer == 'sb':
                                        hp = hg
                                        for hh in range(2):
                                            st = mmbank()
                                            S.op('pe', lambda e: e.matmul(out=st[0:nk, c0:ncols], lhsT=kb['kt'](hp)[64 * hh:64 * hh + 64, :],
                                                                          rhs=QT[64 * hh:64 * hh + 64, hp, qc0 + c0:qc0 + ncols], start=True, stop=True),
                                                 kb['ktb'] + QT.b, st.b)
                                            ee = e1[rot['mm'] % 2]
                                            sp_ = spb[rot['mm'] % 2]
                                            S.op('act', lambda e: e.activation(out=ee[0:nk, c0:ncols], in_=st[0:nk, c0:ncols], func=AF.Exp), st.b, ee.b)
                                            S.op('act', lambda e: e.activation(out=sp_[0:nk, c0:ncols], in_=ee[0:nk, c0:ncols], func=AF.Ln, bias=1.0),
                                                 ee.b, sp_.b)
                                            if kb['diag']:
                                                diag_mask(sp_, lambda a, b: msb[0:a, 0:b], msb.b, nk, c0, ncols)
                                            S.op('pe', lambda e: e.matmul(out=acc1[0:nk, c0:ncols], lhsT=tri[0:nk, 0:nk], rhs=sp_[0:nk, c0:ncols],
                                                                          start=True, stop=True), tri.b + sp_.b, acc1.b)
                                            S.op('pe', lambda e: e.matmul(out=accd1[:, c0:ncols], lhsT=ones[0:nk, :], rhs=sp_[0:nk, c0:ncols],
                                                                          start=True, stop=True), ones.b + sp_.b, accd1.b)
                                            S.op('dve', lambda e: e.tensor_tensor(out=cumr[0:nk, c0:ncols], in0=acc1[0:nk, c0:ncols],
                                                                                  in1=R[0:nk, hh, c0:ncols], op=ALU.add), acc1.b + R.b, cumr.b)
                                            S.op('act', lambda e: e.activation(out=cumr[0:nk, c0:ncols], in_=cumr[0:nk, c0:ncols], func=AF.Exp,
                                                                               scale=-1.0), cumr.b, cumr.b)
                                            pt = npt()
                                            S.op('dve', lambda e: e.tensor_tensor(out=pt[0:nk, c0:ncols], in0=ee[0:nk, c0:ncols],
                                                                                  in1=cumr[0:nk, c0:ncols], op=ALU.mult), ee.b + cumr.b, pt.b)
                                            if kb['diag']:
                                                diag_mask(pt, lambda a, b: msb[0:a, 0:b], msb.b, nk, c0, ncols)
                                            S.op('dve', lambda e: e.tensor_tensor(out=R[:, hh, c0:ncols], in0=R[:, hh, c0:ncols], in1=accd1[:, c0:ncols],
                                                                                  op=ALU.add), R.b + accd1.b, R.b)
                                            S.op('pe', lambda e: e.matmul(out=acc[64 * hh:64 * hh + 64, c0:ncols], lhsT=kb['v'](2 * hp + hh),
                                                                          rhs=pt[0:nk, c0:ncols], start=False, stop=True, skip_group_check=True),
                                                 kb['vb'] + pt.b, acc.b)
                                    else:
                                        hd = hg
                                        hgl = 2 * u + hd
                                        for m in range(2):
                                            st = mmbank()
                                            S.op('pe', lambda e: e.matmul(out=st[0:nk, c0:ncols], lhsT=kb['kt'](hd)[64 * m:64 * m + 64, :],
                                                                          rhs=QT[64 * m:64 * m + 64, hd, qc0 + c0:qc0 + ncols], start=True, stop=True),
                                                 kb['ktb'] + QT.b, st.b)
                                            pt = npt()
                                            if sample:
                                                bi = kb['bi']
                                                S.op('act', lambda e: e.activation(out=pt[0:nk, c0:ncols], in_=st[0:nk, c0:ncols], func=AF.Exp,
                                                                                   bias=bds[0:nk, hgl, bi:bi + 1]), st.b + bds.b, pt.b)
                                            else:
                                                cc = c0
                                                while cc < ncols:
                                                    ce = min(ncols, (cc // 256 + 1) * 256)
                                                    dl = (kb['qt0'] + ce // 128 - 1) - kb['kbi']
                                                    S.op('act', lambda e: e.activation(out=pt[0:nk, cc:ce], in_=st[0:nk, cc:ce], func=AF.Exp,
                                                                                       bias=bdp[0:nk, hgl, dl:dl + 1]), st.b + bdp.b, pt.b)
                                                    cc = ce
                                            if kb['diag']:
                                                diag_mask(pt, lambda a, b: mdf[0:a, hgl, 0:b], mdf.b, nk, c0, ncols)
                                            a_, d_ = (acc, accd) if m == 0 else (acc1, accd1)
                                            S.op('pe', lambda e: e.matmul(out=a_[:, c0:ncols], lhsT=kb['v'](hd), rhs=pt[0:nk, c0:ncols],
                                                                          start=False, stop=True, skip_group_check=True), kb['vb'] + pt.b, a_.b)
                                            S.op('pe', lambda e: e.matmul(out=d_[:, c0:ncols], lhsT=ones[0:nk, :], rhs=pt[0:nk, c0:ncols],
                                                                          start=False, stop=True, skip_group_check=True), ones.b + pt.b, d_.b)

                            def att_end(hg, ncols, ocol0, grp_b):
                                if mixer == 'mla':
                                    S.op('dve', lambda e: e.reciprocal(out=rden[:, 0:ncols], in_=accd[:, 0:ncols]), accd.b, rden.b)
                                    S.op('dve', lambda e: e.tensor_tensor(out=OT[:, u, ocol0:ocol0 + ncols], in0=acc[:, 0:ncols], in1=rden[:, 0:ncols],
                                                                          op=ALU.mult), acc.b + rden.b, grp_b)
                                elif mixer == 'sb':
                                    S.op('act', lambda e: e.copy(out=OT[:, 2 * u + hg, ocol0:ocol0 + ncols], in_=acc[:, 0:ncols]), acc.b, grp_b)
                                else:
                                    hgl = 2 * u + hg
                                    S.op('dve', lambda e: e.reciprocal(out=rden[:, 0:ncols], in_=accd[:, 0:ncols]), accd.b, rden.b)
                                    S.op('dve', lambda e: e.reciprocal(out=rden1[:, 0:ncols], in_=accd1[:, 0:ncols]), accd1.b, rden1.b)
                                    S.op('dve', lambda e: e.tensor_tensor(out=t0[:, 0:ncols], in0=acc[:, 0:ncols], in1=rden[:, 0:ncols], op=ALU.mult),
                                         acc.b + rden.b, t0.b)
                                    S.op('dve', lambda e: e.tensor_tensor(out=t1[:, 0:ncols], in0=acc1[:, 0:ncols], in1=rden1[:, 0:ncols], op=ALU.mult),
                                         acc1.b + rden1.b, t1.b)
                                    S.op('dve', lambda e: e.scalar_tensor_tensor(out=t0[:, 0:ncols], in0=t1[:, 0:ncols], scalar=gcol[:, 2:3],
                                                                                 in1=t0[:, 0:ncols], op0=ALU.mult, op1=ALU.add),
                                         t0.b + t1.b + gcol.b, t0.b)
                                    S.op('act', lambda e: e.activation(out=osq[:, 0:ncols], in_=t0[:, 0:ncols], func=AF.Square), t0.b, osq.b)
                                    S.op('pe', lambda e: e.matmul(out=acc[:, 0:ncols], lhsT=onesf[:, :], rhs=osq[:, 0:ncols], start=True, stop=True),
                                         onesf.b + osq.b, acc.b)
                                    S.op('act', lambda e: e.activation(out=t1[:, 0:ncols], in_=acc[:, 0:ncols], func=AF.Ln, scale=1.0 / 128, bias=EPS),
                                         acc.b, t1.b)
                                    S.op('act', lambda e: e.activation(out=t1[:, 0:ncols], in_=t1[:, 0:ncols], func=AF.Exp, scale=-0.5), t1.b, t1.b)
                                    S.op('dve', lambda e: e.scalar_tensor_tensor(out=OT[:, hgl, ocol0:ocol0 + ncols], in0=t0[:, 0:ncols],
                                                                                 scalar=gcol[:, 0:1], in1=t1[:, 0:ncols], op0=ALU.mult, op1=ALU.mult),
                                         t0.b + t1.b + gcol.b, grp_b)

                            nhg = 1 if mixer == 'mla' else 2

                            def kb_store(kbi, c0, diag, qt0, nk=128, bi=0):
                                rows_, sl = tsl(kbi)
                                if mixer == 'mla':
                                    return dict(kt=lambda hh: KT[:, hh, sl], ktb=[KT.b[kbi]], v=lambda hh: V[0:nk, kbi, 64 * hh:64 * hh + 64],
                                                vb=[V.b[kbi]], nk=nk, c0=c0, diag=diag, kbi=kbi, qt0=qt0, bi=bi)
                                if mixer == 'sb':
                                    return dict(kt=lambda hp: KT[:, hp, sl], ktb=[KT.b[kbi]], v=lambda h: V[0:nk, kbi, 64 * h:64 * h + 64],
                                                vb=[V.b[kbi]], nk=nk, c0=c0, diag=diag, kbi=kbi, qt0=qt0, bi=bi)
                                return dict(kt=lambda hd: KT[:, hd, sl], ktb=[KT.b[kbi]], v=lambda hd: V[0:nk, kbi, 128 * hd:128 * hd + 128],
                                            vb=[V.b[kbi]], nk=nk, c0=c0, diag=diag, kbi=kbi, qt0=qt0, bi=bi)

                            def kb_past(j, kbi):
                                sl = slice(j * 128, (j + 1) * 128)
                                bi = 32 - kbi
                                if mixer == 'mla':
                                    return dict(kt=lambda hh: KTp[:, hh, sl], ktb=KTp.b, v=lambda hh: Vp[:, j, 64 * hh:64 * hh + 64],
                                                vb=Vp.b, nk=128, c0=0, diag=False, kbi=kbi, qt0=0, bi=bi)
                                if mixer == 'sb':
                                    return dict(kt=lambda hp: KTp[:, hp, sl], ktb=KTp.b, v=lambda h: Vp[:, j, 64 * h:64 * h + 64],
                                                vb=Vp.b, nk=128, c0=0, diag=False, kbi=kbi, qt0=0, bi=bi)
                                return dict(kt=lambda hd: KTp[:, hd, sl], ktb=KTp.b, v=lambda hd: Vp[:, j, 128 * hd:128 * hd + 128],
                                            vb=Vp.b, nk=128, c0=0, diag=False, kbi=kbi, qt0=0, bi=bi)

                            def build_chunk(s, ch):
                                r0 = ch * 512
                                if mixer == 'mla':
                                    S.dma('pool', pck[:], c_ckv[l, s, r0:r0 + 512, :].rearrange("(k p) n -> p k n", p=128), [], pck.b)
                                    S.dma('pool', pkr[:], c_kr[l, s, r0:r0 + 512, :].rearrange("(k p) n -> p k n", p=128), [], pkr.b)
                                    for j in range(4):
                                        transpose_to(pck[:, j, :], pck.b, 128, 128, pckT[:, j * 128:(j + 1) * 128], pckT.b, 'act' if j % 2 else 'dve')
                                    for j in range(4):
                                        jsl = slice(j * 128, (j + 1) * 128)
                                        kv_from_latent(pckT[:, jsl], pckT.b, pkr[:, j, :], pkr.b, 128,
                                                       lambda hh: KTp[:, hh, jsl], KTp.b, Vp[:, j, :], Vp.b, kcat[j % 2])
                                else:
                                    ck = c_sbk if mixer == 'sb' else c_dk
                                    cv = c_sbv if mixer == 'sb' else c_dv
                                    S.dma('pool', pk[:], ck[l, s, r0:r0 + 512, u * 256:(u + 1) * 256].rearrange("(k p) n -> p k n", p=128), [], pk.b)
                                    S.dma('pool', Vp[:], cv[l, s, r0:r0 + 512, u * 256:(u + 1) * 256].rearrange("(k p) n -> p k n", p=128), [], Vp.b)
                                    for j in range(4):
                                        for c in range(2):
                                            transpose_to(pk[:, j, c * 128:(c + 1) * 128], pk.b, 128, 128, KTp[:, c, j * 128:(j + 1) * 128], KTp.b,
                                                         'act' if c else 'dve')

                            for g in range(4):
                                for i in range(4):
                                    proj_tile(4 * g + i, i * 128)
                                kbs = []
                                for kbi in range(4 * g + 4):
                                    i = kbi - 4 * g
                                    kbs.append(kb_store(kbi, max(i, 0) * 128, i >= 0, 4 * g))
                                if mixer == 'sb':
                                    kbs = kbs[::-1]
                                for hg in range(nhg):
                                    att_begin(hg, 512)
                                    att_blocks(hg, kbs, 0, 512, False)
                                    att_end(hg, 512, g * 512, [OT.b[g]])
                                _ck('%s%d_u%d_g%d' % (mixer, l, u, g))

                            for s in range(NS):
                                proj_tile(16 + s, 0)
                                knew = kb_store(16 + s, 0, mixer != 'mla', 0, nk=32, bi=0)
                                for hg in range(nhg):
                                    att_begin(hg, 32)
                                    if mixer == 'sb':
                                        att_blocks(hg, [knew], 0, 32, True)
                                        for ch in range(7, -1, -1):
                                            build_chunk(s, ch)
                                            att_blocks(hg, [kb_past(j, ch * 4 + j) for j in range(3, -1, -1)], 0, 32, True)
                                    else:
                                        for ch in range(8):
                                            build_chunk(s, ch)
                                            att_blocks(hg, [kb_past(j, ch * 4 + j) for j in range(4)], 0, 32, True)
                                        att_blocks(hg, [knew], 0, 32, True)
                                    att_end(hg, 32, 2048 + 32 * s, [OT.b[4]])
                                _ck('%s%d_u%d_s%d' % (mixer, l, u, s))
                            S.barrier()
                            _ck('%s%d_u%d' % (mixer, l, u))

                    mi = ('mla', 'sb', 'diff').index(mixer)
                    ms.close()
                    with ExitStack() as gs_:
                        mm_list[0] = [0, 1, 4, 5, 6, 7]
                        wg = sb(gs_, [128, 8, 1024], BF16)
                        g0 = OFF['g'] + 1024 * mi
                        load_w(wg, W['w_in'][l, :, g0:g0 + 1024].rearrange("(k p) n -> p k n", p=128))
                        wbr = sb(gs_, [128, 4, 1024], BF16)
                        load_w(wbr, W['w_br_' + mixer][l].rearrange("(k p) n -> p k n", p=128))
                        wo = sb(gs_, [128, 8, 1024], BF16)
                        load_w(wo, W['w_out'][l].rearrange("(k p) n -> p k n", p=128))
                        gT = [sb(gs_, [128, 512], F32) for _ in range(2)]
                        MG = [sb(gs_, [128, 8, 512], BF16) for _ in range(1)]
                        for gi, (c0, n, tiles) in enumerate(GROUPS):
                            hb_ = [HT.b[t] for t in tiles]
                            mg = MG[0]
                            for nn in range(8):
                                nsl = slice(nn * 128, (nn + 1) * 128)
                                pg = mmbank()
                                for k in range(8):
                                    S.op('pe', lambda e: e.matmul(out=pg[:, 0:n], lhsT=wg[:, k, nsl], rhs=HT[:, k, c0:c0 + n],
                                                                  start=(k == 0), stop=(k == 7)), wg.b + hb_, pg.b, inc=(k == 7))
                                gt = gT[nn % 2]
                                S.op('act', lambda e: e.activation(out=gt[:, 0:n], in_=pg[:, 0:n], func=AF.Sigmoid), pg.b, gt.b)
                                py = mmbank()
                                for c in range(4):
                                    S.op('pe', lambda e: e.matmul(out=py[:, 0:n], lhsT=wbr[:, c, nsl], rhs=OT[:, c, c0:c0 + n],
                                                                  start=(c == 0), stop=(c == 3)), wbr.b + [OT.b[gi]], py.b, inc=(c == 3))
                                S.op('dve', lambda e: e.tensor_tensor(out=mg[:, nn, 0:n], in0=py[:, 0:n], in1=gt[:, 0:n], op=ALU.mult),
                                     py.b + gt.b, mg.b)
                            for tt in tiles:
                                rows, sl = tsl(tt)
                                off = sl.start - c0
                                for half in range(2):
                                    hsl = slice(half * 512, (half + 1) * 512)
                                    po = mmbank()
                                    for k in range(8):
                                        S.op('pe', lambda e: e.matmul(out=po[0:rows, :], lhsT=mg[:, k, off:off + rows], rhs=wo[:, k, hsl],
                                                                      start=(k == 0), stop=(k == 7)), mg.b + wo.b, po.b, inc=(k == 7))
                                    S.op('dve', lambda e: e.tensor_tensor(out=X[0:rows, tt, hsl], in0=X[0:rows, tt, hsl], in1=po[0:rows, :],
                                                                          op=ALU.add), [X.b[tt]] + po.b, [X.b[tt]])
                        S.barrier()
                        mm_list[0] = [0, 1]
                        rot['mm'] = 0
                    _ck('%s%d_merge' % (mixer, l))

            norm_phase(l, 'ffn_norm_g', False)
            with ExitStack() as fs:
                mm_list[0] = [0, 1, 4, 5, 6, 7]
                cw = sb(fs, [128, 22, 3], F32)
                cb = sb(fs, [128, 22], F32)
                cst = sb(fs, [128, NS, 22, 2], F32)
                OC = sb(fs, [128, 3, 22, 2], F32)
                S.dma('sp', cw[:].rearrange("p a b -> p (a b)"), W['ffn_conv_w'][l], [], cw.b)
                S.dma('sp', cb[:], W['ffn_conv_b'][l], [], cb.b)
                for s in range(NS):
                    S.dma('sp', cst[:, s, :, :].rearrange("p a b -> p (a b)"), c_conv[l, s], [], cst.b)
                WA = [sb(fs, [128, 8, 512], BF16) for _ in range(2)]
                WU = [sb(fs, [128, 8, 512], BF16) for _ in range(2)]
                WD = [sb(fs, [128, 4, 1024], BF16) for _ in range(2)]
                AT = [sb(fs, [128, 4, 516], F32, 4) for _ in range(1)]
                carry = sb(fs, [128, 4, 2], F32, 4)
                cc_ = [sb(fs, [128, 512], F32) for _ in range(1)]
                sl_ = [sb(fs, [128, 512], F32) for _ in range(1)]
                MM = [sb(fs, [128, 4, 512], BF16) for _ in range(1)]
                fgroups = [(0, 4), (4, 4), (8, 4), (12, 4), (16, 4), (20, 2)]
                for fi, (fc0, nfc) in enumerate(fgroups):
                    wa, wu, wd = WA[fi % 2], WU[fi % 2], WD[fi % 2]
                    nf = nfc * 128
                    S.dma('pool', wa[:, :, 0:nf], W['ffn_w_up'][l, :, fc0 * 128:fc0 * 128 + nf].rearrange("(k p) n -> p k n", p=128), [], wa.b)
                    S.dma('pool', wu[:, :, 0:nf], W['ffn_w_up'][l, :, DFF + fc0 * 128:DFF + fc0 * 128 + nf].rearrange("(k p) n -> p k n", p=128),
                          [], wu.b)
                    S.dma('pool', wd[:, 0:nfc, :], W['ffn_w_down'][l, fc0 * 128:fc0 * 128 + nf, :].rearrange("(c p) n -> p c n", p=128), [], wd.b)
                    for gi, (c0, n, tiles) in enumerate(GROUPS):
                        hb_ = [HT.b[t] for t in tiles]
                        mmt = MM[0]
                        buf = AT[0]
                        for j in range(nfc):
                            fc = fc0 + j
                            jsl = slice(j * 128, (j + 1) * 128)
                            pa = mmbank()
                            for k in range(8):
                                S.op('pe', lambda e: e.matmul(out=pa[:, 0:n], lhsT=wa[:, k, jsl], rhs=HT[:, k, c0:c0 + n],
                                                              start=(k == 0), stop=(k == 7)), wa.b + hb_, pa.b, inc=(k == 7))
                            cc = cc_[0]
                            if gi < 4:
                                S.op('act', lambda e: e.copy(out=buf[:, j, 2:514], in_=pa[:, 0:512]), pa.b, [buf.b[j]])
                                if gi == 0:
                                    S.op('dve', lambda e: e.memset(buf[:, j, 0:2], 0.0), [], [buf.b[j]])
                                else:
                                    S.op('dve', lambda e: e.tensor_copy(out=buf[:, j, 0:2], in_=carry[:, j, :]), [carry.b[j]], [buf.b[j]])
                                S.op('dve', lambda e: e.tensor_copy(out=carry[:, j, :], in_=buf[:, j, 512:514]), [buf.b[j]], [carry.b[j]])
                                segs = [(0, 0, 512)]
                                if gi == 3:
                                    S.op('dve', lambda e: e.tensor_copy(out=OC[:, 0, fc, :], in_=buf[:, j, 512:514]), [buf.b[j]], OC.b)
                            else:
                                for s in range(NS):
                                    S.op('act', lambda e: e.copy(out=buf[:, j, s * 34 + 2:s * 34 + 34], in_=pa[:, s * 32:s * 32 + 32]), pa.b, [buf.b[j]])
                                    S.op('dve', lambda e: e.tensor_copy(out=buf[:, j, s * 34:s * 34 + 2], in_=cst[:, s, fc, :]), cst.b, [buf.b[j]])
                                    S.op('dve', lambda e: e.tensor_copy(out=OC[:, 1 + s, fc, :], in_=buf[:, j, s * 34 + 32:s * 34 + 34]), [buf.b[j]], OC.b)
                                segs = [(0, 0, 32), (34, 32, 32)]
                            for (b0, o0, nn_) in segs:
                                S.op('dve', lambda e: e.tensor_scalar(out=cc[:, o0:o0 + nn_], in0=buf[:, j, b0 + 2:b0 + 2 + nn_], scalar1=cw[:, fc, 2:3],
                                                                      scalar2=cb[:, fc:fc + 1], op0=ALU.mult, op1=ALU.add),
                                     [buf.b[j]] + cw.b + cb.b, cc.b)
                                S.op('dve', lambda e: e.scalar_tensor_tensor(out=cc[:, o0:o0 + nn_], in0=buf[:, j, b0 + 1:b0 + 1 + nn_], scalar=cw[:, fc, 1:2],
                                                                             in1=cc[:, o0:o0 + nn_], op0=ALU.mult, op1=ALU.add),
                                     [buf.b[j]] + cw.b + cc.b, cc.b)
                                S.op('dve', lambda e: e.scalar_tensor_tensor(out=cc[:, o0:o0 + nn_], in0=buf[:, j, b0:b0 + nn_], scalar=cw[:, fc, 0:1],
                                                                             in1=cc[:, o0:o0 + nn_], op0=ALU.mult, op1=ALU.add),
                                     [buf.b[j]] + cw.b + cc.b, cc.b)
                            sl2 = sl_[0]
                            S.op('act', lambda e: e.activation(out=sl2[:, 0:n], in_=cc[:, 0:n], func=AF.Silu), cc.b, sl2.b)
                            pu = mmbank()
                            for k in range(8):
                                S.op('pe', lambda e: e.matmul(out=pu[:, 0:n], lhsT=wu[:, k, jsl], rhs=HT[:, k, c0:c0 + n],
                                                              start=(k == 0), stop=(k == 7)), wu.b + hb_, pu.b, inc=(k == 7))
                            S.op('dve', lambda e: e.tensor_tensor(out=mmt[:, j, 0:n], in0=pu[:, 0:n], in1=sl2[:, 0:n], op=ALU.mult),
                                 pu.b + sl2.b, mmt.b)
                        for tt in tiles:
                            rows, sl = tsl(tt)
                            off = sl.start - c0
                            for half in range(2):
                                hsl = slice(half * 512, (half + 1) * 512)
                                po = mmbank()
                                for j in range(nfc):
                                    S.op('pe', lambda e: e.matmul(out=po[0:rows, :], lhsT=mmt[:, j, off:off + rows], rhs=wd[:, j, hsl],
                                                                  start=(j == 0), stop=(j == nfc - 1)), mmt.b + wd.b, po.b, inc=(j == nfc - 1))
                                S.op('dve', lambda e: e.tensor_tensor(out=X[0:rows, tt, hsl], in0=X[0:rows, tt, hsl], in1=po[0:rows, :],
                                                                      op=ALU.add), [X.b[tt]] + po.b, [X.b[tt]])
                S.dma('sp', o_conv[l].rearrange("s p n -> p s n"), OC[:].rearrange("p s a b -> p s (a b)"), OC.b, [])
                S.barrier()
                mm_list[0] = [0, 1]
                rot['mm'] = 0
            _ck('ffn%d' % l)

        _DEV['off'] = False
        _DEV['nops'] = 0
        _layers()
        _DEV['off'] = False
        mm_list[0] = [0, 1]
        for tt in range(NT):
            rows, sl = tsl(tt)
            S.dma('sp', y[sl, :], X[0:rows, tt, :], [X.b[tt]], [])
        S.barrier()
    return nc


_SLOPES = [2.0 ** (-8.0 * (h + 1) / 4) for h in range(4)]


def _consts():
    half = 16
    inv = (np.float32(10000.0) ** (-np.arange(half, dtype=np.float32) / np.float32(half))).astype(np.float32)
    pos = np.concatenate([np.arange(SP_), PAST + np.arange(SS), PAST + np.arange(SS)]).astype(np.float32)
    ang = (pos[:, None] * inv[None, :]).astype(np.float32)
    k_cs = np.concatenate([np.cos(ang), np.sin(ang)], axis=1).astype(np.float32)
    k = np.arange(128)[:, None]
    q = np.arange(128)[None, :]
    msb = (k < q).astype(np.float32)
    mch = ((k // 64) <= (q // 64)).astype(np.float32)
    mdf = np.zeros((128, 4, 128), np.float32)
    bdp = np.zeros((128, 4, 17), np.float32)
    bds = np.zeros((128, 4, 33), np.float32)
    for h in range(4):
        sl = _SLOPES[h]
        mdf[:, h, :] = mch * np.where(k > q, np.exp(-2.0 * sl * (k - q)), 1.0)
        for d in range(17):
            bdp[:, h, d] = sl * (np.arange(128) - 127 - 128 * d)
        for j in range(33):
            bds[:, h, j] = sl * (np.arange(128) - 31 - 128 * j)
    tri = (k >= q).astype(np.float32)
    return dict(k_cs=k_cs, k_msb=msb, k_mch=mch, k_mdf=mdf.reshape(128, 512), k_tri=tri,
                k_bdp=bdp.reshape(128, 68), k_bds=bds.reshape(128, 132))


_WNAMES = ["mix_norm_g", "w_in", "mla_q_norm_g", "mla_w_uq", "mla_kv_norm_g", "mla_w_uk", "mla_w_uv", "mla_qn_g", "mla_kn_g",
           "mla_qr_g", "mla_kr_g", "diff_qn_g", "diff_kn_g", "diff_lambda", "diff_subln_g", "w_br_mla", "w_br_sb", "w_br_diff",
           "w_out", "ffn_norm_g", "ffn_w_up", "ffn_conv_w", "ffn_conv_b", "ffn_w_down"]


def kernel(**inputs):
    inp = {k: np.asarray(v) for k, v in inputs.items()}
    nc = build_program()
    shared = {}
    for n in _WNAMES:
        a = np.ascontiguousarray(inp[n], dtype=np.float32)
        if n == "diff_lambda":
            a = a.reshape(L, 256)
        elif n == "ffn_conv_w":
            a = np.ascontiguousarray(a.reshape(L, 3, 22, 128).transpose(0, 3, 2, 1)).reshape(L, 128, 66)
        elif n == "ffn_conv_b":
            a = np.ascontiguousarray(a.reshape(L, 22, 128).transpose(0, 2, 1))
        shared[n] = a
    shared.update(_consts())
    in_maps = []
    for c in range(8):
        m = dict(shared)
        m["xin"] = np.ascontiguousarray(np.concatenate([inp["x_prompt"][c], inp["x_sample"][2 * c], inp["x_sample"][2 * c + 1]], axis=0),
                                        dtype=np.float32)
        sl = slice(2 * c, 2 * c + 2)
        m["c_ckv"] = np.ascontiguousarray(inp["cache_mla_ckv"][:, sl])
        m["c_kr"] = np.ascontiguousarray(inp["cache_mla_krope"][:, sl])
        m["c_sbk"] = np.ascontiguousarray(inp["cache_sb_k"][:, sl]).reshape(L, NS, PAST, 512)
        m["c_sbv"] = np.ascontiguousarray(inp["cache_sb_v"][:, sl]).reshape(L, NS, PAST, 512)
        m["c_dk"] = np.ascontiguousarray(inp["cache_diff_k"][:, sl]).reshape(L, NS, PAST, 512)
        m["c_dv"] = np.ascontiguousarray(inp["cache_diff_v"][:, sl]).reshape(L, NS, PAST, 512)
        st = np.asarray(inp["state_ffn_conv"][:, sl], dtype=np.float32)
        m["c_conv"] = np.ascontiguousarray(st.reshape(L, NS, 2, 22, 128).transpose(0, 1, 4, 3, 2)).reshape(L, NS, 128, 44)
        in_maps.append(m)
    res = run_bass_kernel_spmd(nc, in_maps, core_ids=list(range(8))).results

    def gather(name, width):
        p = np.stack([res[c][name][:, 0:SP_] for c in range(8)], axis=1)
        s = np.stack([res[c][name][:, SP_ + SS * j:SP_ + SS * (j + 1)] for c in range(8) for j in range(NS)], axis=1)
        return p, s

    y_p = np.stack([res[c]["y"][0:SP_] for c in range(8)], axis=0)
    y_s = np.stack([res[c]["y"][SP_ + SS * j:SP_ + SS * (j + 1)] for c in range(8) for j in range(NS)], axis=0)
    p_ckv, s_ckv = gather("o_ckv", 128)
    p_kr, s_kr = gather("o_kr", 32)
    p_sbk, s_sbk = gather("o_sbk", 512)
    p_sbv, s_sbv = gather("o_sbv", 512)
    p_dk, s_dk = gather("o_dk", 512)
    p_dv, s_dv = gather("o_dv", 512)

    def conv_of(c, idx):
        oc = res[c]["o_conv"][:, idx].reshape(L, 128, 22, 2)
        return np.ascontiguousarray(oc.transpose(0, 3, 2, 1)).reshape(L, 2, DFF)
    p_conv = np.stack([conv_of(c, 0) for c in range(8)], axis=1)
    s_conv = np.stack([conv_of(c, 1 + j) for c in range(8) for j in range(NS)], axis=1)
    f = np.float32
    return (y_p.astype(f), y_s.astype(f), p_ckv.astype(f), p_kr.astype(f),
            p_sbk.reshape(L, 8, SP_, 8, 64).astype(f), p_sbv.reshape(L, 8, SP_, 8, 64).astype(f),
            p_dk.reshape(L, 8, SP_, 4, 2, 64).astype(f), p_dv.reshape(L, 8, SP_, 4, 128).astype(f), p_conv.astype(f),
            s_ckv.astype(f), s_kr.astype(f), s_sbk.reshape(L, 16, SS, 8, 64).astype(f), s_sbv.reshape(L, 16, SS, 8, 64).astype(f),
            s_dk.reshape(L, 16, SS, 4, 2, 64).astype(f), s_dv.reshape(L, 16, SS, 4, 128).astype(f), s_conv.astype(f))
```

```python
import math
from contextlib import ExitStack
import numpy as np
import concourse.bass as bass
import concourse.mybir as mybir
from concourse.bass_utils import run_bass_kernel_spmd

F32 = mybir.dt.float32
BF16 = mybir.dt.bfloat16
AF = mybir.ActivationFunctionType
ALU = mybir.AluOpType
AX = mybir.AxisListType

L = 2
D = 1024
SP_ = 2048
NS = 2
SS = 32
PAST = 4096
NTOK = SP_ + NS * SS
NT = 18
DFF = 2816
NIN = 6560
EPS = 1e-6
MLA_SCALE = 96 ** -0.5
SB_SCALE = 64 ** -0.5
DIFF_SCALE = 64 ** -0.5
OFF = dict(cq=0, ckv=256, kr=384, sq=416, sk=928, sv=1440, dq=1952, dk=2464, dv=2976, g=3488)
ENG = ['pe', 'act', 'dve', 'pool', 'sp']
NDS = 20
_DEV = {'stop': None, 'off': False, 'maxops': None, 'nops': 0, 'log': None}


class _Stop(Exception):
    pass


def _ck(name):
    if _DEV['log'] is not None and not _DEV['off']:
        _DEV['log'].append((name, _DEV['nops']))
    if _DEV['stop'] == name:
        _DEV['off'] = True


class Buf:
    __slots__ = ('w', 'r', 'excl')

    def __init__(self):
        self.w = None
        self.r = {}
        self.excl = False


class TT:
    def __init__(self, h, n=1):
        self.h = h
        self.b = [Buf() for _ in range(n)]

    def __getitem__(self, k):
        return self.h[k]


class Sync:
    def __init__(self, nc, es):
        self.nc = nc
        self.e = dict(pe=nc.tensor, act=nc.scalar, dve=nc.vector, pool=nc.gpsimd, sp=nc.sync)
        self.sem = {k: es.enter_context(nc.semaphore("sem_" + k)) for k in ENG}
        self.cnt = {k: 0 for k in ENG}
        self.dsem = {q: [es.enter_context(nc.semaphore("ds_%s%d" % (q, i))) for i in range(NDS)] for q in ('sp', 'pool')}
        self.dcnt = {q: [0] * NDS for q in ('sp', 'pool')}
        self.dnext = {'sp': 0, 'pool': 0}
        self.known = {k: {} for k in ENG}
        self.pend = {k: False for k in ENG}
        self.hist = {}

    def _semof(self, k):
        return self.sem[k] if isinstance(k, str) else self.dsem[k[0]][k[1]]

    def _need(self, eng, toks):
        kn = self.known[eng]
        best = {}
        for (k, v) in toks:
            if best.get(k, 0) < v:
                best[k] = v
        need = []
        for k, v in sorted(best.items(), key=lambda kv: str(kv[0])):
            if kn.get(k, 0) < v:
                need.append((k, v))
        implied = {}
        for k, v in need:
            snap = self.hist.get((k, v))
            if snap:
                for k2, v2 in snap.items():
                    if implied.get(k2, 0) < v2:
                        implied[k2] = v2
        out = [(k, v) for (k, v) in need if implied.get(k, 0) < v]
        for k, v in out:
            kn[k] = v
            snap = self.hist.get((k, v))
            if snap:
                for k2, v2 in snap.items():
                    if k2 != eng and kn.get(k2, 0) < v2:
                        kn[k2] = v2
        return out

    def _wait(self, eng, toks, ins_fn=None):
        need = self._need(eng, toks)
        if ins_fn is None:
            for k, v in need:
                self.e[eng].wait_ge(self._semof(k), v)
            return None
        for k, v in need[:-1]:
            self.e[eng].wait_ge(self._semof(k), v)
        ins = ins_fn()
        if need:
            k, v = need[-1]
            ins._wait_ge(self._semof(k), v)
        return ins

    def _deps(self, eng, reads, writes):
        toks = set()
        for b in reads:
            if b.w is not None:
                toks.add(b.w)
            if b.excl:
                for kv in b.r.items():
                    if kv[0] != eng:
                        toks.add(kv)
        for b in writes:
            if b.w is not None:
                toks.add(b.w)
            for kv in b.r.items():
                toks.add(kv)
        if eng == 'pe':
            toks = {t for t in toks if t[0] != 'pe'}
        return toks

    def op(self, eng, fn, reads=(), writes=(), inc=True):
        if _DEV['off']:
            return
        _DEV['nops'] += 1
        if _DEV['maxops'] is not None and _DEV['nops'] > _DEV['maxops'] and not self.pend[eng]:
            _DEV['off'] = True
            return
        ins = self._wait(eng, self._deps(eng, reads, writes), lambda: fn(self.e[eng]))
        c = self.cnt[eng] + 1
        if inc:
            ins.then_inc(self.sem[eng], 1)
            self.cnt[eng] = c
            self.pend[eng] = False
            self.hist[(eng, c)] = dict(self.known[eng])
        else:
            self.pend[eng] = True
        for b in reads:
            b.r[eng] = c
        for b in writes:
            b.w = (eng, c)
            b.r = {}

    def dma(self, q, out, in_, reads=(), writes=()):
        if _DEV['off']:
            return
        toks = self._deps(q, reads, writes)
        i = self.dnext[q]
        self.dnext[q] = (i + 1) % NDS
        key = (q, i)
        if self.dcnt[q][i] > 0:
            toks.add((key, self.dcnt[q][i]))
        ins = self._wait(q, toks, lambda: self.e[q].dma_start(out=out, in_=in_))
        ins.then_inc(self.dsem[q][i], 16)
        self.dcnt[q][i] += 16
        v = self.dcnt[q][i]
        self.hist[(key, v)] = dict(self.known[q])
        for b in reads:
            b.r[key] = v
        for b in writes:
            b.w = (key, v)
            b.r = {}

    def all_tokens(self):
        toks = {(k, self.cnt[k]) for k in ENG if self.cnt[k] > 0}
        for q in ('sp', 'pool'):
            for i, c in enumerate(self.dcnt[q]):
                if c > 0:
                    toks.add(((q, i), c))
        return toks

    def barrier(self):
        if _DEV['off']:
            return
        for k in ENG:
            assert not self.pend[k]
        toks = self.all_tokens()
        for eng in ENG:
            self._wait(eng, {t for t in toks if t[0] != eng})


def build_program():
    nc = bass.Bass("TRN2", target_bir_lowering=False)

    def din(name, shape):
        return nc.dram_tensor(name, list(shape), F32, kind="ExternalInput").ap()

    def dout(name, shape):
        return nc.dram_tensor(name, list(shape), F32, kind="ExternalOutput").ap()

    xin = din("xin", [NTOK, D])
    c_ckv = din("c_ckv", [L, NS, PAST, 128])
    c_kr = din("c_kr", [L, NS, PAST, 32])
    c_sbk = din("c_sbk", [L, NS, PAST, 512])
    c_sbv = din("c_sbv", [L, NS, PAST, 512])
    c_dk = din("c_dk", [L, NS, PAST, 512])
    c_dv = din("c_dv", [L, NS, PAST, 512])
    c_conv = din("c_conv", [L, NS, 128, 22 * 2])
    W = {}
    for name, shape in [("mix_norm_g", [L, D]), ("w_in", [L, D, NIN]), ("mla_q_norm_g", [L, 256]),
                        ("mla_w_uq", [L, 256, 768]), ("mla_kv_norm_g", [L, 128]), ("mla_w_uk", [L, 128, 512]),
                        ("mla_w_uv", [L, 128, 512]), ("mla_qn_g", [L, 64]), ("mla_kn_g", [L, 64]),
                        ("mla_qr_g", [L, 32]), ("mla_kr_g", [L, 32]), ("diff_qn_g", [L, 64]),
                        ("diff_kn_g", [L, 64]), ("diff_lambda", [L, 256]), ("diff_subln_g", [L, 128]),
                        ("w_br_mla", [L, 512, D]), ("w_br_sb", [L, 512, D]), ("w_br_diff", [L, 512, D]),
                        ("w_out", [L, D, D]), ("ffn_norm_g", [L, D]), ("ffn_w_up", [L, D, 2 * DFF]),
                        ("ffn_conv_w", [L, 128, 22 * 3]), ("ffn_conv_b", [L, 128, 22]), ("ffn_w_down", [L, DFF, D])]:
        W[name] = din(name, shape)
    k_cs = din("k_cs", [NTOK, 32])
    k_msb = din("k_msb", [128, 128])
    k_mch = din("k_mch", [128, 128])
    k_mdf = din("k_mdf", [128, 4 * 128])
    k_tri = din("k_tri", [128, 128])
    k_bdp = din("k_bdp", [128, 4 * 17])
    k_bds = din("k_bds", [128, 4 * 33])

    y = dout("y", [NTOK, D])
    o_ckv = dout("o_ckv", [L, NTOK, 128])
    o_kr = dout("o_kr", [L, NTOK, 32])
    o_sbk = dout("o_sbk", [L, NTOK, 512])
    o_sbv = dout("o_sbv", [L, NTOK, 512])
    o_dk = dout("o_dk", [L, NTOK, 512])
    o_dv = dout("o_dv", [L, NTOK, 512])
    o_conv = dout("o_conv", [L, 3, 128, 22 * 2])

    with ExitStack() as es:
        E = es.enter_context
        S = Sync(nc, es)
        cnt = [0]

        def sb(es_, shape, dt, n=1):
            cnt[0] += 1
            return TT(es_.enter_context(nc.sbuf_tensor("t%d" % cnt[0], list(shape), dt)), n)

        X = sb(es, [128, NT, D], F32, NT)
        HT = sb(es, [128, 8, NTOK], BF16, NT)
        OT = sb(es, [128, 4, NTOK], BF16, 5)
        ident = sb(es, [128, 128], BF16)
        ones = sb(es, [128, 128], BF16)
        zeros = sb(es, [128, 128], BF16)
        onesf = sb(es, [128, 128], F32)
        tri = sb(es, [128, 128], BF16)
        msb = sb(es, [128, 128], F32)
        mch = sb(es, [128, 128], F32)
        mdf = sb(es, [128, 4, 128], F32)
        bdp = sb(es, [128, 4, 17], F32)
        bds = sb(es, [128, 4, 33], F32)
        cs = sb(es, [128, NT, 32], F32)
        gbig = sb(es, [128, D], F32)
        gsm = sb(es, [128, 1024], F32)
        gcol = sb(es, [128, 8], F32)
        PS = [TT(E(nc.psum_tensor("ps%d" % i, [128, 512], F32))) for i in range(8)]
        for p_ in PS:
            p_.b[0].excl = True
        rot = {'mm': 0, 'tp': 0}

        def mmbank():
            rot['mm'] = (rot['mm'] + 1) % len(mm_list[0])
            return PS[mm_list[0][rot['mm']]]

        def tpbank():
            rot['tp'] ^= 1
            return PS[2 + rot['tp']]

        def tsl(tt):
            if tt < 16:
                return 128, slice(tt * 128, tt * 128 + 128)
            return 32, slice(2048 + 32 * (tt - 16), 2048 + 32 * (tt - 16) + 32)

        GROUPS = [(g * 512, 512, [4 * g + i for i in range(4)]) for g in range(4)] + [(2048, 64, [16, 17])]
        mm_list = [[0, 1]]

        S.op('pool', lambda e: e.memset(ident[:], 0.0), [], ident.b)
        S.op('pool', lambda e: e.affine_select(out=ident[:], in_=ident[:], pattern=[[-1, 128]], compare_op=ALU.not_equal,
                                               fill=1.0, base=0, channel_multiplier=1), ident.b, ident.b)
        S.op('pool', lambda e: e.memset(ones[:], 1.0), [], ones.b)
        S.op('pool', lambda e: e.memset(zeros[:], 0.0), [], zeros.b)
        S.op('pool', lambda e: e.memset(onesf[:], 1.0), [], onesf.b)
        S.dma('pool', tri[:], k_tri, [], tri.b)
        S.dma('sp', msb[:], k_msb, [], msb.b)
        S.dma('sp', mch[:], k_mch, [], mch.b)
        S.dma('sp', mdf[:].rearrange("p a b -> p (a b)"), k_mdf, [], mdf.b)
        S.dma('sp', bdp[:].rearrange("p a b -> p (a b)"), k_bdp, [], bdp.b)
        S.dma('sp', bds[:].rearrange("p a b -> p (a b)"), k_bds, [], bds.b)
        S.dma('sp', cs[:, 0:16, :], k_cs[0:2048, :].rearrange("(t p) n -> p t n", p=128), [], cs.b)
        S.dma('sp', cs[0:32, 16, :], k_cs[2048:2080, :], [], cs.b)
        S.dma('sp', cs[0:32, 17, :], k_cs[2080:2112, :], [], cs.b)
        zrhs = sb(es, [128, 512], BF16)
        S.op('pool', lambda e: e.memset(zrhs[:], 0.0), [], zrhs.b)

        GS = dict(q_norm=(0, 256), kv_norm=(256, 128), kr=(384, 32), qn=(416, 64), kn=(480, 64), qr=(544, 32),
                  dqn=(576, 64), dkn=(640, 64), lam=(704, 256))

        def gs(name, rows=128):
            o, n = GS[name]
            return gsm[0:rows, o:o + n]

        def rstd_from_ss(ss_t, rows, G, d):
            S.op('act', lambda e: e.activation(out=ss_t[0:rows, 0:G], in_=ss_t[0:rows, 0:G], func=AF.Ln, scale=1.0 / d, bias=EPS),
                 ss_t.b, ss_t.b)
            S.op('act', lambda e: e.activation(out=ss_t[0:rows, 0:G], in_=ss_t[0:rows, 0:G], func=AF.Exp, scale=-0.5),
                 ss_t.b, ss_t.b)

        def gnorm(ws, src, srcb, rows, G, d, gain, out, outb, post_scale=1.0):
            sq, ss, tmp = ws['sq'], ws['ss'], ws['tmp']
            sqv = sq[0:rows, 0:G * d].rearrange("p (g d) -> p g d", d=d)
            S.op('act', lambda e: e.activation(out=sqv, in_=src, func=AF.Square), srcb, sq.b)
            S.op('dve', lambda e: e.tensor_reduce(out=ss[0:rows, 0:G], in_=sqv, axis=AX.X, op=ALU.add), sq.b, ss.b)
            rstd_from_ss(ss, rows, G, d)
            tv = tmp[0:rows, 0:G * d].rearrange("p (g d) -> p g d", d=d)
            S.op('dve', lambda e: e.tensor_tensor(out=tv, in0=src, in1=ss[0:rows, 0:G].unsqueeze(2).to_broadcast([rows, G, d]),
                                                  op=ALU.mult), list(srcb) + ss.b, tmp.b)
            gb = gain.unsqueeze(1).to_broadcast([rows, G, d])
            if post_scale == 1.0:
                S.op('dve', lambda e: e.tensor_tensor(out=out, in0=tv, in1=gb, op=ALU.mult), tmp.b + gsm.b, outb)
            else:
                S.op('dve', lambda e: e.scalar_tensor_tensor(out=out, in0=tv, scalar=float(post_scale), in1=gb, op0=ALU.mult,
                                                             op1=ALU.mult), tmp.b + gsm.b, outb)

        def rope(ws, src, srcb, rows, H, tt, out, outb):
            t1, t2 = ws['r1'], ws['r2']
            cosb = cs[0:rows, tt, 0:16].unsqueeze(1).to_broadcast([rows, H, 16])
            sinb = cs[0:rows, tt, 16:32].unsqueeze(1).to_broadcast([rows, H, 16])
            a1 = t1[0:rows, 0:H * 16].rearrange("p (h d) -> p h d", d=16)
            a2 = t2[0:rows, 0:H * 16].rearrange("p (h d) -> p h d", d=16)
            x1 = src[:, :, 0:16]
            x2 = src[:, :, 16:32]
            S.op('dve', lambda e: e.tensor_tensor(out=a1, in0=x1, in1=cosb, op=ALU.mult), list(srcb) + cs.b, t1.b)
            S.op('dve', lambda e: e.tensor_tensor(out=a2, in0=x2, in1=sinb, op=ALU.mult), list(srcb) + cs.b, t2.b)
            S.op('dve', lambda e: e.tensor_tensor(out=out[:, :, 0:16], in0=a1, in1=a2, op=ALU.subtract), t1.b + t2.b, outb)
            S.op('dve', lambda e: e.tensor_tensor(out=a1, in0=x1, in1=sinb, op=ALU.mult), list(srcb) + cs.b, t1.b)
            S.op('dve', lambda e: e.tensor_tensor(out=a2, in0=x2, in1=cosb, op=ALU.mult), list(srcb) + cs.b, t2.b)
            S.op('dve', lambda e: e.tensor_tensor(out=out[:, :, 16:32], in0=a1, in1=a2, op=ALU.add), t1.b + t2.b, outb)

        def transpose_to(src, srcb, rows, ncols, dst, dstb, eng='act'):
            pb = tpbank()
            pv = pb.h[:, :].bitcast(BF16)
            S.op('pe', lambda e: e.transpose(out=pv[0:ncols, 0:rows], in_=src, identity=ident[0:rows, 0:rows]), list(srcb) + ident.b, pb.b)
            if eng == 'act':
                S.op('act', lambda e: e.copy(out=dst, in_=pv[0:ncols, 0:rows]), pb.b, dstb)
            else:
                S.op('dve', lambda e: e.tensor_copy(out=dst, in_=pv[0:ncols, 0:rows]), pb.b, dstb)

        def load_w(dst, src_ap):
            S.dma('pool', dst.h[:], src_ap, [], dst.b)

        def norm_phase(l, gname, first):
            with ExitStack() as ps:
                sq = sb(ps, [128, D], F32)
                ssr = [sb(ps, [128, 1], F32) for _ in range(2)]
                hb = [sb(ps, [128, D], BF16) for _ in range(2)]
                S.dma('sp', gbig[:], W[gname][l].partition_broadcast(128), [], gbig.b)
                for tt in range(NT):
                    rows, sl = tsl(tt)
                    if first:
                        S.dma('sp', X[0:rows, tt, :], xin[sl, :], [], [X.b[tt]])
                    ss = ssr[tt % 2]
                    h = hb[tt % 2]
                    S.op('act', lambda e: e.activation(out=sq[0:rows, :], in_=X[0:rows, tt, :], func=AF.Square,
                                                       accum_out=ss[0:rows, :]), [X.b[tt]], sq.b + ss.b)
                    rstd_from_ss(ss, rows, 1, D)
                    S.op('dve', lambda e: e.scalar_tensor_tensor(out=h[0:rows, :], in0=X[0:rows, tt, :], scalar=ss[0:rows, 0:1],
                                                                 in1=gbig[0:rows, :], op0=ALU.mult, op1=ALU.mult),
                         [X.b[tt]] + ss.b + gbig.b, h.b)
                    pb = tpbank()
                    pv = pb.h[:, :].bitcast(BF16).rearrange("p (k n) -> p k n", n=128)
                    for k in range(8):
                        S.op('pe', lambda e: e.transpose(out=pv[:, k, 0:rows], in_=h[0:rows, k * 128:(k + 1) * 128],
                                                         identity=ident[0:rows, 0:rows]), h.b + ident.b, pb.b, inc=(k == 7))
                    S.op('act' if tt % 2 else 'dve',
                         (lambda e: e.copy(out=HT[:, :, sl], in_=pv[:, :, 0:rows])) if tt % 2 else
                         (lambda e: e.tensor_copy(out=HT[:, :, sl], in_=pv[:, :, 0:rows])), pb.b, [HT.b[tt]])
                S.barrier()

        def _layers():
          for l in range(L):
            lam_init = 0.8 - 0.6 * math.exp(-0.3 * l)
            for nm, wn in [('q_norm', 'mla_q_norm_g'), ('kv_norm', 'mla_kv_norm_g'), ('kr', 'mla_kr_g'), ('qn', 'mla_qn_g'),
                           ('kn', 'mla_kn_g'), ('qr', 'mla_qr_g'), ('dqn', 'diff_qn_g'), ('dkn', 'diff_kn_g'),
                           ('lam', 'diff_lambda')]:
                S.dma('sp', gs(nm), W[wn][l].partition_broadcast(128), [], gsm.b)
            S.dma('sp', gcol[:, 0:1], W['diff_subln_g'][l].rearrange("(p o) -> p o", o=1), [], gcol.b)
            with ExitStack() as ps:
                lt = sb(ps, [128, 128], F32)
                l2 = sb(ps, [128, 2], F32)
                lv = gs('lam')
                S.op('dve', lambda e: e.tensor_tensor(out=lt[:, 0:64], in0=lv[:, 0:64], in1=lv[:, 64:128], op=ALU.mult), gsm.b, lt.b)
                S.op('dve', lambda e: e.tensor_tensor(out=lt[:, 64:128], in0=lv[:, 128:192], in1=lv[:, 192:256], op=ALU.mult), gsm.b, lt.b)
                S.op('dve', lambda e: e.tensor_reduce(out=l2[:, 0:2], in_=lt[:, :].rearrange("p (a b) -> p a b", b=64), axis=AX.X,
                                                      op=ALU.add), lt.b, l2.b)
                S.op('act', lambda e: e.activation(out=l2[:, 0:2], in_=l2[:, 0:2], func=AF.Exp), l2.b, l2.b)
                S.op('dve', lambda e: e.tensor_tensor(out=gcol[:, 1:2], in0=l2[:, 0:1], in1=l2[:, 1:2], op=ALU.subtract), l2.b, gcol.b)
                S.op('dve', lambda e: e.tensor_scalar(out=gcol[:, 1:2], in0=gcol[:, 1:2], scalar1=float(lam_init), scalar2=None,
                                                      op0=ALU.add), gcol.b, gcol.b)
                S.op('dve', lambda e: e.tensor_scalar(out=gcol[:, 2:3], in0=gcol[:, 1:2], scalar1=-1.0, scalar2=None,
                                                      op0=ALU.mult), gcol.b, gcol.b)
                S.op('dve', lambda e: e.tensor_scalar(out=gcol[:, 0:1], in0=gcol[:, 0:1], scalar1=float(1.0 - lam_init), scalar2=None,
                                                      op0=ALU.mult), gcol.b, gcol.b)
                S.barrier()

            norm_phase(l, 'mix_norm_g', l == 0)
            _ck('norm%d' % l)

            for mixer in ('mla', 'sb', 'diff'):
                with ExitStack() as ms:
                    if mixer == 'mla':
                        CQT = sb(ms, [128, 2, NTOK], BF16, NT)
                        CKVT = sb(ms, [128, NTOK], BF16, NT)
                        KRT = sb(ms, [128, NT, 32], BF16, NT)
                        nunits, hpu = 4, 2
                    elif mixer == 'sb':
                        nunits, hpu = 2, 4
                    else:
                        nunits, hpu = 2, 2
                    for u in range(nunits):
                        with ExitStack() as us:
                            ws = dict(sq=sb(us, [128, 512], F32), ss=sb(us, [128, 8], F32), tmp=sb(us, [128, 512], F32),
                                      r1=sb(us, [128, 128], F32), r2=sb(us, [128, 128], F32))
                            stage = [sb(us, [128, 256], F32) for _ in range(3)]
                            stg_i = [0]

                            def nstage():
                                stg_i[0] = (stg_i[0] + 1) % 3
                                return stage[stg_i[0]]
                            tokb = [sb(us, [128, 256], BF16) for _ in range(3)]
                            tok_i = [0]

                            def ntok():
                                tok_i[0] = (tok_i[0] + 1) % 3
                                return tokb[tok_i[0]]
                            PT = [sb(us, [128, 512], BF16) for _ in range(3)]
                            pt_i = [0]

                            def npt():
                                pt_i[0] = (pt_i[0] + 1) % 3
                                return PT[pt_i[0]]
                            rden = sb(us, [128, 512], F32)
                            if mixer == 'mla':
                                KT = sb(us, [96, 2, NTOK], BF16, NT)
                                V = sb(us, [128, NT, 128], BF16, NT)
                                QT = sb(us, [96, 2, 512], BF16)
                                if u == 0:
                                    w1 = sb(us, [128, 8, 416], BF16)
                                    load_w(w1, W['w_in'][l, :, 0:416].rearrange("(k p) n -> p k n", p=128))
                                wuq = sb(us, [128, 2, 192], BF16)
                                load_w(wuq, W['mla_w_uq'][l, :, u * 192:(u + 1) * 192].rearrange("(k p) n -> p k n", p=128))
                                wuk = sb(us, [128, 128], BF16)
                                load_w(wuk, W['mla_w_uk'][l, :, u * 128:(u + 1) * 128])
                                wuv = sb(us, [128, 128], BF16)
                                load_w(wuv, W['mla_w_uv'][l, :, u * 128:(u + 1) * 128])
                                kcat = [sb(us, [128, 2, 96], BF16) for _ in range(2)]
                                qcat = [sb(us, [128, 2, 96], BF16) for _ in range(2)]
                                pck = sb(us, [128, 4, 128], BF16)
                                pkr = sb(us, [128, 4, 32], BF16)
                                pckT = sb(us, [128, 512], BF16)
                                KTp = sb(us, [96, 2, 512], BF16)
                                Vp = sb(us, [128, 4, 128], BF16)
                            else:
                                c0q = OFF['sq' if mixer == 'sb' else 'dq'] + u * 256
                                c0k = OFF['sk' if mixer == 'sb' else 'dk'] + u * 256
                                c0v = OFF['sv' if mixer == 'sb' else 'dv'] + u * 256
                                wq = sb(us, [128, 8, 256], BF16)
                                wk = sb(us, [128, 8, 256], BF16)
                                wv = sb(us, [128, 8, 256], BF16)
                                load_w(wq, W['w_in'][l, :, c0q:c0q + 256].rearrange("(k p) n -> p k n", p=128))
                                load_w(wk, W['w_in'][l, :, c0k:c0k + 256].rearrange("(k p) n -> p k n", p=128))
                                load_w(wv, W['w_in'][l, :, c0v:c0v + 256].rearrange("(k p) n -> p k n", p=128))
                                KT = sb(us, [128, 2, NTOK], BF16, NT)
                                V = sb(us, [128, NT, 256], BF16, NT)
                                QT = sb(us, [128, 2, 512], BF16)
                                pk = sb(us, [128, 4, 256], BF16)
                                KTp = sb(us, [128, 2, 512], BF16)
                                Vp = sb(us, [128, 4, 256], BF16)
                                if mixer == 'sb':
                                    R = sb(us, [128, 2, 512], F32)
                                    e1 = [sb(us, [128, 512], F32) for _ in range(2)]
                                    spb = [sb(us, [128, 512], BF16) for _ in range(2)]
                                    cumr = [sb(us, [128, 512], F32) for _ in range(2)]
                                else:
                                    t0 = sb(us, [128, 512], F32)
                                    t1 = sb(us, [128, 512], F32)
                                    rden1 = sb(us, [128, 512], F32)
                                    osq = sb(us, [128, 512], F32)
                            o_k = o_sbk if mixer == 'sb' else o_dk
                            o_v = o_sbv if mixer == 'sb' else o_dv

                            def proj_tile(tt, qcol0):
                                rows, sl = tsl(tt)
                                if mixer == 'mla':
                                    if u == 0:
                                        pz = mmbank()
                                        for k in range(8):
                                            S.op('pe', lambda e: e.matmul(out=pz[0:rows, 0:416], lhsT=HT[:, k, sl], rhs=w1[:, k, :],
                                                                          start=(k == 0), stop=(k == 7)), [HT.b[tt]] + w1.b, pz.b, inc=(k == 7))
                                        tk = ntok()
                                        gnorm(ws, pz[0:rows, 0:256].rearrange("p (g d) -> p g d", g=1), pz.b, rows, 1, 256, gs('q_norm', rows),
                                              tk[0:rows, 0:256].rearrange("p (g d) -> p g d", g=1), tk.b)
                                        for c in range(2):
                                            transpose_to(tk[0:rows, c * 128:(c + 1) * 128], tk.b, rows, 128, CQT[:, c, sl], [CQT.b[tt]],
                                                         'act' if c else 'dve')
                                        st = nstage()
                                        gnorm(ws, pz[0:rows, 256:384].rearrange("p (g d) -> p g d", g=1), pz.b, rows, 1, 128, gs('kv_norm', rows),
                                              st[0:rows, 0:128].rearrange("p (g d) -> p g d", g=1), st.b)
                                        S.dma('sp', o_ckv[l, sl, :], st[0:rows, 0:128], st.b, [])
                                        tk2 = ntok()
                                        S.op('act', lambda e: e.copy(out=tk2[0:rows, 0:128], in_=st[0:rows, 0:128]), st.b, tk2.b)
                                        transpose_to(tk2[0:rows, 0:128], tk2.b, rows, 128, CKVT[:, sl], [CKVT.b[tt]], 'dve')
                                        gnorm(ws, pz[0:rows, 384:416].rearrange("p (g d) -> p g d", g=1), pz.b, rows, 1, 32, gs('kr', rows),
                                              st[0:rows, 128:160].rearrange("p (g d) -> p g d", g=1), st.b)
                                        rope(ws, st[0:rows, 128:160].rearrange("p (g d) -> p g d", g=1), st.b, rows, 1, tt,
                                             st[0:rows, 160:192].rearrange("p (g d) -> p g d", g=1), st.b)
                                        S.dma('sp', o_kr[l, sl, :], st[0:rows, 160:192], st.b, [])
                                        S.op('act', lambda e: e.copy(out=KRT[0:rows, tt, :], in_=st[0:rows, 160:192]), st.b, [KRT.b[tt]])
                                    pq = mmbank()
                                    for c in range(2):
                                        S.op('pe', lambda e: e.matmul(out=pq[0:rows, 0:192], lhsT=CQT[:, c, sl], rhs=wuq[:, c, :],
                                                                      start=(c == 0), stop=(c == 1)), [CQT.b[tt]] + wuq.b, pq.b, inc=(c == 1))
                                    qv = pq[0:rows, 0:192].rearrange("p (h d) -> p h d", d=96)
                                    qc = qcat[tt % 2]
                                    gnorm(ws, qv[:, :, 0:64], pq.b, rows, 2, 64, gs('qn', rows), qc[0:rows, :, 0:64], qc.b, MLA_SCALE)
                                    st = nstage()
                                    stv = st[0:rows, 0:64].rearrange("p (h d) -> p h d", d=32)
                                    gnorm(ws, qv[:, :, 64:96], pq.b, rows, 2, 32, gs('qr', rows), stv, st.b, MLA_SCALE)
                                    rope(ws, stv, st.b, rows, 2, tt, qc[0:rows, :, 64:96], qc.b)
                                    for hh in range(2):
                                        transpose_to(qc[0:rows, hh, :], qc.b, rows, 96, QT[:, hh, qcol0:qcol0 + rows], QT.b, 'act' if hh else 'dve')
                                    kv_from_latent(CKVT[:, sl], [CKVT.b[tt]], KRT[0:rows, tt, :], [KRT.b[tt]], rows,
                                                   lambda hh: KT[:, hh, sl], [KT.b[tt]], V[0:rows, tt, :], [V.b[tt]], kcat[tt % 2])
                                else:
                                    pq = mmbank()
                                    for k in range(8):
                                        S.op('pe', lambda e: e.matmul(out=pq[0:rows, 0:256], lhsT=HT[:, k, sl], rhs=wq[:, k, :],
                                                                      start=(k == 0), stop=(k == 7)), [HT.b[tt]] + wq.b, pq.b, inc=(k == 7))
                                    tk = ntok()
                                    if mixer == 'sb':
                                        S.op('act', lambda e: e.activation(out=tk[0:rows, :], in_=pq[0:rows, 0:256], func=AF.Copy,
                                                                           scale=SB_SCALE), pq.b, tk.b)
                                    else:
                                        gnorm(ws, pq[0:rows, 0:256].rearrange("p (g d) -> p g d", d=64), pq.b, rows, 4, 64, gs('dqn', rows),
                                              tk[0:rows, :].rearrange("p (g d) -> p g d", d=64), tk.b, DIFF_SCALE)
                                    for c in range(2):
                                        transpose_to(tk[0:rows, c * 128:(c + 1) * 128], tk.b, rows, 128, QT[:, c, qcol0:qcol0 + rows], QT.b,
                                                     'act' if c else 'dve')
                                    pk_ = mmbank()
                                    for k in range(8):
                                        S.op('pe', lambda e: e.matmul(out=pk_[0:rows, 0:256], lhsT=HT[:, k, sl], rhs=wk[:, k, :],
                                                                      start=(k == 0), stop=(k == 7)), [HT.b[tt]] + wk.b, pk_.b, inc=(k == 7))
                                    st = nstage()
                                    if mixer == 'sb':
                                        S.op('act', lambda e: e.copy(out=st[0:rows, :], in_=pk_[0:rows, 0:256]), pk_.b, st.b)
                                    else:
                                        gnorm(ws, pk_[0:rows, 0:256].rearrange("p (g d) -> p g d", d=64), pk_.b, rows, 4, 64, gs('dkn', rows),
                                              st[0:rows, :].rearrange("p (g d) -> p g d", d=64), st.b)
                                    S.dma('sp', o_k[l, sl, u * 256:(u + 1) * 256], st[0:rows, :], st.b, [])
                                    tk = ntok()
                                    S.op('dve', lambda e: e.tensor_copy(out=tk[0:rows, :], in_=st[0:rows, :]), st.b, tk.b)
                                    for c in range(2):
                                        transpose_to(tk[0:rows, c * 128:(c + 1) * 128], tk.b, rows, 128, KT[:, c, sl], [KT.b[tt]],
                                                     'act' if c else 'dve')
                                    pv_ = mmbank()
                                    for k in range(8):
                                        S.op('pe', lambda e: e.matmul(out=pv_[0:rows, 0:256], lhsT=HT[:, k, sl], rhs=wv[:, k, :],
                                                                      start=(k == 0), stop=(k == 7)), [HT.b[tt]] + wv.b, pv_.b, inc=(k == 7))
                                    st = nstage()
                                    S.op('act', lambda e: e.copy(out=st[0:rows, :], in_=pv_[0:rows, 0:256]), pv_.b, st.b)
                                    S.dma('sp', o_v[l, sl, u * 256:(u + 1) * 256], st[0:rows, :], st.b, [])
                                    S.op('dve', lambda e: e.tensor_copy(out=V[0:rows, tt, :], in_=pv_[0:rows, 0:256]), pv_.b, [V.b[tt]])

                            def kv_from_latent(ckvT_ap, ckvT_b, kr_ap, kr_b, rows, kt_dst, kt_b, v_dst, v_b, kc):
                                pk_ = mmbank()
                                S.op('pe', lambda e: e.matmul(out=pk_[0:rows, 0:128], lhsT=ckvT_ap, rhs=wuk[:, :], start=True, stop=True),
                                     list(ckvT_b) + wuk.b, pk_.b)
                                gnorm(ws, pk_[0:rows, 0:128].rearrange("p (h d) -> p h d", d=64), pk_.b, rows, 2, 64, gs('kn', rows),
                                      kc[0:rows, :, 0:64], kc.b)
                                S.op('dve', lambda e: e.tensor_copy(out=kc[0:rows, :, 64:96], in_=kr_ap.unsqueeze(1).to_broadcast([rows, 2, 32])),
                                     list(kr_b), kc.b)
                                for hh in range(2):
                                    transpose_to(kc[0:rows, hh, :], kc.b, rows, 96, kt_dst(hh), kt_b, 'act' if hh else 'dve')
                                pv_ = mmbank()
                                S.op('pe', lambda e: e.matmul(out=pv_[0:rows, 0:128], lhsT=ckvT_ap, rhs=wuv[:, :], start=True, stop=True),
                                     list(ckvT_b) + wuv.b, pv_.b)
                                S.op('act', lambda e: e.copy(out=v_dst, in_=pv_[0:rows, 0:128]), pv_.b, v_b)

                            acc, accd, acc1, accd1 = PS[4], PS[5], PS[6], PS[7]

                            def zero_acc(t_, ncols):
                                S.op('pe', lambda e: e.matmul(out=t_[:, 0:ncols], lhsT=zeros[:, :], rhs=zrhs[:, 0:ncols], start=True, stop=True),
                                     zeros.b + zrhs.b, t_.b)

                            acc1s = [(PS[6], PS[7]), (PS[2], PS[3])]
                            itc = [0]

                            def att_begin(ncols):
                                if mixer == 'mla':
                                    zero_acc(acc, ncols)
                                    zero_acc(accd, ncols)
                                elif mixer == 'sb':
                                    zero_acc(acc, ncols)
                                    S.op('dve', lambda e: e.memset(R[:, :, 0:ncols], 0.0), [], R.b)
                                else:
                                    for t_ in (acc, accd, acc1, accd1):
                                        zero_acc(t_, ncols)

                            def diag_mask(t_, mask_ap, mask_b, nk, c0, ncols):
                                dn = min(128, ncols - c0)
                                S.op('dve', lambda e: e.tensor_tensor(out=t_[0:nk, c0:c0 + dn], in0=t_[0:nk, c0:c0 + dn], in1=mask_ap(nk, dn),
                                                                      op=ALU.mult), t_.b + mask_b, t_.b)

                            def stage_a(hg, kb, x, qc0, ncols, sample):
                                nk, c0 = kb['nk'], kb['c0']
                                st = mmbank()
                                if mixer == 'mla':
                                    S.op('pe', lambda e: e.matmul(out=st[0:nk, c0:ncols], lhsT=kb['kt'](x), rhs=QT[:, x, qc0 + c0:qc0 + ncols],
                                                                  start=True, stop=True), kb['ktb'] + QT.b, st.b)
                                    pt = npt()
                                    S.op('act', lambda e: e.activation(out=pt[0:nk, c0:ncols], in_=st[0:nk, c0:ncols], func=AF.Exp), st.b, pt.b)
                                    if kb['diag']:
                                        diag_mask(pt, lambda a_, b_: mch[0:a_, 0:b_], mch.b, nk, c0, ncols)
                                    return dict(pt=pt)
                                if mixer == 'sb':
                                    hp = hg
                                    S.op('pe', lambda e: e.matmul(out=st[0:nk, c0:ncols], lhsT=kb['kt'](hp)[64 * x:64 * x + 64, :],
                                                                  rhs=QT[64 * x:64 * x + 64, hp, qc0 + c0:qc0 + ncols], start=True, stop=True),
                                         kb['ktb'] + QT.b, st.b)
                                    itc[0] += 1
                                    ee = e1[itc[0] % 2]
                                    sp_ = spb[itc[0] % 2]
                                    a1, ad1 = acc1s[itc[0] % 2]
                                    S.op('act', lambda e: e.activation(out=ee[0:nk, c0:ncols], in_=st[0:nk, c0:ncols], func=AF.Exp), st.b, ee.b)
                                    S.op('act', lambda e: e.activation(out=sp_[0:nk, c0:ncols], in_=ee[0:nk, c0:ncols], func=AF.Ln, bias=1.0),
                                         ee.b, sp_.b)
                                    if kb['diag']:
                                        diag_mask(sp_, lambda a_, b_: msb[0:a_, 0:b_], msb.b, nk, c0, ncols)
                                    S.op('pe', lambda e: e.matmul(out=a1[0:nk, c0:ncols], lhsT=tri[0:nk, 0:nk], rhs=sp_[0:nk, c0:ncols],
                                                                  start=True, stop=True), tri.b + sp_.b, a1.b)
                                    S.op('pe', lambda e: e.matmul(out=ad1[:, c0:ncols], lhsT=ones[0:nk, :], rhs=sp_[0:nk, c0:ncols],
                                                                  start=True, stop=True), ones.b + sp_.b, ad1.b)
                                    return dict(ee=ee, a1=a1, ad1=ad1, cu=cumr[itc[0] % 2])
                                hd = hg
                                hgl = 2 * u + hd
                                S.op('pe', lambda e: e.matmul(out=st[0:nk, c0:ncols], lhsT=kb['kt'](hd)[64 * x:64 * x + 64, :],
                                                              rhs=QT[64 * x:64 * x + 64, hd, qc0 + c0:qc0 + ncols], start=True, stop=True),
                                     kb['ktb'] + QT.b, st.b)
                                pt = npt()
                                if sample:
                                    bi = kb['bi']
                                    S.op('act', lambda e: e.activation(out=pt[0:nk, c0:ncols], in_=st[0:nk, c0:ncols], func=AF.Exp,
                                                                       bias=bds[0:nk, hgl, bi:bi + 1]), st.b + bds.b, pt.b)
                                else:
                                    cc = c0
                                    while cc < ncols:
                                        ce = min(ncols, (cc // 256 + 1) * 256)
                                        dl = (kb['qt0'] + ce // 128 - 1) - kb['kbi']
                                        S.op('act', lambda e: e.activation(out=pt[0:nk, cc:ce], in_=st[0:nk, cc:ce], func=AF.Exp,
                                                                           bias=bdp[0:nk, hgl, dl:dl + 1]), st.b + bdp.b, pt.b)
                                        cc = ce
                                if kb['diag']:
                                    diag_mask(pt, lambda a_, b_: mdf[0:a_, hgl, 0:b_], mdf.b, nk, c0, ncols)
                                return dict(pt=pt)

                            def stage_b(hg, kb, x, ncols, acol, cx):
                                nk, c0 = kb['nk'], kb['c0']
                                o0, o1 = acol + c0, acol + ncols
                                if mixer == 'mla':
                                    pt = cx['pt']
                                    S.op('pe', lambda e: e.matmul(out=acc[64 * x:64 * x + 64, o0:o1], lhsT=kb['v'](x),
                                                                  rhs=pt[0:nk, c0:ncols], start=False, stop=True, skip_group_check=True),
                                         kb['vb'] + pt.b, acc.b)
                                    S.op('pe', lambda e: e.matmul(out=accd[64 * x:64 * x + 64, o0:o1], lhsT=ones[0:nk, 0:64],
                                                                  rhs=pt[0:nk, c0:ncols], start=False, stop=True, skip_group_check=True),
                                         ones.b + pt.b, accd.b)
                                elif mixer == 'sb':
                                    hp = hg
                                    ee, a1, ad1, cu = cx['ee'], cx['a1'], cx['ad1'], cx['cu']
                                    S.op('dve', lambda e: e.tensor_tensor(out=cu[0:nk, c0:ncols], in0=a1[0:nk, c0:ncols],
                                                                          in1=R[0:nk, x, o0:o1], op=ALU.add), a1.b + R.b, cu.b)
                                    S.op('act', lambda e: e.activation(out=cu[0:nk, c0:ncols], in_=cu[0:nk, c0:ncols], func=AF.Exp,
                                                                       scale=-1.0), cu.b, cu.b)
                                    pt = npt()
                                    S.op('dve', lambda e: e.tensor_tensor(out=pt[0:nk, c0:ncols], in0=ee[0:nk, c0:ncols],
                                                                          in1=cu[0:nk, c0:ncols], op=ALU.mult), ee.b + cu.b, pt.b)
                                    if kb['diag']:
                                        diag_mask(pt, lambda a_, b_: msb[0:a_, 0:b_], msb.b, nk, c0, ncols)
                                    S.op('dve', lambda e: e.tensor_tensor(out=R[:, x, o0:o1], in0=R[:, x, o0:o1], in1=ad1[:, c0:ncols],
                                                                          op=ALU.add), R.b + ad1.b, R.b)
                                    S.op('pe', lambda e: e.matmul(out=acc[64 * x:64 * x + 64, o0:o1], lhsT=kb['v'](2 * hp + x),
                                                                  rhs=pt[0:nk, c0:ncols], start=False, stop=True, skip_group_check=True),
                                         kb['vb'] + pt.b, acc.b)
                                else:
                                    pt = cx['pt']
                                    a_, d_ = (acc, accd) if x == 0 else (acc1, accd1)
                                    S.op('pe', lambda e: e.matmul(out=a_[:, o0:o1], lhsT=kb['v'](hg), rhs=pt[0:nk, c0:ncols],
                                                                  start=False, stop=True, skip_group_check=True), kb['vb'] + pt.b, a_.b)
                                    S.op('pe', lambda e: e.matmul(out=d_[:, o0:o1], lhsT=ones[0:nk, :], rhs=pt[0:nk, c0:ncols],
                                                                  start=False, stop=True, skip_group_check=True), ones.b + pt.b, d_.b)

                            def att_blocks(hg, kbs, qc0, ncols, sample, acol=0):
                                prev = None
                                for kb in kbs:
                                    for x in range(2):
                                        cx = stage_a(hg, kb, x, qc0, ncols, sample)
                                        if prev is not None:
                                            stage_b(hg, prev[0], prev[1], ncols, acol, prev[2])
                                        prev = (kb, x, cx)
                                stage_b(hg, prev[0], prev[1], ncols, acol, prev[2])

                            def att_end(hg, ncols, ocol0, grp_b, acol=0):
                                a0, a1_ = acol, acol + ncols
                                if mixer == 'mla':
                                    S.op('dve', lambda e: e.reciprocal(out=rden[:, 0:ncols], in_=accd[:, a0:a1_]), accd.b, rden.b)
                                    S.op('dve', lambda e: e.tensor_tensor(out=OT[:, u, ocol0:ocol0 + ncols], in0=acc[:, a0:a1_], in1=rden[:, 0:ncols],
                                                                          op=ALU.mult), acc.b + rden.b, grp_b)
                                elif mixer == 'sb':
                                    S.op('act', lambda e: e.copy(out=OT[:, 2 * u + hg, ocol0:ocol0 + ncols], in_=acc[:, a0:a1_]), acc.b, grp_b)
                                else:
                                    hgl = 2 * u + hg
                                    S.op('dve', lambda e: e.reciprocal(out=rden[:, 0:ncols], in_=accd[:, a0:a1_]), accd.b, rden.b)
                                    S.op('dve', lambda e: e.reciprocal(out=rden1[:, 0:ncols], in_=accd1[:, a0:a1_]), accd1.b, rden1.b)
                                    S.op('dve', lambda e: e.tensor_tensor(out=t0[:, 0:ncols], in0=acc[:, a0:a1_], in1=rden[:, 0:ncols], op=ALU.mult),
                                         acc.b + rden.b, t0.b)
                                    S.op('dve', lambda e: e.tensor_tensor(out=t1[:, 0:ncols], in0=acc1[:, a0:a1_], in1=rden1[:, 0:ncols], op=ALU.mult),
                                         acc1.b + rden1.b, t1.b)
                                    S.op('dve', lambda e: e.scalar_tensor_tensor(out=t0[:, 0:ncols], in0=t1[:, 0:ncols], scalar=gcol[:, 2:3],
                                                                                 in1=t0[:, 0:ncols], op0=ALU.mult, op1=ALU.add),
                                         t0.b + t1.b + gcol.b, t0.b)
                                    S.op('act', lambda e: e.activation(out=osq[:, 0:ncols], in_=t0[:, 0:ncols], func=AF.Square), t0.b, osq.b)
                                    pss = mmbank()
                                    S.op('pe', lambda e: e.matmul(out=pss[:, 0:ncols], lhsT=onesf[:, :], rhs=osq[:, 0:ncols], start=True, stop=True),
                                         onesf.b + osq.b, pss.b)
                                    S.op('act', lambda e: e.activation(out=t1[:, 0:ncols], in_=pss[:, 0:ncols], func=AF.Ln, scale=1.0 / 128, bias=EPS),
                                         pss.b, t1.b)
                                    S.op('act', lambda e: e.activation(out=t1[:, 0:ncols], in_=t1[:, 0:ncols], func=AF.Exp, scale=-0.5), t1.b, t1.b)
                                    S.op('dve', lambda e: e.scalar_tensor_tensor(out=OT[:, hgl, ocol0:ocol0 + ncols], in0=t0[:, 0:ncols],
                                                                                 scalar=gcol[:, 0:1], in1=t1[:, 0:ncols], op0=ALU.mult, op1=ALU.mult),
                                         t0.b + t1.b + gcol.b, grp_b)

                            nhg = 1 if mixer == 'mla' else 2

                            def kb_store(kbi, c0, diag, qt0, nk=128, bi=0):
                                rows_, sl = tsl(kbi)
                                if mixer == 'mla':
                                    return dict(kt=lambda hh: KT[:, hh, sl], ktb=[KT.b[kbi]], v=lambda hh: V[0:nk, kbi, 64 * hh:64 * hh + 64],
                                                vb=[V.b[kbi]], nk=nk, c0=c0, diag=diag, kbi=kbi, qt0=qt0, bi=bi)
                                if mixer == 'sb':
                                    return dict(kt=lambda hp: KT[:, hp, sl], ktb=[KT.b[kbi]], v=lambda h: V[0:nk, kbi, 64 * h:64 * h + 64],
                                                vb=[V.b[kbi]], nk=nk, c0=c0, diag=diag, kbi=kbi, qt0=qt0, bi=bi)
                                return dict(kt=lambda hd: KT[:, hd, sl], ktb=[KT.b[kbi]], v=lambda hd: V[0:nk, kbi, 128 * hd:128 * hd + 128],
                                            vb=[V.b[kbi]], nk=nk, c0=c0, diag=diag, kbi=kbi, qt0=qt0, bi=bi)

                            def kb_past(j, kbi):
                                sl = slice(j * 128, (j + 1) * 128)
                                bi = 32 - kbi
                                if mixer == 'mla':
                                    return dict(kt=lambda hh: KTp[:, hh, sl], ktb=KTp.b, v=lambda hh: Vp[:, j, 64 * hh:64 * hh + 64],
                                                vb=Vp.b, nk=128, c0=0, diag=False, kbi=kbi, qt0=0, bi=bi)
                                if mixer == 'sb':
                                    return dict(kt=lambda hp: KTp[:, hp, sl], ktb=KTp.b, v=lambda h: Vp[:, j, 64 * h:64 * h + 64],
                                                vb=Vp.b, nk=128, c0=0, diag=False, kbi=kbi, qt0=0, bi=bi)
                                return dict(kt=lambda hd: KTp[:, hd, sl], ktb=KTp.b, v=lambda hd: Vp[:, j, 128 * hd:128 * hd + 128],
                                            vb=Vp.b, nk=128, c0=0, diag=False, kbi=kbi, qt0=0, bi=bi)

                            def build_chunk(s, ch):
                                r0 = ch * 512
                                if mixer == 'mla':
                                    S.dma('pool', pck[:], c_ckv[l, s, r0:r0 + 512, :].rearrange("(k p) n -> p k n", p=128), [], pck.b)
                                    S.dma('pool', pkr[:], c_kr[l, s, r0:r0 + 512, :].rearrange("(k p) n -> p k n", p=128), [], pkr.b)
                                    for j in range(4):
                                        transpose_to(pck[:, j, :], pck.b, 128, 128, pckT[:, j * 128:(j + 1) * 128], pckT.b, 'act' if j % 2 else 'dve')
                                    for j in range(4):
                                        jsl = slice(j * 128, (j + 1) * 128)
                                        kv_from_latent(pckT[:, jsl], pckT.b, pkr[:, j, :], pkr.b, 128,
                                                       lambda hh: KTp[:, hh, jsl], KTp.b, Vp[:, j, :], Vp.b, kcat[j % 2])
                                else:
                                    ck = c_sbk if mixer == 'sb' else c_dk
                                    cv = c_sbv if mixer == 'sb' else c_dv
                                    S.dma('pool', pk[:], ck[l, s, r0:r0 + 512, u * 256:(u + 1) * 256].rearrange("(k p) n -> p k n", p=128), [], pk.b)
                                    S.dma('pool', Vp[:], cv[l, s, r0:r0 + 512, u * 256:(u + 1) * 256].rearrange("(k p) n -> p k n", p=128), [], Vp.b)
                                    for j in range(4):
                                        for c in range(2):
                                            transpose_to(pk[:, j, c * 128:(c + 1) * 128], pk.b, 128, 128, KTp[:, c, j * 128:(j + 1) * 128], KTp.b,
                                                         'act' if c else 'dve')

                            for g in range(4):
                                for i in range(4):
                                    proj_tile(4 * g + i, i * 128)
                                kbs = []
                                for kbi in range(4 * g + 4):
                                    i = kbi - 4 * g
                                    kbs.append(kb_store(kbi, max(i, 0) * 128, i >= 0, 4 * g))
                                if mixer == 'sb':
                                    kbs = kbs[::-1]
                                for hg in range(nhg):
                                    att_begin(512)
                                    att_blocks(hg, kbs, 0, 512, False)
                                    att_end(hg, 512, g * 512, [OT.b[g]])
                                _ck('%s%d_u%d_g%d' % (mixer, l, u, g))

                            for s in range(NS):
                                proj_tile(16 + s, 0)
                                knew = kb_store(16 + s, 0, mixer != 'mla', 0, nk=32, bi=0)
                                att_begin(32 * nhg)
                                if mixer == 'sb':
                                    for hg in range(nhg):
                                        att_blocks(hg, [knew], 0, 32, True, acol=32 * hg)
                                    for ch in range(7, -1, -1):
                                        build_chunk(s, ch)
                                        for hg in range(nhg):
                                            att_blocks(hg, [kb_past(j, ch * 4 + j) for j in range(3, -1, -1)], 0, 32, True, acol=32 * hg)
                                else:
                                    for ch in range(8):
                                        build_chunk(s, ch)
                                        for hg in range(nhg):
                                            att_blocks(hg, [kb_past(j, ch * 4 + j) for j in range(4)], 0, 32, True, acol=32 * hg)
                                    for hg in range(nhg):
                                        att_blocks(hg, [knew], 0, 32, True, acol=32 * hg)
                                for hg in range(nhg):
                                    att_end(hg, 32, 2048 + 32 * s, [OT.b[4]], acol=32 * hg)
                                _ck('%s%d_u%d_s%d' % (mixer, l, u, s))
                            S.barrier()
                            _ck('%s%d_u%d' % (mixer, l, u))

                    mi = ('mla', 'sb', 'diff').index(mixer)
                    ms.close()
                    with ExitStack() as gs_:
                        mm_list[0] = [0, 1, 4, 5, 6, 7]
                        wg = sb(gs_, [128, 8, 1024], BF16)
                        g0 = OFF['g'] + 1024 * mi
                        load_w(wg, W['w_in'][l, :, g0:g0 + 1024].rearrange("(k p) n -> p k n", p=128))
                        wbr = sb(gs_, [128, 4, 1024], BF16)
                        load_w(wbr, W['w_br_' + mixer][l].rearrange("(k p) n -> p k n", p=128))
                        wo = sb(gs_, [128, 8, 1024], BF16)
                        load_w(wo, W['w_out'][l].rearrange("(k p) n -> p k n", p=128))
                        gT = [sb(gs_, [128, 512], F32) for _ in range(2)]
                        MG = [sb(gs_, [128, 8, 512], BF16) for _ in range(1)]
                        for gi, (c0, n, tiles) in enumerate(GROUPS):
                            hb_ = [HT.b[t] for t in tiles]
                            mg = MG[0]
                            for nn in range(8):
                                nsl = slice(nn * 128, (nn + 1) * 128)
                                pg = mmbank()
                                for k in range(8):
                                    S.op('pe', lambda e: e.matmul(out=pg[:, 0:n], lhsT=wg[:, k, nsl], rhs=HT[:, k, c0:c0 + n],
                                                                  start=(k == 0), stop=(k == 7)), wg.b + hb_, pg.b, inc=(k == 7))
                                gt = gT[nn % 2]
                                S.op('act', lambda e: e.activation(out=gt[:, 0:n], in_=pg[:, 0:n], func=AF.Sigmoid), pg.b, gt.b)
                                py = mmbank()
                                for c in range(4):
                                    S.op('pe', lambda e: e.matmul(out=py[:, 0:n], lhsT=wbr[:, c, nsl], rhs=OT[:, c, c0:c0 + n],
                                                                  start=(c == 0), stop=(c == 3)), wbr.b + [OT.b[gi]], py.b, inc=(c == 3))
                                S.op('dve', lambda e: e.tensor_tensor(out=mg[:, nn, 0:n], in0=py[:, 0:n], in1=gt[:, 0:n], op=ALU.mult),
                                     py.b + gt.b, mg.b)
                            for tt in tiles:
                                rows, sl = tsl(tt)
                                off = sl.start - c0
                                for half in range(2):
                                    hsl = slice(half * 512, (half + 1) * 512)
                                    po = mmbank()
                                    for k in range(8):
                                        S.op('pe', lambda e: e.matmul(out=po[0:rows, :], lhsT=mg[:, k, off:off + rows], rhs=wo[:, k, hsl],
                                                                      start=(k == 0), stop=(k == 7)), mg.b + wo.b, po.b, inc=(k == 7))
                                    S.op('dve', lambda e: e.tensor_tensor(out=X[0:rows, tt, hsl], in0=X[0:rows, tt, hsl], in1=po[0:rows, :],
                                                                          op=ALU.add), [X.b[tt]] + po.b, [X.b[tt]])
                        S.barrier()
                        mm_list[0] = [0, 1]
                        rot['mm'] = 0
                    _ck('%s%d_merge' % (mixer, l))

            norm_phase(l, 'ffn_norm_g', False)
            with ExitStack() as fs:
                mm_list[0] = [0, 1, 4, 5, 6, 7]
                cw = sb(fs, [128, 22, 3], F32)
                cb = sb(fs, [128, 22], F32)
                cst = sb(fs, [128, NS, 22, 2], F32)
                OC = sb(fs, [128, 3, 22, 2], F32)
                S.dma('sp', cw[:].rearrange("p a b -> p (a b)"), W['ffn_conv_w'][l], [], cw.b)
                S.dma('sp', cb[:], W['ffn_conv_b'][l], [], cb.b)
                for s in range(NS):
                    S.dma('sp', cst[:, s, :, :].rearrange("p a b -> p (a b)"), c_conv[l, s], [], cst.b)
                WA = [sb(fs, [128, 8, 512], BF16) for _ in range(2)]
                WU = [sb(fs, [128, 8, 512], BF16) for _ in range(2)]
                WD = [sb(fs, [128, 4, 1024], BF16) for _ in range(2)]
                AT = [sb(fs, [128, 4, 516], F32, 4) for _ in range(1)]
                carry = sb(fs, [128, 4, 2], F32, 4)
                cc_ = [sb(fs, [128, 512], F32) for _ in range(1)]
                sl_ = [sb(fs, [128, 512], F32) for _ in range(1)]
                MM = [sb(fs, [128, 4, 512], BF16) for _ in range(1)]
                fgroups = [(0, 4), (4, 4), (8, 4), (12, 4), (16, 4), (20, 2)]
                for fi, (fc0, nfc) in enumerate(fgroups):
                    wa, wu, wd = WA[fi % 2], WU[fi % 2], WD[fi % 2]
                    nf = nfc * 128
                    S.dma('pool', wa[:, :, 0:nf], W['ffn_w_up'][l, :, fc0 * 128:fc0 * 128 + nf].rearrange("(k p) n -> p k n", p=128), [], wa.b)
                    S.dma('pool', wu[:, :, 0:nf], W['ffn_w_up'][l, :, DFF + fc0 * 128:DFF + fc0 * 128 + nf].rearrange("(k p) n -> p k n", p=128),
                          [], wu.b)
                    S.dma('pool', wd[:, 0:nfc, :], W['ffn_w_down'][l, fc0 * 128:fc0 * 128 + nf, :].rearrange("(c p) n -> p c n", p=128), [], wd.b)
                    for gi, (c0, n, tiles) in enumerate(GROUPS):
                        hb_ = [HT.b[t] for t in tiles]
                        mmt = MM[0]
                        buf = AT[0]
                        for j in range(nfc):
                            fc = fc0 + j
                            jsl = slice(j * 128, (j + 1) * 128)
                            pa = mmbank()
                            for k in range(8):
                                S.op('pe', lambda e: e.matmul(out=pa[:, 0:n], lhsT=wa[:, k, jsl], rhs=HT[:, k, c0:c0 + n],
                                                              start=(k == 0), stop=(k == 7)), wa.b + hb_, pa.b, inc=(k == 7))
                            cc = cc_[0]
                            if gi < 4:
                                S.op('act', lambda e: e.copy(out=buf[:, j, 2:514], in_=pa[:, 0:512]), pa.b, [buf.b[j]])
                                if gi == 0:
                                    S.op('dve', lambda e: e.memset(buf[:, j, 0:2], 0.0), [], [buf.b[j]])
                                else:
                                    S.op('dve', lambda e: e.tensor_copy(out=buf[:, j, 0:2], in_=carry[:, j, :]), [carry.b[j]], [buf.b[j]])
                                S.op('dve', lambda e: e.tensor_copy(out=carry[:, j, :], in_=buf[:, j, 512:514]), [buf.b[j]], [carry.b[j]])
                                segs = [(0, 0, 512)]
                                if gi == 3:
                                    S.op('dve', lambda e: e.tensor_copy(out=OC[:, 0, fc, :], in_=buf[:, j, 512:514]), [buf.b[j]], OC.b)
                            else:
                                for s in range(NS):
                                    S.op('act', lambda e: e.copy(out=buf[:, j, s * 34 + 2:s * 34 + 34], in_=pa[:, s * 32:s * 32 + 32]), pa.b, [buf.b[j]])
                                    S.op('dve', lambda e: e.tensor_copy(out=buf[:, j, s * 34:s * 34 + 2], in_=cst[:, s, fc, :]), cst.b, [buf.b[j]])
                                    S.op('dve', lambda e: e.tensor_copy(out=OC[:, 1 + s, fc, :], in_=buf[:, j, s * 34 + 32:s * 34 + 34]), [buf.b[j]], OC.b)
                                segs = [(0, 0, 32), (34, 32, 32)]
                            for (b0, o0, nn_) in segs:
                                S.op('dve', lambda e: e.tensor_scalar(out=cc[:, o0:o0 + nn_], in0=buf[:, j, b0 + 2:b0 + 2 + nn_], scalar1=cw[:, fc, 2:3],
                                                                      scalar2=cb[:, fc:fc + 1], op0=ALU.mult, op1=ALU.add),
                                     [buf.b[j]] + cw.b + cb.b, cc.b)
                                S.op('dve', lambda e: e.scalar_tensor_tensor(out=cc[:, o0:o0 + nn_], in0=buf[:, j, b0 + 1:b0 + 1 + nn_], scalar=cw[:, fc, 1:2],
                                                                             in1=cc[:, o0:o0 + nn_], op0=ALU.mult, op1=ALU.add),
                                     [buf.b[j]] + cw.b + cc.b, cc.b)
                                S.op('dve', lambda e: e.scalar_tensor_tensor(out=cc[:, o0:o0 + nn_], in0=buf[:, j, b0:b0 + nn_], scalar=cw[:, fc, 0:1],
                                                                             in1=cc[:, o0:o0 + nn_], op0=ALU.mult, op1=ALU.add),
                                     [buf.b[j]] + cw.b + cc.b, cc.b)
                            sl2 = sl_[0]
                            S.op('act', lambda e: e.activation(out=sl2[:, 0:n], in_=cc[:, 0:n], func=AF.Silu), cc.b, sl2.b)
                            pu = mmbank()
                            for k in range(8):
                                S.op('pe', lambda e: e.matmul(out=pu[:, 0:n], lhsT=wu[:, k, jsl], rhs=HT[:, k, c0:c0 + n],
                                                              start=(k == 0), stop=(k == 7)), wu.b + hb_, pu.b, inc=(k == 7))
                            S.op('dve', lambda e: e.tensor_tensor(out=mmt[:, j, 0:n], in0=pu[:, 0:n], in1=sl2[:, 0:n], op=ALU.mult),
                                 pu.b + sl2.b, mmt.b)
                        for tt in tiles:
                            rows, sl = tsl(tt)
                            off = sl.start - c0
                            for half in range(2):
                                hsl = slice(half * 512, (half + 1) * 512)
                                po = mmbank()
                                for j in range(nfc):
                                    S.op('pe', lambda e: e.matmul(out=po[0:rows, :], lhsT=mmt[:, j, off:off + rows], rhs=wd[:, j, hsl],
                                                                  start=(j == 0), stop=(j == nfc - 1)), mmt.b + wd.b, po.b, inc=(j == nfc - 1))
                                S.op('dve', lambda e: e.tensor_tensor(out=X[0:rows, tt, hsl], in0=X[0:rows, tt, hsl], in1=po[0:rows, :],
                                                                      op=ALU.add), [X.b[tt]] + po.b, [X.b[tt]])
                S.dma('sp', o_conv[l].rearrange("s p n -> p s n"), OC[:].rearrange("p s a b -> p s (a b)"), OC.b, [])
                S.barrier()
                mm_list[0] = [0, 1]
                rot['mm'] = 0
            _ck('ffn%d' % l)

        _DEV['off'] = False
        _DEV['nops'] = 0
        _layers()
        _DEV['off'] = False
        mm_list[0] = [0, 1]
        for tt in range(NT):
            rows, sl = tsl(tt)
            S.dma('sp', y[sl, :], X[0:rows, tt, :], [X.b[tt]], [])
        S.barrier()
    return nc


_SLOPES = [2.0 ** (-8.0 * (h + 1) / 4) for h in range(4)]


def _consts():
    half = 16
    inv = (np.float32(10000.0) ** (-np.arange(half, dtype=np.float32) / np.float32(half))).astype(np.float32)
    pos = np.concatenate([np.arange(SP_), PAST + np.arange(SS), PAST + np.arange(SS)]).astype(np.float32)
    ang = (pos[:, None] * inv[None, :]).astype(np.float32)
    k_cs = np.concatenate([np.cos(ang), np.sin(ang)], axis=1).astype(np.float32)
    k = np.arange(128)[:, None]
    q = np.arange(128)[None, :]
    msb = (k < q).astype(np.float32)
    mch = ((k // 64) <= (q // 64)).astype(np.float32)
    mdf = np.zeros((128, 4, 128), np.float32)
    bdp = np.zeros((128, 4, 17), np.float32)
    bds = np.zeros((128, 4, 33), np.float32)
    for h in range(4):
        sl = _SLOPES[h]
        mdf[:, h, :] = mch * np.where(k > q, np.exp(-2.0 * sl * (k - q)), 1.0)
        for d in range(17):
            bdp[:, h, d] = sl * (np.arange(128) - 127 - 128 * d)
        for j in range(33):
            bds[:, h, j] = sl * (np.arange(128) - 31 - 128 * j)
    tri = (k >= q).astype(np.float32)
    return dict(k_cs=k_cs, k_msb=msb, k_mch=mch, k_mdf=mdf.reshape(128, 512), k_tri=tri,
                k_bdp=bdp.reshape(128, 68), k_bds=bds.reshape(128, 132))


_WNAMES = ["mix_norm_g", "w_in", "mla_q_norm_g", "mla_w_uq", "mla_kv_norm_g", "mla_w_uk", "mla_w_uv", "mla_qn_g", "mla_kn_g",
           "mla_qr_g", "mla_kr_g", "diff_qn_g", "diff_kn_g", "diff_lambda", "diff_subln_g", "w_br_mla", "w_br_sb", "w_br_diff",
           "w_out", "ffn_norm_g", "ffn_w_up", "ffn_conv_w", "ffn_conv_b", "ffn_w_down"]


def kernel(**inputs):
    inp = {k: np.asarray(v) for k, v in inputs.items()}
    nc = build_program()
    shared = {}
    for n in _WNAMES:
        a = np.ascontiguousarray(inp[n], dtype=np.float32)
        if n == "diff_lambda":
            a = a.reshape(L, 256)
        elif n == "ffn_conv_w":
            a = np.ascontiguousarray(a.reshape(L, 3, 22, 128).transpose(0, 3, 2, 1)).reshape(L, 128, 66)
        elif n == "ffn_conv_b":
            a = np.ascontiguousarray(a.reshape(L, 22, 128).transpose(0, 2, 1))
        shared[n] = a
    shared.update(_consts())
    in_maps = []
    for c in range(8):
        m = dict(shared)
        m["xin"] = np.ascontiguousarray(np.concatenate([inp["x_prompt"][c], inp["x_sample"][2 * c], inp["x_sample"][2 * c + 1]], axis=0),
                                        dtype=np.float32)
        sl = slice(2 * c, 2 * c + 2)
        m["c_ckv"] = np.ascontiguousarray(inp["cache_mla_ckv"][:, sl])
        m["c_kr"] = np.ascontiguousarray(inp["cache_mla_krope"][:, sl])
        m["c_sbk"] = np.ascontiguousarray(inp["cache_sb_k"][:, sl]).reshape(L, NS, PAST, 512)
        m["c_sbv"] = np.ascontiguousarray(inp["cache_sb_v"][:, sl]).reshape(L, NS, PAST, 512)
        m["c_dk"] = np.ascontiguousarray(inp["cache_diff_k"][:, sl]).reshape(L, NS, PAST, 512)
        m["c_dv"] = np.ascontiguousarray(inp["cache_diff_v"][:, sl]).reshape(L, NS, PAST, 512)
        st = np.asarray(inp["state_ffn_conv"][:, sl], dtype=np.float32)
        m["c_conv"] = np.ascontiguousarray(st.reshape(L, NS, 2, 22, 128).transpose(0, 1, 4, 3, 2)).reshape(L, NS, 128, 44)
        in_maps.append(m)
    res = run_bass_kernel_spmd(nc, in_maps, core_ids=list(range(8))).results

    def gather(name, width):
        p = np.stack([res[c][name][:, 0:SP_] for c in range(8)], axis=1)
        s = np.stack([res[c][name][:, SP_ + SS * j:SP_ + SS * (j + 1)] for c in range(8) for j in range(NS)], axis=1)
        return p, s

    y_p = np.stack([res[c]["y"][0:SP_] for c in range(8)], axis=0)
    y_s = np.stack([res[c]["y"][SP_ + SS * j:SP_ + SS * (j + 1)] for c in range(8) for j in range(NS)], axis=0)
    p_ckv, s_ckv = gather("o_ckv", 128)
    p_kr, s_kr = gather("o_kr", 32)
    p_sbk, s_sbk = gather("o_sbk", 512)
    p_sbv, s_sbv = gather("o_sbv", 512)
    p_dk, s_dk = gather("o_dk", 512)
    p_dv, s_dv = gather("o_dv", 512)

    def conv_of(c, idx):
        oc = res[c]["o_conv"][:, idx].reshape(L, 128, 22, 2)
        return np.ascontiguousarray(oc.transpose(0, 3, 2, 1)).reshape(L, 2, DFF)
    p_conv = np.stack([conv_of(c, 0) for c in range(8)], axis=1)
    s_conv = np.stack([conv_of(c, 1 + j) for c in range(8) for j in range(NS)], axis=1)
    f = np.float32
    return (y_p.astype(f), y_s.astype(f), p_ckv.astype(f), p_kr.astype(f),
            p_sbk.reshape(L, 8, SP_, 8, 64).astype(f), p_sbv.reshape(L, 8, SP_, 8, 64).astype(f),
            p_dk.reshape(L, 8, SP_, 4, 2, 64).astype(f), p_dv.reshape(L, 8, SP_, 4, 128).astype(f), p_conv.astype(f),
            s_ckv.astype(f), s_kr.astype(f), s_sbk.reshape(L, 16, SS, 8, 64).astype(f), s_sbv.reshape(L, 16, SS, 8, 64).astype(f),
            s_dk.reshape(L, 16, SS, 4, 2, 64).astype(f), s_dv.reshape(L, 16, SS, 4, 128).astype(f), s_conv.astype(f))
```

```python
import math
from contextlib import ExitStack
import numpy as np
import concourse.bass as bass
import concourse.mybir as mybir
from concourse.bass_utils import run_bass_kernel_spmd

F32 = mybir.dt.float32
BF16 = mybir.dt.bfloat16
AF = mybir.ActivationFunctionType
ALU = mybir.AluOpType
AX = mybir.AxisListType

L = 2
D = 1024
SP_ = 2048
NS = 2
SS = 32
PAST = 4096
NTOK = SP_ + NS * SS
NT = 18
DFF = 2816
NIN = 6560
EPS = 1e-6
MLA_SCALE = 96 ** -0.5
SB_SCALE = 64 ** -0.5
DIFF_SCALE = 64 ** -0.5
OFF = dict(cq=0, ckv=256, kr=384, sq=416, sk=928, sv=1440, dq=1952, dk=2464, dv=2976, g=3488)
ENG = ['pe', 'act', 'dve', 'pool', 'sp']
NDS = 20
_DEV = {'stop': None, 'off': False, 'maxops': None, 'nops': 0, 'log': None}


class _Stop(Exception):
    pass


def _ck(name):
    if _DEV['log'] is not None and not _DEV['off']:
        _DEV['log'].append((name, _DEV['nops']))
    if _DEV['stop'] == name:
        _DEV['off'] = True


class Buf:
    __slots__ = ('w', 'r', 'excl')

    def __init__(self):
        self.w = None
        self.r = {}
        self.excl = False


class TT:
    def __init__(self, h, n=1):
        self.h = h
        self.b = [Buf() for _ in range(n)]

    def __getitem__(self, k):
        return self.h[k]


class Sync:
    def __init__(self, nc, es):
        self.nc = nc
        self.e = dict(pe=nc.tensor, act=nc.scalar, dve=nc.vector, pool=nc.gpsimd, sp=nc.sync)
        self.sem = {k: es.enter_context(nc.semaphore("sem_" + k)) for k in ENG}
        self.cnt = {k: 0 for k in ENG}
        self.dsem = {q: [es.enter_context(nc.semaphore("ds_%s%d" % (q, i))) for i in range(NDS)] for q in ('sp', 'pool')}
        self.dcnt = {q: [0] * NDS for q in ('sp', 'pool')}
        self.dnext = {'sp': 0, 'pool': 0}
        self.known = {k: {} for k in ENG}
        self.pend = {k: False for k in ENG}
        self.hist = {}

    def _semof(self, k):
        return self.sem[k] if isinstance(k, str) else self.dsem[k[0]][k[1]]

    def _need(self, eng, toks):
        kn = self.known[eng]
        best = {}
        for (k, v) in toks:
            if best.get(k, 0) < v:
                best[k] = v
        need = []
        for k, v in sorted(best.items(), key=lambda kv: str(kv[0])):
            if kn.get(k, 0) < v:
                need.append((k, v))
        implied = {}
        for k, v in need:
            snap = self.hist.get((k, v))
            if snap:
                for k2, v2 in snap.items():
                    if implied.get(k2, 0) < v2:
                        implied[k2] = v2
        out = [(k, v) for (k, v) in need if implied.get(k, 0) < v]
        for k, v in out:
            kn[k] = v
            snap = self.hist.get((k, v))
            if snap:
                for k2, v2 in snap.items():
                    if k2 != eng and kn.get(k2, 0) < v2:
                        kn[k2] = v2
        return out

    def _wait(self, eng, toks, ins_fn=None):
        need = self._need(eng, toks)
        if ins_fn is None:
            for k, v in need:
                self.e[eng].wait_ge(self._semof(k), v)
            return None
        for k, v in need[:-1]:
            self.e[eng].wait_ge(self._semof(k), v)
        ins = ins_fn()
        if need:
            k, v = need[-1]
            ins._wait_ge(self._semof(k), v)
        return ins

    def _deps(self, eng, reads, writes):
        toks = set()
        for b in reads:
            if b.w is not None:
                toks.add(b.w)
            if b.excl:
                for kv in b.r.items():
                    if kv[0] != eng:
                        toks.add(kv)
        for b in writes:
            if b.w is not None:
                toks.add(b.w)
            for kv in b.r.items():
                toks.add(kv)
        if eng == 'pe':
            toks = {t for t in toks if t[0] != 'pe'}
        return toks

    def op(self, eng, fn, reads=(), writes=(), inc=True):
        if _DEV['off']:
            return
        _DEV['nops'] += 1
        if _DEV['maxops'] is not None and _DEV['nops'] > _DEV['maxops'] and not self.pend[eng]:
            _DEV['off'] = True
            return
        ins = self._wait(eng, self._deps(eng, reads, writes), lambda: fn(self.e[eng]))
        c = self.cnt[eng] + 1
        if inc:
            ins.then_inc(self.sem[eng], 1)
            self.cnt[eng] = c
            self.pend[eng] = False
            self.hist[(eng, c)] = dict(self.known[eng])
        else:
            self.pend[eng] = True
        for b in reads:
            b.r[eng] = c
        for b in writes:
            b.w = (eng, c)
            b.r = {}

    def dma(self, q, out, in_, reads=(), writes=()):
        if _DEV['off']:
            return
        toks = self._deps(q, reads, writes)
        i = self.dnext[q]
        self.dnext[q] = (i + 1) % NDS
        key = (q, i)
        if self.dcnt[q][i] > 0:
            toks.add((key, self.dcnt[q][i]))
        ins = self._wait(q, toks, lambda: self.e[q].dma_start(out=out, in_=in_))
        ins.then_inc(self.dsem[q][i], 16)
        self.dcnt[q][i] += 16
        v = self.dcnt[q][i]
        self.hist[(key, v)] = dict(self.known[q])
        for b in reads:
            b.r[key] = v
        for b in writes:
            b.w = (key, v)
            b.r = {}

    def all_tokens(self):
        toks = {(k, self.cnt[k]) for k in ENG if self.cnt[k] > 0}
        for q in ('sp', 'pool'):
            for i, c in enumerate(self.dcnt[q]):
                if c > 0:
                    toks.add(((q, i), c))
        return toks

    def barrier(self):
        if _DEV['off']:
            return
        for k in ENG:
            assert not self.pend[k]
        toks = self.all_tokens()
        for eng in ENG:
            self._wait(eng, {t for t in toks if t[0] != eng})


def build_program():
    nc = bass.Bass("TRN2", target_bir_lowering=False)

    def din(name, shape):
        return nc.dram_tensor(name, list(shape), F32, kind="ExternalInput").ap()

    def dout(name, shape):
        return nc.dram_tensor(name, list(shape), F32, kind="ExternalOutput").ap()

    xin = din("xin", [NTOK, D])
    c_ckv = din("c_ckv", [L, NS, PAST, 128])
    c_kr = din("c_kr", [L, NS, PAST, 32])
    c_sbk = din("c_sbk", [L, NS, PAST, 512])
    c_sbv = din("c_sbv", [L, NS, PAST, 512])
    c_dk = din("c_dk", [L, NS, PAST, 512])
    c_dv = din("c_dv", [L, NS, PAST, 512])
    c_conv = din("c_conv", [L, NS, 128, 22 * 2])
    W = {}
    for name, shape in [("mix_norm_g", [L, D]), ("w_in", [L, D, NIN]), ("mla_q_norm_g", [L, 256]),
                        ("mla_w_uq", [L, 256, 768]), ("mla_kv_norm_g", [L, 128]), ("mla_w_uk", [L, 128, 512]),
                        ("mla_w_uv", [L, 128, 512]), ("mla_qn_g", [L, 64]), ("mla_kn_g", [L, 64]),
                        ("mla_qr_g", [L, 32]), ("mla_kr_g", [L, 32]), ("diff_qn_g", [L, 64]),
                        ("diff_kn_g", [L, 64]), ("diff_lambda", [L, 256]), ("diff_subln_g", [L, 128]),
                        ("w_br_mla", [L, 512, D]), ("w_br_sb", [L, 512, D]), ("w_br_diff", [L, 512, D]),
                        ("w_out", [L, D, D]), ("ffn_norm_g", [L, D]), ("ffn_w_up", [L, D, 2 * DFF]),
                        ("ffn_conv_w", [L, 128, 22 * 3]), ("ffn_conv_b", [L, 128, 22]), ("ffn_w_down", [L, DFF, D])]:
        W[name] = din(name, shape)
    k_cs = din("k_cs", [NTOK, 32])
    k_msb = din("k_msb", [128, 128])
    k_mch = din("k_mch", [128, 128])
    k_mdf = din("k_mdf", [128, 4 * 128])
    k_tri = din("k_tri", [128, 128])
    k_bdp = din("k_bdp", [128, 4 * 17])
    k_bds = din("k_bds", [128, 4 * 33])

    y = dout("y", [NTOK, D])
    o_ckv = dout("o_ckv", [L, NTOK, 128])
    o_kr = dout("o_kr", [L, NTOK, 32])
    o_sbk = dout("o_sbk", [L, NTOK, 512])
    o_sbv = dout("o_sbv", [L, NTOK, 512])
    o_dk = dout("o_dk", [L, NTOK, 512])
    o_dv = dout("o_dv", [L, NTOK, 512])
    o_conv = dout("o_conv", [L, 3, 128, 22 * 2])

    with ExitStack() as es:
        E = es.enter_context
        S = Sync(nc, es)
        cnt = [0]

        def sb(es_, shape, dt, n=1):
            cnt[0] += 1
            return TT(es_.enter_context(nc.sbuf_tensor("t%d" % cnt[0], list(shape), dt)), n)

        X = sb(es, [128, NT, D], F32, NT)
        HT = sb(es, [128, 8, NTOK], BF16, NT)
        OT = sb(es, [128, 4, NTOK], BF16, 5)
        ident = sb(es, [128, 128], BF16)
        ones = sb(es, [128, 128], BF16)
        zeros = sb(es, [128, 128], BF16)
        onesf = sb(es, [128, 128], F32)
        tri = sb(es, [128, 128], BF16)
        msb = sb(es, [128, 128], F32)
        mch = sb(es, [128, 128], F32)
        mdf = sb(es, [128, 4, 128], F32)
        bdp = sb(es, [128, 4, 17], F32)
        bds = sb(es, [128, 4, 33], F32)
        cs = sb(es, [128, NT, 32], F32)
        gbig = sb(es, [128, D], F32)
        gsm = sb(es, [128, 1024], F32)
        gcol = sb(es, [128, 8], F32)
        PS = [TT(E(nc.psum_tensor("ps%d" % i, [128, 512], F32))) for i in range(8)]
        for p_ in PS:
            p_.b[0].excl = True
        rot = {'mm': 0, 'tp': 0}

        def mmbank():
            rot['mm'] = (rot['mm'] + 1) % len(mm_list[0])
            return PS[mm_list[0][rot['mm']]]

        def tpbank():
            rot['tp'] ^= 1
            return PS[2 + rot['tp']]

        def tsl(tt):
            if tt < 16:
                return 128, slice(tt * 128, tt * 128 + 128)
            return 32, slice(2048 + 32 * (tt - 16), 2048 + 32 * (tt - 16) + 32)

        GROUPS = [(g * 512, 512, [4 * g + i for i in range(4)]) for g in range(4)] + [(2048, 64, [16, 17])]
        mm_list = [[0, 1]]

        S.op('pool', lambda e: e.memset(ident[:], 0.0), [], ident.b)
        S.op('pool', lambda e: e.affine_select(out=ident[:], in_=ident[:], pattern=[[-1, 128]], compare_op=ALU.not_equal,
                                               fill=1.0, base=0, channel_multiplier=1), ident.b, ident.b)
        S.op('pool', lambda e: e.memset(ones[:], 1.0), [], ones.b)
        S.op('pool', lambda e: e.memset(zeros[:], 0.0), [], zeros.b)
        S.op('pool', lambda e: e.memset(onesf[:], 1.0), [], onesf.b)
        S.dma('pool', tri[:], k_tri, [], tri.b)
        S.dma('sp', msb[:], k_msb, [], msb.b)
        S.dma('sp', mch[:], k_mch, [], mch.b)
        S.dma('sp', mdf[:].rearrange("p a b -> p (a b)"), k_mdf, [], mdf.b)
        S.dma('sp', bdp[:].rearrange("p a b -> p (a b)"), k_bdp, [], bdp.b)
        S.dma('sp', bds[:].rearrange("p a b -> p (a b)"), k_bds, [], bds.b)
        S.dma('sp', cs[:, 0:16, :], k_cs[0:2048, :].rearrange("(t p) n -> p t n", p=128), [], cs.b)
        S.dma('sp', cs[0:32, 16, :], k_cs[2048:2080, :], [], cs.b)
        S.dma('sp', cs[0:32, 17, :], k_cs[2080:2112, :], [], cs.b)
        zrhs = sb(es, [128, 512], BF16)
        S.op('pool', lambda e: e.memset(zrhs[:], 0.0), [], zrhs.b)

        GS = dict(q_norm=(0, 256), kv_norm=(256, 128), kr=(384, 32), qn=(416, 64), kn=(480, 64), qr=(544, 32),
                  dqn=(576, 64), dkn=(640, 64), lam=(704, 256))

        def gs(name, rows=128):
            o, n = GS[name]
            return gsm[0:rows, o:o + n]

        def rstd_from_ss(ss_t, rows, G, d):
            S.op('act', lambda e: e.activation(out=ss_t[0:rows, 0:G], in_=ss_t[0:rows, 0:G], func=AF.Ln, scale=1.0 / d, bias=EPS),
                 ss_t.b, ss_t.b)
            S.op('act', lambda e: e.activation(out=ss_t[0:rows, 0:G], in_=ss_t[0:rows, 0:G], func=AF.Exp, scale=-0.5),
                 ss_t.b, ss_t.b)

        def gnorm(ws, src, srcb, rows, G, d, gain, out, outb, post_scale=1.0):
            sq, ss, tmp = ws['sq'], ws['ss'], ws['tmp']
            sqv = sq[0:rows, 0:G * d].rearrange("p (g d) -> p g d", d=d)
            S.op('act', lambda e: e.activation(out=sqv, in_=src, func=AF.Square), srcb, sq.b)
            S.op('dve', lambda e: e.tensor_reduce(out=ss[0:rows, 0:G], in_=sqv, axis=AX.X, op=ALU.add), sq.b, ss.b)
            rstd_from_ss(ss, rows, G, d)
            tv = tmp[0:rows, 0:G * d].rearrange("p (g d) -> p g d", d=d)
            S.op('dve', lambda e: e.tensor_tensor(out=tv, in0=src, in1=ss[0:rows, 0:G].unsqueeze(2).to_broadcast([rows, G, d]),
                                                  op=ALU.mult), list(srcb) + ss.b, tmp.b)
            gb = gain.unsqueeze(1).to_broadcast([rows, G, d])
            if post_scale == 1.0:
                S.op('dve', lambda e: e.tensor_tensor(out=out, in0=tv, in1=gb, op=ALU.mult), tmp.b + gsm.b, outb)
            else:
                S.op('dve', lambda e: e.scalar_tensor_tensor(out=out, in0=tv, scalar=float(post_scale), in1=gb, op0=ALU.mult,
                                                             op1=ALU.mult), tmp.b + gsm.b, outb)

        def rope(ws, src, srcb, rows, H, tt, out, outb):
            t1, t2 = ws['r1'], ws['r2']
            cosb = cs[0:rows, tt, 0:16].unsqueeze(1).to_broadcast([rows, H, 16])
            sinb = cs[0:rows, tt, 16:32].unsqueeze(1).to_broadcast([rows, H, 16])
            a1 = t1[0:rows, 0:H * 16].rearrange("p (h d) -> p h d", d=16)
            a2 = t2[0:rows, 0:H * 16].rearrange("p (h d) -> p h d", d=16)
            x1 = src[:, :, 0:16]
            x2 = src[:, :, 16:32]
            S.op('dve', lambda e: e.tensor_tensor(out=a1, in0=x1, in1=cosb, op=ALU.mult), list(srcb) + cs.b, t1.b)
            S.op('dve', lambda e: e.tensor_tensor(out=a2, in0=x2, in1=sinb, op=ALU.mult), list(srcb) + cs.b, t2.b)
            S.op('dve', lambda e: e.tensor_tensor(out=out[:, :, 0:16], in0=a1, in1=a2, op=ALU.subtract), t1.b + t2.b, outb)
            S.op('dve', lambda e: e.tensor_tensor(out=a1, in0=x1, in1=sinb, op=ALU.mult), list(srcb) + cs.b, t1.b)
            S.op('dve', lambda e: e.tensor_tensor(out=a2, in0=x2, in1=cosb, op=ALU.mult), list(srcb) + cs.b, t2.b)
            S.op('dve', lambda e: e.tensor_tensor(out=out[:, :, 16:32], in0=a1, in1=a2, op=ALU.add), t1.b + t2.b, outb)

        def transpose_to(src, srcb, rows, ncols, dst, dstb, eng='act'):
            pb = tpbank()
            pv = pb.h[:, :].bitcast(BF16)
            S.op('pe', lambda e: e.transpose(out=pv[0:ncols, 0:rows], in_=src, identity=ident[0:rows, 0:rows]), list(srcb) + ident.b, pb.b)
            if eng == 'act':
                S.op('act', lambda e: e.copy(out=dst, in_=pv[0:ncols, 0:rows]), pb.b, dstb)
            else:
                S.op('dve', lambda e: e.tensor_copy(out=dst, in_=pv[0:ncols, 0:rows]), pb.b, dstb)

        def load_w(dst, src_ap):
            S.dma('pool', dst.h[:], src_ap, [], dst.b)

        def norm_phase(l, gname, first):
            with ExitStack() as ps:
                sq = sb(ps, [128, D], F32)
                ssr = [sb(ps, [128, 1], F32) for _ in range(2)]
                hb = [sb(ps, [128, D], BF16) for _ in range(2)]
                S.dma('sp', gbig[:], W[gname][l].partition_broadcast(128), [], gbig.b)
                for tt in range(NT):
                    rows, sl = tsl(tt)
                    if first:
                        S.dma('sp', X[0:rows, tt, :], xin[sl, :], [], [X.b[tt]])
                    ss = ssr[tt % 2]
                    h = hb[tt % 2]
                    S.op('act', lambda e: e.activation(out=sq[0:rows, :], in_=X[0:rows, tt, :], func=AF.Square,
                                                       accum_out=ss[0:rows, :]), [X.b[tt]], sq.b + ss.b)
                    rstd_from_ss(ss, rows, 1, D)
                    S.op('dve', lambda e: e.scalar_tensor_tensor(out=h[0:rows, :], in0=X[0:rows, tt, :], scalar=ss[0:rows, 0:1],
                                                                 in1=gbig[0:rows, :], op0=ALU.mult, op1=ALU.mult),
                         [X.b[tt]] + ss.b + gbig.b, h.b)
                    pb = tpbank()
                    pv = pb.h[:, :].bitcast(BF16).rearrange("p (k n) -> p k n", n=128)
                    for k in range(8):
                        S.op('pe', lambda e: e.transpose(out=pv[:, k, 0:rows], in_=h[0:rows, k * 128:(k + 1) * 128],
                                                         identity=ident[0:rows, 0:rows]), h.b + ident.b, pb.b, inc=(k == 7))
                    S.op('act' if tt % 2 else 'dve',
                         (lambda e: e.copy(out=HT[:, :, sl], in_=pv[:, :, 0:rows])) if tt % 2 else
                         (lambda e: e.tensor_copy(out=HT[:, :, sl], in_=pv[:, :, 0:rows])), pb.b, [HT.b[tt]])
                S.barrier()

        def _layers():
          for l in range(L):
            lam_init = 0.8 - 0.6 * math.exp(-0.3 * l)
            for nm, wn in [('q_norm', 'mla_q_norm_g'), ('kv_norm', 'mla_kv_norm_g'), ('kr', 'mla_kr_g'), ('qn', 'mla_qn_g'),
                           ('kn', 'mla_kn_g'), ('qr', 'mla_qr_g'), ('dqn', 'diff_qn_g'), ('dkn', 'diff_kn_g'),
                           ('lam', 'diff_lambda')]:
                S.dma('sp', gs(nm), W[wn][l].partition_broadcast(128), [], gsm.b)
            S.dma('sp', gcol[:, 0:1], W['diff_subln_g'][l].rearrange("(p o) -> p o", o=1), [], gcol.b)
            with ExitStack() as ps:
                lt = sb(ps, [128, 128], F32)
                l2 = sb(ps, [128, 2], F32)
                lv = gs('lam')
                S.op('dve', lambda e: e.tensor_tensor(out=lt[:, 0:64], in0=lv[:, 0:64], in1=lv[:, 64:128], op=ALU.mult), gsm.b, lt.b)
                S.op('dve', lambda e: e.tensor_tensor(out=lt[:, 64:128], in0=lv[:, 128:192], in1=lv[:, 192:256], op=ALU.mult), gsm.b, lt.b)
                S.op('dve', lambda e: e.tensor_reduce(out=l2[:, 0:2], in_=lt[:, :].rearrange("p (a b) -> p a b", b=64), axis=AX.X,
                                                      op=ALU.add), lt.b, l2.b)
                S.op('act', lambda e: e.activation(out=l2[:, 0:2], in_=l2[:, 0:2], func=AF.Exp), l2.b, l2.b)
                S.op('dve', lambda e: e.tensor_tensor(out=gcol[:, 1:2], in0=l2[:, 0:1], in1=l2[:, 1:2], op=ALU.subtract), l2.b, gcol.b)
                S.op('dve', lambda e: e.tensor_scalar(out=gcol[:, 1:2], in0=gcol[:, 1:2], scalar1=float(lam_init), scalar2=None,
                                                      op0=ALU.add), gcol.b, gcol.b)
                S.op('dve', lambda e: e.tensor_scalar(out=gcol[:, 2:3], in0=gcol[:, 1:2], scalar1=-1.0, scalar2=None,
                                                      op0=ALU.mult), gcol.b, gcol.b)
                S.op('dve', lambda e: e.tensor_scalar(out=gcol[:, 0:1], in0=gcol[:, 0:1], scalar1=float(1.0 - lam_init), scalar2=None,
                                                      op0=ALU.mult), gcol.b, gcol.b)
                S.barrier()

            norm_phase(l, 'mix_norm_g', l == 0)
            _ck('norm%d' % l)

            for mixer in ('mla', 'sb', 'diff'):
                with ExitStack() as ms:
                    if mixer == 'mla':
                        CQT = sb(ms, [128, 2, NTOK], BF16, NT)
                        CKVT = sb(ms, [128, NTOK], BF16, NT)
                        KRT = sb(ms, [128, NT, 32], BF16, NT)
                        nunits, hpu = 4, 2
                    elif mixer == 'sb':
                        nunits, hpu = 2, 4
                    else:
                        nunits, hpu = 2, 2
                    for u in range(nunits):
                        with ExitStack() as us:
                            ws = dict(ss=sb(us, [128, 8], F32))
                            if mixer != 'sb':
                                ws['sq'] = sb(us, [128, 512], F32)
                                ws['tmp'] = sb(us, [128, 512], F32)
                            if mixer == 'mla':
                                ws['r1'] = sb(us, [128, 128], F32)
                                ws['r2'] = sb(us, [128, 128], F32)
                            stage = [sb(us, [128, 256], F32) for _ in range(3)]
                            stg_i = [0]

                            def nstage():
                                stg_i[0] = (stg_i[0] + 1) % 3
                                return stage[stg_i[0]]
                            tokb = [sb(us, [128, 256], BF16) for _ in range(3)]
                            tok_i = [0]

                            def ntok():
                                tok_i[0] = (tok_i[0] + 1) % 3
                                return tokb[tok_i[0]]
                            PT = [sb(us, [128, 512], BF16) for _ in range(3)]
                            pt_i = [0]

                            def npt():
                                pt_i[0] = (pt_i[0] + 1) % 3
                                return PT[pt_i[0]]
                            rden = sb(us, [128, 512], F32)
                            if mixer == 'mla':
                                KT = sb(us, [96, 2, NTOK], BF16, NT)
                                V = sb(us, [128, NT, 128], BF16, NT)
                                QT = sb(us, [96, 2, 512], BF16)
                                if u == 0:
                                    w1 = sb(us, [128, 8, 416], BF16)
                                    load_w(w1, W['w_in'][l, :, 0:416].rearrange("(k p) n -> p k n", p=128))
                                wuq = sb(us, [128, 2, 192], BF16)
                                load_w(wuq, W['mla_w_uq'][l, :, u * 192:(u + 1) * 192].rearrange("(k p) n -> p k n", p=128))
                                wuk = sb(us, [128, 128], BF16)
                                load_w(wuk, W['mla_w_uk'][l, :, u * 128:(u + 1) * 128])
                                wuv = sb(us, [128, 128], BF16)
                                load_w(wuv, W['mla_w_uv'][l, :, u * 128:(u + 1) * 128])
                                kcat = [sb(us, [128, 2, 96], BF16) for _ in range(2)]
                                qcat = [sb(us, [128, 2, 96], BF16) for _ in range(2)]
                                CS = [dict(pck=sb(us, [128, 4, 128], BF16), pkr=sb(us, [128, 4, 32], BF16), pckT=sb(us, [128, 512], BF16),
                                           KTp=sb(us, [96, 2, 512], BF16), Vp=sb(us, [128, 4, 128], BF16),
                                           kc4=sb(us, [128, 4, 2, 96], BF16)) for _ in range(2)]
                            else:
                                c0q = OFF['sq' if mixer == 'sb' else 'dq'] + u * 256
                                c0k = OFF['sk' if mixer == 'sb' else 'dk'] + u * 256
                                c0v = OFF['sv' if mixer == 'sb' else 'dv'] + u * 256
                                wq = sb(us, [128, 8, 256], BF16)
                                wk = sb(us, [128, 8, 256], BF16)
                                wv = sb(us, [128, 8, 256], BF16)
                                load_w(wq, W['w_in'][l, :, c0q:c0q + 256].rearrange("(k p) n -> p k n", p=128))
                                load_w(wk, W['w_in'][l, :, c0k:c0k + 256].rearrange("(k p) n -> p k n", p=128))
                                load_w(wv, W['w_in'][l, :, c0v:c0v + 256].rearrange("(k p) n -> p k n", p=128))
                                KT = sb(us, [128, 2, NTOK], BF16, NT)
                                V = sb(us, [128, NT, 256], BF16, NT)
                                QT = sb(us, [128, 2, 512], BF16)
                                CS = [dict(pk=sb(us, [128, 4, 256], BF16), KTp=sb(us, [128, 2, 512], BF16), Vp=sb(us, [128, 4, 256], BF16))
                                      for _ in range(2)]
                                if mixer == 'sb':
                                    R = sb(us, [128, 2, 512], F32)
                                    e1 = [sb(us, [128, 512], F32) for _ in range(2)]
                                    spb = [sb(us, [128, 512], BF16) for _ in range(2)]
                                    cumr = [sb(us, [128, 512], F32) for _ in range(2)]
                                else:
                                    t0 = sb(us, [128, 512], F32)
                                    t1 = sb(us, [128, 512], F32)
                                    rden1 = sb(us, [128, 512], F32)
                                    osq = sb(us, [128, 512], F32)
                            o_k = o_sbk if mixer == 'sb' else o_dk
                            o_v = o_sbv if mixer == 'sb' else o_dv

                            def proj_tile(tt, qcol0):
                                rows, sl = tsl(tt)
                                if mixer == 'mla':
                                    if u == 0:
                                        pz = mmbank()
                                        for k in range(8):
                                            S.op('pe', lambda e: e.matmul(out=pz[0:rows, 0:416], lhsT=HT[:, k, sl], rhs=w1[:, k, :],
                                                                          start=(k == 0), stop=(k == 7)), [HT.b[tt]] + w1.b, pz.b, inc=(k == 7))
                                        tk = ntok()
                                        gnorm(ws, pz[0:rows, 0:256].rearrange("p (g d) -> p g d", g=1), pz.b, rows, 1, 256, gs('q_norm', rows),
                                              tk[0:rows, 0:256].rearrange("p (g d) -> p g d", g=1), tk.b)
                                        for c in range(2):
                                            transpose_to(tk[0:rows, c * 128:(c + 1) * 128], tk.b, rows, 128, CQT[:, c, sl], [CQT.b[tt]],
                                                         'act' if c else 'dve')
                                        st = nstage()
                                        gnorm(ws, pz[0:rows, 256:384].rearrange("p (g d) -> p g d", g=1), pz.b, rows, 1, 128, gs('kv_norm', rows),
                                              st[0:rows, 0:128].rearrange("p (g d) -> p g d", g=1), st.b)
                                        S.dma('sp', o_ckv[l, sl, :], st[0:rows, 0:128], st.b, [])
                                        tk2 = ntok()
                                        S.op('act', lambda e: e.copy(out=tk2[0:rows, 0:128], in_=st[0:rows, 0:128]), st.b, tk2.b)
                                        transpose_to(tk2[0:rows, 0:128], tk2.b, rows, 128, CKVT[:, sl], [CKVT.b[tt]], 'dve')
                                        gnorm(ws, pz[0:rows, 384:416].rearrange("p (g d) -> p g d", g=1), pz.b, rows, 1, 32, gs('kr', rows),
                                              st[0:rows, 128:160].rearrange("p (g d) -> p g d", g=1), st.b)
                                        rope(ws, st[0:rows, 128:160].rearrange("p (g d) -> p g d", g=1), st.b, rows, 1, tt,
                                             st[0:rows, 160:192].rearrange("p (g d) -> p g d", g=1), st.b)
                                        S.dma('sp', o_kr[l, sl, :], st[0:rows, 160:192], st.b, [])
                                        S.op('act', lambda e: e.copy(out=KRT[0:rows, tt, :], in_=st[0:rows, 160:192]), st.b, [KRT.b[tt]])
                                    pq = mmbank()
                                    for c in range(2):
                                        S.op('pe', lambda e: e.matmul(out=pq[0:rows, 0:192], lhsT=CQT[:, c, sl], rhs=wuq[:, c, :],
                                                                      start=(c == 0), stop=(c == 1)), [CQT.b[tt]] + wuq.b, pq.b, inc=(c == 1))
                                    qv = pq[0:rows, 0:192].rearrange("p (h d) -> p h d", d=96)
                                    qc = qcat[tt % 2]
                                    gnorm(ws, qv[:, :, 0:64], pq.b, rows, 2, 64, gs('qn', rows), qc[0:rows, :, 0:64], qc.b, MLA_SCALE)
                                    st = nstage()
                                    stv = st[0:rows, 0:64].rearrange("p (h d) -> p h d", d=32)
                                    gnorm(ws, qv[:, :, 64:96], pq.b, rows, 2, 32, gs('qr', rows), stv, st.b, MLA_SCALE)
                                    rope(ws, stv, st.b, rows, 2, tt, qc[0:rows, :, 64:96], qc.b)
                                    for hh in range(2):
                                        transpose_to(qc[0:rows, hh, :], qc.b, rows, 96, QT[:, hh, qcol0:qcol0 + rows], QT.b, 'act' if hh else 'dve')
                                    kv_from_latent(CKVT[:, sl], [CKVT.b[tt]], KRT[0:rows, tt, :], [KRT.b[tt]], rows,
                                                   lambda hh: KT[:, hh, sl], [KT.b[tt]], V[0:rows, tt, :], [V.b[tt]], kcat[tt % 2])
                                else:
                                    pq = mmbank()
                                    for k in range(8):
                                        S.op('pe', lambda e: e.matmul(out=pq[0:rows, 0:256], lhsT=HT[:, k, sl], rhs=wq[:, k, :],
                                                                      start=(k == 0), stop=(k == 7)), [HT.b[tt]] + wq.b, pq.b, inc=(k == 7))
                                    tk = ntok()
                                    if mixer == 'sb':
                                        S.op('act', lambda e: e.activation(out=tk[0:rows, :], in_=pq[0:rows, 0:256], func=AF.Copy,
                                                                           scale=SB_SCALE), pq.b, tk.b)
                                    else:
                                        gnorm(ws, pq[0:rows, 0:256].rearrange("p (g d) -> p g d", d=64), pq.b, rows, 4, 64, gs('dqn', rows),
                                              tk[0:rows, :].rearrange("p (g d) -> p g d", d=64), tk.b, DIFF_SCALE)
                                    for c in range(2):
                                        transpose_to(tk[0:rows, c * 128:(c + 1) * 128], tk.b, rows, 128, QT[:, c, qcol0:qcol0 + rows], QT.b,
                                                     'act' if c else 'dve')
                                    pk_ = mmbank()
                                    for k in range(8):
                                        S.op('pe', lambda e: e.matmul(out=pk_[0:rows, 0:256], lhsT=HT[:, k, sl], rhs=wk[:, k, :],
                                                                      start=(k == 0), stop=(k == 7)), [HT.b[tt]] + wk.b, pk_.b, inc=(k == 7))
                                    st = nstage()
                                    if mixer == 'sb':
                                        S.op('act', lambda e: e.copy(out=st[0:rows, :], in_=pk_[0:rows, 0:256]), pk_.b, st.b)
                                    else:
                                        gnorm(ws, pk_[0:rows, 0:256].rearrange("p (g d) -> p g d", d=64), pk_.b, rows, 4, 64, gs('dkn', rows),
                                              st[0:rows, :].rearrange("p (g d) -> p g d", d=64), st.b)
                                    S.dma('sp', o_k[l, sl, u * 256:(u + 1) * 256], st[0:rows, :], st.b, [])
                                    tk = ntok()
                                    S.op('dve', lambda e: e.tensor_copy(out=tk[0:rows, :], in_=st[0:rows, :]), st.b, tk.b)
                                    for c in range(2):
                                        transpose_to(tk[0:rows, c * 128:(c + 1) * 128], tk.b, rows, 128, KT[:, c, sl], [KT.b[tt]],
                                                     'act' if c else 'dve')
                                    pv_ = mmbank()
                                    for k in range(8):
                                        S.op('pe', lambda e: e.matmul(out=pv_[0:rows, 0:256], lhsT=HT[:, k, sl], rhs=wv[:, k, :],
                                                                      start=(k == 0), stop=(k == 7)), [HT.b[tt]] + wv.b, pv_.b, inc=(k == 7))
                                    st = nstage()
                                    S.op('act', lambda e: e.copy(out=st[0:rows, :], in_=pv_[0:rows, 0:256]), pv_.b, st.b)
                                    S.dma('sp', o_v[l, sl, u * 256:(u + 1) * 256], st[0:rows, :], st.b, [])
                                    S.op('dve', lambda e: e.tensor_copy(out=V[0:rows, tt, :], in_=pv_[0:rows, 0:256]), pv_.b, [V.b[tt]])

                            def kv_from_latent(ckvT_ap, ckvT_b, kr_ap, kr_b, rows, kt_dst, kt_b, v_dst, v_b, kc):
                                pk_ = mmbank()
                                S.op('pe', lambda e: e.matmul(out=pk_[0:rows, 0:128], lhsT=ckvT_ap, rhs=wuk[:, :], start=True, stop=True),
                                     list(ckvT_b) + wuk.b, pk_.b)
                                gnorm(ws, pk_[0:rows, 0:128].rearrange("p (h d) -> p h d", d=64), pk_.b, rows, 2, 64, gs('kn', rows),
                                      kc[0:rows, :, 0:64], kc.b)
                                S.op('dve', lambda e: e.tensor_copy(out=kc[0:rows, :, 64:96], in_=kr_ap.unsqueeze(1).to_broadcast([rows, 2, 32])),
                                     list(kr_b), kc.b)
                                for hh in range(2):
                                    transpose_to(kc[0:rows, hh, :], kc.b, rows, 96, kt_dst(hh), kt_b, 'act' if hh else 'dve')
                                pv_ = mmbank()
                                S.op('pe', lambda e: e.matmul(out=pv_[0:rows, 0:128], lhsT=ckvT_ap, rhs=wuv[:, :], start=True, stop=True),
                                     list(ckvT_b) + wuv.b, pv_.b)
                                S.op('act', lambda e: e.copy(out=v_dst, in_=pv_[0:rows, 0:128]), pv_.b, v_b)

                            acc, accd, acc1, accd1 = PS[4], PS[5], PS[6], PS[7]

                            def zero_acc(t_, ncols):
                                S.op('pe', lambda e: e.matmul(out=t_[:, 0:ncols], lhsT=zeros[:, :], rhs=zrhs[:, 0:ncols], start=True, stop=True),
                                     zeros.b + zrhs.b, t_.b)

                            acc1s = [(PS[6], PS[7]), (PS[2], PS[3])]
                            itc = [0]

                            def att_begin(ncols):
                                if mixer == 'mla':
                                    zero_acc(acc, ncols)
                                    zero_acc(accd, ncols)
                                elif mixer == 'sb':
                                    zero_acc(acc, ncols)
                                    S.op('dve', lambda e: e.memset(R[:, :, 0:ncols], 0.0), [], R.b)
                                else:
                                    for t_ in (acc, accd, acc1, accd1):
                                        zero_acc(t_, ncols)

                            def diag_mask(t_, mask_ap, mask_b, nk, c0, ncols):
                                dn = min(128, ncols - c0)
                                S.op('dve', lambda e: e.tensor_tensor(out=t_[0:nk, c0:c0 + dn], in0=t_[0:nk, c0:c0 + dn], in1=mask_ap(nk, dn),
                                                                      op=ALU.mult), t_.b + mask_b, t_.b)

                            def stage_a(hg, kb, x, qc0, ncols, sample):
                                nk, c0 = kb['nk'], kb['c0']
                                st = mmbank()
                                if mixer == 'mla':
                                    S.op('pe', lambda e: e.matmul(out=st[0:nk, c0:ncols], lhsT=kb['kt'](x), rhs=QT[:, x, qc0 + c0:qc0 + ncols],
                                                                  start=True, stop=True), kb['ktb'] + QT.b, st.b)
                                    pt = npt()
                                    S.op('act', lambda e: e.activation(out=pt[0:nk, c0:ncols], in_=st[0:nk, c0:ncols], func=AF.Exp), st.b, pt.b)
                                    if kb['diag']:
                                        diag_mask(pt, lambda a_, b_: mch[0:a_, 0:b_], mch.b, nk, c0, ncols)
                                    return dict(pt=pt)
                                if mixer == 'sb':
                                    hp = hg
                                    S.op('pe', lambda e: e.matmul(out=st[0:nk, c0:ncols], lhsT=kb['kt'](hp)[64 * x:64 * x + 64, :],
                                                                  rhs=QT[64 * x:64 * x + 64, hp, qc0 + c0:qc0 + ncols], start=True, stop=True),
                                         kb['ktb'] + QT.b, st.b)
                                    itc[0] += 1
                                    ee = e1[itc[0] % 2]
                                    sp_ = spb[itc[0] % 2]
                                    a1, ad1 = acc1s[itc[0] % 2]
                                    S.op('act', lambda e: e.activation(out=ee[0:nk, c0:ncols], in_=st[0:nk, c0:ncols], func=AF.Exp), st.b, ee.b)
                                    S.op('act', lambda e: e.activation(out=sp_[0:nk, c0:ncols], in_=ee[0:nk, c0:ncols], func=AF.Ln, bias=1.0),
                                         ee.b, sp_.b)
                                    if kb['diag']:
                                        diag_mask(sp_, lambda a_, b_: msb[0:a_, 0:b_], msb.b, nk, c0, ncols)
                                    S.op('pe', lambda e: e.matmul(out=a1[0:nk, c0:ncols], lhsT=tri[0:nk, 0:nk], rhs=sp_[0:nk, c0:ncols],
                                                                  start=True, stop=True), tri.b + sp_.b, a1.b)
                                    S.op('pe', lambda e: e.matmul(out=ad1[:, c0:ncols], lhsT=ones[0:nk, :], rhs=sp_[0:nk, c0:ncols],
                                                                  start=True, stop=True), ones.b + sp_.b, ad1.b)
                                    return dict(ee=ee, a1=a1, ad1=ad1, cu=cumr[itc[0] % 2])
                                hd = hg
                                hgl = 2 * u + hd
                                S.op('pe', lambda e: e.matmul(out=st[0:nk, c0:ncols], lhsT=kb['kt'](hd)[64 * x:64 * x + 64, :],
                                                              rhs=QT[64 * x:64 * x + 64, hd, qc0 + c0:qc0 + ncols], start=True, stop=True),
                                     kb['ktb'] + QT.b, st.b)
                                pt = npt()
                                if sample:
                                    bi = kb['bi']
                                    S.op('act', lambda e: e.activation(out=pt[0:nk, c0:ncols], in_=st[0:nk, c0:ncols], func=AF.Exp,
                                                                       bias=bds[0:nk, hgl, bi:bi + 1]), st.b + bds.b, pt.b)
                                else:
                                    cc = c0
                                    while cc < ncols:
                                        ce = min(ncols, (cc // 256 + 1) * 256)
                                        dl = (kb['qt0'] + ce // 128 - 1) - kb['kbi']
                                        S.op('act', lambda e: e.activation(out=pt[0:nk, cc:ce], in_=st[0:nk, cc:ce], func=AF.Exp,
                                                                           bias=bdp[0:nk, hgl, dl:dl + 1]), st.b + bdp.b, pt.b)
                                        cc = ce
                                if kb['diag']:
                                    diag_mask(pt, lambda a_, b_: mdf[0:a_, hgl, 0:b_], mdf.b, nk, c0, ncols)
                                return dict(pt=pt)

                            def stage_b(hg, kb, x, ncols, acol, cx):
                                nk, c0 = kb['nk'], kb['c0']
                                o0, o1 = acol + c0, acol + ncols
                                if mixer == 'mla':
                                    pt = cx['pt']
                                    S.op('pe', lambda e: e.matmul(out=acc[64 * x:64 * x + 64, o0:o1], lhsT=kb['v'](x),
                                                                  rhs=pt[0:nk, c0:ncols], start=False, stop=True, skip_group_check=True),
                                         kb['vb'] + pt.b, acc.b)
                                    S.op('pe', lambda e: e.matmul(out=accd[64 * x:64 * x + 64, o0:o1], lhsT=ones[0:nk, 0:64],
                                                                  rhs=pt[0:nk, c0:ncols], start=False, stop=True, skip_group_check=True),
                                         ones.b + pt.b, accd.b)
                                elif mixer == 'sb':
                                    hp = hg
                                    ee, a1, ad1, cu = cx['ee'], cx['a1'], cx['ad1'], cx['cu']
                                    S.op('dve', lambda e: e.tensor_tensor(out=cu[0:nk, c0:ncols], in0=a1[0:nk, c0:ncols],
                                                                          in1=R[0:nk, x, o0:o1], op=ALU.add), a1.b + R.b, cu.b)
                                    S.op('act', lambda e: e.activation(out=cu[0:nk, c0:ncols], in_=cu[0:nk, c0:ncols], func=AF.Exp,
                                                                       scale=-1.0), cu.b, cu.b)
                                    pt = npt()
                                    S.op('dve', lambda e: e.tensor_tensor(out=pt[0:nk, c0:ncols], in0=ee[0:nk, c0:ncols],
                                                                          in1=cu[0:nk, c0:ncols], op=ALU.mult), ee.b + cu.b, pt.b)
                                    if kb['diag']:
                                        diag_mask(pt, lambda a_, b_: msb[0:a_, 0:b_], msb.b, nk, c0, ncols)
                                    S.op('dve', lambda e: e.tensor_tensor(out=R[:, x, o0:o1], in0=R[:, x, o0:o1], in1=ad1[:, c0:ncols],
                                                                          op=ALU.add), R.b + ad1.b, R.b)
                                    S.op('pe', lambda e: e.matmul(out=acc[64 * x:64 * x + 64, o0:o1], lhsT=kb['v'](2 * hp + x),
                                                                  rhs=pt[0:nk, c0:ncols], start=False, stop=True, skip_group_check=True),
                                         kb['vb'] + pt.b, acc.b)
                                else:
                                    pt = cx['pt']
                                    a_, d_ = (acc, accd) if x == 0 else (acc1, accd1)
                                    S.op('pe', lambda e: e.matmul(out=a_[:, o0:o1], lhsT=kb['v'](hg), rhs=pt[0:nk, c0:ncols],
                                                                  start=False, stop=True, skip_group_check=True), kb['vb'] + pt.b, a_.b)
                                    S.op('pe', lambda e: e.matmul(out=d_[:, o0:o1], lhsT=ones[0:nk, :], rhs=pt[0:nk, c0:ncols],
                                                                  start=False, stop=True, skip_group_check=True), ones.b + pt.b, d_.b)

                            def att_blocks(hg, kbs, qc0, ncols, sample, acol=0):
                                prev = None
                                for kb in kbs:
                                    for x in range(2):
                                        cx = stage_a(hg, kb, x, qc0, ncols, sample)
                                        if prev is not None:
                                            stage_b(hg, prev[0], prev[1], ncols, acol, prev[2])
                                        prev = (kb, x, cx)
                                stage_b(hg, prev[0], prev[1], ncols, acol, prev[2])

                            def att_end(hg, ncols, ocol0, grp_b, acol=0):
                                a0, a1_ = acol, acol + ncols
                                if mixer == 'mla':
                                    S.op('dve', lambda e: e.reciprocal(out=rden[:, 0:ncols], in_=accd[:, a0:a1_]), accd.b, rden.b)
                                    S.op('dve', lambda e: e.tensor_tensor(out=OT[:, u, ocol0:ocol0 + ncols], in0=acc[:, a0:a1_], in1=rden[:, 0:ncols],
                                                                          op=ALU.mult), acc.b + rden.b, grp_b)
                                elif mixer == 'sb':
                                    S.op('act', lambda e: e.copy(out=OT[:, 2 * u + hg, ocol0:ocol0 + ncols], in_=acc[:, a0:a1_]), acc.b, grp_b)
                                else:
                                    hgl = 2 * u + hg
                                    S.op('dve', lambda e: e.reciprocal(out=rden[:, 0:ncols], in_=accd[:, a0:a1_]), accd.b, rden.b)
                                    S.op('dve', lambda e: e.reciprocal(out=rden1[:, 0:ncols], in_=accd1[:, a0:a1_]), accd1.b, rden1.b)
                                    S.op('dve', lambda e: e.tensor_tensor(out=t0[:, 0:ncols], in0=acc[:, a0:a1_], in1=rden[:, 0:ncols], op=ALU.mult),
                                         acc.b + rden.b, t0.b)
                                    S.op('dve', lambda e: e.tensor_tensor(out=t1[:, 0:ncols], in0=acc1[:, a0:a1_], in1=rden1[:, 0:ncols], op=ALU.mult),
                                         acc1.b + rden1.b, t1.b)
                                    S.op('dve', lambda e: e.scalar_tensor_tensor(out=t0[:, 0:ncols], in0=t1[:, 0:ncols], scalar=gcol[:, 2:3],
                                                                                 in1=t0[:, 0:ncols], op0=ALU.mult, op1=ALU.add),
                                         t0.b + t1.b + gcol.b, t0.b)
                                    S.op('act', lambda e: e.activation(out=osq[:, 0:ncols], in_=t0[:, 0:ncols], func=AF.Square), t0.b, osq.b)
                                    pss = mmbank()
                                    S.op('pe', lambda e: e.matmul(out=pss[:, 0:ncols], lhsT=onesf[:, :], rhs=osq[:, 0:ncols], start=True, stop=True),
                                         onesf.b + osq.b, pss.b)
                                    S.op('act', lambda e: e.activation(out=t1[:, 0:ncols], in_=pss[:, 0:ncols], func=AF.Ln, scale=1.0 / 128, bias=EPS),
                                         pss.b, t1.b)
                                    S.op('act', lambda e: e.activation(out=t1[:, 0:ncols], in_=t1[:, 0:ncols], func=AF.Exp, scale=-0.5), t1.b, t1.b)
                                    S.op('dve', lambda e: e.scalar_tensor_tensor(out=OT[:, hgl, ocol0:ocol0 + ncols], in0=t0[:, 0:ncols],
                                                                                 scalar=gcol[:, 0:1], in1=t1[:, 0:ncols], op0=ALU.mult, op1=ALU.mult),
                                         t0.b + t1.b + gcol.b, grp_b)

                            nhg = 1 if mixer == 'mla' else 2

                            def kb_store(kbi, c0, diag, qt0, nk=128, bi=0):
                                rows_, sl = tsl(kbi)
                                if mixer == 'mla':
                                    return dict(kt=lambda hh: KT[:, hh, sl], ktb=[KT.b[kbi]], v=lambda hh: V[0:nk, kbi, 64 * hh:64 * hh + 64],
                                                vb=[V.b[kbi]], nk=nk, c0=c0, diag=diag, kbi=kbi, qt0=qt0, bi=bi)
                                if mixer == 'sb':
                                    return dict(kt=lambda hp: KT[:, hp, sl], ktb=[KT.b[kbi]], v=lambda h: V[0:nk, kbi, 64 * h:64 * h + 64],
                                                vb=[V.b[kbi]], nk=nk, c0=c0, diag=diag, kbi=kbi, qt0=qt0, bi=bi)
                                return dict(kt=lambda hd: KT[:, hd, sl], ktb=[KT.b[kbi]], v=lambda hd: V[0:nk, kbi, 128 * hd:128 * hd + 128],
                                            vb=[V.b[kbi]], nk=nk, c0=c0, diag=diag, kbi=kbi, qt0=qt0, bi=bi)

                            chi = [0]

                            def kb_past(j, kbi, cs_):
                                sl = slice(j * 128, (j + 1) * 128)
                                bi = 32 - kbi
                                KTp, Vp = cs_['KTp'], cs_['Vp']
                                if mixer == 'mla':
                                    return dict(kt=lambda hh: KTp[:, hh, sl], ktb=KTp.b, v=lambda hh: Vp[:, j, 64 * hh:64 * hh + 64],
                                                vb=Vp.b, nk=128, c0=0, diag=False, kbi=kbi, qt0=0, bi=bi)
                                if mixer == 'sb':
                                    return dict(kt=lambda hp: KTp[:, hp, sl], ktb=KTp.b, v=lambda h: Vp[:, j, 64 * h:64 * h + 64],
                                                vb=Vp.b, nk=128, c0=0, diag=False, kbi=kbi, qt0=0, bi=bi)
                                return dict(kt=lambda hd: KTp[:, hd, sl], ktb=KTp.b, v=lambda hd: Vp[:, j, 128 * hd:128 * hd + 128],
                                            vb=Vp.b, nk=128, c0=0, diag=False, kbi=kbi, qt0=0, bi=bi)

                            def build_chunk(s, ch):
                                r0 = ch * 512
                                chi[0] += 1
                                cs_ = CS[chi[0] % 2]
                                if mixer == 'mla':
                                    pck, pkr, pckT, KTp, Vp, kc4 = cs_['pck'], cs_['pkr'], cs_['pckT'], cs_['KTp'], cs_['Vp'], cs_['kc4']
                                    S.dma('pool', pck[:], c_ckv[l, s, r0:r0 + 512, :].rearrange("(k p) n -> p k n", p=128), [], pck.b)
                                    S.dma('pool', pkr[:], c_kr[l, s, r0:r0 + 512, :].rearrange("(k p) n -> p k n", p=128), [], pkr.b)
                                    for j in range(4):
                                        transpose_to(pck[:, j, :], pck.b, 128, 128, pckT[:, j * 128:(j + 1) * 128], pckT.b, 'act' if j % 2 else 'dve')
                                    pk_ = mmbank()
                                    for j in range(4):
                                        S.op('pe', lambda e: e.matmul(out=pk_[:, j * 128:(j + 1) * 128], lhsT=pckT[:, j * 128:(j + 1) * 128], rhs=wuk[:, :],
                                                                      start=True, stop=True), pckT.b + wuk.b, pk_.b)
                                    gnorm(ws, pk_[:, 0:512].rearrange("p (g d) -> p g d", d=64), pk_.b, 128, 8, 64, gs('kn', 128),
                                          kc4[:, :, :, 0:64].rearrange("p j h d -> p (j h) d"), kc4.b)
                                    S.op('dve', lambda e: e.tensor_copy(out=kc4[:, :, :, 64:96], in_=pkr[:, :, :].unsqueeze(2).to_broadcast([128, 4, 2, 32])),
                                         pkr.b, kc4.b)
                                    for j in range(4):
                                        for hh in range(2):
                                            transpose_to(kc4[:, j, hh, :], kc4.b, 128, 96, KTp[:, hh, j * 128:(j + 1) * 128], KTp.b, 'act' if hh else 'dve')
                                    pv_ = mmbank()
                                    for j in range(4):
                                        S.op('pe', lambda e: e.matmul(out=pv_[:, j * 128:(j + 1) * 128], lhsT=pckT[:, j * 128:(j + 1) * 128], rhs=wuv[:, :],
                                                                      start=True, stop=True), pckT.b + wuv.b, pv_.b)
                                    S.op('act', lambda e: e.copy(out=Vp[:, :, :].rearrange("p j n -> p (j n)"), in_=pv_[:, 0:512]), pv_.b, Vp.b)
                                else:
                                    pk, KTp, Vp = cs_['pk'], cs_['KTp'], cs_['Vp']
                                    ck = c_sbk if mixer == 'sb' else c_dk
                                    cv = c_sbv if mixer == 'sb' else c_dv
                                    S.dma('pool', pk[:], ck[l, s, r0:r0 + 512, u * 256:(u + 1) * 256].rearrange("(k p) n -> p k n", p=128), [], pk.b)
                                    S.dma('pool', Vp[:], cv[l, s, r0:r0 + 512, u * 256:(u + 1) * 256].rearrange("(k p) n -> p k n", p=128), [], Vp.b)
                                    for j in range(4):
                                        for c in range(2):
                                            transpose_to(pk[:, j, c * 128:(c + 1) * 128], pk.b, 128, 128, KTp[:, c, j * 128:(j + 1) * 128], KTp.b,
                                                         'act' if c else 'dve')
                                return cs_

                            for g in range(4):
                                for i in range(4):
                                    proj_tile(4 * g + i, i * 128)
                                kbs = []
                                for kbi in range(4 * g + 4):
                                    i = kbi - 4 * g
                                    kbs.append(kb_store(kbi, max(i, 0) * 128, i >= 0, 4 * g))
                                if mixer == 'sb':
                                    kbs = kbs[::-1]
                                for hg in range(nhg):
                                    att_begin(512)
                                    att_blocks(hg, kbs, 0, 512, False)
                                    att_end(hg, 512, g * 512, [OT.b[g]])
                                _ck('%s%d_u%d_g%d' % (mixer, l, u, g))

                            for s in range(NS):
                                proj_tile(16 + s, 0)
                                knew = kb_store(16 + s, 0, mixer != 'mla', 0, nk=32, bi=0)
                                att_begin(32 * nhg)
                                if mixer == 'sb':
                                    for hg in range(nhg):
                                        att_blocks(hg, [knew], 0, 32, True, acol=32 * hg)
                                    for ch in range(7, -1, -1):
                                        cs_ = build_chunk(s, ch)
                                        for hg in range(nhg):
                                            att_blocks(hg, [kb_past(j, ch * 4 + j, cs_) for j in range(3, -1, -1)], 0, 32, True, acol=32 * hg)
                                else:
                                    for ch in range(8):
                                        cs_ = build_chunk(s, ch)
                                        for hg in range(nhg):
                                            att_blocks(hg, [kb_past(j, ch * 4 + j, cs_) for j in range(4)], 0, 32, True, acol=32 * hg)
                                    for hg in range(nhg):
                                        att_blocks(hg, [knew], 0, 32, True, acol=32 * hg)
                                for hg in range(nhg):
                                    att_end(hg, 32, 2048 + 32 * s, [OT.b[4]], acol=32 * hg)
                                _ck('%s%d_u%d_s%d' % (mixer, l, u, s))
                            S.barrier()
                            _ck('%s%d_u%d' % (mixer, l, u))

                    mi = ('mla', 'sb', 'diff').index(mixer)
                    ms.close()
                    with ExitStack() as gs_:
                        mm_list[0] = [0, 1, 4, 5, 6, 7]
                        wg = sb(gs_, [128, 8, 1024], BF16)
                        g0 = OFF['g'] + 1024 * mi
                        load_w(wg, W['w_in'][l, :, g0:g0 + 1024].rearrange("(k p) n -> p k n", p=128))
                        wbr = sb(gs_, [128, 4, 1024], BF16)
                        load_w(wbr, W['w_br_' + mixer][l].rearrange("(k p) n -> p k n", p=128))
                        wo = sb(gs_, [128, 8, 1024], BF16)
                        load_w(wo, W['w_out'][l].rearrange("(k p) n -> p k n", p=128))
                        gT = [sb(gs_, [128, 512], F32) for _ in range(2)]
                        MG = [sb(gs_, [128, 8, 512], BF16) for _ in range(1)]
                        for gi, (c0, n, tiles) in enumerate(GROUPS):
                            hb_ = [HT.b[t] for t in tiles]
                            mg = MG[0]
                            for nn in range(8):
                                nsl = slice(nn * 128, (nn + 1) * 128)
                                pg = mmbank()
                                for k in range(8):
                                    S.op('pe', lambda e: e.matmul(out=pg[:, 0:n], lhsT=wg[:, k, nsl], rhs=HT[:, k, c0:c0 + n],
                                                                  start=(k == 0), stop=(k == 7)), wg.b + hb_, pg.b, inc=(k == 7))
                                gt = gT[nn % 2]
                                S.op('act', lambda e: e.activation(out=gt[:, 0:n], in_=pg[:, 0:n], func=AF.Sigmoid), pg.b, gt.b)
                                py = mmbank()
                                for c in range(4):
                                    S.op('pe', lambda e: e.matmul(out=py[:, 0:n], lhsT=wbr[:, c, nsl], rhs=OT[:, c, c0:c0 + n],
                                                                  start=(c == 0), stop=(c == 3)), wbr.b + [OT.b[gi]], py.b, inc=(c == 3))
                                S.op('dve', lambda e: e.tensor_tensor(out=mg[:, nn, 0:n], in0=py[:, 0:n], in1=gt[:, 0:n], op=ALU.mult),
                                     py.b + gt.b, mg.b)
                            for tt in tiles:
                                rows, sl = tsl(tt)
                                off = sl.start - c0
                                for half in range(2):
                                    hsl = slice(half * 512, (half + 1) * 512)
                                    po = mmbank()
                                    for k in range(8):
                                        S.op('pe', lambda e: e.matmul(out=po[0:rows, :], lhsT=mg[:, k, off:off + rows], rhs=wo[:, k, hsl],
                                                                      start=(k == 0), stop=(k == 7)), mg.b + wo.b, po.b, inc=(k == 7))
                                    S.op('dve', lambda e: e.tensor_tensor(out=X[0:rows, tt, hsl], in0=X[0:rows, tt, hsl], in1=po[0:rows, :],
                                                                          op=ALU.add), [X.b[tt]] + po.b, [X.b[tt]])
                        S.barrier()
                        mm_list[0] = [0, 1]
                        rot['mm'] = 0
                    _ck('%s%d_merge' % (mixer, l))

            norm_phase(l, 'ffn_norm_g', False)
            with ExitStack() as fs:
                mm_list[0] = [0, 1, 4, 5, 6, 7]
                cw = sb(fs, [128, 22, 3], F32)
                cb = sb(fs, [128, 22], F32)
                cst = sb(fs, [128, NS, 22, 2], F32)
                OC = sb(fs, [128, 3, 22, 2], F32)
                S.dma('sp', cw[:].rearrange("p a b -> p (a b)"), W['ffn_conv_w'][l], [], cw.b)
                S.dma('sp', cb[:], W['ffn_conv_b'][l], [], cb.b)
                for s in range(NS):
                    S.dma('sp', cst[:, s, :, :].rearrange("p a b -> p (a b)"), c_conv[l, s], [], cst.b)
                WA = [sb(fs, [128, 8, 512], BF16) for _ in range(2)]
                WU = [sb(fs, [128, 8, 512], BF16) for _ in range(2)]
                WD = [sb(fs, [128, 4, 1024], BF16) for _ in range(2)]
                AT = [sb(fs, [128, 4, 516], F32, 4) for _ in range(1)]
                carry = sb(fs, [128, 4, 2], F32, 4)
                cc_ = [sb(fs, [128, 512], F32) for _ in range(1)]
                sl_ = [sb(fs, [128, 512], F32) for _ in range(1)]
                MM = [sb(fs, [128, 4, 512], BF16) for _ in range(1)]
                fgroups = [(0, 4), (4, 4), (8, 4), (12, 4), (16, 4), (20, 2)]
                for fi, (fc0, nfc) in enumerate(fgroups):
                    wa, wu, wd = WA[fi % 2], WU[fi % 2], WD[fi % 2]
                    nf = nfc * 128
                    S.dma('pool', wa[:, :, 0:nf], W['ffn_w_up'][l, :, fc0 * 128:fc0 * 128 + nf].rearrange("(k p) n -> p k n", p=128), [], wa.b)
                    S.dma('pool', wu[:, :, 0:nf], W['ffn_w_up'][l, :, DFF + fc0 * 128:DFF + fc0 * 128 + nf].rearrange("(k p) n -> p k n", p=128),
                          [], wu.b)
                    S.dma('pool', wd[:, 0:nfc, :], W['ffn_w_down'][l, fc0 * 128:fc0 * 128 + nf, :].rearrange("(c p) n -> p c n", p=128), [], wd.b)
                    for gi, (c0, n, tiles) in enumerate(GROUPS):
                        hb_ = [HT.b[t] for t in tiles]
                        mmt = MM[0]
                        buf = AT[0]
                        for j in range(nfc):
                            fc = fc0 + j
                            jsl = slice(j * 128, (j + 1) * 128)
                            pa = mmbank()
                            for k in range(8):
                                S.op('pe', lambda e: e.matmul(out=pa[:, 0:n], lhsT=wa[:, k, jsl], rhs=HT[:, k, c0:c0 + n],
                                                              start=(k == 0), stop=(k == 7)), wa.b + hb_, pa.b, inc=(k == 7))
                            cc = cc_[0]
                            if gi < 4:
                                S.op('act', lambda e: e.copy(out=buf[:, j, 2:514], in_=pa[:, 0:512]), pa.b, [buf.b[j]])
                                if gi == 0:
                                    S.op('dve', lambda e: e.memset(buf[:, j, 0:2], 0.0), [], [buf.b[j]])
                                else:
                                    S.op('dve', lambda e: e.tensor_copy(out=buf[:, j, 0:2], in_=carry[:, j, :]), [carry.b[j]], [buf.b[j]])
                                S.op('dve', lambda e: e.tensor_copy(out=carry[:, j, :], in_=buf[:, j, 512:514]), [buf.b[j]], [carry.b[j]])
                                segs = [(0, 0, 512)]
                                if gi == 3:
                                    S.op('dve', lambda e: e.tensor_copy(out=OC[:, 0, fc, :], in_=buf[:, j, 512:514]), [buf.b[j]], OC.b)
                            else:
                                for s in range(NS):
                                    S.op('act', lambda e: e.copy(out=buf[:, j, s * 34 + 2:s * 34 + 34], in_=pa[:, s * 32:s * 32 + 32]), pa.b, [buf.b[j]])
                                    S.op('dve', lambda e: e.tensor_copy(out=buf[:, j, s * 34:s * 34 + 2], in_=cst[:, s, fc, :]), cst.b, [buf.b[j]])
                                    S.op('dve', lambda e: e.tensor_copy(out=OC[:, 1 + s, fc, :], in_=buf[:, j, s * 34 + 32:s * 34 + 34]), [buf.b[j]], OC.b)
                                segs = [(0, 0, 32), (34, 32, 32)]
                            for (b0, o0, nn_) in segs:
                                S.op('dve', lambda e: e.tensor_scalar(out=cc[:, o0:o0 + nn_], in0=buf[:, j, b0 + 2:b0 + 2 + nn_], scalar1=cw[:, fc, 2:3],
                                                                      scalar2=cb[:, fc:fc + 1], op0=ALU.mult, op1=ALU.add),
                                     [buf.b[j]] + cw.b + cb.b, cc.b)
                                S.op('dve', lambda e: e.scalar_tensor_tensor(out=cc[:, o0:o0 + nn_], in0=buf[:, j, b0 + 1:b0 + 1 + nn_], scalar=cw[:, fc, 1:2],
                                                                             in1=cc[:, o0:o0 + nn_], op0=ALU.mult, op1=ALU.add),
                                     [buf.b[j]] + cw.b + cc.b, cc.b)
                                S.op('dve', lambda e: e.scalar_tensor_tensor(out=cc[:, o0:o0 + nn_], in0=buf[:, j, b0:b0 + nn_], scalar=cw[:, fc, 0:1],
                                                                             in1=cc[:, o0:o0 + nn_], op0=ALU.mult, op1=ALU.add),
                                     [buf.b[j]] + cw.b + cc.b, cc.b)
                            sl2 = sl_[0]
                            S.op('act', lambda e: e.activation(out=sl2[:, 0:n], in_=cc[:, 0:n], func=AF.Silu), cc.b, sl2.b)
                            pu = mmbank()
                            for k in range(8):
                                S.op('pe', lambda e: e.matmul(out=pu[:, 0:n], lhsT=wu[:, k, jsl], rhs=HT[:, k, c0:c0 + n],
                                                              start=(k == 0), stop=(k == 7)), wu.b + hb_, pu.b, inc=(k == 7))
                            S.op('dve', lambda e: e.tensor_tensor(out=mmt[:, j, 0:n], in0=pu[:, 0:n], in1=sl2[:, 0:n], op=ALU.mult),
                                 pu.b + sl2.b, mmt.b)
                        for tt in tiles:
                            rows, sl = tsl(tt)
                            off = sl.start - c0
                            for half in range(2):
                                hsl = slice(half * 512, (half + 1) * 512)
                                po = mmbank()
                                for j in range(nfc):
                                    S.op('pe', lambda e: e.matmul(out=po[0:rows, :], lhsT=mmt[:, j, off:off + rows], rhs=wd[:, j, hsl],
                                                                  start=(j == 0), stop=(j == nfc - 1)), mmt.b + wd.b, po.b, inc=(j == nfc - 1))
                                S.op('dve', lambda e: e.tensor_tensor(out=X[0:rows, tt, hsl], in0=X[0:rows, tt, hsl], in1=po[0:rows, :],
                                                                      op=ALU.add), [X.b[tt]] + po.b, [X.b[tt]])
                S.dma('sp', o_conv[l].rearrange("s p n -> p s n"), OC[:].rearrange("p s a b -> p s (a b)"), OC.b, [])
                S.barrier()
                mm_list[0] = [0, 1]
                rot['mm'] = 0
            _ck('ffn%d' % l)

        _DEV['off'] = False
        _DEV['nops'] = 0
        _layers()
        _DEV['off'] = False
        mm_list[0] = [0, 1]
        for tt in range(NT):
            rows, sl = tsl(tt)
            S.dma('sp', y[sl, :], X[0:rows, tt, :], [X.b[tt]], [])
        S.barrier()
    return nc


_SLOPES = [2.0 ** (-8.0 * (h + 1) / 4) for h in range(4)]


def _consts():
    half = 16
    inv = (np.float32(10000.0) ** (-np.arange(half, dtype=np.float32) / np.float32(half))).astype(np.float32)
    pos = np.concatenate([np.arange(SP_), PAST + np.arange(SS), PAST + np.arange(SS)]).astype(np.float32)
    ang = (pos[:, None] * inv[None, :]).astype(np.float32)
    k_cs = np.concatenate([np.cos(ang), np.sin(ang)], axis=1).astype(np.float32)
    k = np.arange(128)[:, None]
    q = np.arange(128)[None, :]
    msb = (k < q).astype(np.float32)
    mch = ((k // 64) <= (q // 64)).astype(np.float32)
    mdf = np.zeros((128, 4, 128), np.float32)
    bdp = np.zeros((128, 4, 17), np.float32)
    bds = np.zeros((128, 4, 33), np.float32)
    for h in range(4):
        sl = _SLOPES[h]
        mdf[:, h, :] = mch * np.where(k > q, np.exp(-2.0 * sl * (k - q)), 1.0)
        for d in range(17):
            bdp[:, h, d] = sl * (np.arange(128) - 127 - 128 * d)
        for j in range(33):
            bds[:, h, j] = sl * (np.arange(128) - 31 - 128 * j)
    tri = (k >= q).astype(np.float32)
    return dict(k_cs=k_cs, k_msb=msb, k_mch=mch, k_mdf=mdf.reshape(128, 512), k_tri=tri,
                k_bdp=bdp.reshape(128, 68), k_bds=bds.reshape(128, 132))


_WNAMES = ["mix_norm_g", "w_in", "mla_q_norm_g", "mla_w_uq", "mla_kv_norm_g", "mla_w_uk", "mla_w_uv", "mla_qn_g", "mla_kn_g",
           "mla_qr_g", "mla_kr_g", "diff_qn_g", "diff_kn_g", "diff_lambda", "diff_subln_g", "w_br_mla", "w_br_sb", "w_br_diff",
           "w_out", "ffn_norm_g", "ffn_w_up", "ffn_conv_w", "ffn_conv_b", "ffn_w_down"]


def kernel(**inputs):
    inp = {k: np.asarray(v) for k, v in inputs.items()}
    nc = build_program()
    shared = {}
    for n in _WNAMES:
        a = np.ascontiguousarray(inp[n], dtype=np.float32)
        if n == "diff_lambda":
            a = a.reshape(L, 256)
        elif n == "ffn_conv_w":
            a = np.ascontiguousarray(a.reshape(L, 3, 22, 128).transpose(0, 3, 2, 1)).reshape(L, 128, 66)
        elif n == "ffn_conv_b":
            a = np.ascontiguousarray(a.reshape(L, 22, 128).transpose(0, 2, 1))
        shared[n] = a
    shared.update(_consts())
    in_maps = []
    for c in range(8):
        m = dict(shared)
        m["xin"] = np.ascontiguousarray(np.concatenate([inp["x_prompt"][c], inp["x_sample"][2 * c], inp["x_sample"][2 * c + 1]], axis=0),
                                        dtype=np.float32)
        sl = slice(2 * c, 2 * c + 2)
        m["c_ckv"] = np.ascontiguousarray(inp["cache_mla_ckv"][:, sl])
        m["c_kr"] = np.ascontiguousarray(inp["cache_mla_krope"][:, sl])
        m["c_sbk"] = np.ascontiguousarray(inp["cache_sb_k"][:, sl]).reshape(L, NS, PAST, 512)
        m["c_sbv"] = np.ascontiguousarray(inp["cache_sb_v"][:, sl]).reshape(L, NS, PAST, 512)
        m["c_dk"] = np.ascontiguousarray(inp["cache_diff_k"][:, sl]).reshape(L, NS, PAST, 512)
        m["c_dv"] = np.ascontiguousarray(inp["cache_diff_v"][:, sl]).reshape(L, NS, PAST, 512)
        st = np.asarray(inp["state_ffn_conv"][:, sl], dtype=np.float32)
        m["c_conv"] = np.ascontiguousarray(st.reshape(L, NS, 2, 22, 128).transpose(0, 1, 4, 3, 2)).reshape(L, NS, 128, 44)
        in_maps.append(m)
    res = run_bass_kernel_spmd(nc, in_maps, core_ids=list(range(8))).results

    def gather(name, width):
        p = np.stack([res[c][name][:, 0:SP_] for c in range(8)], axis=1)
        s = np.stack([res[c][name][:, SP_ + SS * j:SP_ + SS * (j + 1)] for c in range(8) for j in range(NS)], axis=1)
        return p, s

    y_p = np.stack([res[c]["y"][0:SP_] for c in range(8)], axis=0)
    y_s = np.stack([res[c]["y"][SP_ + SS * j:SP_ + SS * (j + 1)] for c in range(8) for j in range(NS)], axis=0)
    p_ckv, s_ckv = gather("o_ckv", 128)
    p_kr, s_kr = gather("o_kr", 32)
    p_sbk, s_sbk = gather("o_sbk", 512)
    p_sbv, s_sbv = gather("o_sbv", 512)
    p_dk, s_dk = gather("o_dk", 512)
    p_dv, s_dv = gather("o_dv", 512)

    def conv_of(c, idx):
        oc = res[c]["o_conv"][:, idx].reshape(L, 128, 22, 2)
        return np.ascontiguousarray(oc.transpose(0, 3, 2, 1)).reshape(L, 2, DFF)
    p_conv = np.stack([conv_of(c, 0) for c in range(8)], axis=1)
    s_conv = np.stack([conv_of(c, 1 + j) for c in range(8) for j in range(NS)], axis=1)
    f = np.float32
    return (y_p.astype(f), y_s.astype(f), p_ckv.astype(f), p_kr.astype(f),
            p_sbk.reshape(L, 8, SP_, 8, 64).astype(f), p_sbv.reshape(L, 8, SP_, 8, 64).astype(f),
            p_dk.reshape(L, 8, SP_, 4, 2, 64).astype(f), p_dv.reshape(L, 8, SP_, 4, 128).astype(f), p_conv.astype(f),
            s_ckv.astype(f), s_kr.astype(f), s_sbk.reshape(L, 16, SS, 8, 64).astype(f), s_sbv.reshape(L, 16, SS, 8, 64).astype(f),
            s_dk.reshape(L, 16, SS, 4, 2, 64).astype(f), s_dv.reshape(L, 16, SS, 4, 128).astype(f), s_conv.astype(f))
```

```python
import math
from contextlib import ExitStack
import numpy as np
import concourse.bass as bass
import concourse.mybir as mybir
from concourse.bass_utils import run_bass_kernel_spmd

F32 = mybir.dt.float32
BF16 = mybir.dt.bfloat16
AF = mybir.ActivationFunctionType
ALU = mybir.AluOpType
AX = mybir.AxisListType

L = 2
D = 1024
SP_ = 2048
NS = 2
SS = 32
PAST = 4096
NTOK = SP_ + NS * SS
NT = 18
DFF = 2816
NIN = 6560
EPS = 1e-6
MLA_SCALE = 96 ** -0.5
SB_SCALE = 64 ** -0.5
DIFF_SCALE = 64 ** -0.5
OFF = dict(cq=0, ckv=256, kr=384, sq=416, sk=928, sv=1440, dq=1952, dk=2464, dv=2976, g=3488)
ENG = ['pe', 'act', 'dve', 'pool', 'sp']
NDS = 20
_DEV = {'stop': None, 'off': False, 'maxops': None, 'nops': 0, 'log': None}


class _Stop(Exception):
    pass


def _ck(name):
    if _DEV['log'] is not None and not _DEV['off']:
        _DEV['log'].append((name, _DEV['nops']))
    if _DEV['stop'] == name:
        _DEV['off'] = True


class Buf:
    __slots__ = ('w', 'r', 'excl')

    def __init__(self):
        self.w = None
        self.r = {}
        self.excl = False


class TT:
    def __init__(self, h, n=1):
        self.h = h
        self.b = [Buf() for _ in range(n)]

    def __getitem__(self, k):
        return self.h[k]


class Sync:
    def __init__(self, nc, es):
        self.nc = nc
        self.e = dict(pe=nc.tensor, act=nc.scalar, dve=nc.vector, pool=nc.gpsimd, sp=nc.sync)
        self.sem = {k: es.enter_context(nc.semaphore("sem_" + k)) for k in ENG}
        self.cnt = {k: 0 for k in ENG}
        self.dsem = {q: [es.enter_context(nc.semaphore("ds_%s%d" % (q, i))) for i in range(NDS)] for q in ('sp', 'pool')}
        self.dcnt = {q: [0] * NDS for q in ('sp', 'pool')}
        self.dnext = {'sp': 0, 'pool': 0}
        self.known = {k: {} for k in ENG}
        self.pend = {k: False for k in ENG}
        self.hist = {}

    def _semof(self, k):
        return self.sem[k] if isinstance(k, str) else self.dsem[k[0]][k[1]]

    def _need(self, eng, toks):
        kn = self.known[eng]
        best = {}
        for (k, v) in toks:
            if best.get(k, 0) < v:
                best[k] = v
        need = []
        for k, v in sorted(best.items(), key=lambda kv: str(kv[0])):
            if kn.get(k, 0) < v:
                need.append((k, v))
        implied = {}
        for k, v in need:
            snap = self.hist.get((k, v))
            if snap:
                for k2, v2 in snap.items():
                    if implied.get(k2, 0) < v2:
                        implied[k2] = v2
        out = [(k, v) for (k, v) in need if implied.get(k, 0) < v]
        for k, v in out:
            kn[k] = v
            snap = self.hist.get((k, v))
            if snap:
                for k2, v2 in snap.items():
                    if k2 != eng and kn.get(k2, 0) < v2:
                        kn[k2] = v2
        return out

    def _wait(self, eng, toks, ins_fn=None):
        need = self._need(eng, toks)
        if ins_fn is None:
            for k, v in need:
                self.e[eng].wait_ge(self._semof(k), v)
            return None
        for k, v in need[:-1]:
            self.e[eng].wait_ge(self._semof(k), v)
        ins = ins_fn()
        if need:
            k, v = need[-1]
            ins._wait_ge(self._semof(k), v)
        return ins

    def _deps(self, eng, reads, writes):
        toks = set()
        for b in reads:
            if b.w is not None:
                toks.add(b.w)
            if b.excl:
                for kv in b.r.items():
                    if kv[0] != eng:
                        toks.add(kv)
        for b in writes:
            if b.w is not None:
                toks.add(b.w)
            for kv in b.r.items():
                toks.add(kv)
        if eng == 'pe':
            toks = {t for t in toks if t[0] != 'pe'}
        return toks

    def op(self, eng, fn, reads=(), writes=(), inc=True):
        if _DEV['off']:
            return
        _DEV['nops'] += 1
        if _DEV['maxops'] is not None and _DEV['nops'] > _DEV['maxops'] and not self.pend[eng]:
            _DEV['off'] = True
            return
        ins = self._wait(eng, self._deps(eng, reads, writes), lambda: fn(self.e[eng]))
        c = self.cnt[eng] + 1
        if inc:
            ins.then_inc(self.sem[eng], 1)
            self.cnt[eng] = c
            self.pend[eng] = False
            self.hist[(eng, c)] = dict(self.known[eng])
        else:
            self.pend[eng] = True
        for b in reads:
            b.r[eng] = c
        for b in writes:
            b.w = (eng, c)
            b.r = {}

    def dma(self, q, out, in_, reads=(), writes=()):
        if _DEV['off']:
            return
        toks = self._deps(q, reads, writes)
        i = self.dnext[q]
        self.dnext[q] = (i + 1) % NDS
        key = (q, i)
        if self.dcnt[q][i] > 0:
            toks.add((key, self.dcnt[q][i]))
        ins = self._wait(q, toks, lambda: self.e[q].dma_start(out=out, in_=in_))
        ins.then_inc(self.dsem[q][i], 16)
        self.dcnt[q][i] += 16
        v = self.dcnt[q][i]
        self.hist[(key, v)] = dict(self.known[q])
        for b in reads:
            b.r[key] = v
        for b in writes:
            b.w = (key, v)
            b.r = {}

    def all_tokens(self):
        toks = {(k, self.cnt[k]) for k in ENG if self.cnt[k] > 0}
        for q in ('sp', 'pool'):
            for i, c in enumerate(self.dcnt[q]):
                if c > 0:
                    toks.add(((q, i), c))
        return toks

    def barrier(self):
        if _DEV['off']:
            return
        for k in ENG:
            assert not self.pend[k]
        toks = self.all_tokens()
        for eng in ENG:
            self._wait(eng, {t for t in toks if t[0] != eng})


def build_program():
    nc = bass.Bass("TRN2", target_bir_lowering=False)

    def din(name, shape):
        return nc.dram_tensor(name, list(shape), F32, kind="ExternalInput").ap()

    def dout(name, shape):
        return nc.dram_tensor(name, list(shape), F32, kind="ExternalOutput").ap()

    xin = din("xin", [NTOK, D])
    c_ckv = din("c_ckv", [L, NS, PAST, 128])
    c_kr = din("c_kr", [L, NS, PAST, 32])
    c_sbk = din("c_sbk", [L, NS, PAST, 512])
    c_sbv = din("c_sbv", [L, NS, PAST, 512])
    c_dk = din("c_dk", [L, NS, PAST, 512])
    c_dv = din("c_dv", [L, NS, PAST, 512])
    c_conv = din("c_conv", [L, NS, 128, 22 * 2])
    W = {}
    for name, shape in [("mix_norm_g", [L, D]), ("w_in", [L, D, NIN]), ("mla_q_norm_g", [L, 256]),
                        ("mla_w_uq", [L, 256, 768]), ("mla_kv_norm_g", [L, 128]), ("mla_w_uk", [L, 128, 512]),
                        ("mla_w_uv", [L, 128, 512]), ("mla_qn_g", [L, 64]), ("mla_kn_g", [L, 64]),
                        ("mla_qr_g", [L, 32]), ("mla_kr_g", [L, 32]), ("diff_qn_g", [L, 64]),
                        ("diff_kn_g", [L, 64]), ("diff_lambda", [L, 256]), ("diff_subln_g", [L, 128]),
                        ("w_br_mla", [L, 512, D]), ("w_br_sb", [L, 512, D]), ("w_br_diff", [L, 512, D]),
                        ("w_out", [L, D, D]), ("ffn_norm_g", [L, D]), ("ffn_w_up", [L, D, 2 * DFF]),
                        ("ffn_conv_w", [L, 128, 22 * 3]), ("ffn_conv_b", [L, 128, 22]), ("ffn_w_down", [L, DFF, D])]:
        W[name] = din(name, shape)
    k_cs = din("k_cs", [NTOK, 32])
    k_msb = din("k_msb", [128, 128])
    k_mch = din("k_mch", [128, 128])
    k_mdf = din("k_mdf", [128, 4 * 128])
    k_tri = din("k_tri", [128, 128])
    k_bdp = din("k_bdp", [128, 4 * 17])
    k_bds = din("k_bds", [128, 4 * 33])

    y = dout("y", [NTOK, D])
    o_ckv = dout("o_ckv", [L, NTOK, 128])
    o_kr = dout("o_kr", [L, NTOK, 32])
    o_sbk = dout("o_sbk", [L, NTOK, 512])
    o_sbv = dout("o_sbv", [L, NTOK, 512])
    o_dk = dout("o_dk", [L, NTOK, 512])
    o_dv = dout("o_dv", [L, NTOK, 512])
    o_conv = dout("o_conv", [L, 3, 128, 22 * 2])

    with ExitStack() as es:
        E = es.enter_context
        S = Sync(nc, es)
        cnt = [0]

        def sb(es_, shape, dt, n=1):
            cnt[0] += 1
            return TT(es_.enter_context(nc.sbuf_tensor("t%d" % cnt[0], list(shape), dt)), n)

        X = sb(es, [128, NT, D], F32, NT)
        HT = sb(es, [128, 8, NTOK], BF16, NT)
        OT = sb(es, [128, 4, NTOK], BF16, 5)
        ident = sb(es, [128, 128], BF16)
        ones = sb(es, [128, 128], BF16)
        zeros = sb(es, [128, 128], BF16)
        onesf = sb(es, [128, 128], F32)
        tri = sb(es, [128, 128], BF16)
        msb = sb(es, [128, 128], F32)
        mch = sb(es, [128, 128], F32)
        mdf = sb(es, [128, 4, 128], F32)
        bdp = sb(es, [128, 4, 17], F32)
        bds = sb(es, [128, 4, 33], F32)
        cs = sb(es, [128, NT, 32], F32)
        gbig = sb(es, [128, D], F32)
        gsm = sb(es, [128, 1024], F32)
        gcol = sb(es, [128, 8], F32)
        PS = [TT(E(nc.psum_tensor("ps%d" % i, [128, 512], F32))) for i in range(8)]
        for p_ in PS:
            p_.b[0].excl = True
        rot = {'mm': 0, 'tp': 0}

        def mmbank():
            rot['mm'] = (rot['mm'] + 1) % len(mm_list[0])
            return PS[mm_list[0][rot['mm']]]

        def tpbank():
            rot['tp'] ^= 1
            return PS[2 + rot['tp']]

        def tsl(tt):
            if tt < 16:
                return 128, slice(tt * 128, tt * 128 + 128)
            return 32, slice(2048 + 32 * (tt - 16), 2048 + 32 * (tt - 16) + 32)

        GROUPS = [(g * 512, 512, [4 * g + i for i in range(4)]) for g in range(4)] + [(2048, 64, [16, 17])]
        mm_list = [[0, 1]]

        S.op('pool', lambda e: e.memset(ident[:], 0.0), [], ident.b)
        S.op('pool', lambda e: e.affine_select(out=ident[:], in_=ident[:], pattern=[[-1, 128]], compare_op=ALU.not_equal,
                                               fill=1.0, base=0, channel_multiplier=1), ident.b, ident.b)
        S.op('pool', lambda e: e.memset(ones[:], 1.0), [], ones.b)
        S.op('pool', lambda e: e.memset(zeros[:], 0.0), [], zeros.b)
        S.op('pool', lambda e: e.memset(onesf[:], 1.0), [], onesf.b)
        S.dma('pool', tri[:], k_tri, [], tri.b)
        S.dma('sp', msb[:], k_msb, [], msb.b)
        S.dma('sp', mch[:], k_mch, [], mch.b)
        S.dma('sp', mdf[:].rearrange("p a b -> p (a b)"), k_mdf, [], mdf.b)
        S.dma('sp', bdp[:].rearrange("p a b -> p (a b)"), k_bdp, [], bdp.b)
        S.dma('sp', bds[:].rearrange("p a b -> p (a b)"), k_bds, [], bds.b)
        S.dma('sp', cs[:, 0:16, :], k_cs[0:2048, :].rearrange("(t p) n -> p t n", p=128), [], cs.b)
        S.dma('sp', cs[0:32, 16, :], k_cs[2048:2080, :], [], cs.b)
        S.dma('sp', cs[0:32, 17, :], k_cs[2080:2112, :], [], cs.b)
        zrhs = sb(es, [128, 512], BF16)
        S.op('pool', lambda e: e.memset(zrhs[:], 0.0), [], zrhs.b)

        GS = dict(q_norm=(0, 256), kv_norm=(256, 128), kr=(384, 32), qn=(416, 64), kn=(480, 64), qr=(544, 32),
                  dqn=(576, 64), dkn=(640, 64), lam=(704, 256))

        def gs(name, rows=128):
            o, n = GS[name]
            return gsm[0:rows, o:o + n]

        def rstd_from_ss(ss_t, rows, G, d):
            S.op('act', lambda e: e.activation(out=ss_t[0:rows, 0:G], in_=ss_t[0:rows, 0:G], func=AF.Ln, scale=1.0 / d, bias=EPS),
                 ss_t.b, ss_t.b)
            S.op('act', lambda e: e.activation(out=ss_t[0:rows, 0:G], in_=ss_t[0:rows, 0:G], func=AF.Exp, scale=-0.5),
                 ss_t.b, ss_t.b)

        def gnorm(ws, src, srcb, rows, G, d, gain, out, outb, post_scale=1.0):
            sq, ss, tmp = ws['sq'], ws['ss'], ws['tmp']
            sqv = sq[0:rows, 0:G * d].rearrange("p (g d) -> p g d", d=d)
            S.op('act', lambda e: e.activation(out=sqv, in_=src, func=AF.Square), srcb, sq.b)
            S.op('dve', lambda e: e.tensor_reduce(out=ss[0:rows, 0:G], in_=sqv, axis=AX.X, op=ALU.add), sq.b, ss.b)
            rstd_from_ss(ss, rows, G, d)
            tv = tmp[0:rows, 0:G * d].rearrange("p (g d) -> p g d", d=d)
            S.op('dve', lambda e: e.tensor_tensor(out=tv, in0=src, in1=ss[0:rows, 0:G].unsqueeze(2).to_broadcast([rows, G, d]),
                                                  op=ALU.mult), list(srcb) + ss.b, tmp.b)
            gb = gain.unsqueeze(1).to_broadcast([rows, G, d])
            if post_scale == 1.0:
                S.op('dve', lambda e: e.tensor_tensor(out=out, in0=tv, in1=gb, op=ALU.mult), tmp.b + gsm.b, outb)
            else:
                S.op('dve', lambda e: e.scalar_tensor_tensor(out=out, in0=tv, scalar=float(post_scale), in1=gb, op0=ALU.mult,
                                                             op1=ALU.mult), tmp.b + gsm.b, outb)

        def rope(ws, src, srcb, rows, H, tt, out, outb):
            t1, t2 = ws['r1'], ws['r2']
            cosb = cs[0:rows, tt, 0:16].unsqueeze(1).to_broadcast([rows, H, 16])
            sinb = cs[0:rows, tt, 16:32].unsqueeze(1).to_broadcast([rows, H, 16])
            a1 = t1[0:rows, 0:H * 16].rearrange("p (h d) -> p h d", d=16)
            a2 = t2[0:rows, 0:H * 16].rearrange("p (h d) -> p h d", d=16)
            x1 = src[:, :, 0:16]
            x2 = src[:, :, 16:32]
            S.op('dve', lambda e: e.tensor_tensor(out=a1, in0=x1, in1=cosb, op=ALU.mult), list(srcb) + cs.b, t1.b)
            S.op('dve', lambda e: e.tensor_tensor(out=a2, in0=x2, in1=sinb, op=ALU.mult), list(srcb) + cs.b, t2.b)
            S.op('dve', lambda e: e.tensor_tensor(out=out[:, :, 0:16], in0=a1, in1=a2, op=ALU.subtract), t1.b + t2.b, outb)
            S.op('dve', lambda e: e.tensor_tensor(out=a1, in0=x1, in1=sinb, op=ALU.mult), list(srcb) + cs.b, t1.b)
            S.op('dve', lambda e: e.tensor_tensor(out=a2, in0=x2, in1=cosb, op=ALU.mult), list(srcb) + cs.b, t2.b)
            S.op('dve', lambda e: e.tensor_tensor(out=out[:, :, 16:32], in0=a1, in1=a2, op=ALU.add), t1.b + t2.b, outb)

        def transpose_to(src, srcb, rows, ncols, dst, dstb, eng='act'):
            pb = tpbank()
            pv = pb.h[:, :].bitcast(BF16)
            S.op('pe', lambda e: e.transpose(out=pv[0:ncols, 0:rows], in_=src, identity=ident[0:rows, 0:rows]), list(srcb) + ident.b, pb.b)
            if eng == 'act':
                S.op('act', lambda e: e.copy(out=dst, in_=pv[0:ncols, 0:rows]), pb.b, dstb)
            else:
                S.op('dve', lambda e: e.tensor_copy(out=dst, in_=pv[0:ncols, 0:rows]), pb.b, dstb)

        def load_w(dst, src_ap):
            S.dma('pool', dst.h[:], src_ap, [], dst.b)

        def norm_phase(l, gname, first):
            with ExitStack() as ps:
                sq = sb(ps, [128, D], F32)
                ssr = [sb(ps, [128, 1], F32) for _ in range(2)]
                hb = [sb(ps, [128, D], BF16) for _ in range(2)]
                S.dma('sp', gbig[:], W[gname][l].partition_broadcast(128), [], gbig.b)
                for tt in range(NT):
                    rows, sl = tsl(tt)
                    if first:
                        S.dma('sp', X[0:rows, tt, :], xin[sl, :], [], [X.b[tt]])
                    ss = ssr[tt % 2]
                    h = hb[tt % 2]
                    S.op('act', lambda e: e.activation(out=sq[0:rows, :], in_=X[0:rows, tt, :], func=AF.Square,
                                                       accum_out=ss[0:rows, :]), [X.b[tt]], sq.b + ss.b)
                    rstd_from_ss(ss, rows, 1, D)
                    S.op('dve', lambda e: e.scalar_tensor_tensor(out=h[0:rows, :], in0=X[0:rows, tt, :], scalar=ss[0:rows, 0:1],
                                                                 in1=gbig[0:rows, :], op0=ALU.mult, op1=ALU.mult),
                         [X.b[tt]] + ss.b + gbig.b, h.b)
                    pb = tpbank()
                    pv = pb.h[:, :].bitcast(BF16).rearrange("p (k n) -> p k n", n=128)
                    for k in range(8):
                        S.op('pe', lambda e: e.transpose(out=pv[:, k, 0:rows], in_=h[0:rows, k * 128:(k + 1) * 128],
                                                         identity=ident[0:rows, 0:rows]), h.b + ident.b, pb.b, inc=(k == 7))
                    S.op('act' if tt % 2 else 'dve',
                         (lambda e: e.copy(out=HT[:, :, sl], in_=pv[:, :, 0:rows])) if tt % 2 else
                         (lambda e: e.tensor_copy(out=HT[:, :, sl], in_=pv[:, :, 0:rows])), pb.b, [HT.b[tt]])
                S.barrier()

        def _layers():
          for l in range(L):
            lam_init = 0.8 - 0.6 * math.exp(-0.3 * l)
            for nm, wn in [('q_norm', 'mla_q_norm_g'), ('kv_norm', 'mla_kv_norm_g'), ('kr', 'mla_kr_g'), ('qn', 'mla_qn_g'),
                           ('kn', 'mla_kn_g'), ('qr', 'mla_qr_g'), ('dqn', 'diff_qn_g'), ('dkn', 'diff_kn_g'),
                           ('lam', 'diff_lambda')]:
                S.dma('sp', gs(nm), W[wn][l].partition_broadcast(128), [], gsm.b)
            S.dma('sp', gcol[:, 0:1], W['diff_subln_g'][l].rearrange("(p o) -> p o", o=1), [], gcol.b)
            with ExitStack() as ps:
                lt = sb(ps, [128, 128], F32)
                l2 = sb(ps, [128, 2], F32)
                lv = gs('lam')
                S.op('dve', lambda e: e.tensor_tensor(out=lt[:, 0:64], in0=lv[:, 0:64], in1=lv[:, 64:128], op=ALU.mult), gsm.b, lt.b)
                S.op('dve', lambda e: e.tensor_tensor(out=lt[:, 64:128], in0=lv[:, 128:192], in1=lv[:, 192:256], op=ALU.mult), gsm.b, lt.b)
                S.op('dve', lambda e: e.tensor_reduce(out=l2[:, 0:2], in_=lt[:, :].rearrange("p (a b) -> p a b", b=64), axis=AX.X,
                                                      op=ALU.add), lt.b, l2.b)
                S.op('act', lambda e: e.activation(out=l2[:, 0:2], in_=l2[:, 0:2], func=AF.Exp), l2.b, l2.b)
                S.op('dve', lambda e: e.tensor_tensor(out=gcol[:, 1:2], in0=l2[:, 0:1], in1=l2[:, 1:2], op=ALU.subtract), l2.b, gcol.b)
                S.op('dve', lambda e: e.tensor_scalar(out=gcol[:, 1:2], in0=gcol[:, 1:2], scalar1=float(lam_init), scalar2=None,
                                                      op0=ALU.add), gcol.b, gcol.b)
                S.op('dve', lambda e: e.tensor_scalar(out=gcol[:, 2:3], in0=gcol[:, 1:2], scalar1=-1.0, scalar2=None,
                                                      op0=ALU.mult), gcol.b, gcol.b)
                S.op('dve', lambda e: e.tensor_scalar(out=gcol[:, 0:1], in0=gcol[:, 0:1], scalar1=float(1.0 - lam_init), scalar2=None,
                                                      op0=ALU.mult), gcol.b, gcol.b)
                S.barrier()

            norm_phase(l, 'mix_norm_g', l == 0)
            _ck('norm%d' % l)

            for mixer in ('mla', 'sb', 'diff'):
                with ExitStack() as ms:
                    if mixer == 'mla':
                        CQT = sb(ms, [128, 2, NTOK], BF16, NT)
                        CKVT = sb(ms, [128, NTOK], BF16, NT)
                        KRT = sb(ms, [128, NT, 32], BF16, NT)
                        nunits, hpu = 4, 2
                    elif mixer == 'sb':
                        nunits, hpu = 2, 4
                    else:
                        nunits, hpu = 2, 2
                    for u in range(nunits):
                        with ExitStack() as us:
                            ws = dict(ss=sb(us, [128, 8], F32))
                            if mixer != 'sb':
                                ws['sq'] = sb(us, [128, 512], F32)
                                ws['tmp'] = sb(us, [128, 512], F32)
                            if mixer == 'mla':
                                ws['r1'] = sb(us, [128, 128], F32)
                                ws['r2'] = sb(us, [128, 128], F32)
                            stage = [sb(us, [128, 256], F32) for _ in range(3 if mixer != 'sb' else 2)]
                            stg_i = [0]

                            def nstage():
                                stg_i[0] = (stg_i[0] + 1) % len(stage)
                                return stage[stg_i[0]]
                            tokb = [sb(us, [128, 256], BF16) for _ in range(3)]
                            tok_i = [0]

                            def ntok():
                                tok_i[0] = (tok_i[0] + 1) % 3
                                return tokb[tok_i[0]]
                            PT = [sb(us, [128, 512], BF16) for _ in range(4 if mixer == 'mla' else 3)]
                            pt_i = [0]

                            def npt():
                                pt_i[0] = (pt_i[0] + 1) % len(PT)
                                return PT[pt_i[0]]
                            rden = sb(us, [128, 512], F32) if mixer != 'sb' else None
                            if mixer == 'mla':
                                mm_list[0] = [0, 1, 6, 7]
                                KT = sb(us, [96, 2, NTOK], BF16, NT)
                                V = sb(us, [128, NT, 128], BF16, NT)
                                QT = sb(us, [96, 2, 512], BF16)
                                if u == 0:
                                    w1 = sb(us, [128, 8, 416], BF16)
                                    load_w(w1, W['w_in'][l, :, 0:416].rearrange("(k p) n -> p k n", p=128))
                                wuq = sb(us, [128, 2, 192], BF16)
                                load_w(wuq, W['mla_w_uq'][l, :, u * 192:(u + 1) * 192].rearrange("(k p) n -> p k n", p=128))
                                wuk = sb(us, [128, 128], BF16)
                                load_w(wuk, W['mla_w_uk'][l, :, u * 128:(u + 1) * 128])
                                wuv = sb(us, [128, 128], BF16)
                                load_w(wuv, W['mla_w_uv'][l, :, u * 128:(u + 1) * 128])
                                kcat = [sb(us, [128, 2, 96], BF16) for _ in range(2)]
                                qcat = [sb(us, [128, 2, 96], BF16) for _ in range(2)]
                                CS = [dict(pck=sb(us, [128, 4, 128], BF16), pkr=sb(us, [128, 4, 32], BF16), pckT=sb(us, [128, 512], BF16),
                                           KTp=sb(us, [96, 2, 512], BF16), Vp=sb(us, [128, 4, 128], BF16),
                                           kc4=sb(us, [128, 4, 2, 96], BF16)) for _ in range(2)]
                            else:
                                c0q = OFF['sq' if mixer == 'sb' else 'dq'] + u * 256
                                c0k = OFF['sk' if mixer == 'sb' else 'dk'] + u * 256
                                c0v = OFF['sv' if mixer == 'sb' else 'dv'] + u * 256
                                wq = sb(us, [128, 8, 256], BF16)
                                wk = sb(us, [128, 8, 256], BF16)
                                wv = sb(us, [128, 8, 256], BF16)
                                load_w(wq, W['w_in'][l, :, c0q:c0q + 256].rearrange("(k p) n -> p k n", p=128))
                                load_w(wk, W['w_in'][l, :, c0k:c0k + 256].rearrange("(k p) n -> p k n", p=128))
                                load_w(wv, W['w_in'][l, :, c0v:c0v + 256].rearrange("(k p) n -> p k n", p=128))
                                KT = sb(us, [128, 2, NTOK], BF16, NT)
                                V = sb(us, [128, NT, 256], BF16, NT)
                                QT = sb(us, [128, 2, 512], BF16)
                                CS = [dict(pk=sb(us, [128, 4, 256], BF16), KTp=sb(us, [128, 2, 512], BF16), Vp=sb(us, [128, 4, 256], BF16))
                                      for _ in range(2)]
                                if mixer == 'sb':
                                    R = sb(us, [128, 2, 512], F32)
                                    e1 = [sb(us, [128, 512], F32) for _ in range(3)]
                                    spb = [sb(us, [128, 512], BF16) for _ in range(3)]
                                    cumr = [sb(us, [128, 512], F32) for _ in range(2)]
                                else:
                                    t0 = sb(us, [128, 512], F32)
                                    t1 = sb(us, [128, 512], F32)
                                    rden1 = sb(us, [128, 512], F32)
                                    osq = sb(us, [128, 512], F32)
                            o_k = o_sbk if mixer == 'sb' else o_dk
                            o_v = o_sbv if mixer == 'sb' else o_dv

                            def proj_tile(tt, qcol0):
                                rows, sl = tsl(tt)
                                if mixer == 'mla':
                                    if u == 0:
                                        pz = mmbank()
                                        for k in range(8):
                                            S.op('pe', lambda e: e.matmul(out=pz[0:rows, 0:416], lhsT=HT[:, k, sl], rhs=w1[:, k, :],
                                                                          start=(k == 0), stop=(k == 7)), [HT.b[tt]] + w1.b, pz.b, inc=(k == 7))
                                        tk = ntok()
                                        gnorm(ws, pz[0:rows, 0:256].rearrange("p (g d) -> p g d", g=1), pz.b, rows, 1, 256, gs('q_norm', rows),
                                              tk[0:rows, 0:256].rearrange("p (g d) -> p g d", g=1), tk.b)
                                        for c in range(2):
                                            transpose_to(tk[0:rows, c * 128:(c + 1) * 128], tk.b, rows, 128, CQT[:, c, sl], [CQT.b[tt]],
                                                         'act' if c else 'dve')
                                        st = nstage()
                                        gnorm(ws, pz[0:rows, 256:384].rearrange("p (g d) -> p g d", g=1), pz.b, rows, 1, 128, gs('kv_norm', rows),
                                              st[0:rows, 0:128].rearrange("p (g d) -> p g d", g=1), st.b)
                                        S.dma('sp', o_ckv[l, sl, :], st[0:rows, 0:128], st.b, [])
                                        tk2 = ntok()
                                        S.op('act', lambda e: e.copy(out=tk2[0:rows, 0:128], in_=st[0:rows, 0:128]), st.b, tk2.b)
                                        transpose_to(tk2[0:rows, 0:128], tk2.b, rows, 128, CKVT[:, sl], [CKVT.b[tt]], 'dve')
                                        gnorm(ws, pz[0:rows, 384:416].rearrange("p (g d) -> p g d", g=1), pz.b, rows, 1, 32, gs('kr', rows),
                                              st[0:rows, 128:160].rearrange("p (g d) -> p g d", g=1), st.b)
                                        rope(ws, st[0:rows, 128:160].rearrange("p (g d) -> p g d", g=1), st.b, rows, 1, tt,
                                             st[0:rows, 160:192].rearrange("p (g d) -> p g d", g=1), st.b)
                                        S.dma('sp', o_kr[l, sl, :], st[0:rows, 160:192], st.b, [])
                                        S.op('act', lambda e: e.copy(out=KRT[0:rows, tt, :], in_=st[0:rows, 160:192]), st.b, [KRT.b[tt]])
                                    pq = mmbank()
                                    for c in range(2):
                                        S.op('pe', lambda e: e.matmul(out=pq[0:rows, 0:192], lhsT=CQT[:, c, sl], rhs=wuq[:, c, :],
                                                                      start=(c == 0), stop=(c == 1)), [CQT.b[tt]] + wuq.b, pq.b, inc=(c == 1))
                                    qv = pq[0:rows, 0:192].rearrange("p (h d) -> p h d", d=96)
                                    qc = qcat[tt % 2]
                                    gnorm(ws, qv[:, :, 0:64], pq.b, rows, 2, 64, gs('qn', rows), qc[0:rows, :, 0:64], qc.b, MLA_SCALE)
                                    st = nstage()
                                    stv = st[0:rows, 0:64].rearrange("p (h d) -> p h d", d=32)
                                    gnorm(ws, qv[:, :, 64:96], pq.b, rows, 2, 32, gs('qr', rows), stv, st.b, MLA_SCALE)
                                    rope(ws, stv, st.b, rows, 2, tt, qc[0:rows, :, 64:96], qc.b)
                                    for hh in range(2):
                                        transpose_to(qc[0:rows, hh, :], qc.b, rows, 96, QT[:, hh, qcol0:qcol0 + rows], QT.b, 'act' if hh else 'dve')
                                    kv_from_latent(CKVT[:, sl], [CKVT.b[tt]], KRT[0:rows, tt, :], [KRT.b[tt]], rows,
                                                   lambda hh: KT[:, hh, sl], [KT.b[tt]], V[0:rows, tt, :], [V.b[tt]], kcat[tt % 2])
                                else:
                                    pq = mmbank()
                                    for k in range(8):
                                        S.op('pe', lambda e: e.matmul(out=pq[0:rows, 0:256], lhsT=HT[:, k, sl], rhs=wq[:, k, :],
                                                                      start=(k == 0), stop=(k == 7)), [HT.b[tt]] + wq.b, pq.b, inc=(k == 7))
                                    tk = ntok()
                                    if mixer == 'sb':
                                        S.op('act', lambda e: e.activation(out=tk[0:rows, :], in_=pq[0:rows, 0:256], func=AF.Copy,
                                                                           scale=SB_SCALE), pq.b, tk.b)
                                    else:
                                        gnorm(ws, pq[0:rows, 0:256].rearrange("p (g d) -> p g d", d=64), pq.b, rows, 4, 64, gs('dqn', rows),
                                              tk[0:rows, :].rearrange("p (g d) -> p g d", d=64), tk.b, DIFF_SCALE)
                                    for c in range(2):
                                        transpose_to(tk[0:rows, c * 128:(c + 1) * 128], tk.b, rows, 128, QT[:, c, qcol0:qcol0 + rows], QT.b,
                                                     'act' if c else 'dve')
                                    pk_ = mmbank()
                                    for k in range(8):
                                        S.op('pe', lambda e: e.matmul(out=pk_[0:rows, 0:256], lhsT=HT[:, k, sl], rhs=wk[:, k, :],
                                                                      start=(k == 0), stop=(k == 7)), [HT.b[tt]] + wk.b, pk_.b, inc=(k == 7))
                                    st = nstage()
                                    if mixer == 'sb':
                                        S.op('act', lambda e: e.copy(out=st[0:rows, :], in_=pk_[0:rows, 0:256]), pk_.b, st.b)
                                    else:
                                        gnorm(ws, pk_[0:rows, 0:256].rearrange("p (g d) -> p g d", d=64), pk_.b, rows, 4, 64, gs('dkn', rows),
                                              st[0:rows, :].rearrange("p (g d) -> p g d", d=64), st.b)
                                    S.dma('sp', o_k[l, sl, u * 256:(u + 1) * 256], st[0:rows, :], st.b, [])
                                    tk = ntok()
                                    S.op('dve', lambda e: e.tensor_copy(out=tk[0:rows, :], in_=st[0:rows, :]), st.b, tk.b)
                                    for c in range(2):
                                        transpose_to(tk[0:rows, c * 128:(c + 1) * 128], tk.b, rows, 128, KT[:, c, sl], [KT.b[tt]],
                                                     'act' if c else 'dve')
                                    pv_ = mmbank()
                                    for k in range(8):
                                        S.op('pe', lambda e: e.matmul(out=pv_[0:rows, 0:256], lhsT=HT[:, k, sl], rhs=wv[:, k, :],
                                                                      start=(k == 0), stop=(k == 7)), [HT.b[tt]] + wv.b, pv_.b, inc=(k == 7))
                                    st = nstage()
                                    S.op('act', lambda e: e.copy(out=st[0:rows, :], in_=pv_[0:rows, 0:256]), pv_.b, st.b)
                                    S.dma('sp', o_v[l, sl, u * 256:(u + 1) * 256], st[0:rows, :], st.b, [])
                                    S.op('dve', lambda e: e.tensor_copy(out=V[0:rows, tt, :], in_=pv_[0:rows, 0:256]), pv_.b, [V.b[tt]])

                            def kv_from_latent(ckvT_ap, ckvT_b, kr_ap, kr_b, rows, kt_dst, kt_b, v_dst, v_b, kc):
                                pk_ = mmbank()
                                S.op('pe', lambda e: e.matmul(out=pk_[0:rows, 0:128], lhsT=ckvT_ap, rhs=wuk[:, :], start=True, stop=True),
                                     list(ckvT_b) + wuk.b, pk_.b)
                                gnorm(ws, pk_[0:rows, 0:128].rearrange("p (h d) -> p h d", d=64), pk_.b, rows, 2, 64, gs('kn', rows),
                                      kc[0:rows, :, 0:64], kc.b)
                                S.op('dve', lambda e: e.tensor_copy(out=kc[0:rows, :, 64:96], in_=kr_ap.unsqueeze(1).to_broadcast([rows, 2, 32])),
                                     list(kr_b), kc.b)
                                for hh in range(2):
                                    transpose_to(kc[0:rows, hh, :], kc.b, rows, 96, kt_dst(hh), kt_b, 'act' if hh else 'dve')
                                pv_ = mmbank()
                                S.op('pe', lambda e: e.matmul(out=pv_[0:rows, 0:128], lhsT=ckvT_ap, rhs=wuv[:, :], start=True, stop=True),
                                     list(ckvT_b) + wuv.b, pv_.b)
                                S.op('act', lambda e: e.copy(out=v_dst, in_=pv_[0:rows, 0:128]), pv_.b, v_b)

                            acc, accd, acc1, accd1 = PS[4], PS[5], PS[6], PS[7]

                            def zero_acc(t_, ncols):
                                S.op('pe', lambda e: e.matmul(out=t_[:, 0:ncols], lhsT=zeros[:, :], rhs=zrhs[:, 0:ncols], start=True, stop=True),
                                     zeros.b + zrhs.b, t_.b)

                            acc1s = [(PS[6], PS[7]), (PS[2], PS[3])]
                            itc = [0]

                            def att_begin(ncols):
                                if mixer == 'mla':
                                    zero_acc(acc, ncols)
                                    zero_acc(accd, ncols)
                                elif mixer == 'sb':
                                    zero_acc(acc, ncols)
                                    S.op('dve', lambda e: e.memset(R[:, :, 0:ncols], 0.0), [], R.b)
                                else:
                                    for t_ in (acc, accd, acc1, accd1):
                                        zero_acc(t_, ncols)

                            def diag_mask(t_, mask_ap, mask_b, nk, c0, ncols):
                                dn = min(128, ncols - c0)
                                S.op('dve', lambda e: e.tensor_tensor(out=t_[0:nk, c0:c0 + dn], in0=t_[0:nk, c0:c0 + dn], in1=mask_ap(nk, dn),
                                                                      op=ALU.mult), t_.b + mask_b, t_.b)

                            def stage_a(hg, kb, x, qc0, ncols, sample):
                                nk, c0 = kb['nk'], kb['c0']
                                st = mmbank()
                                if mixer == 'mla':
                                    S.op('pe', lambda e: e.matmul(out=st[0:nk, c0:ncols], lhsT=kb['kt'](x), rhs=QT[:, x, qc0 + c0:qc0 + ncols],
                                                                  start=True, stop=True), kb['ktb'] + QT.b, st.b)
                                    pt = npt()
                                    S.op('act', lambda e: e.activation(out=pt[0:nk, c0:ncols], in_=st[0:nk, c0:ncols], func=AF.Exp), st.b, pt.b)
                                    if kb['diag']:
                                        diag_mask(pt, lambda a_, b_: mch[0:a_, 0:b_], mch.b, nk, c0, ncols)
                                    return dict(pt=pt)
                                if mixer == 'sb':
                                    hp = hg
                                    S.op('pe', lambda e: e.matmul(out=st[0:nk, c0:ncols], lhsT=kb['kt'](hp)[64 * x:64 * x + 64, :],
                                                                  rhs=QT[64 * x:64 * x + 64, hp, qc0 + c0:qc0 + ncols], start=True, stop=True),
                                         kb['ktb'] + QT.b, st.b)
                                    itc[0] += 1
                                    ee = e1[itc[0] % 3]
                                    sp_ = spb[itc[0] % 3]
                                    a1, ad1 = acc1s[itc[0] % 2]
                                    S.op('act', lambda e: e.activation(out=ee[0:nk, c0:ncols], in_=st[0:nk, c0:ncols], func=AF.Exp), st.b, ee.b)
                                    S.op('act', lambda e: e.activation(out=sp_[0:nk, c0:ncols], in_=ee[0:nk, c0:ncols], func=AF.Ln, bias=1.0),
                                         ee.b, sp_.b)
                                    if kb['diag']:
                                        diag_mask(sp_, lambda a_, b_: msb[0:a_, 0:b_], msb.b, nk, c0, ncols)
                                    return dict(ee=ee, sp=sp_, a1=a1, ad1=ad1, cu=cumr[itc[0] % 2])
                                hd = hg
                                hgl = 2 * u + hd
                                S.op('pe', lambda e: e.matmul(out=st[0:nk, c0:ncols], lhsT=kb['kt'](hd)[64 * x:64 * x + 64, :],
                                                              rhs=QT[64 * x:64 * x + 64, hd, qc0 + c0:qc0 + ncols], start=True, stop=True),
                                     kb['ktb'] + QT.b, st.b)
                                pt = npt()
                                if sample:
                                    bi = kb['bi']
                                    S.op('act', lambda e: e.activation(out=pt[0:nk, c0:ncols], in_=st[0:nk, c0:ncols], func=AF.Exp,
                                                                       bias=bds[0:nk, hgl, bi:bi + 1]), st.b + bds.b, pt.b)
                                else:
                                    cc = c0
                                    while cc < ncols:
                                        ce = min(ncols, (cc // 256 + 1) * 256)
                                        dl = (kb['qt0'] + ce // 128 - 1) - kb['kbi']
                                        S.op('act', lambda e: e.activation(out=pt[0:nk, cc:ce], in_=st[0:nk, cc:ce], func=AF.Exp,
                                                                           bias=bdp[0:nk, hgl, dl:dl + 1]), st.b + bdp.b, pt.b)
                                        cc = ce
                                if kb['diag']:
                                    diag_mask(pt, lambda a_, b_: mdf[0:a_, hgl, 0:b_], mdf.b, nk, c0, ncols)
                                return dict(pt=pt)

                            def stage_b(hg, kb, x, ncols, acol, cx):
                                nk, c0 = kb['nk'], kb['c0']
                                o0, o1 = acol + c0, acol + ncols
                                if mixer == 'mla':
                                    pt = cx['pt']
                                    S.op('pe', lambda e: e.matmul(out=acc[64 * x:64 * x + 64, o0:o1], lhsT=kb['v'](x),
                                                                  rhs=pt[0:nk, c0:ncols], start=False, stop=True, skip_group_check=True),
                                         kb['vb'] + pt.b, acc.b)
                                    S.op('pe', lambda e: e.matmul(out=accd[64 * x:64 * x + 64, o0:o1], lhsT=ones[0:nk, 0:64],
                                                                  rhs=pt[0:nk, c0:ncols], start=False, stop=True, skip_group_check=True),
                                         ones.b + pt.b, accd.b)
                                elif mixer == 'sb':
                                    hp = hg
                                    ee, a1, ad1, cu = cx['ee'], cx['a1'], cx['ad1'], cx['cu']
                                    S.op('dve', lambda e: e.tensor_tensor(out=cu[0:nk, c0:ncols], in0=a1[0:nk, c0:ncols],
                                                                          in1=R[0:nk, x, o0:o1], op=ALU.add), a1.b + R.b, cu.b)
                                    S.op('act', lambda e: e.activation(out=cu[0:nk, c0:ncols], in_=cu[0:nk, c0:ncols], func=AF.Exp,
                                                                       scale=-1.0), cu.b, cu.b)
                                    pt = npt()
                                    S.op('dve', lambda e: e.tensor_tensor(out=pt[0:nk, c0:ncols], in0=ee[0:nk, c0:ncols],
                                                                          in1=cu[0:nk, c0:ncols], op=ALU.mult), ee.b + cu.b, pt.b)
                                    if kb['diag']:
                                        diag_mask(pt, lambda a_, b_: msb[0:a_, 0:b_], msb.b, nk, c0, ncols)
                                    S.op('dve', lambda e: e.tensor_tensor(out=R[:, x, o0:o1], in0=R[:, x, o0:o1], in1=ad1[:, c0:ncols],
                                                                          op=ALU.add), R.b + ad1.b, R.b)
                                    S.op('pe', lambda e: e.matmul(out=acc[64 * x:64 * x + 64, o0:o1], lhsT=kb['v'](2 * hp + x),
                                                                  rhs=pt[0:nk, c0:ncols], start=False, stop=True, skip_group_check=True),
                                         kb['vb'] + pt.b, acc.b)
                                else:
                                    pt = cx['pt']
                                    a_, d_ = (acc, accd) if x == 0 else (acc1, accd1)
                                    S.op('pe', lambda e: e.matmul(out=a_[:, o0:o1], lhsT=kb['v'](hg), rhs=pt[0:nk, c0:ncols],
                                                                  start=False, stop=True, skip_group_check=True), kb['vb'] + pt.b, a_.b)
                                    S.op('pe', lambda e: e.matmul(out=d_[:, o0:o1], lhsT=ones[0:nk, :], rhs=pt[0:nk, c0:ncols],
                                                                  start=False, stop=True, skip_group_check=True), ones.b + pt.b, d_.b)

                            def stage_a1(kb, ncols, cx):
                                nk, c0 = kb['nk'], kb['c0']
                                sp_, a1, ad1 = cx['sp'], cx['a1'], cx['ad1']
                                S.op('pe', lambda e: e.matmul(out=a1[0:nk, c0:ncols], lhsT=tri[0:nk, 0:nk], rhs=sp_[0:nk, c0:ncols],
                                                              start=True, stop=True), tri.b + sp_.b, a1.b)
                                S.op('pe', lambda e: e.matmul(out=ad1[:, c0:ncols], lhsT=ones[0:nk, :], rhs=sp_[0:nk, c0:ncols],
                                                              start=True, stop=True), ones.b + sp_.b, ad1.b)

                            def att_blocks(hg, kbs, qc0, ncols, sample, acol=0):
                                items = [(kb, x) for kb in kbs for x in range(2)]
                                n = len(items)
                                dep = 1 if mixer == 'diff' else 2
                                ctx = {}
                                for t in range(n + dep):
                                    if t < n:
                                        ctx[t] = stage_a(hg, items[t][0], items[t][1], qc0, ncols, sample)
                                    if mixer == 'sb' and 0 <= t - 1 < n:
                                        stage_a1(items[t - 1][0], ncols, ctx[t - 1])
                                    if 0 <= t - dep < n:
                                        stage_b(hg, items[t - dep][0], items[t - dep][1], ncols, acol, ctx.pop(t - dep))

                            def att_end(hg, ncols, ocol0, grp_b, acol=0):
                                a0, a1_ = acol, acol + ncols
                                if mixer == 'mla':
                                    S.op('dve', lambda e: e.reciprocal(out=rden[:, 0:ncols], in_=accd[:, a0:a1_]), accd.b, rden.b)
                                    S.op('dve', lambda e: e.tensor_tensor(out=OT[:, u, ocol0:ocol0 + ncols], in0=acc[:, a0:a1_], in1=rden[:, 0:ncols],
                                                                          op=ALU.mult), acc.b + rden.b, grp_b)
                                elif mixer == 'sb':
                                    S.op('act', lambda e: e.copy(out=OT[:, 2 * u + hg, ocol0:ocol0 + ncols], in_=acc[:, a0:a1_]), acc.b, grp_b)
                                else:
                                    hgl = 2 * u + hg
                                    S.op('dve', lambda e: e.reciprocal(out=rden[:, 0:ncols], in_=accd[:, a0:a1_]), accd.b, rden.b)
                                    S.op('dve', lambda e: e.reciprocal(out=rden1[:, 0:ncols], in_=accd1[:, a0:a1_]), accd1.b, rden1.b)
                                    S.op('dve', lambda e: e.tensor_tensor(out=t0[:, 0:ncols], in0=acc[:, a0:a1_], in1=rden[:, 0:ncols], op=ALU.mult),
                                         acc.b + rden.b, t0.b)
                                    S.op('dve', lambda e: e.tensor_tensor(out=t1[:, 0:ncols], in0=acc1[:, a0:a1_], in1=rden1[:, 0:ncols], op=ALU.mult),
                                         acc1.b + rden1.b, t1.b)
                                    S.op('dve', lambda e: e.scalar_tensor_tensor(out=t0[:, 0:ncols], in0=t1[:, 0:ncols], scalar=gcol[:, 2:3],
                                                                                 in1=t0[:, 0:ncols], op0=ALU.mult, op1=ALU.add),
                                         t0.b + t1.b + gcol.b, t0.b)
                                    S.op('act', lambda e: e.activation(out=osq[:, 0:ncols], in_=t0[:, 0:ncols], func=AF.Square), t0.b, osq.b)
                                    pss = mmbank()
                                    S.op('pe', lambda e: e.matmul(out=pss[:, 0:ncols], lhsT=onesf[:, :], rhs=osq[:, 0:ncols], start=True, stop=True),
                                         onesf.b + osq.b, pss.b)
                                    S.op('act', lambda e: e.activation(out=t1[:, 0:ncols], in_=pss[:, 0:ncols], func=AF.Ln, scale=1.0 / 128, bias=EPS),
                                         pss.b, t1.b)
                                    S.op('act', lambda e: e.activation(out=t1[:, 0:ncols], in_=t1[:, 0:ncols], func=AF.Exp, scale=-0.5), t1.b, t1.b)
                                    S.op('dve', lambda e: e.scalar_tensor_tensor(out=OT[:, hgl, ocol0:ocol0 + ncols], in0=t0[:, 0:ncols],
                                                                                 scalar=gcol[:, 0:1], in1=t1[:, 0:ncols], op0=ALU.mult, op1=ALU.mult),
                                         t0.b + t1.b + gcol.b, grp_b)

                            nhg = 1 if mixer == 'mla' else 2

                            def kb_store(kbi, c0, diag, qt0, nk=128, bi=0):
                                rows_, sl = tsl(kbi)
                                if mixer == 'mla':
                                    return dict(kt=lambda hh: KT[:, hh, sl], ktb=[KT.b[kbi]], v=lambda hh: V[0:nk, kbi, 64 * hh:64 * hh + 64],
                                                vb=[V.b[kbi]], nk=nk, c0=c0, diag=diag, kbi=kbi, qt0=qt0, bi=bi)
                                if mixer == 'sb':
                                    return dict(kt=lambda hp: KT[:, hp, sl], ktb=[KT.b[kbi]], v=lambda h: V[0:nk, kbi, 64 * h:64 * h + 64],
                                                vb=[V.b[kbi]], nk=nk, c0=c0, diag=diag, kbi=kbi, qt0=qt0, bi=bi)
                                return dict(kt=lambda hd: KT[:, hd, sl], ktb=[KT.b[kbi]], v=lambda hd: V[0:nk, kbi, 128 * hd:128 * hd + 128],
                                            vb=[V.b[kbi]], nk=nk, c0=c0, diag=diag, kbi=kbi, qt0=qt0, bi=bi)

                            chi = [0]

                            def kb_past(j, kbi, cs_):
                                sl = slice(j * 128, (j + 1) * 128)
                                bi = 32 - kbi
                                KTp, Vp = cs_['KTp'], cs_['Vp']
                                if mixer == 'mla':
                                    return dict(kt=lambda hh: KTp[:, hh, sl], ktb=KTp.b, v=lambda hh: Vp[:, j, 64 * hh:64 * hh + 64],
                                                vb=Vp.b, nk=128, c0=0, diag=False, kbi=kbi, qt0=0, bi=bi)
                                if mixer == 'sb':
                                    return dict(kt=lambda hp: KTp[:, hp, sl], ktb=KTp.b, v=lambda h: Vp[:, j, 64 * h:64 * h + 64],
                                                vb=Vp.b, nk=128, c0=0, diag=False, kbi=kbi, qt0=0, bi=bi)
                                return dict(kt=lambda hd: KTp[:, hd, sl], ktb=KTp.b, v=lambda hd: Vp[:, j, 128 * hd:128 * hd + 128],
                                            vb=Vp.b, nk=128, c0=0, diag=False, kbi=kbi, qt0=0, bi=bi)

                            def build_chunk(s, ch):
                                r0 = ch * 512
                                chi[0] += 1
                                cs_ = CS[chi[0] % 2]
                                if mixer == 'mla':
                                    pck, pkr, pckT, KTp, Vp, kc4 = cs_['pck'], cs_['pkr'], cs_['pckT'], cs_['KTp'], cs_['Vp'], cs_['kc4']
                                    S.dma('pool', pck[:], c_ckv[l, s, r0:r0 + 512, :].rearrange("(k p) n -> p k n", p=128), [], pck.b)
                                    S.dma('pool', pkr[:], c_kr[l, s, r0:r0 + 512, :].rearrange("(k p) n -> p k n", p=128), [], pkr.b)
                                    for j in range(4):
                                        transpose_to(pck[:, j, :], pck.b, 128, 128, pckT[:, j * 128:(j + 1) * 128], pckT.b, 'act' if j % 2 else 'dve')
                                    pk_ = mmbank()
                                    for j in range(4):
                                        S.op('pe', lambda e: e.matmul(out=pk_[:, j * 128:(j + 1) * 128], lhsT=pckT[:, j * 128:(j + 1) * 128], rhs=wuk[:, :],
                                                                      start=True, stop=True), pckT.b + wuk.b, pk_.b)
                                    gnorm(ws, pk_[:, 0:512].rearrange("p (g d) -> p g d", d=64), pk_.b, 128, 8, 64, gs('kn', 128),
                                          kc4[:, :, :, 0:64].rearrange("p j h d -> p (j h) d"), kc4.b)
                                    S.op('dve', lambda e: e.tensor_copy(out=kc4[:, :, :, 64:96], in_=pkr[:, :, :].unsqueeze(2).to_broadcast([128, 4, 2, 32])),
                                         pkr.b, kc4.b)
                                    for j in range(4):
                                        for hh in range(2):
                                            transpose_to(kc4[:, j, hh, :], kc4.b, 128, 96, KTp[:, hh, j * 128:(j + 1) * 128], KTp.b, 'act' if hh else 'dve')
                                    pv_ = mmbank()
                                    for j in range(4):
                                        S.op('pe', lambda e: e.matmul(out=pv_[:, j * 128:(j + 1) * 128], lhsT=pckT[:, j * 128:(j + 1) * 128], rhs=wuv[:, :],
                                                                      start=True, stop=True), pckT.b + wuv.b, pv_.b)
                                    S.op('act', lambda e: e.copy(out=Vp[:, :, :].rearrange("p j n -> p (j n)"), in_=pv_[:, 0:512]), pv_.b, Vp.b)
                                else:
                                    pk, KTp, Vp = cs_['pk'], cs_['KTp'], cs_['Vp']
                                    ck = c_sbk if mixer == 'sb' else c_dk
                                    cv = c_sbv if mixer == 'sb' else c_dv
                                    S.dma('pool', pk[:], ck[l, s, r0:r0 + 512, u * 256:(u + 1) * 256].rearrange("(k p) n -> p k n", p=128), [], pk.b)
                                    S.dma('pool', Vp[:], cv[l, s, r0:r0 + 512, u * 256:(u + 1) * 256].rearrange("(k p) n -> p k n", p=128), [], Vp.b)
                                    for j in range(4):
                                        for c in range(2):
                                            transpose_to(pk[:, j, c * 128:(c + 1) * 128], pk.b, 128, 128, KTp[:, c, j * 128:(j + 1) * 128], KTp.b,
                                                         'act' if c else 'dve')
                                return cs_

                            for g in range(4):
                                for i in range(4):
                                    proj_tile(4 * g + i, i * 128)
                                kbs = []
                                for kbi in range(4 * g + 4):
                                    i = kbi - 4 * g
                                    kbs.append(kb_store(kbi, max(i, 0) * 128, i >= 0, 4 * g))
                                if mixer == 'sb':
                                    kbs = kbs[::-1]
                                for hg in range(nhg):
                                    att_begin(512)
                                    att_blocks(hg, kbs, 0, 512, False)
                                    att_end(hg, 512, g * 512, [OT.b[g]])
                                _ck('%s%d_u%d_g%d' % (mixer, l, u, g))

                            for s in range(NS):
                                proj_tile(16 + s, 0)
                                knew = kb_store(16 + s, 0, mixer != 'mla', 0, nk=32, bi=0)
                                att_begin(32 * nhg)
                                if mixer == 'sb':
                                    for hg in range(nhg):
                                        att_blocks(hg, [knew], 0, 32, True, acol=32 * hg)
                                    for ch in range(7, -1, -1):
                                        cs_ = build_chunk(s, ch)
                                        for hg in range(nhg):
                                            att_blocks(hg, [kb_past(j, ch * 4 + j, cs_) for j in range(3, -1, -1)], 0, 32, True, acol=32 * hg)
                                else:
                                    for ch in range(8):
                                        cs_ = build_chunk(s, ch)
                                        for hg in range(nhg):
                                            att_blocks(hg, [kb_past(j, ch * 4 + j, cs_) for j in range(4)], 0, 32, True, acol=32 * hg)
                                    for hg in range(nhg):
                                        att_blocks(hg, [knew], 0, 32, True, acol=32 * hg)
                                for hg in range(nhg):
                                    att_end(hg, 32, 2048 + 32 * s, [OT.b[4]], acol=32 * hg)
                                _ck('%s%d_u%d_s%d' % (mixer, l, u, s))
                            S.barrier()
                            mm_list[0] = [0, 1]
                            rot['mm'] = 0
                            _ck('%s%d_u%d' % (mixer, l, u))

                    mi = ('mla', 'sb', 'diff').index(mixer)
                    ms.close()
                    with ExitStack() as gs_:
                        mm_list[0] = [0, 1, 4, 5, 6, 7]
                        wg = sb(gs_, [128, 8, 1024], BF16)
                        g0 = OFF['g'] + 1024 * mi
                        load_w(wg, W['w_in'][l, :, g0:g0 + 1024].rearrange("(k p) n -> p k n", p=128))
                        wbr = sb(gs_, [128, 4, 1024], BF16)
                        load_w(wbr, W['w_br_' + mixer][l].rearrange("(k p) n -> p k n", p=128))
                        wo = sb(gs_, [128, 8, 1024], BF16)
                        load_w(wo, W['w_out'][l].rearrange("(k p) n -> p k n", p=128))
                        gT = [sb(gs_, [128, 512], F32) for _ in range(2)]
                        MG = [sb(gs_, [128, 8, 512], BF16) for _ in range(1)]
                        for gi, (c0, n, tiles) in enumerate(GROUPS):
                            hb_ = [HT.b[t] for t in tiles]
                            mg = MG[0]
                            for nn in range(8):
                                nsl = slice(nn * 128, (nn + 1) * 128)
                                pg = mmbank()
                                for k in range(8):
                                    S.op('pe', lambda e: e.matmul(out=pg[:, 0:n], lhsT=wg[:, k, nsl], rhs=HT[:, k, c0:c0 + n],
                                                                  start=(k == 0), stop=(k == 7)), wg.b + hb_, pg.b, inc=(k == 7))
                                gt = gT[nn % 2]
                                S.op('act', lambda e: e.activation(out=gt[:, 0:n], in_=pg[:, 0:n], func=AF.Sigmoid), pg.b, gt.b)
                                py = mmbank()
                                for c in range(4):
                                    S.op('pe', lambda e: e.matmul(out=py[:, 0:n], lhsT=wbr[:, c, nsl], rhs=OT[:, c, c0:c0 + n],
                                                                  start=(c == 0), stop=(c == 3)), wbr.b + [OT.b[gi]], py.b, inc=(c == 3))
                                S.op('dve', lambda e: e.tensor_tensor(out=mg[:, nn, 0:n], in0=py[:, 0:n], in1=gt[:, 0:n], op=ALU.mult),
                                     py.b + gt.b, mg.b)
                            for tt in tiles:
                                rows, sl = tsl(tt)
                                off = sl.start - c0
                                for half in range(2):
                                    hsl = slice(half * 512, (half + 1) * 512)
                                    po = mmbank()
                                    for k in range(8):
                                        S.op('pe', lambda e: e.matmul(out=po[0:rows, :], lhsT=mg[:, k, off:off + rows], rhs=wo[:, k, hsl],
                                                                      start=(k == 0), stop=(k == 7)), mg.b + wo.b, po.b, inc=(k == 7))
                                    S.op('dve', lambda e: e.tensor_tensor(out=X[0:rows, tt, hsl], in0=X[0:rows, tt, hsl], in1=po[0:rows, :],
                                                                          op=ALU.add), [X.b[tt]] + po.b, [X.b[tt]])
                        S.barrier()
                        mm_list[0] = [0, 1]
                        rot['mm'] = 0
                    _ck('%s%d_merge' % (mixer, l))

            norm_phase(l, 'ffn_norm_g', False)
            with ExitStack() as fs:
                mm_list[0] = [0, 1, 4, 5, 6, 7]
                cw = sb(fs, [128, 22, 3], F32)
                cb = sb(fs, [128, 22], F32)
                cst = sb(fs, [128, NS, 22, 2], F32)
                OC = sb(fs, [128, 3, 22, 2], F32)
                S.dma('sp', cw[:].rearrange("p a b -> p (a b)"), W['ffn_conv_w'][l], [], cw.b)
                S.dma('sp', cb[:], W['ffn_conv_b'][l], [], cb.b)
                for s in range(NS):
                    S.dma('sp', cst[:, s, :, :].rearrange("p a b -> p (a b)"), c_conv[l, s], [], cst.b)
                WA = [sb(fs, [128, 8, 512], BF16) for _ in range(2)]
                WU = [sb(fs, [128, 8, 512], BF16) for _ in range(2)]
                WD = [sb(fs, [128, 4, 1024], BF16) for _ in range(2)]
                AT = [sb(fs, [128, 4, 516], F32, 4) for _ in range(1)]
                carry = sb(fs, [128, 4, 2], F32, 4)
                cc_ = [sb(fs, [128, 512], F32) for _ in range(1)]
                sl_ = [sb(fs, [128, 512], F32) for _ in range(1)]
                MM = [sb(fs, [128, 4, 512], BF16) for _ in range(1)]
                fgroups = [(0, 4), (4, 4), (8, 4), (12, 4), (16, 4), (20, 2)]
                for fi, (fc0, nfc) in enumerate(fgroups):
                    wa, wu, wd = WA[fi % 2], WU[fi % 2], WD[fi % 2]
                    nf = nfc * 128
                    S.dma('pool', wa[:, :, 0:nf], W['ffn_w_up'][l, :, fc0 * 128:fc0 * 128 + nf].rearrange("(k p) n -> p k n", p=128), [], wa.b)
                    S.dma('pool', wu[:, :, 0:nf], W['ffn_w_up'][l, :, DFF + fc0 * 128:DFF + fc0 * 128 + nf].rearrange("(k p) n -> p k n", p=128),
                          [], wu.b)
                    S.dma('pool', wd[:, 0:nfc, :], W['ffn_w_down'][l, fc0 * 128:fc0 * 128 + nf, :].rearrange("(c p) n -> p c n", p=128), [], wd.b)
                    for gi, (c0, n, tiles) in enumerate(GROUPS):
                        hb_ = [HT.b[t] for t in tiles]
                        mmt = MM[0]
                        buf = AT[0]
                        for j in range(nfc):
                            fc = fc0 + j
                            jsl = slice(j * 128, (j + 1) * 128)
                            pa = mmbank()
                            for k in range(8):
                                S.op('pe', lambda e: e.matmul(out=pa[:, 0:n], lhsT=wa[:, k, jsl], rhs=HT[:, k, c0:c0 + n],
                                                              start=(k == 0), stop=(k == 7)), wa.b + hb_, pa.b, inc=(k == 7))
                            cc = cc_[0]
                            if gi < 4:
                                S.op('act', lambda e: e.copy(out=buf[:, j, 2:514], in_=pa[:, 0:512]), pa.b, [buf.b[j]])
                                if gi == 0:
                                    S.op('dve', lambda e: e.memset(buf[:, j, 0:2], 0.0), [], [buf.b[j]])
                                else:
                                    S.op('dve', lambda e: e.tensor_copy(out=buf[:, j, 0:2], in_=carry[:, j, :]), [carry.b[j]], [buf.b[j]])
                                S.op('dve', lambda e: e.tensor_copy(out=carry[:, j, :], in_=buf[:, j, 512:514]), [buf.b[j]], [carry.b[j]])
                                segs = [(0, 0, 512)]
                                if gi == 3:
                                    S.op('dve', lambda e: e.tensor_copy(out=OC[:, 0, fc, :], in_=buf[:, j, 512:514]), [buf.b[j]], OC.b)
                            else:
                                for s in range(NS):
                                    S.op('act', lambda e: e.copy(out=buf[:, j, s * 34 + 2:s * 34 + 34], in_=pa[:, s * 32:s * 32 + 32]), pa.b, [buf.b[j]])
                                    S.op('dve', lambda e: e.tensor_copy(out=buf[:, j, s * 34:s * 34 + 2], in_=cst[:, s, fc, :]), cst.b, [buf.b[j]])
                                    S.op('dve', lambda e: e.tensor_copy(out=OC[:, 1 + s, fc, :], in_=buf[:, j, s * 34 + 32:s * 34 + 34]), [buf.b[j]], OC.b)
                                segs = [(0, 0, 32), (34, 32, 32)]
                            for (b0, o0, nn_) in segs:
                                S.op('dve', lambda e: e.tensor_scalar(out=cc[:, o0:o0 + nn_], in0=buf[:, j, b0 + 2:b0 + 2 + nn_], scalar1=cw[:, fc, 2:3],
                                                                      scalar2=cb[:, fc:fc + 1], op0=ALU.mult, op1=ALU.add),
                                     [buf.b[j]] + cw.b + cb.b, cc.b)
                                S.op('dve', lambda e: e.scalar_tensor_tensor(out=cc[:, o0:o0 + nn_], in0=buf[:, j, b0 + 1:b0 + 1 + nn_], scalar=cw[:, fc, 1:2],
                                                                             in1=cc[:, o0:o0 + nn_], op0=ALU.mult, op1=ALU.add),
                                     [buf.b[j]] + cw.b + cc.b, cc.b)
                                S.op('dve', lambda e: e.scalar_tensor_tensor(out=cc[:, o0:o0 + nn_], in0=buf[:, j, b0:b0 + nn_], scalar=cw[:, fc, 0:1],
                                                                             in1=cc[:, o0:o0 + nn_], op0=ALU.mult, op1=ALU.add),
                                     [buf.b[j]] + cw.b + cc.b, cc.b)
                            sl2 = sl_[0]
                            S.op('act', lambda e: e.activation(out=sl2[:, 0:n], in_=cc[:, 0:n], func=AF.Silu), cc.b, sl2.b)
                            pu = mmbank()
                            for k in range(8):
                                S.op('pe', lambda e: e.matmul(out=pu[:, 0:n], lhsT=wu[:, k, jsl], rhs=HT[:, k, c0:c0 + n],
                                                              start=(k == 0), stop=(k == 7)), wu.b + hb_, pu.b, inc=(k == 7))
                            S.op('dve', lambda e: e.tensor_tensor(out=mmt[:, j, 0:n], in0=pu[:, 0:n], in1=sl2[:, 0:n], op=ALU.mult),
                                 pu.b + sl2.b, mmt.b)
                        for tt in tiles:
                            rows, sl = tsl(tt)
                            off = sl.start - c0
                            for half in range(2):
                                hsl = slice(half * 512, (half + 1) * 512)
                                po = mmbank()
                                for j in range(nfc):
                                    S.op('pe', lambda e: e.matmul(out=po[0:rows, :], lhsT=mmt[:, j, off:off + rows], rhs=wd[:, j, hsl],
                                                                  start=(j == 0), stop=(j == nfc - 1)), mmt.b + wd.b, po.b, inc=(j == nfc - 1))
                                S.op('dve', lambda e: e.tensor_tensor(out=X[0:rows, tt, hsl], in0=X[0:rows, tt, hsl], in1=po[0:rows, :],
                                                                      op=ALU.add), [X.b[tt]] + po.b, [X.b[tt]])
                S.dma('sp', o_conv[l].rearrange("s p n -> p s n"), OC[:].rearrange("p s a b -> p s (a b)"), OC.b, [])
                S.barrier()
                mm_list[0] = [0, 1]
                rot['mm'] = 0
            _ck('ffn%d' % l)

        _DEV['off'] = False
        _DEV['nops'] = 0
        _layers()
        _DEV['off'] = False
        mm_list[0] = [0, 1]
        for tt in range(NT):
            rows, sl = tsl(tt)
            S.dma('sp', y[sl, :], X[0:rows, tt, :], [X.b[tt]], [])
        S.barrier()
    return nc


_SLOPES = [2.0 ** (-8.0 * (h + 1) / 4) for h in range(4)]


def _consts():
    half = 16
    inv = (np.float32(10000.0) ** (-np.arange(half, dtype=np.float32) / np.float32(half))).astype(np.float32)
    pos = np.concatenate([np.arange(SP_), PAST + np.arange(SS), PAST + np.arange(SS)]).astype(np.float32)
    ang = (pos[:, None] * inv[None, :]).astype(np.float32)
    k_cs = np.concatenate([np.cos(ang), np.sin(ang)], axis=1).astype(np.float32)
    k = np.arange(128)[:, None]
    q = np.arange(128)[None, :]
    msb = (k < q).astype(np.float32)
    mch = ((k // 64) <= (q // 64)).astype(np.float32)
    mdf = np.zeros((128, 4, 128), np.float32)
    bdp = np.zeros((128, 4, 17), np.float32)
    bds = np.zeros((128, 4, 33), np.float32)
    for h in range(4):
        sl = _SLOPES[h]
        mdf[:, h, :] = mch * np.where(k > q, np.exp(-2.0 * sl * (k - q)), 1.0)
        for d in range(17):
            bdp[:, h, d] = sl * (np.arange(128) - 127 - 128 * d)
        for j in range(33):
            bds[:, h, j] = sl * (np.arange(128) - 31 - 128 * j)
    tri = (k >= q).astype(np.float32)
    return dict(k_cs=k_cs, k_msb=msb, k_mch=mch, k_mdf=mdf.reshape(128, 512), k_tri=tri,
                k_bdp=bdp.reshape(128, 68), k_bds=bds.reshape(128, 132))


_WNAMES = ["mix_norm_g", "w_in", "mla_q_norm_g", "mla_w_uq", "mla_kv_norm_g", "mla_w_uk", "mla_w_uv", "mla_qn_g", "mla_kn_g",
           "mla_qr_g", "mla_kr_g", "diff_qn_g", "diff_kn_g", "diff_lambda", "diff_subln_g", "w_br_mla", "w_br_sb", "w_br_diff",
           "w_out", "ffn_norm_g", "ffn_w_up", "ffn_conv_w", "ffn_conv_b", "ffn_w_down"]


def kernel(**inputs):
    inp = {k: np.asarray(v) for k, v in inputs.items()}
    nc = build_program()
    shared = {}
    for n in _WNAMES:
        a = np.ascontiguousarray(inp[n], dtype=np.float32)
        if n == "diff_lambda":
            a = a.reshape(L, 256)
        elif n == "ffn_conv_w":
            a = np.ascontiguousarray(a.reshape(L, 3, 22, 128).transpose(0, 3, 2, 1)).reshape(L, 128, 66)
        elif n == "ffn_conv_b":
            a = np.ascontiguousarray(a.reshape(L, 22, 128).transpose(0, 2, 1))
        shared[n] = a
    shared.update(_consts())
    in_maps = []
    for c in range(8):
        m = dict(shared)
        m["xin"] = np.ascontiguousarray(np.concatenate([inp["x_prompt"][c], inp["x_sample"][2 * c], inp["x_sample"][2 * c + 1]], axis=0),
                                        dtype=np.float32)
        sl = slice(2 * c, 2 * c + 2)
        m["c_ckv"] = np.ascontiguousarray(inp["cache_mla_ckv"][:, sl])
        m["c_kr"] = np.ascontiguousarray(inp["cache_mla_krope"][:, sl])
        m["c_sbk"] = np.ascontiguousarray(inp["cache_sb_k"][:, sl]).reshape(L, NS, PAST, 512)
        m["c_sbv"] = np.ascontiguousarray(inp["cache_sb_v"][:, sl]).reshape(L, NS, PAST, 512)
        m["c_dk"] = np.ascontiguousarray(inp["cache_diff_k"][:, sl]).reshape(L, NS, PAST, 512)
        m["c_dv"] = np.ascontiguousarray(inp["cache_diff_v"][:, sl]).reshape(L, NS, PAST, 512)
        st = np.asarray(inp["state_ffn_conv"][:, sl], dtype=np.float32)
        m["c_conv"] = np.ascontiguousarray(st.reshape(L, NS, 2, 22, 128).transpose(0, 1, 4, 3, 2)).reshape(L, NS, 128, 44)
        in_maps.append(m)
    res = run_bass_kernel_spmd(nc, in_maps, core_ids=list(range(8))).results

    def gather(name, width):
        p = np.stack([res[c][name][:, 0:SP_] for c in range(8)], axis=1)
        s = np.stack([res[c][name][:, SP_ + SS * j:SP_ + SS * (j + 1)] for c in range(8) for j in range(NS)], axis=1)
        return p, s

    y_p = np.stack([res[c]["y"][0:SP_] for c in range(8)], axis=0)
    y_s = np.stack([res[c]["y"][SP_ + SS * j:SP_ + SS * (j + 1)] for c in range(8) for j in range(NS)], axis=0)
    p_ckv, s_ckv = gather("o_ckv", 128)
    p_kr, s_kr = gather("o_kr", 32)
    p_sbk, s_sbk = gather("o_sbk", 512)
    p_sbv, s_sbv = gather("o_sbv", 512)
    p_dk, s_dk = gather("o_dk", 512)
    p_dv, s_dv = gather("o_dv", 512)

    def conv_of(c, idx):
        oc = res[c]["o_conv"][:, idx].reshape(L, 128, 22, 2)
        return np.ascontiguousarray(oc.transpose(0, 3, 2, 1)).reshape(L, 2, DFF)
    p_conv = np.stack([conv_of(c, 0) for c in range(8)], axis=1)
    s_conv = np.stack([conv_of(c, 1 + j) for c in range(8) for j in range(NS)], axis=1)
    f = np.float32
    return (y_p.astype(f), y_s.astype(f), p_ckv.astype(f), p_kr.astype(f),
            p_sbk.reshape(L, 8, SP_, 8, 64).astype(f), p_sbv.reshape(L, 8, SP_, 8, 64).astype(f),
            p_dk.reshape(L, 8, SP_, 4, 2, 64).astype(f), p_dv.reshape(L, 8, SP_, 4, 128).astype(f), p_conv.astype(f),
            s_ckv.astype(f), s_kr.astype(f), s_sbk.reshape(L, 16, SS, 8, 64).astype(f), s_sbv.reshape(L, 16, SS, 8, 64).astype(f),
            s_dk.reshape(L, 16, SS, 4, 2, 64).astype(f), s_dv.reshape(L, 16, SS, 4, 128).astype(f), s_conv.astype(f))
```

```python
import math
from contextlib import ExitStack
import numpy as np
import concourse.bass as bass
import concourse.mybir as mybir
from concourse.bass_utils import run_bass_kernel_spmd

F32 = mybir.dt.float32
BF16 = mybir.dt.bfloat16
AF = mybir.ActivationFunctionType
ALU = mybir.AluOpType
AX = mybir.AxisListType

L = 2
D = 1024
SP_ = 2048
NS = 2
SS = 32
PAST = 4096
NTOK = SP_ + NS * SS
NT = 18
DFF = 2816
NIN = 6560
EPS = 1e-6
MLA_SCALE = 96 ** -0.5
SB_SCALE = 64 ** -0.5
DIFF_SCALE = 64 ** -0.5
OFF = dict(cq=0, ckv=256, kr=384, sq=416, sk=928, sv=1440, dq=1952, dk=2464, dv=2976, g=3488)
ENG = ['pe', 'act', 'dve', 'pool', 'sp']
NDS = 20
_DEV = {'stop': None, 'off': False, 'maxops': None, 'nops': 0, 'log': None}


class _Stop(Exception):
    pass


def _ck(name):
    if _DEV['log'] is not None and not _DEV['off']:
        _DEV['log'].append((name, _DEV['nops']))
    if _DEV['stop'] == name:
        _DEV['off'] = True


class Buf:
    __slots__ = ('w', 'r', 'excl')

    def __init__(self):
        self.w = None
        self.r = {}
        self.excl = False


class TT:
    def __init__(self, h, n=1):
        self.h = h
        self.b = [Buf() for _ in range(n)]

    def __getitem__(self, k):
        return self.h[k]


class Sync:
    def __init__(self, nc, es):
        self.nc = nc
        self.e = dict(pe=nc.tensor, act=nc.scalar, dve=nc.vector, pool=nc.gpsimd, sp=nc.sync)
        self.sem = {k: es.enter_context(nc.semaphore("sem_" + k)) for k in ENG}
        self.cnt = {k: 0 for k in ENG}
        self.dsem = {q: [es.enter_context(nc.semaphore("ds_%s%d" % (q, i))) for i in range(NDS)] for q in ('sp', 'pool')}
        self.dcnt = {q: [0] * NDS for q in ('sp', 'pool')}
        self.dnext = {'sp': 0, 'pool': 0}
        self.known = {k: {} for k in ENG}
        self.pend = {k: False for k in ENG}
        self.hist = {}

    def _semof(self, k):
        return self.sem[k] if isinstance(k, str) else self.dsem[k[0]][k[1]]

    def _need(self, eng, toks):
        kn = self.known[eng]
        best = {}
        for (k, v) in toks:
            if best.get(k, 0) < v:
                best[k] = v
        need = []
        for k, v in sorted(best.items(), key=lambda kv: str(kv[0])):
            if kn.get(k, 0) < v:
                need.append((k, v))
        implied = {}
        for k, v in need:
            snap = self.hist.get((k, v))
            if snap:
                for k2, v2 in snap.items():
                    if implied.get(k2, 0) < v2:
                        implied[k2] = v2
        out = [(k, v) for (k, v) in need if implied.get(k, 0) < v]
        for k, v in out:
            kn[k] = v
            snap = self.hist.get((k, v))
            if snap:
                for k2, v2 in snap.items():
                    if k2 != eng and kn.get(k2, 0) < v2:
                        kn[k2] = v2
        return out

    def _wait(self, eng, toks, ins_fn=None):
        need = self._need(eng, toks)
        if ins_fn is None:
            for k, v in need:
                self.e[eng].wait_ge(self._semof(k), v)
            return None
        for k, v in need[:-1]:
            self.e[eng].wait_ge(self._semof(k), v)
        ins = ins_fn()
        if need:
            k, v = need[-1]
            ins._wait_ge(self._semof(k), v)
        return ins

    def _deps(self, eng, reads, writes):
        toks = set()
        for b in reads:
            if b.w is not None:
                toks.add(b.w)
            if b.excl:
                for kv in b.r.items():
                    if kv[0] != eng:
                        toks.add(kv)
        for b in writes:
            if b.w is not None:
                toks.add(b.w)
            for kv in b.r.items():
                toks.add(kv)
        if eng == 'pe':
            toks = {t for t in toks if t[0] != 'pe'}
        return toks

    def op(self, eng, fn, reads=(), writes=(), inc=True):
        if _DEV['off']:
            return
        _DEV['nops'] += 1
        if _DEV['maxops'] is not None and _DEV['nops'] > _DEV['maxops'] and not self.pend[eng]:
            _DEV['off'] = True
            return
        ins = self._wait(eng, self._deps(eng, reads, writes), lambda: fn(self.e[eng]))
        c = self.cnt[eng] + 1
        if inc:
            ins.then_inc(self.sem[eng], 1)
            self.cnt[eng] = c
            self.pend[eng] = False
            self.hist[(eng, c)] = dict(self.known[eng])
        else:
            self.pend[eng] = True
        for b in reads:
            b.r[eng] = c
        for b in writes:
            b.w = (eng, c)
            b.r = {}

    def dma(self, q, out, in_, reads=(), writes=()):
        if _DEV['off']:
            return
        toks = self._deps(q, reads, writes)
        i = self.dnext[q]
        self.dnext[q] = (i + 1) % NDS
        key = (q, i)
        if self.dcnt[q][i] > 0:
            toks.add((key, self.dcnt[q][i]))
        ins = self._wait(q, toks, lambda: self.e[q].dma_start(out=out, in_=in_))
        ins.then_inc(self.dsem[q][i], 16)
        self.dcnt[q][i] += 16
        v = self.dcnt[q][i]
        self.hist[(key, v)] = dict(self.known[q])
        for b in reads:
            b.r[key] = v
        for b in writes:
            b.w = (key, v)
            b.r = {}

    def all_tokens(self):
        toks = {(k, self.cnt[k]) for k in ENG if self.cnt[k] > 0}
        for q in ('sp', 'pool'):
            for i, c in enumerate(self.dcnt[q]):
                if c > 0:
                    toks.add(((q, i), c))
        return toks

    def barrier(self):
        if _DEV['off']:
            return
        for k in ENG:
            assert not self.pend[k]
        toks = self.all_tokens()
        for eng in ENG:
            self._wait(eng, {t for t in toks if t[0] != eng})


def build_program():
    nc = bass.Bass("TRN2", target_bir_lowering=False)

    def din(name, shape):
        return nc.dram_tensor(name, list(shape), F32, kind="ExternalInput").ap()

    def dout(name, shape):
        return nc.dram_tensor(name, list(shape), F32, kind="ExternalOutput").ap()

    xin = din("xin", [NTOK, D])
    c_ckv = din("c_ckv", [L, NS, PAST, 128])
    c_kr = din("c_kr", [L, NS, PAST, 32])
    c_sbk = din("c_sbk", [L, NS, PAST, 512])
    c_sbv = din("c_sbv", [L, NS, PAST, 512])
    c_dk = din("c_dk", [L, NS, PAST, 512])
    c_dv = din("c_dv", [L, NS, PAST, 512])
    c_conv = din("c_conv", [L, NS, 128, 22 * 2])
    W = {}
    for name, shape in [("mix_norm_g", [L, D]), ("w_in", [L, D, NIN]), ("mla_q_norm_g", [L, 256]),
                        ("mla_w_uq", [L, 256, 768]), ("mla_kv_norm_g", [L, 128]), ("mla_w_uk", [L, 128, 512]),
                        ("mla_w_uv", [L, 128, 512]), ("mla_qn_g", [L, 64]), ("mla_kn_g", [L, 64]),
                        ("mla_qr_g", [L, 32]), ("mla_kr_g", [L, 32]), ("diff_qn_g", [L, 64]),
                        ("diff_kn_g", [L, 64]), ("diff_lambda", [L, 256]), ("diff_subln_g", [L, 128]),
                        ("w_br_mla", [L, 512, D]), ("w_br_sb", [L, 512, D]), ("w_br_diff", [L, 512, D]),
                        ("w_out", [L, D, D]), ("ffn_norm_g", [L, D]), ("ffn_w_up", [L, D, 2 * DFF]),
                        ("ffn_conv_w", [L, 128, 22 * 3]), ("ffn_conv_b", [L, 128, 22]), ("ffn_w_down", [L, DFF, D])]:
        W[name] = din(name, shape)
    k_cs = din("k_cs", [NTOK, 32])
    k_msb = din("k_msb", [128, 128])
    k_mch = din("k_mch", [128, 128])
    k_mdf = din("k_mdf", [128, 4 * 128])
    k_tri = din("k_tri", [128, 128])
    k_bdp = din("k_bdp", [128, 4 * 17])
    k_bds = din("k_bds", [128, 4 * 33])

    y = dout("y", [NTOK, D])
    o_ckv = dout("o_ckv", [L, NTOK, 128])
    o_kr = dout("o_kr", [L, NTOK, 32])
    o_sbk = dout("o_sbk", [L, NTOK, 512])
    o_sbv = dout("o_sbv", [L, NTOK, 512])
    o_dk = dout("o_dk", [L, NTOK, 512])
    o_dv = dout("o_dv", [L, NTOK, 512])
    o_conv = dout("o_conv", [L, 3, 128, 22 * 2])

    with ExitStack() as es:
        E = es.enter_context
        S = Sync(nc, es)
        cnt = [0]

        def sb(es_, shape, dt, n=1):
            cnt[0] += 1
            return TT(es_.enter_context(nc.sbuf_tensor("t%d" % cnt[0], list(shape), dt)), n)

        X = sb(es, [128, NT, D], F32, NT)
        HT = sb(es, [128, 8, NTOK], BF16, NT)
        OT = sb(es, [128, 4, NTOK], BF16, 5)
        ident = sb(es, [128, 128], BF16)
        ones = sb(es, [128, 128], BF16)
        zeros = sb(es, [128, 128], BF16)
        onesf = sb(es, [128, 128], F32)
        tri = sb(es, [128, 128], BF16)
        msb = sb(es, [128, 128], F32)
        mch = sb(es, [128, 128], F32)
        mdf = sb(es, [128, 4, 128], F32)
        bdp = sb(es, [128, 4, 17], F32)
        bds = sb(es, [128, 4, 33], F32)
        cs = sb(es, [128, NT, 32], F32)
        gbig = sb(es, [128, D], F32)
        gsm = sb(es, [128, 1024], F32)
        gcol = sb(es, [128, 8], F32)
        PS = [TT(E(nc.psum_tensor("ps%d" % i, [128, 512], F32))) for i in range(8)]
        for p_ in PS:
            p_.b[0].excl = True
        rot = {'mm': 0, 'tp': 0}

        def mmbank():
            rot['mm'] = (rot['mm'] + 1) % len(mm_list[0])
            return PS[mm_list[0][rot['mm']]]

        def tpbank():
            rot['tp'] ^= 1
            return PS[2 + rot['tp']]

        def tsl(tt):
            if tt < 16:
                return 128, slice(tt * 128, tt * 128 + 128)
            return 32, slice(2048 + 32 * (tt - 16), 2048 + 32 * (tt - 16) + 32)

        GROUPS = [(g * 512, 512, [4 * g + i for i in range(4)]) for g in range(4)] + [(2048, 64, [16, 17])]
        mm_list = [[0, 1]]

        S.op('pool', lambda e: e.memset(ident[:], 0.0), [], ident.b)
        S.op('pool', lambda e: e.affine_select(out=ident[:], in_=ident[:], pattern=[[-1, 128]], compare_op=ALU.not_equal,
                                               fill=1.0, base=0, channel_multiplier=1), ident.b, ident.b)
        S.op('pool', lambda e: e.memset(ones[:], 1.0), [], ones.b)
        S.op('pool', lambda e: e.memset(zeros[:], 0.0), [], zeros.b)
        S.op('pool', lambda e: e.memset(onesf[:], 1.0), [], onesf.b)
        S.dma('pool', tri[:], k_tri, [], tri.b)
        S.dma('sp', msb[:], k_msb, [], msb.b)
        S.dma('sp', mch[:], k_mch, [], mch.b)
        S.dma('sp', mdf[:].rearrange("p a b -> p (a b)"), k_mdf, [], mdf.b)
        S.dma('sp', bdp[:].rearrange("p a b -> p (a b)"), k_bdp, [], bdp.b)
        S.dma('sp', bds[:].rearrange("p a b -> p (a b)"), k_bds, [], bds.b)
        S.dma('sp', cs[:, 0:16, :], k_cs[0:2048, :].rearrange("(t p) n -> p t n", p=128), [], cs.b)
        S.dma('sp', cs[0:32, 16, :], k_cs[2048:2080, :], [], cs.b)
        S.dma('sp', cs[0:32, 17, :], k_cs[2080:2112, :], [], cs.b)
        zrhs = sb(es, [128, 512], BF16)
        S.op('pool', lambda e: e.memset(zrhs[:], 0.0), [], zrhs.b)

        GS = dict(q_norm=(0, 256), kv_norm=(256, 128), kr=(384, 32), qn=(416, 64), kn=(480, 64), qr=(544, 32),
                  dqn=(576, 64), dkn=(640, 64), lam=(704, 256))

        def gs(name, rows=128):
            o, n = GS[name]
            return gsm[0:rows, o:o + n]

        def rstd_from_ss(ss_t, rows, G, d):
            S.op('act', lambda e: e.activation(out=ss_t[0:rows, 0:G], in_=ss_t[0:rows, 0:G], func=AF.Ln, scale=1.0 / d, bias=EPS),
                 ss_t.b, ss_t.b)
            S.op('act', lambda e: e.activation(out=ss_t[0:rows, 0:G], in_=ss_t[0:rows, 0:G], func=AF.Exp, scale=-0.5),
                 ss_t.b, ss_t.b)

        def gnorm(ws, src, srcb, rows, G, d, gain, out, outb, post_scale=1.0):
            sq, ss, tmp = ws['sq'], ws['ss'], ws['tmp']
            sqv = sq[0:rows, 0:G * d].rearrange("p (g d) -> p g d", d=d)
            S.op('act', lambda e: e.activation(out=sqv, in_=src, func=AF.Square), srcb, sq.b)
            S.op('dve', lambda e: e.tensor_reduce(out=ss[0:rows, 0:G], in_=sqv, axis=AX.X, op=ALU.add), sq.b, ss.b)
            rstd_from_ss(ss, rows, G, d)
            tv = tmp[0:rows, 0:G * d].rearrange("p (g d) -> p g d", d=d)
            S.op('dve', lambda e: e.tensor_tensor(out=tv, in0=src, in1=ss[0:rows, 0:G].unsqueeze(2).to_broadcast([rows, G, d]),
                                                  op=ALU.mult), list(srcb) + ss.b, tmp.b)
            gb = gain.unsqueeze(1).to_broadcast([rows, G, d])
            if post_scale == 1.0:
                S.op('dve', lambda e: e.tensor_tensor(out=out, in0=tv, in1=gb, op=ALU.mult), tmp.b + gsm.b, outb)
            else:
                S.op('dve', lambda e: e.scalar_tensor_tensor(out=out, in0=tv, scalar=float(post_scale), in1=gb, op0=ALU.mult,
                                                             op1=ALU.mult), tmp.b + gsm.b, outb)

        def rope(ws, src, srcb, rows, H, tt, out, outb):
            t1, t2 = ws['r1'], ws['r2']
            cosb = cs[0:rows, tt, 0:16].unsqueeze(1).to_broadcast([rows, H, 16])
            sinb = cs[0:rows, tt, 16:32].unsqueeze(1).to_broadcast([rows, H, 16])
            a1 = t1[0:rows, 0:H * 16].rearrange("p (h d) -> p h d", d=16)
            a2 = t2[0:rows, 0:H * 16].rearrange("p (h d) -> p h d", d=16)
            x1 = src[:, :, 0:16]
            x2 = src[:, :, 16:32]
            S.op('dve', lambda e: e.tensor_tensor(out=a1, in0=x1, in1=cosb, op=ALU.mult), list(srcb) + cs.b, t1.b)
            S.op('dve', lambda e: e.tensor_tensor(out=a2, in0=x2, in1=sinb, op=ALU.mult), list(srcb) + cs.b, t2.b)
            S.op('dve', lambda e: e.tensor_tensor(out=out[:, :, 0:16], in0=a1, in1=a2, op=ALU.subtract), t1.b + t2.b, outb)
            S.op('dve', lambda e: e.tensor_tensor(out=a1, in0=x1, in1=sinb, op=ALU.mult), list(srcb) + cs.b, t1.b)
            S.op('dve', lambda e: e.tensor_tensor(out=a2, in0=x2, in1=cosb, op=ALU.mult), list(srcb) + cs.b, t2.b)
            S.op('dve', lambda e: e.tensor_tensor(out=out[:, :, 16:32], in0=a1, in1=a2, op=ALU.add), t1.b + t2.b, outb)

        def transpose_to(src, srcb, rows, ncols, dst, dstb, eng='act'):
            pb = tpbank()
            pv = pb.h[:, :].bitcast(BF16)
            S.op('pe', lambda e: e.transpose(out=pv[0:ncols, 0:rows], in_=src, identity=ident[0:rows, 0:rows]), list(srcb) + ident.b, pb.b)
            if eng == 'act':
                S.op('act', lambda e: e.copy(out=dst, in_=pv[0:ncols, 0:rows]), pb.b, dstb)
            else:
                S.op('dve', lambda e: e.tensor_copy(out=dst, in_=pv[0:ncols, 0:rows]), pb.b, dstb)

        def load_w(dst, src_ap):
            S.dma('pool', dst.h[:], src_ap, [], dst.b)

        def norm_phase(l, gname, first):
            with ExitStack() as ps:
                sq = sb(ps, [128, D], F32)
                ssr = [sb(ps, [128, 1], F32) for _ in range(2)]
                hb = [sb(ps, [128, D], BF16) for _ in range(2)]
                S.dma('sp', gbig[:], W[gname][l].partition_broadcast(128), [], gbig.b)
                for tt in range(NT):
                    rows, sl = tsl(tt)
                    if first:
                        S.dma('sp', X[0:rows, tt, :], xin[sl, :], [], [X.b[tt]])
                    ss = ssr[tt % 2]
                    h = hb[tt % 2]
                    S.op('act', lambda e: e.activation(out=sq[0:rows, :], in_=X[0:rows, tt, :], func=AF.Square,
                                                       accum_out=ss[0:rows, :]), [X.b[tt]], sq.b + ss.b)
                    rstd_from_ss(ss, rows, 1, D)
                    S.op('dve', lambda e: e.scalar_tensor_tensor(out=h[0:rows, :], in0=X[0:rows, tt, :], scalar=ss[0:rows, 0:1],
                                                                 in1=gbig[0:rows, :], op0=ALU.mult, op1=ALU.mult),
                         [X.b[tt]] + ss.b + gbig.b, h.b)
                    pb = tpbank()
                    pv = pb.h[:, :].bitcast(BF16).rearrange("p (k n) -> p k n", n=128)
                    for k in range(8):
                        S.op('pe', lambda e: e.transpose(out=pv[:, k, 0:rows], in_=h[0:rows, k * 128:(k + 1) * 128],
                                                         identity=ident[0:rows, 0:rows]), h.b + ident.b, pb.b, inc=(k == 7))
                    S.op('act' if tt % 2 else 'dve',
                         (lambda e: e.copy(out=HT[:, :, sl], in_=pv[:, :, 0:rows])) if tt % 2 else
                         (lambda e: e.tensor_copy(out=HT[:, :, sl], in_=pv[:, :, 0:rows])), pb.b, [HT.b[tt]])
                S.barrier()

        def _layers():
          for l in range(L):
            lam_init = 0.8 - 0.6 * math.exp(-0.3 * l)
            for nm, wn in [('q_norm', 'mla_q_norm_g'), ('kv_norm', 'mla_kv_norm_g'), ('kr', 'mla_kr_g'), ('qn', 'mla_qn_g'),
                           ('kn', 'mla_kn_g'), ('qr', 'mla_qr_g'), ('dqn', 'diff_qn_g'), ('dkn', 'diff_kn_g'),
                           ('lam', 'diff_lambda')]:
                S.dma('sp', gs(nm), W[wn][l].partition_broadcast(128), [], gsm.b)
            S.dma('sp', gcol[:, 0:1], W['diff_subln_g'][l].rearrange("(p o) -> p o", o=1), [], gcol.b)
            with ExitStack() as ps:
                lt = sb(ps, [128, 128], F32)
                l2 = sb(ps, [128, 2], F32)
                lv = gs('lam')
                S.op('dve', lambda e: e.tensor_tensor(out=lt[:, 0:64], in0=lv[:, 0:64], in1=lv[:, 64:128], op=ALU.mult), gsm.b, lt.b)
                S.op('dve', lambda e: e.tensor_tensor(out=lt[:, 64:128], in0=lv[:, 128:192], in1=lv[:, 192:256], op=ALU.mult), gsm.b, lt.b)
                S.op('dve', lambda e: e.tensor_reduce(out=l2[:, 0:2], in_=lt[:, :].rearrange("p (a b) -> p a b", b=64), axis=AX.X,
                                                      op=ALU.add), lt.b, l2.b)
                S.op('act', lambda e: e.activation(out=l2[:, 0:2], in_=l2[:, 0:2], func=AF.Exp), l2.b, l2.b)
                S.op('dve', lambda e: e.tensor_tensor(out=gcol[:, 1:2], in0=l2[:, 0:1], in1=l2[:, 1:2], op=ALU.subtract), l2.b, gcol.b)
                S.op('dve', lambda e: e.tensor_scalar(out=gcol[:, 1:2], in0=gcol[:, 1:2], scalar1=float(lam_init), scalar2=None,
                                                      op0=ALU.add), gcol.b, gcol.b)
                S.op('dve', lambda e: e.tensor_scalar(out=gcol[:, 2:3], in0=gcol[:, 1:2], scalar1=-1.0, scalar2=None,
                                                      op0=ALU.mult), gcol.b, gcol.b)
                S.op('dve', lambda e: e.tensor_scalar(out=gcol[:, 0:1], in0=gcol[:, 0:1], scalar1=float(1.0 - lam_init), scalar2=None,
                                                      op0=ALU.mult), gcol.b, gcol.b)
                S.barrier()

            norm_phase(l, 'mix_norm_g', l == 0)
            _ck('norm%d' % l)

            for mixer in ('mla', 'sb', 'diff'):
                with ExitStack() as ms:
                    if mixer == 'mla':
                        CQT = sb(ms, [128, 2, NTOK], BF16, NT)
                        CKVT = sb(ms, [128, NTOK], BF16, NT)
                        KRT = sb(ms, [128, NT, 32], BF16, NT)
                        nunits, hpu = 4, 2
                    elif mixer == 'sb':
                        nunits, hpu = 2, 4
                    else:
                        nunits, hpu = 2, 2
                    uso = ExitStack()
                    ucache = []
                    for u in range(nunits):
                        with ExitStack() as us:
                            upos = [0]

                            def sbu(es_, shape, dt, n=1):
                                if u == 0:
                                    t_ = sb(uso, shape, dt, n)
                                    ucache.append(t_)
                                    return t_
                                t_ = ucache[upos[0]]
                                upos[0] += 1
                                return t_
                            ws = dict(ss=sbu(us, [128, 8], F32))
                            if mixer != 'sb':
                                ws['sq'] = sbu(us, [128, 512], F32)
                                ws['tmp'] = sbu(us, [128, 512], F32)
                            if mixer == 'mla':
                                ws['r1'] = sbu(us, [128, 128], F32)
                                ws['r2'] = sbu(us, [128, 128], F32)
                            stage = [sbu(us, [128, 256], F32) for _ in range(3 if mixer != 'sb' else 2)]
                            stg_i = [0]

                            def nstage():
                                stg_i[0] = (stg_i[0] + 1) % len(stage)
                                return stage[stg_i[0]]
                            tokb = [sbu(us, [128, 256], BF16) for _ in range(3)]
                            tok_i = [0]

                            def ntok():
                                tok_i[0] = (tok_i[0] + 1) % 3
                                return tokb[tok_i[0]]
                            PT = [sbu(us, [128, 512], BF16) for _ in range(4 if mixer == 'mla' else 3)]
                            pt_i = [0]

                            def npt():
                                pt_i[0] = (pt_i[0] + 1) % len(PT)
                                return PT[pt_i[0]]
                            rden = sbu(us, [128, 512], F32) if mixer != 'sb' else None
                            if mixer == 'mla':
                                mm_list[0] = [0, 1, 6, 7]
                                KT = sbu(us, [96, 2, NTOK], BF16, NT)
                                V = sbu(us, [128, NT, 128], BF16, NT)
                                QT = sbu(us, [96, 2, 512], BF16)
                                w1 = sbu(us, [128, 8, 416], BF16)
                                if u == 0:
                                    load_w(w1, W['w_in'][l, :, 0:416].rearrange("(k p) n -> p k n", p=128))
                                wuq = sbu(us, [128, 2, 192], BF16)
                                load_w(wuq, W['mla_w_uq'][l, :, u * 192:(u + 1) * 192].rearrange("(k p) n -> p k n", p=128))
                                wuk = sbu(us, [128, 128], BF16)
                                load_w(wuk, W['mla_w_uk'][l, :, u * 128:(u + 1) * 128])
                                wuv = sbu(us, [128, 128], BF16)
                                load_w(wuv, W['mla_w_uv'][l, :, u * 128:(u + 1) * 128])
                                kcat = [sbu(us, [128, 2, 96], BF16) for _ in range(2)]
                                qcat = [sbu(us, [128, 2, 96], BF16) for _ in range(2)]
                                CS = [dict(pck=sbu(us, [128, 4, 128], BF16), pkr=sbu(us, [128, 4, 32], BF16), pckT=sbu(us, [128, 512], BF16),
                                           KTp=sbu(us, [96, 2, 512], BF16), Vp=sbu(us, [128, 4, 128], BF16),
                                           kc4=sbu(us, [128, 4, 2, 96], BF16)) for _ in range(2)]
                            else:
                                c0q = OFF['sq' if mixer == 'sb' else 'dq'] + u * 256
                                c0k = OFF['sk' if mixer == 'sb' else 'dk'] + u * 256
                                c0v = OFF['sv' if mixer == 'sb' else 'dv'] + u * 256
                                wq = sbu(us, [128, 8, 256], BF16)
                                wk = sbu(us, [128, 8, 256], BF16)
                                wv = sbu(us, [128, 8, 256], BF16)
                                load_w(wq, W['w_in'][l, :, c0q:c0q + 256].rearrange("(k p) n -> p k n", p=128))
                                load_w(wk, W['w_in'][l, :, c0k:c0k + 256].rearrange("(k p) n -> p k n", p=128))
                                load_w(wv, W['w_in'][l, :, c0v:c0v + 256].rearrange("(k p) n -> p k n", p=128))
                                KT = sbu(us, [128, 2, NTOK], BF16, NT)
                                V = sbu(us, [128, NT, 256], BF16, NT)
                                QT = sbu(us, [128, 2, 512], BF16)
                                CS = [dict(pk=sbu(us, [128, 4, 256], BF16), KTp=sbu(us, [128, 2, 512], BF16), Vp=sbu(us, [128, 4, 256], BF16))
                                      for _ in range(2)]
                                if mixer == 'sb':
                                    R = sbu(us, [128, 2, 512], F32)
                                    e1 = [sbu(us, [128, 512], F32) for _ in range(3)]
                                    spb = [sbu(us, [128, 512], BF16) for _ in range(3)]
                                    cumr = [sbu(us, [128, 512], F32) for _ in range(2)]
                                else:
                                    t0 = sbu(us, [128, 512], F32)
                                    t1 = sbu(us, [128, 512], F32)
                                    rden1 = sbu(us, [128, 512], F32)
                                    osq = sbu(us, [128, 512], F32)
                            o_k = o_sbk if mixer == 'sb' else o_dk
                            o_v = o_sbv if mixer == 'sb' else o_dv

                            def proj_tile(tt, qcol0):
                                rows, sl = tsl(tt)
                                if mixer == 'mla':
                                    if u == 0:
                                        pz = mmbank()
                                        for k in range(8):
                                            S.op('pe', lambda e: e.matmul(out=pz[0:rows, 0:416], lhsT=HT[:, k, sl], rhs=w1[:, k, :],
                                                                          start=(k == 0), stop=(k == 7)), [HT.b[tt]] + w1.b, pz.b, inc=(k == 7))
                                        tk = ntok()
                                        gnorm(ws, pz[0:rows, 0:256].rearrange("p (g d) -> p g d", g=1), pz.b, rows, 1, 256, gs('q_norm', rows),
                                              tk[0:rows, 0:256].rearrange("p (g d) -> p g d", g=1), tk.b)
                                        for c in range(2):
                                            transpose_to(tk[0:rows, c * 128:(c + 1) * 128], tk.b, rows, 128, CQT[:, c, sl], [CQT.b[tt]],
                                                         'act' if c else 'dve')
                                        st = nstage()
                                        gnorm(ws, pz[0:rows, 256:384].rearrange("p (g d) -> p g d", g=1), pz.b, rows, 1, 128, gs('kv_norm', rows),
                                              st[0:rows, 0:128].rearrange("p (g d) -> p g d", g=1), st.b)
                                        S.dma('sp', o_ckv[l, sl, :], st[0:rows, 0:128], st.b, [])
                                        tk2 = ntok()
                                        S.op('act', lambda e: e.copy(out=tk2[0:rows, 0:128], in_=st[0:rows, 0:128]), st.b, tk2.b)
                                        transpose_to(tk2[0:rows, 0:128], tk2.b, rows, 128, CKVT[:, sl], [CKVT.b[tt]], 'dve')
                                        gnorm(ws, pz[0:rows, 384:416].rearrange("p (g d) -> p g d", g=1), pz.b, rows, 1, 32, gs('kr', rows),
                                              st[0:rows, 128:160].rearrange("p (g d) -> p g d", g=1), st.b)
                                        rope(ws, st[0:rows, 128:160].rearrange("p (g d) -> p g d", g=1), st.b, rows, 1, tt,
                                             st[0:rows, 160:192].rearrange("p (g d) -> p g d", g=1), st.b)
                                        S.dma('sp', o_kr[l, sl, :], st[0:rows, 160:192], st.b, [])
                                        S.op('act', lambda e: e.copy(out=KRT[0:rows, tt, :], in_=st[0:rows, 160:192]), st.b, [KRT.b[tt]])
                                    pq = mmbank()
                                    for c in range(2):
                                        S.op('pe', lambda e: e.matmul(out=pq[0:rows, 0:192], lhsT=CQT[:, c, sl], rhs=wuq[:, c, :],
                                                                      start=(c == 0), stop=(c == 1)), [CQT.b[tt]] + wuq.b, pq.b, inc=(c == 1))
                                    qv = pq[0:rows, 0:192].rearrange("p (h d) -> p h d", d=96)
                                    qc = qcat[tt % 2]
                                    gnorm(ws, qv[:, :, 0:64], pq.b, rows, 2, 64, gs('qn', rows), qc[0:rows, :, 0:64], qc.b, MLA_SCALE)
                                    st = nstage()
                                    stv = st[0:rows, 0:64].rearrange("p (h d) -> p h d", d=32)
                                    gnorm(ws, qv[:, :, 64:96], pq.b, rows, 2, 32, gs('qr', rows), stv, st.b, MLA_SCALE)
                                    rope(ws, stv, st.b, rows, 2, tt, qc[0:rows, :, 64:96], qc.b)
                                    for hh in range(2):
                                        transpose_to(qc[0:rows, hh, :], qc.b, rows, 96, QT[:, hh, qcol0:qcol0 + rows], QT.b, 'act' if hh else 'dve')
                                    kv_from_latent(CKVT[:, sl], [CKVT.b[tt]], KRT[0:rows, tt, :], [KRT.b[tt]], rows,
                                                   lambda hh: KT[:, hh, sl], [KT.b[tt]], V[0:rows, tt, :], [V.b[tt]], kcat[tt % 2])
                                else:
                                    pq = mmbank()
                                    for k in range(8):
                                        S.op('pe', lambda e: e.matmul(out=pq[0:rows, 0:256], lhsT=HT[:, k, sl], rhs=wq[:, k, :],
                                                                      start=(k == 0), stop=(k == 7)), [HT.b[tt]] + wq.b, pq.b, inc=(k == 7))
                                    tk = ntok()
                                    if mixer == 'sb':
                                        S.op('act', lambda e: e.activation(out=tk[0:rows, :], in_=pq[0:rows, 0:256], func=AF.Copy,
                                                                           scale=SB_SCALE), pq.b, tk.b)
                                    else:
                                        gnorm(ws, pq[0:rows, 0:256].rearrange("p (g d) -> p g d", d=64), pq.b, rows, 4, 64, gs('dqn', rows),
                                              tk[0:rows, :].rearrange("p (g d) -> p g d", d=64), tk.b, DIFF_SCALE)
                                    for c in range(2):
                                        transpose_to(tk[0:rows, c * 128:(c + 1) * 128], tk.b, rows, 128, QT[:, c, qcol0:qcol0 + rows], QT.b,
                                                     'act' if c else 'dve')
                                    pk_ = mmbank()
                                    for k in range(8):
                                        S.op('pe', lambda e: e.matmul(out=pk_[0:rows, 0:256], lhsT=HT[:, k, sl], rhs=wk[:, k, :],
                                                                      start=(k == 0), stop=(k == 7)), [HT.b[tt]] + wk.b, pk_.b, inc=(k == 7))
                                    st = nstage()
                                    if mixer == 'sb':
                                        S.op('act', lambda e: e.copy(out=st[0:rows, :], in_=pk_[0:rows, 0:256]), pk_.b, st.b)
                                    else:
                                        gnorm(ws, pk_[0:rows, 0:256].rearrange("p (g d) -> p g d", d=64), pk_.b, rows, 4, 64, gs('dkn', rows),
                                              st[0:rows, :].rearrange("p (g d) -> p g d", d=64), st.b)
                                    S.dma('sp', o_k[l, sl, u * 256:(u + 1) * 256], st[0:rows, :], st.b, [])
                                    tk = ntok()
                                    S.op('dve', lambda e: e.tensor_copy(out=tk[0:rows, :], in_=st[0:rows, :]), st.b, tk.b)
                                    for c in range(2):
                                        transpose_to(tk[0:rows, c * 128:(c + 1) * 128], tk.b, rows, 128, KT[:, c, sl], [KT.b[tt]],
                                                     'act' if c else 'dve')
                                    pv_ = mmbank()
                                    for k in range(8):
                                        S.op('pe', lambda e: e.matmul(out=pv_[0:rows, 0:256], lhsT=HT[:, k, sl], rhs=wv[:, k, :],
                                                                      start=(k == 0), stop=(k == 7)), [HT.b[tt]] + wv.b, pv_.b, inc=(k == 7))
                                    st = nstage()
                                    S.op('act', lambda e: e.copy(out=st[0:rows, :], in_=pv_[0:rows, 0:256]), pv_.b, st.b)
                                    S.dma('sp', o_v[l, sl, u * 256:(u + 1) * 256], st[0:rows, :], st.b, [])
                                    S.op('dve', lambda e: e.tensor_copy(out=V[0:rows, tt, :], in_=pv_[0:rows, 0:256]), pv_.b, [V.b[tt]])

                            def kv_from_latent(ckvT_ap, ckvT_b, kr_ap, kr_b, rows, kt_dst, kt_b, v_dst, v_b, kc):
                                pk_ = mmbank()
                                S.op('pe', lambda e: e.matmul(out=pk_[0:rows, 0:128], lhsT=ckvT_ap, rhs=wuk[:, :], start=True, stop=True),
                                     list(ckvT_b) + wuk.b, pk_.b)
                                gnorm(ws, pk_[0:rows, 0:128].rearrange("p (h d) -> p h d", d=64), pk_.b, rows, 2, 64, gs('kn', rows),
                                      kc[0:rows, :, 0:64], kc.b)
                                S.op('dve', lambda e: e.tensor_copy(out=kc[0:rows, :, 64:96], in_=kr_ap.unsqueeze(1).to_broadcast([rows, 2, 32])),
                                     list(kr_b), kc.b)
                                for hh in range(2):
                                    transpose_to(kc[0:rows, hh, :], kc.b, rows, 96, kt_dst(hh), kt_b, 'act' if hh else 'dve')
                                pv_ = mmbank()
                                S.op('pe', lambda e: e.matmul(out=pv_[0:rows, 0:128], lhsT=ckvT_ap, rhs=wuv[:, :], start=True, stop=True),
                                     list(ckvT_b) + wuv.b, pv_.b)
                                S.op('act', lambda e: e.copy(out=v_dst, in_=pv_[0:rows, 0:128]), pv_.b, v_b)

                            acc, accd, acc1, accd1 = PS[4], PS[5], PS[6], PS[7]

                            def zero_acc(t_, ncols):
                                S.op('pe', lambda e: e.matmul(out=t_[:, 0:ncols], lhsT=zeros[:, :], rhs=zrhs[:, 0:ncols], start=True, stop=True),
                                     zeros.b + zrhs.b, t_.b)

                            acc1s = [(PS[6], PS[7]), (PS[2], PS[3])]
                            itc = [0]

                            def att_begin(ncols):
                                if mixer == 'mla':
                                    zero_acc(acc, ncols)
                                    zero_acc(accd, ncols)
                                elif mixer == 'sb':
                                    zero_acc(acc, ncols)
                                    S.op('dve', lambda e: e.memset(R[:, :, 0:ncols], 0.0), [], R.b)
                                else:
                                    for t_ in (acc, accd, acc1, accd1):
                                        zero_acc(t_, ncols)

                            def diag_mask(t_, mask_ap, mask_b, nk, c0, ncols):
                                dn = min(128, ncols - c0)
                                S.op('dve', lambda e: e.tensor_tensor(out=t_[0:nk, c0:c0 + dn], in0=t_[0:nk, c0:c0 + dn], in1=mask_ap(nk, dn),
                                                                      op=ALU.mult), t_.b + mask_b, t_.b)

                            def stage_a(hg, kb, x, qc0, ncols, sample):
                                nk, c0 = kb['nk'], kb['c0']
                                st = mmbank()
                                if mixer == 'mla':
                                    S.op('pe', lambda e: e.matmul(out=st[0:nk, c0:ncols], lhsT=kb['kt'](x), rhs=QT[:, x, qc0 + c0:qc0 + ncols],
                                                                  start=True, stop=True), kb['ktb'] + QT.b, st.b)
                                    pt = npt()
                                    S.op('act', lambda e: e.activation(out=pt[0:nk, c0:ncols], in_=st[0:nk, c0:ncols], func=AF.Exp), st.b, pt.b)
                                    if kb['diag']:
                                        diag_mask(pt, lambda a_, b_: mch[0:a_, 0:b_], mch.b, nk, c0, ncols)
                                    return dict(pt=pt)
                                if mixer == 'sb':
                                    hp = hg
                                    S.op('pe', lambda e: e.matmul(out=st[0:nk, c0:ncols], lhsT=kb['kt'](hp)[64 * x:64 * x + 64, :],
                                                                  rhs=QT[64 * x:64 * x + 64, hp, qc0 + c0:qc0 + ncols], start=True, stop=True),
                                         kb['ktb'] + QT.b, st.b)
                                    itc[0] += 1
                                    ee = e1[itc[0] % 3]
                                    sp_ = spb[itc[0] % 3]
                                    a1, ad1 = acc1s[itc[0] % 2]
                                    S.op('act', lambda e: e.activation(out=ee[0:nk, c0:ncols], in_=st[0:nk, c0:ncols], func=AF.Exp), st.b, ee.b)
                                    S.op('act', lambda e: e.activation(out=sp_[0:nk, c0:ncols], in_=ee[0:nk, c0:ncols], func=AF.Ln, bias=1.0),
                                         ee.b, sp_.b)
                                    if kb['diag']:
                                        diag_mask(sp_, lambda a_, b_: msb[0:a_, 0:b_], msb.b, nk, c0, ncols)
                                    return dict(ee=ee, sp=sp_, a1=a1, ad1=ad1, cu=cumr[itc[0] % 2])
                                hd = hg
                                hgl = 2 * u + hd
                                S.op('pe', lambda e: e.matmul(out=st[0:nk, c0:ncols], lhsT=kb['kt'](hd)[64 * x:64 * x + 64, :],
                                                              rhs=QT[64 * x:64 * x + 64, hd, qc0 + c0:qc0 + ncols], start=True, stop=True),
                                     kb['ktb'] + QT.b, st.b)
                                pt = npt()
                                if sample:
                                    bi = kb['bi']
                                    S.op('act', lambda e: e.activation(out=pt[0:nk, c0:ncols], in_=st[0:nk, c0:ncols], func=AF.Exp,
                                                                       bias=bds[0:nk, hgl, bi:bi + 1]), st.b + bds.b, pt.b)
                                else:
                                    cc = c0
                                    while cc < ncols:
                                        ce = min(ncols, (cc // 256 + 1) * 256)
                                        dl = (kb['qt0'] + ce // 128 - 1) - kb['kbi']
                                        S.op('act', lambda e: e.activation(out=pt[0:nk, cc:ce], in_=st[0:nk, cc:ce], func=AF.Exp,
                                                                           bias=bdp[0:nk, hgl, dl:dl + 1]), st.b + bdp.b, pt.b)
                                        cc = ce
                                if kb['diag']:
                                    diag_mask(pt, lambda a_, b_: mdf[0:a_, hgl, 0:b_], mdf.b, nk, c0, ncols)
                                return dict(pt=pt)

                            def stage_b(hg, kb, x, ncols, acol, cx):
                                nk, c0 = kb['nk'], kb['c0']
                                o0, o1 = acol + c0, acol + ncols
                                if mixer == 'mla':
                                    pt = cx['pt']
                                    S.op('pe', lambda e: e.matmul(out=acc[64 * x:64 * x + 64, o0:o1], lhsT=kb['v'](x),
                                                                  rhs=pt[0:nk, c0:ncols], start=False, stop=True, skip_group_check=True),
                                         kb['vb'] + pt.b, acc.b)
                                    S.op('pe', lambda e: e.matmul(out=accd[64 * x:64 * x + 64, o0:o1], lhsT=ones[0:nk, 0:64],
                                                                  rhs=pt[0:nk, c0:ncols], start=False, stop=True, skip_group_check=True),
                                         ones.b + pt.b, accd.b)
                                elif mixer == 'sb':
                                    hp = hg
                                    ee, a1, ad1, cu = cx['ee'], cx['a1'], cx['ad1'], cx['cu']
                                    S.op('dve', lambda e: e.tensor_tensor(out=cu[0:nk, c0:ncols], in0=a1[0:nk, c0:ncols],
                                                                          in1=R[0:nk, x, o0:o1], op=ALU.add), a1.b + R.b, cu.b)
                                    S.op('act', lambda e: e.activation(out=cu[0:nk, c0:ncols], in_=cu[0:nk, c0:ncols], func=AF.Exp,
                                                                       scale=-1.0), cu.b, cu.b)
                                    pt = npt()
                                    S.op('dve', lambda e: e.tensor_tensor(out=pt[0:nk, c0:ncols], in0=ee[0:nk, c0:ncols],
                                                                          in1=cu[0:nk, c0:ncols], op=ALU.mult), ee.b + cu.b, pt.b)
                                    if kb['diag']:
                                        diag_mask(pt, lambda a_, b_: msb[0:a_, 0:b_], msb.b, nk, c0, ncols)
                                    S.op('dve', lambda e: e.tensor_tensor(out=R[:, x, o0:o1], in0=R[:, x, o0:o1], in1=ad1[:, c0:ncols],
                                                                          op=ALU.add), R.b + ad1.b, R.b)
                                    S.op('pe', lambda e: e.matmul(out=acc[64 * x:64 * x + 64, o0:o1], lhsT=kb['v'](2 * hp + x),
                                                                  rhs=pt[0:nk, c0:ncols], start=False, stop=True, skip_group_check=True),
                                         kb['vb'] + pt.b, acc.b)
                                else:
                                    pt = cx['pt']
                                    a_, d_ = (acc, accd) if x == 0 else (acc1, accd1)
                                    S.op('pe', lambda e: e.matmul(out=a_[:, o0:o1], lhsT=kb['v'](hg), rhs=pt[0:nk, c0:ncols],
                                                                  start=False, stop=True, skip_group_check=True), kb['vb'] + pt.b, a_.b)
                                    S.op('pe', lambda e: e.matmul(out=d_[:, o0:o1], lhsT=ones[0:nk, :], rhs=pt[0:nk, c0:ncols],
                                                                  start=False, stop=True, skip_group_check=True), ones.b + pt.b, d_.b)

                            def stage_a1(kb, ncols, cx):
                                nk, c0 = kb['nk'], kb['c0']
                                sp_, a1, ad1 = cx['sp'], cx['a1'], cx['ad1']
                                S.op('pe', lambda e: e.matmul(out=a1[0:nk, c0:ncols], lhsT=tri[0:nk, 0:nk], rhs=sp_[0:nk, c0:ncols],
                                                              start=True, stop=True), tri.b + sp_.b, a1.b)
                                S.op('pe', lambda e: e.matmul(out=ad1[:, c0:ncols], lhsT=ones[0:nk, :], rhs=sp_[0:nk, c0:ncols],
                                                              start=True, stop=True), ones.b + sp_.b, ad1.b)

                            def att_blocks(hg, kbs, qc0, ncols, sample, acol=0):
                                items = [(kb, x) for kb in kbs for x in range(2)]
                                n = len(items)
                                dep = 1 if mixer == 'diff' else 2
                                ctx = {}
                                for t in range(n + dep):
                                    if t < n:
                                        ctx[t] = stage_a(hg, items[t][0], items[t][1], qc0, ncols, sample)
                                    if mixer == 'sb' and 0 <= t - 1 < n:
                                        stage_a1(items[t - 1][0], ncols, ctx[t - 1])
                                    if 0 <= t - dep < n:
                                        stage_b(hg, items[t - dep][0], items[t - dep][1], ncols, acol, ctx.pop(t - dep))

                            def att_end(hg, ncols, ocol0, grp_b, acol=0):
                                a0, a1_ = acol, acol + ncols
                                if mixer == 'mla':
                                    S.op('dve', lambda e: e.reciprocal(out=rden[:, 0:ncols], in_=accd[:, a0:a1_]), accd.b, rden.b)
                                    S.op('dve', lambda e: e.tensor_tensor(out=OT[:, u, ocol0:ocol0 + ncols], in0=acc[:, a0:a1_], in1=rden[:, 0:ncols],
                                                                          op=ALU.mult), acc.b + rden.b, grp_b)
                                elif mixer == 'sb':
                                    S.op('act', lambda e: e.copy(out=OT[:, 2 * u + hg, ocol0:ocol0 + ncols], in_=acc[:, a0:a1_]), acc.b, grp_b)
                                else:
                                    hgl = 2 * u + hg
                                    S.op('dve', lambda e: e.reciprocal(out=rden[:, 0:ncols], in_=accd[:, a0:a1_]), accd.b, rden.b)
                                    S.op('dve', lambda e: e.reciprocal(out=rden1[:, 0:ncols], in_=accd1[:, a0:a1_]), accd1.b, rden1.b)
                                    S.op('dve', lambda e: e.tensor_tensor(out=t0[:, 0:ncols], in0=acc[:, a0:a1_], in1=rden[:, 0:ncols], op=ALU.mult),
                                         acc.b + rden.b, t0.b)
                                    S.op('dve', lambda e: e.tensor_tensor(out=t1[:, 0:ncols], in0=acc1[:, a0:a1_], in1=rden1[:, 0:ncols], op=ALU.mult),
                                         acc1.b + rden1.b, t1.b)
                                    S.op('dve', lambda e: e.scalar_tensor_tensor(out=t0[:, 0:ncols], in0=t1[:, 0:ncols], scalar=gcol[:, 2:3],
                                                                                 in1=t0[:, 0:ncols], op0=ALU.mult, op1=ALU.add),
                                         t0.b + t1.b + gcol.b, t0.b)
                                    S.op('act', lambda e: e.activation(out=osq[:, 0:ncols], in_=t0[:, 0:ncols], func=AF.Square), t0.b, osq.b)
                                    pss = mmbank()
                                    S.op('pe', lambda e: e.matmul(out=pss[:, 0:ncols], lhsT=onesf[:, :], rhs=osq[:, 0:ncols], start=True, stop=True),
                                         onesf.b + osq.b, pss.b)
                                    S.op('act', lambda e: e.activation(out=t1[:, 0:ncols], in_=pss[:, 0:ncols], func=AF.Ln, scale=1.0 / 128, bias=EPS),
                                         pss.b, t1.b)
                                    S.op('act', lambda e: e.activation(out=t1[:, 0:ncols], in_=t1[:, 0:ncols], func=AF.Exp, scale=-0.5), t1.b, t1.b)
                                    S.op('dve', lambda e: e.scalar_tensor_tensor(out=OT[:, hgl, ocol0:ocol0 + ncols], in0=t0[:, 0:ncols],
                                                                                 scalar=gcol[:, 0:1], in1=t1[:, 0:ncols], op0=ALU.mult, op1=ALU.mult),
                                         t0.b + t1.b + gcol.b, grp_b)

                            nhg = 1 if mixer == 'mla' else 2

                            def kb_store(kbi, c0, diag, qt0, nk=128, bi=0):
                                rows_, sl = tsl(kbi)
                                if mixer == 'mla':
                                    return dict(kt=lambda hh: KT[:, hh, sl], ktb=[KT.b[kbi]], v=lambda hh: V[0:nk, kbi, 64 * hh:64 * hh + 64],
                                                vb=[V.b[kbi]], nk=nk, c0=c0, diag=diag, kbi=kbi, qt0=qt0, bi=bi)
                                if mixer == 'sb':
                                    return dict(kt=lambda hp: KT[:, hp, sl], ktb=[KT.b[kbi]], v=lambda h: V[0:nk, kbi, 64 * h:64 * h + 64],
                                                vb=[V.b[kbi]], nk=nk, c0=c0, diag=diag, kbi=kbi, qt0=qt0, bi=bi)
                                return dict(kt=lambda hd: KT[:, hd, sl], ktb=[KT.b[kbi]], v=lambda hd: V[0:nk, kbi, 128 * hd:128 * hd + 128],
                                            vb=[V.b[kbi]], nk=nk, c0=c0, diag=diag, kbi=kbi, qt0=qt0, bi=bi)

                            chi = [0]

                            def kb_past(j, kbi, cs_):
                                sl = slice(j * 128, (j + 1) * 128)
                                bi = 32 - kbi
                                KTp, Vp = cs_['KTp'], cs_['Vp']
                                if mixer == 'mla':
                                    return dict(kt=lambda hh: KTp[:, hh, sl], ktb=KTp.b, v=lambda hh: Vp[:, j, 64 * hh:64 * hh + 64],
                                                vb=Vp.b, nk=128, c0=0, diag=False, kbi=kbi, qt0=0, bi=bi)
                                if mixer == 'sb':
                                    return dict(kt=lambda hp: KTp[:, hp, sl], ktb=KTp.b, v=lambda h: Vp[:, j, 64 * h:64 * h + 64],
                                                vb=Vp.b, nk=128, c0=0, diag=False, kbi=kbi, qt0=0, bi=bi)
                                return dict(kt=lambda hd: KTp[:, hd, sl], ktb=KTp.b, v=lambda hd: Vp[:, j, 128 * hd:128 * hd + 128],
                                            vb=Vp.b, nk=128, c0=0, diag=False, kbi=kbi, qt0=0, bi=bi)

                            def build_chunk(s, ch):
                                r0 = ch * 512
                                chi[0] += 1
                                cs_ = CS[chi[0] % 2]
                                if mixer == 'mla':
                                    pck, pkr, pckT, KTp, Vp, kc4 = cs_['pck'], cs_['pkr'], cs_['pckT'], cs_['KTp'], cs_['Vp'], cs_['kc4']
                                    S.dma('pool', pck[:], c_ckv[l, s, r0:r0 + 512, :].rearrange("(k p) n -> p k n", p=128), [], pck.b)
                                    S.dma('pool', pkr[:], c_kr[l, s, r0:r0 + 512, :].rearrange("(k p) n -> p k n", p=128), [], pkr.b)
                                    for j in range(4):
                                        transpose_to(pck[:, j, :], pck.b, 128, 128, pckT[:, j * 128:(j + 1) * 128], pckT.b, 'act' if j % 2 else 'dve')
                                    pk_ = mmbank()
                                    for j in range(4):
                                        S.op('pe', lambda e: e.matmul(out=pk_[:, j * 128:(j + 1) * 128], lhsT=pckT[:, j * 128:(j + 1) * 128], rhs=wuk[:, :],
                                                                      start=True, stop=True), pckT.b + wuk.b, pk_.b)
                                    gnorm(ws, pk_[:, 0:512].rearrange("p (g d) -> p g d", d=64), pk_.b, 128, 8, 64, gs('kn', 128),
                                          kc4[:, :, :, 0:64].rearrange("p j h d -> p (j h) d"), kc4.b)
                                    S.op('dve', lambda e: e.tensor_copy(out=kc4[:, :, :, 64:96], in_=pkr[:, :, :].unsqueeze(2).to_broadcast([128, 4, 2, 32])),
                                         pkr.b, kc4.b)
                                    for j in range(4):
                                        for hh in range(2):
                                            transpose_to(kc4[:, j, hh, :], kc4.b, 128, 96, KTp[:, hh, j * 128:(j + 1) * 128], KTp.b, 'act' if hh else 'dve')
                                    pv_ = mmbank()
                                    for j in range(4):
                                        S.op('pe', lambda e: e.matmul(out=pv_[:, j * 128:(j + 1) * 128], lhsT=pckT[:, j * 128:(j + 1) * 128], rhs=wuv[:, :],
                                                                      start=True, stop=True), pckT.b + wuv.b, pv_.b)
                                    S.op('act', lambda e: e.copy(out=Vp[:, :, :].rearrange("p j n -> p (j n)"), in_=pv_[:, 0:512]), pv_.b, Vp.b)
                                else:
                                    pk, KTp, Vp = cs_['pk'], cs_['KTp'], cs_['Vp']
                                    ck = c_sbk if mixer == 'sb' else c_dk
                                    cv = c_sbv if mixer == 'sb' else c_dv
                                    S.dma('pool', pk[:], ck[l, s, r0:r0 + 512, u * 256:(u + 1) * 256].rearrange("(k p) n -> p k n", p=128), [], pk.b)
                                    S.dma('pool', Vp[:], cv[l, s, r0:r0 + 512, u * 256:(u + 1) * 256].rearrange("(k p) n -> p k n", p=128), [], Vp.b)
                                    for j in range(4):
                                        for c in range(2):
                                            transpose_to(pk[:, j, c * 128:(c + 1) * 128], pk.b, 128, 128, KTp[:, c, j * 128:(j + 1) * 128], KTp.b,
                                                         'act' if c else 'dve')
                                return cs_

                            for g in range(4):
                                for i in range(4):
                                    proj_tile(4 * g + i, i * 128)
                                kbs = []
                                for kbi in range(4 * g + 4):
                                    i = kbi - 4 * g
                                    kbs.append(kb_store(kbi, max(i, 0) * 128, i >= 0, 4 * g))
                                if mixer == 'sb':
                                    kbs = kbs[::-1]
                                for hg in range(nhg):
                                    att_begin(512)
                                    att_blocks(hg, kbs, 0, 512, False)
                                    att_end(hg, 512, g * 512, [OT.b[g]])
                                _ck('%s%d_u%d_g%d' % (mixer, l, u, g))

                            for s in range(NS):
                                proj_tile(16 + s, 0)
                                knew = kb_store(16 + s, 0, mixer != 'mla', 0, nk=32, bi=0)
                                att_begin(32 * nhg)
                                if mixer == 'sb':
                                    for hg in range(nhg):
                                        att_blocks(hg, [knew], 0, 32, True, acol=32 * hg)
                                    for ch in range(7, -1, -1):
                                        cs_ = build_chunk(s, ch)
                                        for hg in range(nhg):
                                            att_blocks(hg, [kb_past(j, ch * 4 + j, cs_) for j in range(3, -1, -1)], 0, 32, True, acol=32 * hg)
                                else:
                                    for ch in range(8):
                                        cs_ = build_chunk(s, ch)
                                        for hg in range(nhg):
                                            att_blocks(hg, [kb_past(j, ch * 4 + j, cs_) for j in range(4)], 0, 32, True, acol=32 * hg)
                                    for hg in range(nhg):
                                        att_blocks(hg, [knew], 0, 32, True, acol=32 * hg)
                                for hg in range(nhg):
                                    att_end(hg, 32, 2048 + 32 * s, [OT.b[4]], acol=32 * hg)
                                _ck('%s%d_u%d_s%d' % (mixer, l, u, s))
                            if u == nunits - 1:
                                S.barrier()
                            mm_list[0] = [0, 1]
                            rot['mm'] = 0
                            _ck('%s%d_u%d' % (mixer, l, u))

                    mi = ('mla', 'sb', 'diff').index(mixer)
                    uso.close()
                    ms.close()
                    with ExitStack() as gs_:
                        mm_list[0] = [0, 1, 4, 5, 6, 7]
                        wg = sb(gs_, [128, 8, 1024], BF16)
                        g0 = OFF['g'] + 1024 * mi
                        load_w(wg, W['w_in'][l, :, g0:g0 + 1024].rearrange("(k p) n -> p k n", p=128))
                        wbr = sb(gs_, [128, 4, 1024], BF16)
                        load_w(wbr, W['w_br_' + mixer][l].rearrange("(k p) n -> p k n", p=128))
                        wo = sb(gs_, [128, 8, 1024], BF16)
                        load_w(wo, W['w_out'][l].rearrange("(k p) n -> p k n", p=128))
                        gT = [sb(gs_, [128, 512], F32) for _ in range(2)]
                        MG = [sb(gs_, [128, 8, 512], BF16) for _ in range(1)]
                        for gi, (c0, n, tiles) in enumerate(GROUPS):
                            hb_ = [HT.b[t] for t in tiles]
                            mg = MG[0]
                            for nn in range(8):
                                nsl = slice(nn * 128, (nn + 1) * 128)
                                pg = mmbank()
                                for k in range(8):
                                    S.op('pe', lambda e: e.matmul(out=pg[:, 0:n], lhsT=wg[:, k, nsl], rhs=HT[:, k, c0:c0 + n],
                                                                  start=(k == 0), stop=(k == 7)), wg.b + hb_, pg.b, inc=(k == 7))
                                gt = gT[nn % 2]
                                S.op('act', lambda e: e.activation(out=gt[:, 0:n], in_=pg[:, 0:n], func=AF.Sigmoid), pg.b, gt.b)
                                py = mmbank()
                                for c in range(4):
                                    S.op('pe', lambda e: e.matmul(out=py[:, 0:n], lhsT=wbr[:, c, nsl], rhs=OT[:, c, c0:c0 + n],
                                                                  start=(c == 0), stop=(c == 3)), wbr.b + [OT.b[gi]], py.b, inc=(c == 3))
                                S.op('dve', lambda e: e.tensor_tensor(out=mg[:, nn, 0:n], in0=py[:, 0:n], in1=gt[:, 0:n], op=ALU.mult),
                                     py.b + gt.b, mg.b)
                            for tt in tiles:
                                rows, sl = tsl(tt)
                                off = sl.start - c0
                                for half in range(2):
                                    hsl = slice(half * 512, (half + 1) * 512)
                                    po = mmbank()
                                    for k in range(8):
                                        S.op('pe', lambda e: e.matmul(out=po[0:rows, :], lhsT=mg[:, k, off:off + rows], rhs=wo[:, k, hsl],
                                                                      start=(k == 0), stop=(k == 7)), mg.b + wo.b, po.b, inc=(k == 7))
                                    S.op('dve', lambda e: e.tensor_tensor(out=X[0:rows, tt, hsl], in0=X[0:rows, tt, hsl], in1=po[0:rows, :],
                                                                          op=ALU.add), [X.b[tt]] + po.b, [X.b[tt]])
                        S.barrier()
                        mm_list[0] = [0, 1]
                        rot['mm'] = 0
                    _ck('%s%d_merge' % (mixer, l))

            norm_phase(l, 'ffn_norm_g', False)
            with ExitStack() as fs:
                mm_list[0] = [0, 1, 4, 5, 6, 7]
                cw = sb(fs, [128, 22, 3], F32)
                cb = sb(fs, [128, 22], F32)
                cst = sb(fs, [128, NS, 22, 2], F32)
                OC = sb(fs, [128, 3, 22, 2], F32)
                S.dma('sp', cw[:].rearrange("p a b -> p (a b)"), W['ffn_conv_w'][l], [], cw.b)
                S.dma('sp', cb[:], W['ffn_conv_b'][l], [], cb.b)
                for s in range(NS):
                    S.dma('sp', cst[:, s, :, :].rearrange("p a b -> p (a b)"), c_conv[l, s], [], cst.b)
                WA = [sb(fs, [128, 8, 512], BF16) for _ in range(2)]
                WU = [sb(fs, [128, 8, 512], BF16) for _ in range(2)]
                WD = [sb(fs, [128, 4, 1024], BF16) for _ in range(2)]
                AT = [sb(fs, [128, 4, 516], F32, 4) for _ in range(1)]
                carry = sb(fs, [128, 4, 2], F32, 4)
                cc_ = [sb(fs, [128, 512], F32) for _ in range(1)]
                sl_ = [sb(fs, [128, 512], F32) for _ in range(1)]
                MM = [sb(fs, [128, 4, 512], BF16) for _ in range(1)]
                fgroups = [(0, 4), (4, 4), (8, 4), (12, 4), (16, 4), (20, 2)]
                for fi, (fc0, nfc) in enumerate(fgroups):
                    wa, wu, wd = WA[fi % 2], WU[fi % 2], WD[fi % 2]
                    nf = nfc * 128
                    S.dma('pool', wa[:, :, 0:nf], W['ffn_w_up'][l, :, fc0 * 128:fc0 * 128 + nf].rearrange("(k p) n -> p k n", p=128), [], wa.b)
                    S.dma('pool', wu[:, :, 0:nf], W['ffn_w_up'][l, :, DFF + fc0 * 128:DFF + fc0 * 128 + nf].rearrange("(k p) n -> p k n", p=128),
                          [], wu.b)
                    S.dma('pool', wd[:, 0:nfc, :], W['ffn_w_down'][l, fc0 * 128:fc0 * 128 + nf, :].rearrange("(c p) n -> p c n", p=128), [], wd.b)
                    for gi, (c0, n, tiles) in enumerate(GROUPS):
                        hb_ = [HT.b[t] for t in tiles]
                        mmt = MM[0]
                        buf = AT[0]
                        for j in range(nfc):
                            fc = fc0 + j
                            jsl = slice(j * 128, (j + 1) * 128)
                            pa = mmbank()
                            for k in range(8):
                                S.op('pe', lambda e: e.matmul(out=pa[:, 0:n], lhsT=wa[:, k, jsl], rhs=HT[:, k, c0:c0 + n],
                                                              start=(k == 0), stop=(k == 7)), wa.b + hb_, pa.b, inc=(k == 7))
                            cc = cc_[0]
                            if gi < 4:
                                S.op('act', lambda e: e.copy(out=buf[:, j, 2:514], in_=pa[:, 0:512]), pa.b, [buf.b[j]])
                                if gi == 0:
                                    S.op('dve', lambda e: e.memset(buf[:, j, 0:2], 0.0), [], [buf.b[j]])
                                else:
                                    S.op('dve', lambda e: e.tensor_copy(out=buf[:, j, 0:2], in_=carry[:, j, :]), [carry.b[j]], [buf.b[j]])
                                S.op('dve', lambda e: e.tensor_copy(out=carry[:, j, :], in_=buf[:, j, 512:514]), [buf.b[j]], [carry.b[j]])
                                segs = [(0, 0, 512)]
                                if gi == 3:
                                    S.op('dve', lambda e: e.tensor_copy(out=OC[:, 0, fc, :], in_=buf[:, j, 512:514]), [buf.b[j]], OC.b)
                            else:
                                for s in range(NS):
                                    S.op('act', lambda e: e.copy(out=buf[:, j, s * 34 + 2:s * 34 + 34], in_=pa[:, s * 32:s * 32 + 32]), pa.b, [buf.b[j]])
                                    S.op('dve', lambda e: e.tensor_copy(out=buf[:, j, s * 34:s * 34 + 2], in_=cst[:, s, fc, :]), cst.b, [buf.b[j]])
                                    S.op('dve', lambda e: e.tensor_copy(out=OC[:, 1 + s, fc, :], in_=buf[:, j, s * 34 + 32:s * 34 + 34]), [buf.b[j]], OC.b)
                                segs = [(0, 0, 32), (34, 32, 32)]
                            for (b0, o0, nn_) in segs:
                                S.op('dve', lambda e: e.tensor_scalar(out=cc[:, o0:o0 + nn_], in0=buf[:, j, b0 + 2:b0 + 2 + nn_], scalar1=cw[:, fc, 2:3],
                                                                      scalar2=cb[:, fc:fc + 1], op0=ALU.mult, op1=ALU.add),
                                     [buf.b[j]] + cw.b + cb.b, cc.b)
                                S.op('dve', lambda e: e.scalar_tensor_tensor(out=cc[:, o0:o0 + nn_], in0=buf[:, j, b0 + 1:b0 + 1 + nn_], scalar=cw[:, fc, 1:2],
                                                                             in1=cc[:, o0:o0 + nn_], op0=ALU.mult, op1=ALU.add),
                                     [buf.b[j]] + cw.b + cc.b, cc.b)
                                S.op('dve', lambda e: e.scalar_tensor_tensor(out=cc[:, o0:o0 + nn_], in0=buf[:, j, b0:b0 + nn_], scalar=cw[:, fc, 0:1],
                                                                             in1=cc[:, o0:o0 + nn_], op0=ALU.mult, op1=ALU.add),
                                     [buf.b[j]] + cw.b + cc.b, cc.b)
                            sl2 = sl_[0]
                            S.op('act', lambda e: e.activation(out=sl2[:, 0:n], in_=cc[:, 0:n], func=AF.Silu), cc.b, sl2.b)
                            pu = mmbank()
                            for k in range(8):
                                S.op('pe', lambda e: e.matmul(out=pu[:, 0:n], lhsT=wu[:, k, jsl], rhs=HT[:, k, c0:c0 + n],
                                                              start=(k == 0), stop=(k == 7)), wu.b + hb_, pu.b, inc=(k == 7))
                            S.op('dve', lambda e: e.tensor_tensor(out=mmt[:, j, 0:n], in0=pu[:, 0:n], in1=sl2[:, 0:n], op=ALU.mult),
                                 pu.b + sl2.b, mmt.b)
                        for tt in tiles:
                            rows, sl = tsl(tt)
                            off = sl.start - c0
                            for half in range(2):
                                hsl = slice(half * 512, (half + 1) * 512)
                                po = mmbank()
                                for j in range(nfc):
                                    S.op('pe', lambda e: e.matmul(out=po[0:rows, :], lhsT=mmt[:, j, off:off + rows], rhs=wd[:, j, hsl],
                                                                  start=(j == 0), stop=(j == nfc - 1)), mmt.b + wd.b, po.b, inc=(j == nfc - 1))
                                S.op('dve', lambda e: e.tensor_tensor(out=X[0:rows, tt, hsl], in0=X[0:rows, tt, hsl], in1=po[0:rows, :],
                                                                      op=ALU.add), [X.b[tt]] + po.b, [X.b[tt]])
                S.dma('sp', o_conv[l].rearrange("s p n -> p s n"), OC[:].rearrange("p s a b -> p s (a b)"), OC.b, [])
                S.barrier()
                mm_list[0] = [0, 1]
                rot['mm'] = 0
            _ck('ffn%d' % l)

        _DEV['off'] = False
        _DEV['nops'] = 0
        _layers()
        _DEV['off'] = False
        mm_list[0] = [0, 1]
        for tt in range(NT):
            rows, sl = tsl(tt)
            S.dma('sp', y[sl, :], X[0:rows, tt, :], [X.b[tt]], [])
        S.barrier()
    return nc


_SLOPES = [2.0 ** (-8.0 * (h + 1) / 4) for h in range(4)]


def _consts():
    half = 16
    inv = (np.float32(10000.0) ** (-np.arange(half, dtype=np.float32) / np.float32(half))).astype(np.float32)
    pos = np.concatenate([np.arange(SP_), PAST + np.arange(SS), PAST + np.arange(SS)]).astype(np.float32)
    ang = (pos[:, None] * inv[None, :]).astype(np.float32)
    k_cs = np.concatenate([np.cos(ang), np.sin(ang)], axis=1).astype(np.float32)
    k = np.arange(128)[:, None]
    q = np.arange(128)[None, :]
    msb = (k < q).astype(np.float32)
    mch = ((k // 64) <= (q // 64)).astype(np.float32)
    mdf = np.zeros((128, 4, 128), np.float32)
    bdp = np.zeros((128, 4, 17), np.float32)
    bds = np.zeros((128, 4, 33), np.float32)
    for h in range(4):
        sl = _SLOPES[h]
        mdf[:, h, :] = mch * np.where(k > q, np.exp(-2.0 * sl * (k - q)), 1.0)
        for d in range(17):
            bdp[:, h, d] = sl * (np.arange(128) - 127 - 128 * d)
        for j in range(33):
            bds[:, h, j] = sl * (np.arange(128) - 31 - 128 * j)
    tri = (k >= q).astype(np.float32)
    return dict(k_cs=k_cs, k_msb=msb, k_mch=mch, k_mdf=mdf.reshape(128, 512), k_tri=tri,
                k_bdp=bdp.reshape(128, 68), k_bds=bds.reshape(128, 132))


_WNAMES = ["mix_norm_g", "w_in", "mla_q_norm_g", "mla_w_uq", "mla_kv_norm_g", "mla_w_uk", "mla_w_uv", "mla_qn_g", "mla_kn_g",
           "mla_qr_g", "mla_kr_g", "diff_qn_g", "diff_kn_g", "diff_lambda", "diff_subln_g", "w_br_mla", "w_br_sb", "w_br_diff",
           "w_out", "ffn_norm_g", "ffn_w_up", "ffn_conv_w", "ffn_conv_b", "ffn_w_down"]


def kernel(**inputs):
    inp = {k: np.asarray(v) for k, v in inputs.items()}
    nc = build_program()
    shared = {}
    for n in _WNAMES:
        a = np.ascontiguousarray(inp[n], dtype=np.float32)
        if n == "diff_lambda":
            a = a.reshape(L, 256)
        elif n == "ffn_conv_w":
            a = np.ascontiguousarray(a.reshape(L, 3, 22, 128).transpose(0, 3, 2, 1)).reshape(L, 128, 66)
        elif n == "ffn_conv_b":
            a = np.ascontiguousarray(a.reshape(L, 22, 128).transpose(0, 2, 1))
        shared[n] = a
    shared.update(_consts())
    in_maps = []
    for c in range(8):
        m = dict(shared)
        m["xin"] = np.ascontiguousarray(np.concatenate([inp["x_prompt"][c], inp["x_sample"][2 * c], inp["x_sample"][2 * c + 1]], axis=0),
                                        dtype=np.float32)
        sl = slice(2 * c, 2 * c + 2)
        m["c_ckv"] = np.ascontiguousarray(inp["cache_mla_ckv"][:, sl])
        m["c_kr"] = np.ascontiguousarray(inp["cache_mla_krope"][:, sl])
        m["c_sbk"] = np.ascontiguousarray(inp["cache_sb_k"][:, sl]).reshape(L, NS, PAST, 512)
        m["c_sbv"] = np.ascontiguousarray(inp["cache_sb_v"][:, sl]).reshape(L, NS, PAST, 512)
        m["c_dk"] = np.ascontiguousarray(inp["cache_diff_k"][:, sl]).reshape(L, NS, PAST, 512)
        m["c_dv"] = np.ascontiguousarray(inp["cache_diff_v"][:, sl]).reshape(L, NS, PAST, 512)
        st = np.asarray(inp["state_ffn_conv"][:, sl], dtype=np.float32)
        m["c_conv"] = np.ascontiguousarray(st.reshape(L, NS, 2, 22, 128).transpose(0, 1, 4, 3, 2)).reshape(L, NS, 128, 44)
        in_maps.append(m)
    res = run_bass_kernel_spmd(nc, in_maps, core_ids=list(range(8))).results

    def gather(name, width):
        p = np.stack([res[c][name][:, 0:SP_] for c in range(8)], axis=1)
        s = np.stack([res[c][name][:, SP_ + SS * j:SP_ + SS * (j + 1)] for c in range(8) for j in range(NS)], axis=1)
        return p, s

    y_p = np.stack([res[c]["y"][0:SP_] for c in range(8)], axis=0)
    y_s = np.stack([res[c]["y"][SP_ + SS * j:SP_ + SS * (j + 1)] for c in range(8) for j in range(NS)], axis=0)
    p_ckv, s_ckv = gather("o_ckv", 128)
    p_kr, s_kr = gather("o_kr", 32)
    p_sbk, s_sbk = gather("o_sbk", 512)
    p_sbv, s_sbv = gather("o_sbv", 512)
    p_dk, s_dk = gather("o_dk", 512)
    p_dv, s_dv = gather("o_dv", 512)

    def conv_of(c, idx):
        oc = res[c]["o_conv"][:, idx].reshape(L, 128, 22, 2)
        return np.ascontiguousarray(oc.transpose(0, 3, 2, 1)).reshape(L, 2, DFF)
    p_conv = np.stack([conv_of(c, 0) for c in range(8)], axis=1)
    s_conv = np.stack([conv_of(c, 1 + j) for c in range(8) for j in range(NS)], axis=1)
    f = np.float32
    return (y_p.astype(f), y_s.astype(f), p_ckv.astype(f), p_kr.astype(f),
            p_sbk.reshape(L, 8, SP_, 8, 64).astype(f), p_sbv.reshape(L, 8, SP_, 8, 64).astype(f),
            p_dk.reshape(L, 8, SP_, 4, 2, 64).astype(f), p_dv.reshape(L, 8, SP_, 4, 128).astype(f), p_conv.astype(f),
            s_ckv.astype(f), s_kr.astype(f), s_sbk.reshape(L, 16, SS, 8, 64).astype(f), s_sbv.reshape(L, 16, SS, 8, 64).astype(f),
            s_dk.reshape(L, 16, SS, 4, 2, 64).astype(f), s_dv.reshape(L, 16, SS, 4, 128).astype(f), s_conv.astype(f))
```

```python
import math
from contextlib import ExitStack
import numpy as np
import concourse.bass as bass
import concourse.mybir as mybir
from concourse.bass_utils import run_bass_kernel_spmd

F32 = mybir.dt.float32
BF16 = mybir.dt.bfloat16
AF = mybir.ActivationFunctionType
ALU = mybir.AluOpType
AX = mybir.AxisListType

L = 2
D = 1024
SP_ = 2048
NS = 2
SS = 32
PAST = 4096
NTOK = SP_ + NS * SS
NT = 18
DFF = 2816
NIN = 6560
EPS = 1e-6
MLA_SCALE = 96 ** -0.5
SB_SCALE = 64 ** -0.5
DIFF_SCALE = 64 ** -0.5
OFF = dict(cq=0, ckv=256, kr=384, sq=416, sk=928, sv=1440, dq=1952, dk=2464, dv=2976, g=3488)
ENG = ['pe', 'act', 'dve', 'pool', 'sp']
NDS = 20
_DEV = {'stop': None, 'off': False, 'maxops': None, 'nops': 0, 'log': None}


class _Stop(Exception):
    pass


def _ck(name):
    if _DEV['log'] is not None and not _DEV['off']:
        _DEV['log'].append((name, _DEV['nops']))
    if _DEV['stop'] == name:
        _DEV['off'] = True


class Buf:
    __slots__ = ('w', 'r', 'excl')

    def __init__(self):
        self.w = None
        self.r = {}
        self.excl = False


class TT:
    def __init__(self, h, n=1):
        self.h = h
        self.b = [Buf() for _ in range(n)]

    def __getitem__(self, k):
        return self.h[k]


class Sync:
    def __init__(self, nc, es):
        self.nc = nc
        self.e = dict(pe=nc.tensor, act=nc.scalar, dve=nc.vector, pool=nc.gpsimd, sp=nc.sync)
        self.sem = {k: es.enter_context(nc.semaphore("sem_" + k)) for k in ENG}
        self.cnt = {k: 0 for k in ENG}
        self.dsem = {q: [es.enter_context(nc.semaphore("ds_%s%d" % (q, i))) for i in range(NDS)] for q in ('sp', 'pool')}
        self.dcnt = {q: [0] * NDS for q in ('sp', 'pool')}
        self.dnext = {'sp': 0, 'pool': 0}
        self.known = {k: {} for k in ENG}
        self.pend = {k: False for k in ENG}
        self.hist = {}

    def _semof(self, k):
        return self.sem[k] if isinstance(k, str) else self.dsem[k[0]][k[1]]

    def _need(self, eng, toks):
        kn = self.known[eng]
        best = {}
        for (k, v) in toks:
            if best.get(k, 0) < v:
                best[k] = v
        need = []
        for k, v in sorted(best.items(), key=lambda kv: str(kv[0])):
            if kn.get(k, 0) < v:
                need.append((k, v))
        implied = {}
        for k, v in need:
            snap = self.hist.get((k, v))
            if snap:
                for k2, v2 in snap.items():
                    if implied.get(k2, 0) < v2:
                        implied[k2] = v2
        out = [(k, v) for (k, v) in need if implied.get(k, 0) < v]
        for k, v in out:
            kn[k] = v
            snap = self.hist.get((k, v))
            if snap:
                for k2, v2 in snap.items():
                    if k2 != eng and kn.get(k2, 0) < v2:
                        kn[k2] = v2
        return out

    def _wait(self, eng, toks, ins_fn=None):
        need = self._need(eng, toks)
        if ins_fn is None:
            for k, v in need:
                self.e[eng].wait_ge(self._semof(k), v)
            return None
        for k, v in need[:-1]:
            self.e[eng].wait_ge(self._semof(k), v)
        ins = ins_fn()
        if need:
            k, v = need[-1]
            ins._wait_ge(self._semof(k), v)
        return ins

    def _deps(self, eng, reads, writes):
        toks = set()
        for b in reads:
            if b.w is not None:
                toks.add(b.w)
            if b.excl:
                for kv in b.r.items():
                    if kv[0] != eng:
                        toks.add(kv)
        for b in writes:
            if b.w is not None:
                toks.add(b.w)
            for kv in b.r.items():
                toks.add(kv)
        if eng == 'pe':
            toks = {t for t in toks if t[0] != 'pe'}
        return toks

    def op(self, eng, fn, reads=(), writes=(), inc=True):
        if _DEV['off']:
            return
        _DEV['nops'] += 1
        if _DEV['maxops'] is not None and _DEV['nops'] > _DEV['maxops'] and not self.pend[eng]:
            _DEV['off'] = True
            return
        ins = self._wait(eng, self._deps(eng, reads, writes), lambda: fn(self.e[eng]))
        c = self.cnt[eng] + 1
        if inc:
            ins.then_inc(self.sem[eng], 1)
            self.cnt[eng] = c
            self.pend[eng] = False
            self.hist[(eng, c)] = dict(self.known[eng])
        else:
            self.pend[eng] = True
        for b in reads:
            b.r[eng] = c
        for b in writes:
            b.w = (eng, c)
            b.r = {}

    def dma(self, q, out, in_, reads=(), writes=()):
        if _DEV['off']:
            return
        toks = self._deps(q, reads, writes)
        i = self.dnext[q]
        self.dnext[q] = (i + 1) % NDS
        key = (q, i)
        if self.dcnt[q][i] > 0:
            toks.add((key, self.dcnt[q][i]))
        ins = self._wait(q, toks, lambda: self.e[q].dma_start(out=out, in_=in_))
        ins.then_inc(self.dsem[q][i], 16)
        self.dcnt[q][i] += 16
        v = self.dcnt[q][i]
        self.hist[(key, v)] = dict(self.known[q])
        for b in reads:
            b.r[key] = v
        for b in writes:
            b.w = (key, v)
            b.r = {}

    def all_tokens(self):
        toks = {(k, self.cnt[k]) for k in ENG if self.cnt[k] > 0}
        for q in ('sp', 'pool'):
            for i, c in enumerate(self.dcnt[q]):
                if c > 0:
                    toks.add(((q, i), c))
        return toks

    def barrier(self):
        if _DEV['off']:
            return
        for k in ENG:
            assert not self.pend[k]
        toks = self.all_tokens()
        for eng in ENG:
            self._wait(eng, {t for t in toks if t[0] != eng})


def build_program():
    nc = bass.Bass("TRN2", target_bir_lowering=False)

    def din(name, shape):
        return nc.dram_tensor(name, list(shape), F32, kind="ExternalInput").ap()

    def dout(name, shape):
        return nc.dram_tensor(name, list(shape), F32, kind="ExternalOutput").ap()

    xin = din("xin", [NTOK, D])
    c_ckv = din("c_ckv", [L, NS, PAST, 128])
    c_kr = din("c_kr", [L, NS, PAST, 32])
    c_sbk = din("c_sbk", [L, NS, PAST, 512])
    c_sbv = din("c_sbv", [L, NS, PAST, 512])
    c_dk = din("c_dk", [L, NS, PAST, 512])
    c_dv = din("c_dv", [L, NS, PAST, 512])
    c_conv = din("c_conv", [L, NS, 128, 22 * 2])
    W = {}
    for name, shape in [("mix_norm_g", [L, D]), ("w_in", [L, D, NIN]), ("mla_q_norm_g", [L, 256]),
                        ("mla_w_uq", [L, 256, 768]), ("mla_kv_norm_g", [L, 128]), ("mla_w_uk", [L, 128, 512]),
                        ("mla_w_uv", [L, 128, 512]), ("mla_qn_g", [L, 64]), ("mla_kn_g", [L, 64]),
                        ("mla_qr_g", [L, 32]), ("mla_kr_g", [L, 32]), ("diff_qn_g", [L, 64]),
                        ("diff_kn_g", [L, 64]), ("diff_lambda", [L, 256]), ("diff_subln_g", [L, 128]),
                        ("w_br_mla", [L, 512, D]), ("w_br_sb", [L, 512, D]), ("w_br_diff", [L, 512, D]),
                        ("w_out", [L, D, D]), ("ffn_norm_g", [L, D]), ("ffn_w_up", [L, D, 2 * DFF]),
                        ("ffn_conv_w", [L, 128, 22 * 3]), ("ffn_conv_b", [L, 128, 22]), ("ffn_w_down", [L, DFF, D])]:
        W[name] = din(name, shape)
    k_cs = din("k_cs", [NTOK, 32])
    k_msb = din("k_msb", [128, 128])
    k_mch = din("k_mch", [128, 128])
    k_mdf = din("k_mdf", [128, 4 * 128])
    k_tri = din("k_tri", [128, 128])
    k_bdp = din("k_bdp", [128, 4 * 17])
    k_bds = din("k_bds", [128, 4 * 33])

    y = dout("y", [NTOK, D])
    o_ckv = dout("o_ckv", [L, NTOK, 128])
    o_kr = dout("o_kr", [L, NTOK, 32])
    o_sbk = dout("o_sbk", [L, NTOK, 512])
    o_sbv = dout("o_sbv", [L, NTOK, 512])
    o_dk = dout("o_dk", [L, NTOK, 512])
    o_dv = dout("o_dv", [L, NTOK, 512])
    o_conv = dout("o_conv", [L, 3, 128, 22 * 2])

    with ExitStack() as es:
        E = es.enter_context
        S = Sync(nc, es)
        cnt = [0]

        def sb(es_, shape, dt, n=1):
            cnt[0] += 1
            return TT(es_.enter_context(nc.sbuf_tensor("t%d" % cnt[0], list(shape), dt)), n)

        X = sb(es, [128, NT, D], F32, NT)
        HT = sb(es, [128, 8, NTOK], BF16, NT)
        OT = sb(es, [128, 4, NTOK], BF16, 5)
        ident = sb(es, [128, 128], BF16)
        ones = sb(es, [128, 128], BF16)
        zeros = sb(es, [128, 128], BF16)
        onesf = sb(es, [128, 128], F32)
        tri = sb(es, [128, 128], BF16)
        msb = sb(es, [128, 128], F32)
        mch = sb(es, [128, 128], F32)
        mdf = sb(es, [128, 4, 128], F32)
        bdp = sb(es, [128, 4, 17], F32)
        bds = sb(es, [128, 4, 33], F32)
        cs = sb(es, [128, NT, 32], F32)
        gbig = sb(es, [128, D], F32)
        gsm = sb(es, [128, 1024], F32)
        gcol = sb(es, [128, 8], F32)
        PS = [TT(E(nc.psum_tensor("ps%d" % i, [128, 512], F32))) for i in range(8)]
        for p_ in PS:
            p_.b[0].excl = True
        rot = {'mm': 0, 'tp': 0}

        def mmbank():
            rot['mm'] = (rot['mm'] + 1) % len(mm_list[0])
            return PS[mm_list[0][rot['mm']]]

        def tpbank():
            rot['tp'] ^= 1
            return PS[2 + rot['tp']]

        def tsl(tt):
            if tt < 16:
                return 128, slice(tt * 128, tt * 128 + 128)
            return 32, slice(2048 + 32 * (tt - 16), 2048 + 32 * (tt - 16) + 32)

        GROUPS = [(g * 512, 512, [4 * g + i for i in range(4)]) for g in range(4)] + [(2048, 64, [16, 17])]
        mm_list = [[0, 1]]

        S.op('pool', lambda e: e.memset(ident[:], 0.0), [], ident.b)
        S.op('pool', lambda e: e.affine_select(out=ident[:], in_=ident[:], pattern=[[-1, 128]], compare_op=ALU.not_equal,
                                               fill=1.0, base=0, channel_multiplier=1), ident.b, ident.b)
        S.op('pool', lambda e: e.memset(ones[:], 1.0), [], ones.b)
        S.op('pool', lambda e: e.memset(zeros[:], 0.0), [], zeros.b)
        S.op('pool', lambda e: e.memset(onesf[:], 1.0), [], onesf.b)
        S.dma('pool', tri[:], k_tri, [], tri.b)
        S.dma('sp', msb[:], k_msb, [], msb.b)
        S.dma('sp', mch[:], k_mch, [], mch.b)
        S.dma('sp', mdf[:].rearrange("p a b -> p (a b)"), k_mdf, [], mdf.b)
        S.dma('sp', bdp[:].rearrange("p a b -> p (a b)"), k_bdp, [], bdp.b)
        S.dma('sp', bds[:].rearrange("p a b -> p (a b)"), k_bds, [], bds.b)
        S.dma('sp', cs[:, 0:16, :], k_cs[0:2048, :].rearrange("(t p) n -> p t n", p=128), [], cs.b)
        S.dma('sp', cs[0:32, 16, :], k_cs[2048:2080, :], [], cs.b)
        S.dma('sp', cs[0:32, 17, :], k_cs[2080:2112, :], [], cs.b)
        zrhs = sb(es, [128, 512], BF16)
        S.op('pool', lambda e: e.memset(zrhs[:], 0.0), [], zrhs.b)

        GS = dict(q_norm=(0, 256), kv_norm=(256, 128), kr=(384, 32), qn=(416, 64), kn=(480, 64), qr=(544, 32),
                  dqn=(576, 64), dkn=(640, 64), lam=(704, 256))

        def gs(name, rows=128):
            o, n = GS[name]
            return gsm[0:rows, o:o + n]

        def rstd_from_ss(ss_t, rows, G, d):
            S.op('act', lambda e: e.activation(out=ss_t[0:rows, 0:G], in_=ss_t[0:rows, 0:G], func=AF.Ln, scale=1.0 / d, bias=EPS),
                 ss_t.b, ss_t.b)
            S.op('act', lambda e: e.activation(out=ss_t[0:rows, 0:G], in_=ss_t[0:rows, 0:G], func=AF.Exp, scale=-0.5),
                 ss_t.b, ss_t.b)

        def gnorm(ws, src, srcb, rows, G, d, gain, out, outb, post_scale=1.0):
            sq, ss, tmp = ws['sq'], ws['ss'], ws['tmp']
            sqv = sq[0:rows, 0:G * d].rearrange("p (g d) -> p g d", d=d)
            S.op('act', lambda e: e.activation(out=sqv, in_=src, func=AF.Square), srcb, sq.b)
            S.op('dve', lambda e: e.tensor_reduce(out=ss[0:rows, 0:G], in_=sqv, axis=AX.X, op=ALU.add), sq.b, ss.b)
            rstd_from_ss(ss, rows, G, d)
            tv = tmp[0:rows, 0:G * d].rearrange("p (g d) -> p g d", d=d)
            S.op('dve', lambda e: e.tensor_tensor(out=tv, in0=src, in1=ss[0:rows, 0:G].unsqueeze(2).to_broadcast([rows, G, d]),
                                                  op=ALU.mult), list(srcb) + ss.b, tmp.b)
            gb = gain.unsqueeze(1).to_broadcast([rows, G, d])
            if post_scale == 1.0:
                S.op('dve', lambda e: e.tensor_tensor(out=out, in0=tv, in1=gb, op=ALU.mult), tmp.b + gsm.b, outb)
            else:
                S.op('dve', lambda e: e.scalar_tensor_tensor(out=out, in0=tv, scalar=float(post_scale), in1=gb, op0=ALU.mult,
                                                             op1=ALU.mult), tmp.b + gsm.b, outb)

        def rope(ws, src, srcb, rows, H, tt, out, outb):
            t1, t2 = ws['r1'], ws['r2']
            cosb = cs[0:rows, tt, 0:16].unsqueeze(1).to_broadcast([rows, H, 16])
            sinb = cs[0:rows, tt, 16:32].unsqueeze(1).to_broadcast([rows, H, 16])
            a1 = t1[0:rows, 0:H * 16].rearrange("p (h d) -> p h d", d=16)
            a2 = t2[0:rows, 0:H * 16].rearrange("p (h d) -> p h d", d=16)
            x1 = src[:, :, 0:16]
            x2 = src[:, :, 16:32]
            S.op('dve', lambda e: e.tensor_tensor(out=a1, in0=x1, in1=cosb, op=ALU.mult), list(srcb) + cs.b, t1.b)
            S.op('dve', lambda e: e.tensor_tensor(out=a2, in0=x2, in1=sinb, op=ALU.mult), list(srcb) + cs.b, t2.b)
            S.op('dve', lambda e: e.tensor_tensor(out=out[:, :, 0:16], in0=a1, in1=a2, op=ALU.subtract), t1.b + t2.b, outb)
            S.op('dve', lambda e: e.tensor_tensor(out=a1, in0=x1, in1=sinb, op=ALU.mult), list(srcb) + cs.b, t1.b)
            S.op('dve', lambda e: e.tensor_tensor(out=a2, in0=x2, in1=cosb, op=ALU.mult), list(srcb) + cs.b, t2.b)
            S.op('dve', lambda e: e.tensor_tensor(out=out[:, :, 16:32], in0=a1, in1=a2, op=ALU.add), t1.b + t2.b, outb)

        def transpose_to(src, srcb, rows, ncols, dst, dstb, eng='act'):
            pb = tpbank()
            pv = pb.h[:, :].bitcast(BF16)
            S.op('pe', lambda e: e.transpose(out=pv[0:ncols, 0:rows], in_=src, identity=ident[0:rows, 0:rows]), list(srcb) + ident.b, pb.b)
            if eng == 'act':
                S.op('act', lambda e: e.copy(out=dst, in_=pv[0:ncols, 0:rows]), pb.b, dstb)
            else:
                S.op('dve', lambda e: e.tensor_copy(out=dst, in_=pv[0:ncols, 0:rows]), pb.b, dstb)

        def load_w(dst, src_ap):
            S.dma('pool', dst.h[:], src_ap, [], dst.b)

        def norm_phase(l, gname, first):
            with ExitStack() as ps:
                sq = sb(ps, [128, D], F32)
                ssr = [sb(ps, [128, 1], F32) for _ in range(2)]
                hb = [sb(ps, [128, D], BF16) for _ in range(2)]
                S.dma('sp', gbig[:], W[gname][l].partition_broadcast(128), [], gbig.b)
                for tt in range(NT):
                    rows, sl = tsl(tt)
                    if first:
                        S.dma('sp', X[0:rows, tt, :], xin[sl, :], [], [X.b[tt]])
                    ss = ssr[tt % 2]
                    h = hb[tt % 2]
                    S.op('act', lambda e: e.activation(out=sq[0:rows, :], in_=X[0:rows, tt, :], func=AF.Square,
                                                       accum_out=ss[0:rows, :]), [X.b[tt]], sq.b + ss.b)
                    rstd_from_ss(ss, rows, 1, D)
                    S.op('dve', lambda e: e.scalar_tensor_tensor(out=h[0:rows, :], in0=X[0:rows, tt, :], scalar=ss[0:rows, 0:1],
                                                                 in1=gbig[0:rows, :], op0=ALU.mult, op1=ALU.mult),
                         [X.b[tt]] + ss.b + gbig.b, h.b)
                    pb = tpbank()
                    pv = pb.h[:, :].bitcast(BF16).rearrange("p (k n) -> p k n", n=128)
                    for k in range(8):
                        S.op('pe', lambda e: e.transpose(out=pv[:, k, 0:rows], in_=h[0:rows, k * 128:(k + 1) * 128],
                                                         identity=ident[0:rows, 0:rows]), h.b + ident.b, pb.b, inc=(k == 7))
                    S.op('act' if tt % 2 else 'dve',
                         (lambda e: e.copy(out=HT[:, :, sl], in_=pv[:, :, 0:rows])) if tt % 2 else
                         (lambda e: e.tensor_copy(out=HT[:, :, sl], in_=pv[:, :, 0:rows])), pb.b, [HT.b[tt]])
                S.barrier()

        def _layers():
          for l in range(L):
            lam_init = 0.8 - 0.6 * math.exp(-0.3 * l)
            for nm, wn in [('q_norm', 'mla_q_norm_g'), ('kv_norm', 'mla_kv_norm_g'), ('kr', 'mla_kr_g'), ('qn', 'mla_qn_g'),
                           ('kn', 'mla_kn_g'), ('qr', 'mla_qr_g'), ('dqn', 'diff_qn_g'), ('dkn', 'diff_kn_g'),
                           ('lam', 'diff_lambda')]:
                S.dma('sp', gs(nm), W[wn][l].partition_broadcast(128), [], gsm.b)
            S.dma('sp', gcol[:, 0:1], W['diff_subln_g'][l].rearrange("(p o) -> p o", o=1), [], gcol.b)
            with ExitStack() as ps:
                lt = sb(ps, [128, 128], F32)
                l2 = sb(ps, [128, 2], F32)
                lv = gs('lam')
                S.op('dve', lambda e: e.tensor_tensor(out=lt[:, 0:64], in0=lv[:, 0:64], in1=lv[:, 64:128], op=ALU.mult), gsm.b, lt.b)
                S.op('dve', lambda e: e.tensor_tensor(out=lt[:, 64:128], in0=lv[:, 128:192], in1=lv[:, 192:256], op=ALU.mult), gsm.b, lt.b)
                S.op('dve', lambda e: e.tensor_reduce(out=l2[:, 0:2], in_=lt[:, :].rearrange("p (a b) -> p a b", b=64), axis=AX.X,
                                                      op=ALU.add), lt.b, l2.b)
                S.op('act', lambda e: e.activation(out=l2[:, 0:2], in_=l2[:, 0:2], func=AF.Exp), l2.b, l2.b)
                S.op('dve', lambda e: e.tensor_tensor(out=gcol[:, 1:2], in0=l2[:, 0:1], in1=l2[:, 1:2], op=ALU.subtract), l2.b, gcol.b)
                S.op('dve', lambda e: e.tensor_scalar(out=gcol[:, 1:2], in0=gcol[:, 1:2], scalar1=float(lam_init), scalar2=None,
                                                      op0=ALU.add), gcol.b, gcol.b)
                S.op('dve', lambda e: e.tensor_scalar(out=gcol[:, 2:3], in0=gcol[:, 1:2], scalar1=-1.0, scalar2=None,
                                                      op0=ALU.mult), gcol.b, gcol.b)
                S.op('dve', lambda e: e.tensor_scalar(out=gcol[:, 0:1], in0=gcol[:, 0:1], scalar1=float(1.0 - lam_init), scalar2=None,
                                                      op0=ALU.mult), gcol.b, gcol.b)
                S.barrier()

            norm_phase(l, 'mix_norm_g', l == 0)
            _ck('norm%d' % l)

            for mixer in ('mla', 'sb', 'diff'):
                with ExitStack() as ms:
                    if mixer == 'mla':
                        CQT = sb(ms, [128, 2, NTOK], BF16, NT)
                        CKVT = sb(ms, [128, NTOK], BF16, NT)
                        KRT = sb(ms, [128, NT, 32], BF16, NT)
                        nunits, hpu = 4, 2
                    elif mixer == 'sb':
                        nunits, hpu = 2, 4
                    else:
                        nunits, hpu = 2, 2
                    uso = ExitStack()
                    ucache = []
                    for u in range(nunits):
                        with ExitStack() as us:
                            upos = [0]

                            def sbu(es_, shape, dt, n=1):
                                if u == 0:
                                    t_ = sb(uso, shape, dt, n)
                                    ucache.append(t_)
                                    return t_
                                t_ = ucache[upos[0]]
                                upos[0] += 1
                                return t_
                            ws = dict(ss=sbu(us, [128, 8], F32))
                            if mixer != 'sb':
                                ws['sq'] = sbu(us, [128, 512], F32)
                                ws['tmp'] = sbu(us, [128, 512], F32)
                            if mixer == 'mla':
                                ws['r1'] = sbu(us, [128, 128], F32)
                                ws['r2'] = sbu(us, [128, 128], F32)
                            stage = [sbu(us, [128, 256], F32) for _ in range(3 if mixer != 'sb' else 2)]
                            stg_i = [0]

                            def nstage():
                                stg_i[0] = (stg_i[0] + 1) % len(stage)
                                return stage[stg_i[0]]
                            tokb = [sbu(us, [128, 256], BF16) for _ in range(3)]
                            tok_i = [0]

                            def ntok():
                                tok_i[0] = (tok_i[0] + 1) % 3
                                return tokb[tok_i[0]]
                            PT = [sbu(us, [128, 512], BF16) for _ in range(3 if mixer == 'sb' else 4)]
                            pt_i = [0]

                            def npt():
                                pt_i[0] = (pt_i[0] + 1) % len(PT)
                                return PT[pt_i[0]]
                            rden = sbu(us, [128, 512], F32) if mixer != 'sb' else None
                            if mixer == 'mla':
                                mm_list[0] = [0, 1, 6, 7]
                                KT = sbu(us, [96, 2, NTOK], BF16, NT)
                                V = sbu(us, [128, NT, 128], BF16, NT)
                                QT = sbu(us, [96, 2, 512], BF16)
                                w1 = sbu(us, [128, 8, 416], BF16)
                                if u == 0:
                                    load_w(w1, W['w_in'][l, :, 0:416].rearrange("(k p) n -> p k n", p=128))
                                wuq = sbu(us, [128, 2, 192], BF16)
                                load_w(wuq, W['mla_w_uq'][l, :, u * 192:(u + 1) * 192].rearrange("(k p) n -> p k n", p=128))
                                wuk = sbu(us, [128, 128], BF16)
                                load_w(wuk, W['mla_w_uk'][l, :, u * 128:(u + 1) * 128])
                                wuv = sbu(us, [128, 128], BF16)
                                load_w(wuv, W['mla_w_uv'][l, :, u * 128:(u + 1) * 128])
                                kcat = [sbu(us, [128, 2, 96], BF16) for _ in range(2)]
                                qcat = [sbu(us, [128, 2, 96], BF16) for _ in range(2)]
                                CS = [dict(pck=sbu(us, [128, 4, 128], BF16), pkr=sbu(us, [128, 4, 32], BF16), pckT=sbu(us, [128, 512], BF16),
                                           KTp=sbu(us, [96, 2, 512], BF16), Vp=sbu(us, [128, 4, 128], BF16),
                                           kc4=sbu(us, [128, 4, 2, 96], BF16)) for _ in range(2)]
                            else:
                                c0q = OFF['sq' if mixer == 'sb' else 'dq'] + u * 256
                                c0k = OFF['sk' if mixer == 'sb' else 'dk'] + u * 256
                                c0v = OFF['sv' if mixer == 'sb' else 'dv'] + u * 256
                                wq = sbu(us, [128, 8, 256], BF16)
                                wk = sbu(us, [128, 8, 256], BF16)
                                wv = sbu(us, [128, 8, 256], BF16)
                                load_w(wq, W['w_in'][l, :, c0q:c0q + 256].rearrange("(k p) n -> p k n", p=128))
                                load_w(wk, W['w_in'][l, :, c0k:c0k + 256].rearrange("(k p) n -> p k n", p=128))
                                load_w(wv, W['w_in'][l, :, c0v:c0v + 256].rearrange("(k p) n -> p k n", p=128))
                                KT = sbu(us, [128, 2, NTOK], BF16, NT)
                                V = sbu(us, [128, NT, 256], BF16, NT)
                                QT = sbu(us, [128, 2, 512], BF16)
                                CS = [dict(pk=sbu(us, [128, 4, 256], BF16), KTp=sbu(us, [128, 2, 512], BF16), Vp=sbu(us, [128, 4, 256], BF16))
                                      for _ in range(2)]
                                if mixer == 'sb':
                                    R = sbu(us, [128, 2, 512], F32)
                                    e1 = [sbu(us, [128, 512], F32) for _ in range(3)]
                                    spb = [sbu(us, [128, 512], BF16) for _ in range(3)]
                                    cumr = [sbu(us, [128, 512], F32) for _ in range(2)]
                                else:
                                    t0 = sbu(us, [128, 512], F32)
                                    t1 = sbu(us, [128, 512], F32)
                                    rden1 = sbu(us, [128, 512], F32)
                                    osq = sbu(us, [128, 512], F32)
                            o_k = o_sbk if mixer == 'sb' else o_dk
                            o_v = o_sbv if mixer == 'sb' else o_dv

                            def proj_tile(tt, qcol0):
                                rows, sl = tsl(tt)
                                if mixer == 'mla':
                                    if u == 0:
                                        pz = mmbank()
                                        for k in range(8):
                                            S.op('pe', lambda e: e.matmul(out=pz[0:rows, 0:416], lhsT=HT[:, k, sl], rhs=w1[:, k, :],
                                                                          start=(k == 0), stop=(k == 7)), [HT.b[tt]] + w1.b, pz.b, inc=(k == 7))
                                        tk = ntok()
                                        gnorm(ws, pz[0:rows, 0:256].rearrange("p (g d) -> p g d", g=1), pz.b, rows, 1, 256, gs('q_norm', rows),
                                              tk[0:rows, 0:256].rearrange("p (g d) -> p g d", g=1), tk.b)
                                        for c in range(2):
                                            transpose_to(tk[0:rows, c * 128:(c + 1) * 128], tk.b, rows, 128, CQT[:, c, sl], [CQT.b[tt]],
                                                         'act' if c else 'dve')
                                        st = nstage()
                                        gnorm(ws, pz[0:rows, 256:384].rearrange("p (g d) -> p g d", g=1), pz.b, rows, 1, 128, gs('kv_norm', rows),
                                              st[0:rows, 0:128].rearrange("p (g d) -> p g d", g=1), st.b)
                                        S.dma('sp', o_ckv[l, sl, :], st[0:rows, 0:128], st.b, [])
                                        tk2 = ntok()
                                        S.op('act', lambda e: e.copy(out=tk2[0:rows, 0:128], in_=st[0:rows, 0:128]), st.b, tk2.b)
                                        transpose_to(tk2[0:rows, 0:128], tk2.b, rows, 128, CKVT[:, sl], [CKVT.b[tt]], 'dve')
                                        gnorm(ws, pz[0:rows, 384:416].rearrange("p (g d) -> p g d", g=1), pz.b, rows, 1, 32, gs('kr', rows),
                                              st[0:rows, 128:160].rearrange("p (g d) -> p g d", g=1), st.b)
                                        rope(ws, st[0:rows, 128:160].rearrange("p (g d) -> p g d", g=1), st.b, rows, 1, tt,
                                             st[0:rows, 160:192].rearrange("p (g d) -> p g d", g=1), st.b)
                                        S.dma('sp', o_kr[l, sl, :], st[0:rows, 160:192], st.b, [])
                                        S.op('act', lambda e: e.copy(out=KRT[0:rows, tt, :], in_=st[0:rows, 160:192]), st.b, [KRT.b[tt]])
                                    pq = mmbank()
                                    for c in range(2):
                                        S.op('pe', lambda e: e.matmul(out=pq[0:rows, 0:192], lhsT=CQT[:, c, sl], rhs=wuq[:, c, :],
                                                                      start=(c == 0), stop=(c == 1)), [CQT.b[tt]] + wuq.b, pq.b, inc=(c == 1))
                                    qv = pq[0:rows, 0:192].rearrange("p (h d) -> p h d", d=96)
                                    qc = qcat[tt % 2]
                                    gnorm(ws, qv[:, :, 0:64], pq.b, rows, 2, 64, gs('qn', rows), qc[0:rows, :, 0:64], qc.b, MLA_SCALE)
                                    st = nstage()
                                    stv = st[0:rows, 0:64].rearrange("p (h d) -> p h d", d=32)
                                    gnorm(ws, qv[:, :, 64:96], pq.b, rows, 2, 32, gs('qr', rows), stv, st.b, MLA_SCALE)
                                    rope(ws, stv, st.b, rows, 2, tt, qc[0:rows, :, 64:96], qc.b)
                                    for hh in range(2):
                                        transpose_to(qc[0:rows, hh, :], qc.b, rows, 96, QT[:, hh, qcol0:qcol0 + rows], QT.b, 'act' if hh else 'dve')
                                    kv_from_latent(CKVT[:, sl], [CKVT.b[tt]], KRT[0:rows, tt, :], [KRT.b[tt]], rows,
                                                   lambda hh: KT[:, hh, sl], [KT.b[tt]], V[0:rows, tt, :], [V.b[tt]], kcat[tt % 2])
                                else:
                                    pq = mmbank()
                                    for k in range(8):
                                        S.op('pe', lambda e: e.matmul(out=pq[0:rows, 0:256], lhsT=HT[:, k, sl], rhs=wq[:, k, :],
                                                                      start=(k == 0), stop=(k == 7)), [HT.b[tt]] + wq.b, pq.b, inc=(k == 7))
                                    tk = ntok()
                                    if mixer == 'sb':
                                        S.op('act', lambda e: e.activation(out=tk[0:rows, :], in_=pq[0:rows, 0:256], func=AF.Copy,
                                                                           scale=SB_SCALE), pq.b, tk.b)
                                    else:
                                        gnorm(ws, pq[0:rows, 0:256].rearrange("p (g d) -> p g d", d=64), pq.b, rows, 4, 64, gs('dqn', rows),
                                              tk[0:rows, :].rearrange("p (g d) -> p g d", d=64), tk.b, DIFF_SCALE)
                                    for c in range(2):
                                        transpose_to(tk[0:rows, c * 128:(c + 1) * 128], tk.b, rows, 128, QT[:, c, qcol0:qcol0 + rows], QT.b,
                                                     'act' if c else 'dve')
                                    pk_ = mmbank()
                                    for k in range(8):
                                        S.op('pe', lambda e: e.matmul(out=pk_[0:rows, 0:256], lhsT=HT[:, k, sl], rhs=wk[:, k, :],
                                                                      start=(k == 0), stop=(k == 7)), [HT.b[tt]] + wk.b, pk_.b, inc=(k == 7))
                                    st = nstage()
                                    if mixer == 'sb':
                                        S.op('act', lambda e: e.copy(out=st[0:rows, :], in_=pk_[0:rows, 0:256]), pk_.b, st.b)
                                    else:
                                        gnorm(ws, pk_[0:rows, 0:256].rearrange("p (g d) -> p g d", d=64), pk_.b, rows, 4, 64, gs('dkn', rows),
                                              st[0:rows, :].rearrange("p (g d) -> p g d", d=64), st.b)
                                    S.dma('sp', o_k[l, sl, u * 256:(u + 1) * 256], st[0:rows, :], st.b, [])
                                    tk = ntok()
                                    S.op('dve', lambda e: e.tensor_copy(out=tk[0:rows, :], in_=st[0:rows, :]), st.b, tk.b)
                                    for c in range(2):
                                        transpose_to(tk[0:rows, c * 128:(c + 1) * 128], tk.b, rows, 128, KT[:, c, sl], [KT.b[tt]],
                                                     'act' if c else 'dve')
                                    pv_ = mmbank()
                                    for k in range(8):
                                        S.op('pe', lambda e: e.matmul(out=pv_[0:rows, 0:256], lhsT=HT[:, k, sl], rhs=wv[:, k, :],
                                                                      start=(k == 0), stop=(k == 7)), [HT.b[tt]] + wv.b, pv_.b, inc=(k == 7))
                                    st = nstage()
                                    S.op('act', lambda e: e.copy(out=st[0:rows, :], in_=pv_[0:rows, 0:256]), pv_.b, st.b)
                                    S.dma('sp', o_v[l, sl, u * 256:(u + 1) * 256], st[0:rows, :], st.b, [])
                                    S.op('dve', lambda e: e.tensor_copy(out=V[0:rows, tt, :], in_=pv_[0:rows, 0:256]), pv_.b, [V.b[tt]])

                            def kv_from_latent(ckvT_ap, ckvT_b, kr_ap, kr_b, rows, kt_dst, kt_b, v_dst, v_b, kc):
                                pk_ = mmbank()
                                S.op('pe', lambda e: e.matmul(out=pk_[0:rows, 0:128], lhsT=ckvT_ap, rhs=wuk[:, :], start=True, stop=True),
                                     list(ckvT_b) + wuk.b, pk_.b)
                                gnorm(ws, pk_[0:rows, 0:128].rearrange("p (h d) -> p h d", d=64), pk_.b, rows, 2, 64, gs('kn', rows),
                                      kc[0:rows, :, 0:64], kc.b)
                                S.op('dve', lambda e: e.tensor_copy(out=kc[0:rows, :, 64:96], in_=kr_ap.unsqueeze(1).to_broadcast([rows, 2, 32])),
                                     list(kr_b), kc.b)
                                for hh in range(2):
                                    transpose_to(kc[0:rows, hh, :], kc.b, rows, 96, kt_dst(hh), kt_b, 'act' if hh else 'dve')
                                pv_ = mmbank()
                                S.op('pe', lambda e: e.matmul(out=pv_[0:rows, 0:128], lhsT=ckvT_ap, rhs=wuv[:, :], start=True, stop=True),
                                     list(ckvT_b) + wuv.b, pv_.b)
                                S.op('act', lambda e: e.copy(out=v_dst, in_=pv_[0:rows, 0:128]), pv_.b, v_b)

                            acc, accd, acc1, accd1 = PS[4], PS[5], PS[6], PS[7]

                            def zero_acc(t_, ncols):
                                S.op('pe', lambda e: e.matmul(out=t_[:, 0:ncols], lhsT=zeros[:, :], rhs=zrhs[:, 0:ncols], start=True, stop=True),
                                     zeros.b + zrhs.b, t_.b)

                            acc1s = [(PS[6], PS[7]), (PS[2], PS[3])]
                            itc = [0]

                            def att_begin(ncols):
                                if mixer == 'mla':
                                    zero_acc(acc, ncols)
                                    zero_acc(accd, ncols)
                                elif mixer == 'sb':
                                    zero_acc(acc, ncols)
                                    S.op('dve', lambda e: e.memset(R[:, :, 0:ncols], 0.0), [], R.b)
                                else:
                                    for t_ in (acc, accd, acc1, accd1):
                                        zero_acc(t_, ncols)

                            def diag_mask(t_, mask_ap, mask_b, nk, c0, ncols):
                                dn = min(128, ncols - c0)
                                S.op('dve', lambda e: e.tensor_tensor(out=t_[0:nk, c0:c0 + dn], in0=t_[0:nk, c0:c0 + dn], in1=mask_ap(nk, dn),
                                                                      op=ALU.mult), t_.b + mask_b, t_.b)

                            def stage_a(hg, kb, x, qc0, ncols, sample):
                                nk, c0 = kb['nk'], kb['c0']
                                st = mmbank()
                                if mixer == 'mla':
                                    S.op('pe', lambda e: e.matmul(out=st[0:nk, c0:ncols], lhsT=kb['kt'](x), rhs=QT[:, x, qc0 + c0:qc0 + ncols],
                                                                  start=True, stop=True), kb['ktb'] + QT.b, st.b)
                                    pt = npt()
                                    S.op('act', lambda e: e.activation(out=pt[0:nk, c0:ncols], in_=st[0:nk, c0:ncols], func=AF.Exp), st.b, pt.b)
                                    if kb['diag']:
                                        diag_mask(pt, lambda a_, b_: mch[0:a_, 0:b_], mch.b, nk, c0, ncols)
                                    return dict(pt=pt)
                                if mixer == 'sb':
                                    hp = hg
                                    S.op('pe', lambda e: e.matmul(out=st[0:nk, c0:ncols], lhsT=kb['kt'](hp)[64 * x:64 * x + 64, :],
                                                                  rhs=QT[64 * x:64 * x + 64, hp, qc0 + c0:qc0 + ncols], start=True, stop=True),
                                         kb['ktb'] + QT.b, st.b)
                                    itc[0] += 1
                                    ee = e1[itc[0] % 3]
                                    sp_ = spb[itc[0] % 3]
                                    a1, ad1 = acc1s[itc[0] % 2]
                                    S.op('act', lambda e: e.activation(out=ee[0:nk, c0:ncols], in_=st[0:nk, c0:ncols], func=AF.Exp), st.b, ee.b)
                                    S.op('act', lambda e: e.activation(out=sp_[0:nk, c0:ncols], in_=ee[0:nk, c0:ncols], func=AF.Ln, bias=1.0),
                                         ee.b, sp_.b)
                                    if kb['diag']:
                                        diag_mask(sp_, lambda a_, b_: msb[0:a_, 0:b_], msb.b, nk, c0, ncols)
                                    return dict(ee=ee, sp=sp_, a1=a1, ad1=ad1, cu=cumr[itc[0] % 2])
                                hd = hg
                                hgl = 2 * u + hd
                                S.op('pe', lambda e: e.matmul(out=st[0:nk, c0:ncols], lhsT=kb['kt'](hd)[64 * x:64 * x + 64, :],
                                                              rhs=QT[64 * x:64 * x + 64, hd, qc0 + c0:qc0 + ncols], start=True, stop=True),
                                     kb['ktb'] + QT.b, st.b)
                                pt = npt()
                                if sample:
                                    bi = kb['bi']
                                    S.op('act', lambda e: e.activation(out=pt[0:nk, c0:ncols], in_=st[0:nk, c0:ncols], func=AF.Exp,
                                                                       bias=bds[0:nk, hgl, bi:bi + 1]), st.b + bds.b, pt.b)
                                else:
                                    cc = c0
                                    while cc < ncols:
                                        ce = min(ncols, (cc // 256 + 1) * 256)
                                        dl = (kb['qt0'] + ce // 128 - 1) - kb['kbi']
                                        S.op('act', lambda e: e.activation(out=pt[0:nk, cc:ce], in_=st[0:nk, cc:ce], func=AF.Exp,
                                                                           bias=bdp[0:nk, hgl, dl:dl + 1]), st.b + bdp.b, pt.b)
                                        cc = ce
                                if kb['diag']:
                                    diag_mask(pt, lambda a_, b_: mdf[0:a_, hgl, 0:b_], mdf.b, nk, c0, ncols)
                                return dict(pt=pt)

                            def stage_b(hg, kb, x, ncols, acol, cx):
                                nk, c0 = kb['nk'], kb['c0']
                                o0, o1 = acol + c0, acol + ncols
                                if mixer == 'mla':
                                    pt = cx['pt']
                                    S.op('pe', lambda e: e.matmul(out=acc[64 * x:64 * x + 64, o0:o1], lhsT=kb['v'](x),
                                                                  rhs=pt[0:nk, c0:ncols], start=False, stop=True, skip_group_check=True),
                                         kb['vb'] + pt.b, acc.b)
                                    S.op('pe', lambda e: e.matmul(out=accd[64 * x:64 * x + 64, o0:o1], lhsT=ones[0:nk, 0:64],
                                                                  rhs=pt[0:nk, c0:ncols], start=False, stop=True, skip_group_check=True),
                                         ones.b + pt.b, accd.b)
                                elif mixer == 'sb':
                                    hp = hg
                                    ee, a1, ad1, cu = cx['ee'], cx['a1'], cx['ad1'], cx['cu']
                                    S.op('dve', lambda e: e.tensor_tensor(out=cu[0:nk, c0:ncols], in0=a1[0:nk, c0:ncols],
                                                                          in1=R[0:nk, x, o0:o1], op=ALU.add), a1.b + R.b, cu.b)
                                    S.op('act', lambda e: e.activation(out=cu[0:nk, c0:ncols], in_=cu[0:nk, c0:ncols], func=AF.Exp,
                                                                       scale=-1.0), cu.b, cu.b)
                                    pt = npt()
                                    S.op('dve', lambda e: e.tensor_tensor(out=pt[0:nk, c0:ncols], in0=ee[0:nk, c0:ncols],
                                                                          in1=cu[0:nk, c0:ncols], op=ALU.mult), ee.b + cu.b, pt.b)
                                    if kb['diag']:
                                        diag_mask(pt, lambda a_, b_: msb[0:a_, 0:b_], msb.b, nk, c0, ncols)
                                    S.op('dve', lambda e: e.tensor_tensor(out=R[:, x, o0:o1], in0=R[:, x, o0:o1], in1=ad1[:, c0:ncols],
                                                                          op=ALU.add), R.b + ad1.b, R.b)
                                    S.op('pe', lambda e: e.matmul(out=acc[64 * x:64 * x + 64, o0:o1], lhsT=kb['v'](2 * hp + x),
                                                                  rhs=pt[0:nk, c0:ncols], start=False, stop=True, skip_group_check=True),
                                         kb['vb'] + pt.b, acc.b)
                                else:
                                    pt = cx['pt']
                                    a_, d_ = (acc, accd) if x == 0 else (acc1, accd1)
                                    S.op('pe', lambda e: e.matmul(out=a_[:, o0:o1], lhsT=kb['v'](hg), rhs=pt[0:nk, c0:ncols],
                                                                  start=False, stop=True, skip_group_check=True), kb['vb'] + pt.b, a_.b)
                                    S.op('pe', lambda e: e.matmul(out=d_[:, o0:o1], lhsT=ones[0:nk, :], rhs=pt[0:nk, c0:ncols],
                                                                  start=False, stop=True, skip_group_check=True), ones.b + pt.b, d_.b)

                            def stage_a1(kb, ncols, cx):
                                nk, c0 = kb['nk'], kb['c0']
                                sp_, a1, ad1 = cx['sp'], cx['a1'], cx['ad1']
                                S.op('pe', lambda e: e.matmul(out=a1[0:nk, c0:ncols], lhsT=tri[0:nk, 0:nk], rhs=sp_[0:nk, c0:ncols],
                                                              start=True, stop=True), tri.b + sp_.b, a1.b)
                                S.op('pe', lambda e: e.matmul(out=ad1[:, c0:ncols], lhsT=ones[0:nk, :], rhs=sp_[0:nk, c0:ncols],
                                                              start=True, stop=True), ones.b + sp_.b, ad1.b)

                            def att_blocks(hg, kbs, qc0, ncols, sample, acol=0):
                                items = [(kb, x) for kb in kbs for x in range(2)]
                                n = len(items)
                                dep = 1 if mixer == 'diff' else 2
                                ctx = {}
                                if mixer != 'sb':
                                    prevb = []
                                    for i0 in range(0, n, 2):
                                        cur = [(t, stage_a(hg, items[t][0], items[t][1], qc0, ncols, sample)) for t in range(i0, min(n, i0 + 2))]
                                        for (t, cx) in prevb:
                                            stage_b(hg, items[t][0], items[t][1], ncols, acol, cx)
                                        prevb = cur
                                    for (t, cx) in prevb:
                                        stage_b(hg, items[t][0], items[t][1], ncols, acol, cx)
                                    return
                                for t in range(n + dep):
                                    if t < n:
                                        ctx[t] = stage_a(hg, items[t][0], items[t][1], qc0, ncols, sample)
                                    if mixer == 'sb' and 0 <= t - 1 < n:
                                        stage_a1(items[t - 1][0], ncols, ctx[t - 1])
                                    if 0 <= t - dep < n:
                                        stage_b(hg, items[t - dep][0], items[t - dep][1], ncols, acol, ctx.pop(t - dep))

                            def att_end(hg, ncols, ocol0, grp_b, acol=0):
                                a0, a1_ = acol, acol + ncols
                                if mixer == 'mla':
                                    S.op('dve', lambda e: e.reciprocal(out=rden[:, 0:ncols], in_=accd[:, a0:a1_]), accd.b, rden.b)
                                    S.op('dve', lambda e: e.tensor_tensor(out=OT[:, u, ocol0:ocol0 + ncols], in0=acc[:, a0:a1_], in1=rden[:, 0:ncols],
                                                                          op=ALU.mult), acc.b + rden.b, grp_b)
                                elif mixer == 'sb':
                                    S.op('act', lambda e: e.copy(out=OT[:, 2 * u + hg, ocol0:ocol0 + ncols], in_=acc[:, a0:a1_]), acc.b, grp_b)
                                else:
                                    hgl = 2 * u + hg
                                    S.op('dve', lambda e: e.reciprocal(out=rden[:, 0:ncols], in_=accd[:, a0:a1_]), accd.b, rden.b)
                                    S.op('dve', lambda e: e.reciprocal(out=rden1[:, 0:ncols], in_=accd1[:, a0:a1_]), accd1.b, rden1.b)
                                    S.op('dve', lambda e: e.tensor_tensor(out=t0[:, 0:ncols], in0=acc[:, a0:a1_], in1=rden[:, 0:ncols], op=ALU.mult),
                                         acc.b + rden.b, t0.b)
                                    S.op('dve', lambda e: e.tensor_tensor(out=t1[:, 0:ncols], in0=acc1[:, a0:a1_], in1=rden1[:, 0:ncols], op=ALU.mult),
                                         acc1.b + rden1.b, t1.b)
                                    S.op('dve', lambda e: e.scalar_tensor_tensor(out=t0[:, 0:ncols], in0=t1[:, 0:ncols], scalar=gcol[:, 2:3],
                                                                                 in1=t0[:, 0:ncols], op0=ALU.mult, op1=ALU.add),
                                         t0.b + t1.b + gcol.b, t0.b)
                                    S.op('act', lambda e: e.activation(out=osq[:, 0:ncols], in_=t0[:, 0:ncols], func=AF.Square), t0.b, osq.b)
                                    pss = mmbank()
                                    S.op('pe', lambda e: e.matmul(out=pss[:, 0:ncols], lhsT=onesf[:, :], rhs=osq[:, 0:ncols], start=True, stop=True),
                                         onesf.b + osq.b, pss.b)
                                    S.op('act', lambda e: e.activation(out=t1[:, 0:ncols], in_=pss[:, 0:ncols], func=AF.Ln, scale=1.0 / 128, bias=EPS),
                                         pss.b, t1.b)
                                    S.op('act', lambda e: e.activation(out=t1[:, 0:ncols], in_=t1[:, 0:ncols], func=AF.Exp, scale=-0.5), t1.b, t1.b)
                                    S.op('dve', lambda e: e.scalar_tensor_tensor(out=OT[:, hgl, ocol0:ocol0 + ncols], in0=t0[:, 0:ncols],
                                                                                 scalar=gcol[:, 0:1], in1=t1[:, 0:ncols], op0=ALU.mult, op1=ALU.mult),
                                         t0.b + t1.b + gcol.b, grp_b)

                            nhg = 1 if mixer == 'mla' else 2

                            def kb_store(kbi, c0, diag, qt0, nk=128, bi=0):
                                rows_, sl = tsl(kbi)
                                if mixer == 'mla':
                                    return dict(kt=lambda hh: KT[:, hh, sl], ktb=[KT.b[kbi]], v=lambda hh: V[0:nk, kbi, 64 * hh:64 * hh + 64],
                                                vb=[V.b[kbi]], nk=nk, c0=c0, diag=diag, kbi=kbi, qt0=qt0, bi=bi)
                                if mixer == 'sb':
                                    return dict(kt=lambda hp: KT[:, hp, sl], ktb=[KT.b[kbi]], v=lambda h: V[0:nk, kbi, 64 * h:64 * h + 64],
                                                vb=[V.b[kbi]], nk=nk, c0=c0, diag=diag, kbi=kbi, qt0=qt0, bi=bi)
                                return dict(kt=lambda hd: KT[:, hd, sl], ktb=[KT.b[kbi]], v=lambda hd: V[0:nk, kbi, 128 * hd:128 * hd + 128],
                                            vb=[V.b[kbi]], nk=nk, c0=c0, diag=diag, kbi=kbi, qt0=qt0, bi=bi)

                            chi = [0]

                            def kb_past(j, kbi, cs_):
                                sl = slice(j * 128, (j + 1) * 128)
                                bi = 32 - kbi
                                KTp, Vp = cs_['KTp'], cs_['Vp']
                                if mixer == 'mla':
                                    return dict(kt=lambda hh: KTp[:, hh, sl], ktb=KTp.b, v=lambda hh: Vp[:, j, 64 * hh:64 * hh + 64],
                                                vb=Vp.b, nk=128, c0=0, diag=False, kbi=kbi, qt0=0, bi=bi)
                                if mixer == 'sb':
                                    return dict(kt=lambda hp: KTp[:, hp, sl], ktb=KTp.b, v=lambda h: Vp[:, j, 64 * h:64 * h + 64],
                                                vb=Vp.b, nk=128, c0=0, diag=False, kbi=kbi, qt0=0, bi=bi)
                                return dict(kt=lambda hd: KTp[:, hd, sl], ktb=KTp.b, v=lambda hd: Vp[:, j, 128 * hd:128 * hd + 128],
                                            vb=Vp.b, nk=128, c0=0, diag=False, kbi=kbi, qt0=0, bi=bi)

                            def build_chunk(s, ch):
                                r0 = ch * 512
                                chi[0] += 1
                                cs_ = CS[chi[0] % 2]
                                if mixer == 'mla':
                                    pck, pkr, pckT, KTp, Vp, kc4 = cs_['pck'], cs_['pkr'], cs_['pckT'], cs_['KTp'], cs_['Vp'], cs_['kc4']
                                    S.dma('pool', pck[:], c_ckv[l, s, r0:r0 + 512, :].rearrange("(k p) n -> p k n", p=128), [], pck.b)
                                    S.dma('pool', pkr[:], c_kr[l, s, r0:r0 + 512, :].rearrange("(k p) n -> p k n", p=128), [], pkr.b)
                                    for j in range(4):
                                        transpose_to(pck[:, j, :], pck.b, 128, 128, pckT[:, j * 128:(j + 1) * 128], pckT.b, 'act' if j % 2 else 'dve')
                                    pk_ = mmbank()
                                    for j in range(4):
                                        S.op('pe', lambda e: e.matmul(out=pk_[:, j * 128:(j + 1) * 128], lhsT=pckT[:, j * 128:(j + 1) * 128], rhs=wuk[:, :],
                                                                      start=True, stop=True), pckT.b + wuk.b, pk_.b)
                                    gnorm(ws, pk_[:, 0:512].rearrange("p (g d) -> p g d", d=64), pk_.b, 128, 8, 64, gs('kn', 128),
                                          kc4[:, :, :, 0:64].rearrange("p j h d -> p (j h) d"), kc4.b)
                                    S.op('dve', lambda e: e.tensor_copy(out=kc4[:, :, :, 64:96], in_=pkr[:, :, :].unsqueeze(2).to_broadcast([128, 4, 2, 32])),
                                         pkr.b, kc4.b)
                                    for j in range(4):
                                        for hh in range(2):
                                            transpose_to(kc4[:, j, hh, :], kc4.b, 128, 96, KTp[:, hh, j * 128:(j + 1) * 128], KTp.b, 'act' if hh else 'dve')
                                    pv_ = mmbank()
                                    for j in range(4):
                                        S.op('pe', lambda e: e.matmul(out=pv_[:, j * 128:(j + 1) * 128], lhsT=pckT[:, j * 128:(j + 1) * 128], rhs=wuv[:, :],
                                                                      start=True, stop=True), pckT.b + wuv.b, pv_.b)
                                    S.op('act', lambda e: e.copy(out=Vp[:, :, :].rearrange("p j n -> p (j n)"), in_=pv_[:, 0:512]), pv_.b, Vp.b)
                                else:
                                    pk, KTp, Vp = cs_['pk'], cs_['KTp'], cs_['Vp']
                                    ck = c_sbk if mixer == 'sb' else c_dk
                                    cv = c_sbv if mixer == 'sb' else c_dv
                                    S.dma('pool', pk[:], ck[l, s, r0:r0 + 512, u * 256:(u + 1) * 256].rearrange("(k p) n -> p k n", p=128), [], pk.b)
                                    S.dma('pool', Vp[:], cv[l, s, r0:r0 + 512, u * 256:(u + 1) * 256].rearrange("(k p) n -> p k n", p=128), [], Vp.b)
                                    for j in range(4):
                                        for c in range(2):
                                            transpose_to(pk[:, j, c * 128:(c + 1) * 128], pk.b, 128, 128, KTp[:, c, j * 128:(j + 1) * 128], KTp.b,
                                                         'act' if c else 'dve')
                                return cs_

                            for g in range(4):
                                for i in range(4):
                                    proj_tile(4 * g + i, i * 128)
                                kbs = []
                                for kbi in range(4 * g + 4):
                                    i = kbi - 4 * g
                                    kbs.append(kb_store(kbi, max(i, 0) * 128, i >= 0, 4 * g))
                                if mixer == 'sb':
                                    kbs = kbs[::-1]
                                for hg in range(nhg):
                                    att_begin(512)
                                    att_blocks(hg, kbs, 0, 512, False)
                                    att_end(hg, 512, g * 512, [OT.b[g]])
                                _ck('%s%d_u%d_g%d' % (mixer, l, u, g))

                            for s in range(NS):
                                proj_tile(16 + s, 0)
                                knew = kb_store(16 + s, 0, mixer != 'mla', 0, nk=32, bi=0)
                                att_begin(32 * nhg)
                                if mixer == 'sb':
                                    for hg in range(nhg):
                                        att_blocks(hg, [knew], 0, 32, True, acol=32 * hg)
                                    for ch in range(7, -1, -1):
                                        cs_ = build_chunk(s, ch)
                                        for hg in range(nhg):
                                            att_blocks(hg, [kb_past(j, ch * 4 + j, cs_) for j in range(3, -1, -1)], 0, 32, True, acol=32 * hg)
                                else:
                                    for ch in range(8):
                                        cs_ = build_chunk(s, ch)
                                        for hg in range(nhg):
                                            att_blocks(hg, [kb_past(j, ch * 4 + j, cs_) for j in range(4)], 0, 32, True, acol=32 * hg)
                                    for hg in range(nhg):
                                        att_blocks(hg, [knew], 0, 32, True, acol=32 * hg)
                                for hg in range(nhg):
                                    att_end(hg, 32, 2048 + 32 * s, [OT.b[4]], acol=32 * hg)
                                _ck('%s%d_u%d_s%d' % (mixer, l, u, s))
                            if u == nunits - 1:
                                S.barrier()
                            mm_list[0] = [0, 1]
                            rot['mm'] = 0
                            _ck('%s%d_u%d' % (mixer, l, u))

                    mi = ('mla', 'sb', 'diff').index(mixer)
                    uso.close()
                    ms.close()
                    with ExitStack() as gs_:
                        mm_list[0] = [0, 1, 4, 5, 6, 7]
                        wg = sb(gs_, [128, 8, 1024], BF16)
                        g0 = OFF['g'] + 1024 * mi
                        load_w(wg, W['w_in'][l, :, g0:g0 + 1024].rearrange("(k p) n -> p k n", p=128))
                        wbr = sb(gs_, [128, 4, 1024], BF16)
                        load_w(wbr, W['w_br_' + mixer][l].rearrange("(k p) n -> p k n", p=128))
                        wo = sb(gs_, [128, 8, 1024], BF16)
                        load_w(wo, W['w_out'][l].rearrange("(k p) n -> p k n", p=128))
                        gT = [sb(gs_, [128, 512], F32) for _ in range(2)]
                        MG = [sb(gs_, [128, 8, 512], BF16) for _ in range(1)]
                        for gi, (c0, n, tiles) in enumerate(GROUPS):
                            hb_ = [HT.b[t] for t in tiles]
                            mg = MG[0]
                            for nn in range(8):
                                nsl = slice(nn * 128, (nn + 1) * 128)
                                pg = mmbank()
                                for k in range(8):
                                    S.op('pe', lambda e: e.matmul(out=pg[:, 0:n], lhsT=wg[:, k, nsl], rhs=HT[:, k, c0:c0 + n],
                                                                  start=(k == 0), stop=(k == 7)), wg.b + hb_, pg.b, inc=(k == 7))
                                gt = gT[nn % 2]
                                S.op('act', lambda e: e.activation(out=gt[:, 0:n], in_=pg[:, 0:n], func=AF.Sigmoid), pg.b, gt.b)
                                py = mmbank()
                                for c in range(4):
                                    S.op('pe', lambda e: e.matmul(out=py[:, 0:n], lhsT=wbr[:, c, nsl], rhs=OT[:, c, c0:c0 + n],
                                                                  start=(c == 0), stop=(c == 3)), wbr.b + [OT.b[gi]], py.b, inc=(c == 3))
                                S.op('dve', lambda e: e.tensor_tensor(out=mg[:, nn, 0:n], in0=py[:, 0:n], in1=gt[:, 0:n], op=ALU.mult),
                                     py.b + gt.b, mg.b)
                            for tt in tiles:
                                rows, sl = tsl(tt)
                                off = sl.start - c0
                                for half in range(2):
                                    hsl = slice(half * 512, (half + 1) * 512)
                                    po = mmbank()
                                    for k in range(8):
                                        S.op('pe', lambda e: e.matmul(out=po[0:rows, :], lhsT=mg[:, k, off:off + rows], rhs=wo[:, k, hsl],
                                                                      start=(k == 0), stop=(k == 7)), mg.b + wo.b, po.b, inc=(k == 7))
                                    S.op('dve', lambda e: e.tensor_tensor(out=X[0:rows, tt, hsl], in0=X[0:rows, tt, hsl], in1=po[0:rows, :],
                                                                          op=ALU.add), [X.b[tt]] + po.b, [X.b[tt]])
                        S.barrier()
                        mm_list[0] = [0, 1]
                        rot['mm'] = 0
                    _ck('%s%d_merge' % (mixer, l))

            norm_phase(l, 'ffn_norm_g', False)
            with ExitStack() as fs:
                mm_list[0] = [0, 1, 4, 5, 6, 7]
                cw = sb(fs, [128, 22, 3], F32)
                cb = sb(fs, [128, 22], F32)
                cst = sb(fs, [128, NS, 22, 2], F32)
                OC = sb(fs, [128, 3, 22, 2], F32)
                S.dma('sp', cw[:].rearrange("p a b -> p (a b)"), W['ffn_conv_w'][l], [], cw.b)
                S.dma('sp', cb[:], W['ffn_conv_b'][l], [], cb.b)
                for s in range(NS):
                    S.dma('sp', cst[:, s, :, :].rearrange("p a b -> p (a b)"), c_conv[l, s], [], cst.b)
                WA = [sb(fs, [128, 8, 512], BF16) for _ in range(2)]
                WU = [sb(fs, [128, 8, 512], BF16) for _ in range(2)]
                WD = [sb(fs, [128, 4, 1024], BF16) for _ in range(2)]
                AT = [sb(fs, [128, 4, 516], F32, 4) for _ in range(1)]
                carry = sb(fs, [128, 4, 2], F32, 4)
                cc_ = [sb(fs, [128, 512], F32) for _ in range(1)]
                sl_ = [sb(fs, [128, 512], F32) for _ in range(1)]
                MM = [sb(fs, [128, 4, 512], BF16) for _ in range(1)]
                fgroups = [(0, 4), (4, 4), (8, 4), (12, 4), (16, 4), (20, 2)]
                for fi, (fc0, nfc) in enumerate(fgroups):
                    wa, wu, wd = WA[fi % 2], WU[fi % 2], WD[fi % 2]
                    nf = nfc * 128
                    S.dma('pool', wa[:, :, 0:nf], W['ffn_w_up'][l, :, fc0 * 128:fc0 * 128 + nf].rearrange("(k p) n -> p k n", p=128), [], wa.b)
                    S.dma('pool', wu[:, :, 0:nf], W['ffn_w_up'][l, :, DFF + fc0 * 128:DFF + fc0 * 128 + nf].rearrange("(k p) n -> p k n", p=128),
                          [], wu.b)
                    S.dma('pool', wd[:, 0:nfc, :], W['ffn_w_down'][l, fc0 * 128:fc0 * 128 + nf, :].rearrange("(c p) n -> p c n", p=128), [], wd.b)
                    for gi, (c0, n, tiles) in enumerate(GROUPS):
                        hb_ = [HT.b[t] for t in tiles]
                        mmt = MM[0]
                        buf = AT[0]
                        for j in range(nfc):
                            fc = fc0 + j
                            jsl = slice(j * 128, (j + 1) * 128)
                            pa = mmbank()
                            for k in range(8):
                                S.op('pe', lambda e: e.matmul(out=pa[:, 0:n], lhsT=wa[:, k, jsl], rhs=HT[:, k, c0:c0 + n],
                                                              start=(k == 0), stop=(k == 7)), wa.b + hb_, pa.b, inc=(k == 7))
                            cc = cc_[0]
                            if gi < 4:
                                S.op('act', lambda e: e.copy(out=buf[:, j, 2:514], in_=pa[:, 0:512]), pa.b, [buf.b[j]])
                                if gi == 0:
                                    S.op('dve', lambda e: e.memset(buf[:, j, 0:2], 0.0), [], [buf.b[j]])
                                else:
                                    S.op('dve', lambda e: e.tensor_copy(out=buf[:, j, 0:2], in_=carry[:, j, :]), [carry.b[j]], [buf.b[j]])
                                S.op('dve', lambda e: e.tensor_copy(out=carry[:, j, :], in_=buf[:, j, 512:514]), [buf.b[j]], [carry.b[j]])
                                segs = [(0, 0, 512)]
                                if gi == 3:
                                    S.op('dve', lambda e: e.tensor_copy(out=OC[:, 0, fc, :], in_=buf[:, j, 512:514]), [buf.b[j]], OC.b)
                            else:
                                for s in range(NS):
                                    S.op('act', lambda e: e.copy(out=buf[:, j, s * 34 + 2:s * 34 + 34], in_=pa[:, s * 32:s * 32 + 32]), pa.b, [buf.b[j]])
                                    S.op('dve', lambda e: e.tensor_copy(out=buf[:, j, s * 34:s * 34 + 2], in_=cst[:, s, fc, :]), cst.b, [buf.b[j]])
                                    S.op('dve', lambda e: e.tensor_copy(out=OC[:, 1 + s, fc, :], in_=buf[:, j, s * 34 + 32:s * 34 + 34]), [buf.b[j]], OC.b)
                                segs = [(0, 0, 32), (34, 32, 32)]
                            for (b0, o0, nn_) in segs:
                                S.op('dve', lambda e: e.tensor_scalar(out=cc[:, o0:o0 + nn_], in0=buf[:, j, b0 + 2:b0 + 2 + nn_], scalar1=cw[:, fc, 2:3],
                                                                      scalar2=cb[:, fc:fc + 1], op0=ALU.mult, op1=ALU.add),
                                     [buf.b[j]] + cw.b + cb.b, cc.b)
                                S.op('dve', lambda e: e.scalar_tensor_tensor(out=cc[:, o0:o0 + nn_], in0=buf[:, j, b0 + 1:b0 + 1 + nn_], scalar=cw[:, fc, 1:2],
                                                                             in1=cc[:, o0:o0 + nn_], op0=ALU.mult, op1=ALU.add),
                                     [buf.b[j]] + cw.b + cc.b, cc.b)
                                S.op('dve', lambda e: e.scalar_tensor_tensor(out=cc[:, o0:o0 + nn_], in0=buf[:, j, b0:b0 + nn_], scalar=cw[:, fc, 0:1],
                                                                             in1=cc[:, o0:o0 + nn_], op0=ALU.mult, op1=ALU.add),
                                     [buf.b[j]] + cw.b + cc.b, cc.b)
                            sl2 = sl_[0]
                            S.op('act', lambda e: e.activation(out=sl2[:, 0:n], in_=cc[:, 0:n], func=AF.Silu), cc.b, sl2.b)
                            pu = mmbank()
                            for k in range(8):
                                S.op('pe', lambda e: e.matmul(out=pu[:, 0:n], lhsT=wu[:, k, jsl], rhs=HT[:, k, c0:c0 + n],
                                                              start=(k == 0), stop=(k == 7)), wu.b + hb_, pu.b, inc=(k == 7))
                            S.op('dve', lambda e: e.tensor_tensor(out=mmt[:, j, 0:n], in0=pu[:, 0:n], in1=sl2[:, 0:n], op=ALU.mult),
                                 pu.b + sl2.b, mmt.b)
                        for tt in tiles:
                            rows, sl = tsl(tt)
                            off = sl.start - c0
                            for half in range(2):
                                hsl = slice(half * 512, (half + 1) * 512)
                                po = mmbank()
                                for j in range(nfc):
                                    S.op('pe', lambda e: e.matmul(out=po[0:rows, :], lhsT=mmt[:, j, off:off + rows], rhs=wd[:, j, hsl],
                                                                  start=(j == 0), stop=(j == nfc - 1)), mmt.b + wd.b, po.b, inc=(j == nfc - 1))
                                S.op('dve', lambda e: e.tensor_tensor(out=X[0:rows, tt, hsl], in0=X[0:rows, tt, hsl], in1=po[0:rows, :],
                                                                      op=ALU.add), [X.b[tt]] + po.b, [X.b[tt]])
                S.dma('sp', o_conv[l].rearrange("s p n -> p s n"), OC[:].rearrange("p s a b -> p s (a b)"), OC.b, [])
                S.barrier()
                mm_list[0] = [0, 1]
                rot['mm'] = 0
            _ck('ffn%d' % l)

        _DEV['off'] = False
        _DEV['nops'] = 0
        _layers()
        _DEV['off'] = False
        mm_list[0] = [0, 1]
        for tt in range(NT):
            rows, sl = tsl(tt)
            S.dma('sp', y[sl, :], X[0:rows, tt, :], [X.b[tt]], [])
        S.barrier()
    return nc


_SLOPES = [2.0 ** (-8.0 * (h + 1) / 4) for h in range(4)]


def _consts():
    half = 16
    inv = (np.float32(10000.0) ** (-np.arange(half, dtype=np.float32) / np.float32(half))).astype(np.float32)
    pos = np.concatenate([np.arange(SP_), PAST + np.arange(SS), PAST + np.arange(SS)]).astype(np.float32)
    ang = (pos[:, None] * inv[None, :]).astype(np.float32)
    k_cs = np.concatenate([np.cos(ang), np.sin(ang)], axis=1).astype(np.float32)
    k = np.arange(128)[:, None]
    q = np.arange(128)[None, :]
    msb = (k < q).astype(np.float32)
    mch = ((k // 64) <= (q // 64)).astype(np.float32)
    mdf = np.zeros((128, 4, 128), np.float32)
    bdp = np.zeros((128, 4, 17), np.float32)
    bds = np.zeros((128, 4, 33), np.float32)
    for h in range(4):
        sl = _SLOPES[h]
        mdf[:, h, :] = mch * np.where(k > q, np.exp(-2.0 * sl * (k - q)), 1.0)
        for d in range(17):
            bdp[:, h, d] = sl * (np.arange(128) - 127 - 128 * d)
        for j in range(33):
            bds[:, h, j] = sl * (np.arange(128) - 31 - 128 * j)
    tri = (k >= q).astype(np.float32)
    return dict(k_cs=k_cs, k_msb=msb, k_mch=mch, k_mdf=mdf.reshape(128, 512), k_tri=tri,
                k_bdp=bdp.reshape(128, 68), k_bds=bds.reshape(128, 132))


_WNAMES = ["mix_norm_g", "w_in", "mla_q_norm_g", "mla_w_uq", "mla_kv_norm_g", "mla_w_uk", "mla_w_uv", "mla_qn_g", "mla_kn_g",
           "mla_qr_g", "mla_kr_g", "diff_qn_g", "diff_kn_g", "diff_lambda", "diff_subln_g", "w_br_mla", "w_br_sb", "w_br_diff",
           "w_out", "ffn_norm_g", "ffn_w_up", "ffn_conv_w", "ffn_conv_b", "ffn_w_down"]


def kernel(**inputs):
    inp = {k: np.asarray(v) for k, v in inputs.items()}
    nc = build_program()
    shared = {}
    for n in _WNAMES:
        a = np.ascontiguousarray(inp[n], dtype=np.float32)
        if n == "diff_lambda":
            a = a.reshape(L, 256)
        elif n == "ffn_conv_w":
            a = np.ascontiguousarray(a.reshape(L, 3, 22, 128).transpose(0, 3, 2, 1)).reshape(L, 128, 66)
        elif n == "ffn_conv_b":
            a = np.ascontiguousarray(a.reshape(L, 22, 128).transpose(0, 2, 1))
        shared[n] = a
    shared.update(_consts())
    in_maps = []
    for c in range(8):
        m = dict(shared)
        m["xin"] = np.ascontiguousarray(np.concatenate([inp["x_prompt"][c], inp["x_sample"][2 * c], inp["x_sample"][2 * c + 1]], axis=0),
                                        dtype=np.float32)
        sl = slice(2 * c, 2 * c + 2)
        m["c_ckv"] = np.ascontiguousarray(inp["cache_mla_ckv"][:, sl])
        m["c_kr"] = np.ascontiguousarray(inp["cache_mla_krope"][:, sl])
        m["c_sbk"] = np.ascontiguousarray(inp["cache_sb_k"][:, sl]).reshape(L, NS, PAST, 512)
        m["c_sbv"] = np.ascontiguousarray(inp["cache_sb_v"][:, sl]).reshape(L, NS, PAST, 512)
        m["c_dk"] = np.ascontiguousarray(inp["cache_diff_k"][:, sl]).reshape(L, NS, PAST, 512)
        m["c_dv"] = np.ascontiguousarray(inp["cache_diff_v"][:, sl]).reshape(L, NS, PAST, 512)
        st = np.asarray(inp["state_ffn_conv"][:, sl], dtype=np.float32)
        m["c_conv"] = np.ascontiguousarray(st.reshape(L, NS, 2, 22, 128).transpose(0, 1, 4, 3, 2)).reshape(L, NS, 128, 44)
        in_maps.append(m)
    res = run_bass_kernel_spmd(nc, in_maps, core_ids=list(range(8))).results

    def gather(name, width):
        p = np.stack([res[c][name][:, 0:SP_] for c in range(8)], axis=1)
        s = np.stack([res[c][name][:, SP_ + SS * j:SP_ + SS * (j + 1)] for c in range(8) for j in range(NS)], axis=1)
        return p, s

    y_p = np.stack([res[c]["y"][0:SP_] for c in range(8)], axis=0)
    y_s = np.stack([res[c]["y"][SP_ + SS * j:SP_ + SS * (j + 1)] for c in range(8) for j in range(NS)], axis=0)
    p_ckv, s_ckv = gather("o_ckv", 128)
    p_kr, s_kr = gather("o_kr", 32)
    p_sbk, s_sbk = gather("o_sbk", 512)
    p_sbv, s_sbv = gather("o_sbv", 512)
    p_dk, s_dk = gather("o_dk", 512)
    p_dv, s_dv = gather("o_dv", 512)

    def conv_of(c, idx):
        oc = res[c]["o_conv"][:, idx].reshape(L, 128, 22, 2)
        return np.ascontiguousarray(oc.transpose(0, 3, 2, 1)).reshape(L, 2, DFF)
    p_conv = np.stack([conv_of(c, 0) for c in range(8)], axis=1)
    s_conv = np.stack([conv_of(c, 1 + j) for c in range(8) for j in range(NS)], axis=1)
    f = np.float32
    return (y_p.astype(f), y_s.astype(f), p_ckv.astype(f), p_kr.astype(f),
            p_sbk.reshape(L, 8, SP_, 8, 64).astype(f), p_sbv.reshape(L, 8, SP_, 8, 64).astype(f),
            p_dk.reshape(L, 8, SP_, 4, 2, 64).astype(f), p_dv.reshape(L, 8, SP_, 4, 128).astype(f), p_conv.astype(f),
            s_ckv.astype(f), s_kr.astype(f), s_sbk.reshape(L, 16, SS, 8, 64).astype(f), s_sbv.reshape(L, 16, SS, 8, 64).astype(f),
            s_dk.reshape(L, 16, SS, 4, 2, 64).astype(f), s_dv.reshape(L, 16, SS, 4, 128).astype(f), s_conv.astype(f))
```
